# Optimizing a Trainium2 kernel written in Bass

```python
import jax, jax.numpy as jnp
from jax import lax
import numpy as np

D_MODEL = 1024
BATCH = 4
SEQ = 4096
DEPTH = 4

BRANCH_W = D_MODEL // 2
N_BRANCHES = 3
POOL_WINDOWS = (2, 4, 8, 16)
POOL_GROUPS = 4
POOL_GROUP_W = BRANCH_W // POOL_GROUPS
RET_HEADS = 4
RET_HEAD_DIM = BRANCH_W // RET_HEADS
RET_CHUNK = 128
ROPE_BASE = 10000.0
RWKV_HEAD_DIM = 64
RWKV_HEADS = BRANCH_W // RWKV_HEAD_DIM
RWKV_W_LORA = 64
RWKV_A_LORA = 64
RWKV_SHIFT_W = 3 * BRANCH_W + RWKV_W_LORA + RWKV_A_LORA
IN_SPLIT_SIZES = (BRANCH_W, BRANCH_W, BRANCH_W, BRANCH_W, BRANCH_W, BRANCH_W,
                  RWKV_SHIFT_W, BRANCH_W, N_BRANCHES * D_MODEL)
D_IN = sum(IN_SPLIT_SIZES)
NORM_EPS = 1e-6
RET_NORM_EPS = 1e-5
RWKV_NORM_EPS = 64e-5

kernel_name = "hybrid_pool_retention_rwkv7_gated_parallel"


def rms_norm(x, g):
    x32 = x.astype(jnp.float32)
    y = x32 * lax.rsqrt(jnp.mean(x32 * x32, axis=-1, keepdims=True) + NORM_EPS)
    return (y * g).astype(x.dtype)


def head_norm(x, eps):
    x32 = x.astype(jnp.float32)
    mu = jnp.mean(x32, axis=-1, keepdims=True)
    xc = x32 - mu
    var = jnp.mean(xc * xc, axis=-1, keepdims=True)
    return xc * lax.rsqrt(var + eps)


def causal_pool_mixer(u, pool_w, pool_scale):
    B, T, _ = u.shape
    u32 = u.astype(jnp.float32)
    cs_pad = jnp.pad(jnp.cumsum(u32, axis=1), ((0, 0), (1, 0), (0, 0)))
    t_idx = jnp.arange(T)
    outs = []
    for gi, w in enumerate(POOL_WINDOWS):
        sl = slice(gi * POOL_GROUP_W, (gi + 1) * POOL_GROUP_W)
        c = cs_pad[..., sl]
        lower = jnp.pad(c[:, :T + 1 - w], ((0, 0), (w - 1, 0), (0, 0)))
        win_sum = c[:, 1:] - lower
        count = jnp.minimum(t_idx + 1, w).astype(jnp.float32)[None, :, None]
        outs.append(win_sum / count - u32[..., sl])
    d = jnp.stack(outs, axis=2).astype(u.dtype)
    y = jnp.einsum('btgc,gcd->btgd', d, pool_w).reshape(B, T, BRANCH_W)
    return y * pool_scale


def rotary(x, positions):
    half = x.shape[-1] // 2
    freqs = ROPE_BASE ** (-jnp.arange(half, dtype=jnp.float32) / half)
    ang = positions.astype(jnp.float32)[..., None] * freqs
    cos = jnp.cos(ang)[:, :, None, :]
    sin = jnp.sin(ang)[:, :, None, :]
    x32 = x.astype(jnp.float32)
    x1, x2 = x32[..., :half], x32[..., half:]
    return jnp.concatenate([x1 * cos - x2 * sin, x1 * sin + x2 * cos], axis=-1)


def retention(q, k, v, positions, ret_norm_g):
    dtype = q.dtype
    B, T, _ = q.shape
    H, dh, C = RET_HEADS, RET_HEAD_DIM, RET_CHUNK
    N = T // C
    qh = rotary(q.reshape(B, T, H, dh), positions)
    kh = rotary(k.reshape(B, T, H, dh), positions) * (dh ** -0.5)
    vh = v.reshape(B, T, H, dh).astype(jnp.float32)

    def to_chunks(a):
        return a.reshape(B, N, C, H, dh).transpose(0, 3, 1, 2, 4)

    qc, kc, vc = to_chunks(qh), to_chunks(kh), to_chunks(vh)
    log_gamma = jnp.log1p(-(2.0 ** (-5.0 - jnp.arange(H, dtype=jnp.float32))))
    i = jnp.arange(C, dtype=jnp.float32)
    diff = i[:, None] - i[None, :]
    decay_mask = jnp.where(diff >= 0,
                           jnp.exp(log_gamma[:, None, None] * jnp.maximum(diff, 0.0)),
                           0.0)
    scores = jnp.einsum('bhnid,bhnjd->bhnij', qc, kc) * decay_mask[None, :, None]
    o_intra = jnp.einsum('bhnij,bhnje->bhnie', scores, vc)
    k_tail = kc * jnp.exp(log_gamma[:, None] * (C - 1 - i))[None, :, None, :, None]
    kv = jnp.einsum('bhnjd,bhnje->bhnde', k_tail, vc)
    chunk_decay = jnp.exp(log_gamma * C)[None, :, None, None]

    def step(R, kv_n):
        return chunk_decay * R + kv_n, R

    _, R_prev = lax.scan(step, jnp.zeros((B, H, dh, dh), jnp.float32), jnp.moveaxis(kv, 2, 0))
    R_prev = jnp.moveaxis(R_prev, 0, 2)
    q_dec = qc * jnp.exp(log_gamma[:, None] * (i + 1.0))[None, :, None, :, None]
    o = o_intra + jnp.einsum('bhnid,bhnde->bhnie', q_dec, R_prev)
    o = o.transpose(0, 2, 3, 1, 4).reshape(B, T, H, dh)
    o = head_norm(o, RET_NORM_EPS).reshape(B, T, BRANCH_W) * ret_norm_g
    return o.astype(dtype)


def rwkv7_time_mix(p, shift_mu, w0, w2, a0, a2, k_k, k_a, r_k, ln_w, ln_b):
    dtype = p.dtype
    B, T, _ = p.shape
    H, N = RWKV_HEADS, RWKV_HEAD_DIM
    p_prev = jnp.pad(p[:, :-1], ((0, 0), (1, 0), (0, 0)))
    p = p + (p_prev - p) * shift_mu
    split_pts = (BRANCH_W, BRANCH_W + RWKV_W_LORA, 2 * BRANCH_W + RWKV_W_LORA,
                 3 * BRANCH_W + RWKV_W_LORA)
    r, wl, k, v, al = jnp.split(p, split_pts, axis=-1)
    w = -jax.nn.softplus(-(w0 + jnp.tanh(wl) @ w2)) - 0.5
    decay = jnp.exp(-jnp.exp(w.astype(jnp.float32)))
    a = jax.nn.sigmoid(a0 + al @ a2)

    def heads(z):
        return z.reshape(B, T, H, N).astype(jnp.float32)

    kk = heads(k * k_k)
    kk = kk / jnp.maximum(jnp.sqrt(jnp.sum(kk * kk, axis=-1, keepdims=True)), 1e-12)
    k = k * (1.0 + (a - 1.0) * k_a)
    r_h, k_h, v_h, a_h, w_h = heads(r), heads(k), heads(v), heads(a), heads(decay)
    xs = tuple(jnp.moveaxis(z, 1, 0) for z in (r_h, w_h, k_h, v_h, -kk, kk * a_h))

    def step(S, inp):
        r_t, w_t, k_t, v_t, a_t, b_t = inp
        sa = jnp.einsum('bhvk,bhk->bhv', S, a_t)
        S = (S * w_t[:, :, None, :] + sa[..., None] * b_t[:, :, None, :]
             + v_t[..., None] * k_t[:, :, None, :])
        return S, jnp.einsum('bhvk,bhk->bhv', S, r_t)

    _, y = lax.scan(step, jnp.zeros((B, H, N, N), jnp.float32), xs)
    y = jnp.moveaxis(y, 0, 1)
    y = head_norm(y, RWKV_NORM_EPS).reshape(B, T, BRANCH_W) * ln_w + ln_b
    bonus = jnp.sum(r_h * k_h * r_k, axis=-1, keepdims=True) * v_h
    y = y + bonus.reshape(B, T, BRANCH_W)
    return y.astype(dtype)


def hybrid_layer(x, c_act, positions, norm_g, w_ada, b_ada, w_in, pool_w, pool_scale,
                 ret_norm_g, shift_mu, w0, w2, a0, a2, k_k, k_a, r_k, ln_w, ln_b,
                 w_branch, w_out):
    B, T, D = x.shape
    mod = c_act @ w_ada + b_ada
    shift, scale, gate = jnp.split(mod[:, None, :], 3, axis=-1)
    h = rms_norm(x, norm_g) * (1.0 + scale) + shift
    proj = h @ w_in
    points = np.cumsum(IN_SPLIT_SIZES)[:-1].tolist()
    pool_u, pool_g, q, k, v, ret_g, rw, rw_g, gate_logits = jnp.split(proj, points, axis=-1)
    o_pool = causal_pool_mixer(pool_u, pool_w, pool_scale) * jax.nn.silu(pool_g)
    o_ret = retention(q, k, v, positions, ret_norm_g) * jax.nn.silu(ret_g)
    o_rwkv = rwkv7_time_mix(rw, shift_mu, w0, w2, a0, a2, k_k, k_a, r_k,
                            ln_w, ln_b) * jax.nn.silu(rw_g)
    branches = jnp.stack([o_pool, o_ret, o_rwkv], axis=2)
    y = jnp.einsum('btnw,nwd->btnd', branches, w_branch)
    gates = jax.nn.sigmoid(gate_logits.reshape(B, T, N_BRANCHES, D))
    merged = jnp.sum(gates * y, axis=2)
    return x + gate * (merged @ w_out)


def setup_inputs(seed: int = 0) -> dict:
    key = jax.random.key(seed)
    ks = jax.random.split(key, 24)
    L, D, W = DEPTH, D_MODEL, BRANCH_W
    nrm = jax.random.normal
    f32 = jnp.float32
    return {
        "x": nrm(ks[0], (BATCH, SEQ, D), f32),
        "c": nrm(ks[1], (BATCH, D), f32),
        "positions": jnp.broadcast_to(jnp.arange(SEQ, dtype=jnp.int32), (BATCH, SEQ)),
        "norm_g": 1.0 + 0.05 * nrm(ks[2], (L, D), f32),
        "w_ada": 0.5 * D ** -0.5 * nrm(ks[3], (L, D, 3 * D), f32),
        "b_ada": 0.01 * nrm(ks[4], (L, 3 * D), f32),
        "w_in": D ** -0.5 * nrm(ks[5], (L, D, D_IN), f32),
        "pool_w": POOL_GROUP_W ** -0.5 * nrm(ks[6], (L, POOL_GROUPS, POOL_GROUP_W, POOL_GROUP_W), f32),
        "pool_scale": 1.0 + 0.1 * nrm(ks[7], (L, W), f32),
        "ret_norm_g": 1.0 + 0.05 * nrm(ks[8], (L, W), f32),
        "rwkv_shift_mu": jax.random.uniform(ks[9], (L, RWKV_SHIFT_W), f32),
        "rwkv_w0": jax.random.uniform(ks[10], (L, W), f32, -6.0, 1.0),
        "rwkv_w2": 0.5 * RWKV_W_LORA ** -0.5 * nrm(ks[11], (L, RWKV_W_LORA, W), f32),
        "rwkv_a0": 0.1 * nrm(ks[12], (L, W), f32),
        "rwkv_a2": 0.5 * RWKV_A_LORA ** -0.5 * nrm(ks[13], (L, RWKV_A_LORA, W), f32),
        "rwkv_k_k": 0.85 + 0.05 * nrm(ks[14], (L, W), f32),
        "rwkv_k_a": 1.0 + 0.05 * nrm(ks[15], (L, W), f32),
        "rwkv_r_k": 0.1 * nrm(ks[16], (L, RWKV_HEADS, RWKV_HEAD_DIM), f32),
        "rwkv_ln_w": 1.0 + 0.05 * nrm(ks[17], (L, W), f32),
        "rwkv_ln_b": 0.01 * nrm(ks[18], (L, W), f32),
        "w_branch": W ** -0.5 * nrm(ks[19], (L, N_BRANCHES, W, D), f32),
        "w_out": D ** -0.5 * nrm(ks[20], (L, D, D), f32),
        "final_g": 1.0 + 0.05 * nrm(ks[21], (D,), f32),
    }


def reference(x, c, positions, norm_g, w_ada, b_ada, w_in, pool_w, pool_scale, ret_norm_g,
              rwkv_shift_mu, rwkv_w0, rwkv_w2, rwkv_a0, rwkv_a2, rwkv_k_k, rwkv_k_a,
              rwkv_r_k, rwkv_ln_w, rwkv_ln_b, w_branch, w_out, final_g):
    c_act = jax.nn.silu(c)
    for l in range(DEPTH):
        x = hybrid_layer(x, c_act, positions, norm_g[l], w_ada[l], b_ada[l], w_in[l],
                         pool_w[l], pool_scale[l], ret_norm_g[l], rwkv_shift_mu[l],
                         rwkv_w0[l], rwkv_w2[l], rwkv_a0[l], rwkv_a2[l], rwkv_k_k[l],
                         rwkv_k_a[l], rwkv_r_k[l], rwkv_ln_w[l], rwkv_ln_b[l],
                         w_branch[l], w_out[l])
    return rms_norm(x, final_g)
```

```python
import contextlib
import numpy as np
import concourse.bass as bass
import concourse.mybir as mybir
from concourse.bass_utils import run_bass_kernel_spmd

F32 = mybir.dt.float32
BF16 = mybir.dt.bfloat16
I32 = mybir.dt.int32
AF = mybir.ActivationFunctionType
ALU = mybir.AluOpType
AX = mybir.AxisListType

D = 1024
W = 512
DIN = 8320
NBLKW = 24
NV = 40
C0 = float(np.exp(-0.5))
GAM = [1.0 - 2.0 ** (-5.0 - h) for h in range(4)]
ENGS = ["pe", "dve", "act", "pool", "sp"]


class Buf:
    __slots__ = ("name", "lw", "rd")

    def __init__(self, name):
        self.name = name
        self.lw = None
        self.rd = []


class _Rec:
    def __getattr__(self, name):
        return lambda *a, **k: (name, a, k)


_REC = _Rec()


class Sched:
    def __init__(self, nc):
        self.nc = nc
        self.ops = {e: [] for e in ENGS}
        self.cnt = {}
        self.keys = []
        self.epoch = 0

    def _key(self, eng):
        return "%s_%d" % (eng, self.epoch)

    def _deps(self, reads, writes):
        deps = {}
        for b in reads:
            t = b.lw
            if t is not None and deps.get(t[0], 0) < t[1]:
                deps[t[0]] = t[1]
        for b in writes:
            t = b.lw
            if t is not None and deps.get(t[0], 0) < t[1]:
                deps[t[0]] = t[1]
            for t in b.rd:
                if deps.get(t[0], 0) < t[1]:
                    deps[t[0]] = t[1]
        return deps

    def _commit(self, tok, reads, writes):
        for b in reads:
            b.rd.append(tok)
        for b in writes:
            b.lw = tok
            b.rd = []

    def _bump(self, key, inc):
        if key not in self.cnt:
            self.cnt[key] = 0
            self.keys.append(key)
        self.cnt[key] += inc
        return (key, self.cnt[key])

    def op(self, eng, fn, reads=(), writes=()):
        deps = self._deps(reads, writes)
        key = self._key(eng)
        tok = self._bump(key, 1)
        self.ops[eng].append((fn(_REC), deps, key, 1))
        self._commit(tok, reads, writes)
        return tok

    def dma(self, eng, fn, semkey, reads=(), writes=()):
        deps = self._deps(reads, writes)
        tok = self._bump(semkey, 16)
        self.ops[eng].append((fn(_REC), deps, semkey, 16))
        self._commit(tok, reads, writes)
        return tok

    def wait_all(self, eng, bufs):
        deps = self._deps(bufs, ())
        self.ops[eng].append((None, deps, None, 0))

    def emit(self):
        nc = self.nc
        with contextlib.ExitStack() as st:
            sems = {}
            for i, k in enumerate(self.keys):
                sems[k] = st.enter_context(nc.semaphore("s%d" % i))
            block = st.enter_context(nc.Block())

            def run(engname, e):
                seen = {}
                for fn, deps, key, inc in self.ops[engname]:
                    for k, v in deps.items():
                        if engname == "pe" and k.startswith("pe_"):
                            continue
                        if seen.get(k, 0) >= v:
                            continue
                        e.wait_ge(sems[k], v)
                        seen[k] = v
                    if fn is not None:
                        getattr(e, fn[0])(*fn[1], **fn[2]).then_inc(sems[key], inc)

            @block.tensor
            def _(e):
                run("pe", e)

            @block.vector
            def _(e):
                run("dve", e)

            @block.scalar
            def _(e):
                run("act", e)

            @block.gpsimd
            def _(e):
                run("pool", e)

            @block.sync
            def _(e):
                run("sp", e)


class Tile:
    def __init__(self, nc, name, shape, dtype, psum=False):
        if psum:
            self.t = nc.alloc_psum_tensor("p_" + name, shape, dtype)
        else:
            self.t = nc.alloc_sbuf_tensor("t_" + name, shape, dtype)
        self.b = Buf(name)

    def __getitem__(self, idx):
        return self.t[idx]


def make_consts():
    c = {}
    p = np.arange(128)
    i = np.arange(128)
    mt = np.zeros((128, 4, 128), np.float64)
    dq = np.zeros((128, 4, 128), np.float64)
    kt = np.zeros((128, 4), np.float64)
    for h in range(4):
        g = GAM[h]
        diff = i[None, :] - p[:, None]
        mt[:, h, :] = np.where(diff >= 0, g ** np.maximum(diff, 0), 0.0)
        dq[:, h, :] = (g ** (i + 1.0))[None, :]
        kt[:, h] = g ** (127.0 - p)
    c["maskT"] = mt.reshape(128, 512)
    c["decq"] = dq.reshape(128, 512)
    c["ktail"] = kt
    same = (p[:, None] // 64) == (i[None, :] // 64)
    mus = (same & (i[None, :] > p[:, None])).astype(np.float64)
    mui = (same & (i[None, :] >= p[:, None])).astype(np.float64)
    mls = (same & (i[None, :] < p[:, None])).astype(np.float64)
    mu2 = np.concatenate([mus, mui], 1)
    c["MU"] = np.concatenate([mu2, mu2], 1)
    c["ML"] = np.concatenate([mls, mls], 1)
    c["onesbd"] = same.astype(np.float64)
    c["ident"] = np.eye(128)
    sm = np.ones((128, 512)); sm[:, ::64] = 0.0
    c["scanm"] = sm
    half = 64
    fr = (np.float32(10000.0) ** (-(np.arange(half, dtype=np.float32)) / np.float32(half))).astype(np.float64)
    c["freq"] = np.concatenate([fr, fr])[:, None]
    c["sgn"] = np.concatenate([-np.ones(64), np.ones(64)])[:, None]
    ic = np.zeros((128, 4, 16))
    for g, w in enumerate((2, 4, 8, 16)):
        ic[:, g, :] = (1.0 / np.minimum(np.arange(16) + 1, w))[None, :]
    c["invc"] = ic.reshape(128, 64)
    c["eps"] = np.array([1e-6, 1e-5, 64e-5, 1e-30, np.pi / 2])[None, :].repeat(128, 0)
    off = {}
    cols = []
    o = 0
    for k, v in c.items():
        off[k] = (o, v.shape[1])
        o += v.shape[1]
        cols.append(v)
    return np.concatenate(cols, 1).astype(np.float32), off


CONSTS, COFF = make_consts()
NCONST = CONSTS.shape[1]


def prep_weights(inp):
    L = inp["w_in"].shape[0]
    w_in = np.asarray(inp["w_in"], np.float32)
    wblk = np.zeros((L, NBLKW, 128, 4096), np.float32)

    def inblk(cols):
        a = w_in[:, :, cols]
        return a.reshape(L, 8, 128, len(cols)).transpose(0, 2, 1, 3)

    def put(b, cols):
        a = inblk(cols)
        n = a.shape[3]
        tmp = np.zeros((L, 128, 8, 512), np.float32)
        tmp[:, :, :, :n] = a
        wblk[:, b] = tmp.reshape(L, 128, 4096)

    ar = np.arange
    sw = np.concatenate([np.concatenate([ar(h * 128 + 64, h * 128 + 128), ar(h * 128, h * 128 + 64)]) for h in range(4)])
    put(0, ar(0, 512)); put(1, ar(512, 1024))
    put(2, ar(1024, 1536)); put(3, 1024 + sw)
    put(4, ar(1536, 2048)); put(5, 1536 + sw)
    put(6, ar(2048, 2560)); put(7, ar(2560, 3072))
    put(8, ar(3072, 3584)); put(9, ar(3648, 4160)); put(10, ar(4160, 4672))
    put(11, np.concatenate([ar(3584, 3648), ar(4672, 4736)]))
    put(12, ar(4736, 5248))
    wb = np.asarray(inp["w_branch"], np.float32)
    for n in range(3):
        wblk[:, 13 + 3 * n] = wb[:, n].reshape(L, 4, 128, 1024).transpose(0, 2, 1, 3).reshape(L, 128, 4096)
        for hf in range(2):
            put(13 + 3 * n + 1 + hf, 5248 + n * 1024 + hf * 512 + ar(512))
    wo = np.asarray(inp["w_out"], np.float32)
    for hf in range(2):
        a = wo[:, :, hf * 512:(hf + 1) * 512].reshape(L, 8, 128, 512).transpose(0, 2, 1, 3)
        wblk[:, 22 + hf] = a.reshape(L, 128, 4096)
    wsm = np.zeros((L, 128, 1536), np.float32)
    pw = np.asarray(inp["pool_w"], np.float32)
    wsm[:, :, 0:512] = pw.transpose(0, 2, 1, 3).reshape(L, 128, 512)
    wsm[:, 0:64, 512:1024] = np.asarray(inp["rwkv_w2"], np.float32)
    wsm[:, 64:128, 1024:1536] = np.asarray(inp["rwkv_a2"], np.float32)

    def fm(v):
        return np.asarray(v, np.float32).reshape(L, 4, 128).transpose(0, 2, 1)
    mu = np.asarray(inp["rwkv_shift_mu"], np.float32)
    vecs = np.zeros((L, 128, NV), np.float32)
    vecs[:, :, 0:4] = fm(inp["pool_scale"])
    vecs[:, :, 4:8] = fm(mu[:, 0:512])
    vecs[:, :, 8:12] = fm(mu[:, 576:1088])
    vecs[:, :, 12:16] = fm(mu[:, 1088:1600])
    vecs[:, 0:64, 16] = mu[:, 512:576]
    vecs[:, 64:128, 16] = mu[:, 1600:1664]
    vecs[:, :, 17:21] = fm(inp["rwkv_w0"])
    vecs[:, :, 21:25] = fm(inp["rwkv_a0"])
    vecs[:, :, 25:29] = fm(inp["rwkv_k_k"])
    vecs[:, :, 29:33] = fm(inp["rwkv_k_a"])
    vecs[:, :, 33:37] = fm(np.asarray(inp["rwkv_r_k"], np.float32).reshape(L, 512))
    rows = np.stack([np.asarray(inp["ret_norm_g"], np.float32), np.asarray(inp["rwkv_ln_w"], np.float32),
                     np.asarray(inp["rwkv_ln_b"], np.float32)], 1)
    ngb = np.concatenate([np.asarray(inp["norm_g"], np.float32), np.asarray(inp["b_ada"], np.float32)], 1)
    return dict(wblk=wblk, wsm=wsm, vecs=vecs, rows=np.ascontiguousarray(rows), ngb=np.ascontiguousarray(ngb),
                wada=np.ascontiguousarray(np.asarray(inp["w_ada"], np.float32)),
                fg=np.asarray(inp["final_g"], np.float32).reshape(1, D), consts=CONSTS)


def build(T, L, NB=2, dbg=None):
    TT = NB * 128
    NT = T // TT
    NCH = NB * 2
    CPB = 512 // TT
    nc = bass.Bass("TRN2", target_bir_lowering=False)
    S = Sched(nc)

    def dram(name, shape, dt, kind):
        return nc.dram_tensor(name, shape, dt, kind=kind).ap()
    x_in = dram("x", [T, D], F32, "ExternalInput")
    c_in = dram("c", [128, 8], F32, "ExternalInput")
    pos_in = dram("pos", [1, T], I32, "ExternalInput")
    wblk = dram("wblk", [L, NBLKW, 128, 4096], F32, "ExternalInput")
    wsm_in = dram("wsm", [L, 128, 1536], F32, "ExternalInput")
    vecs_in = dram("vecs", [L, 128, NV], F32, "ExternalInput")
    rows_in = dram("rows", [L, 3, 512], F32, "ExternalInput")
    ngb_in = dram("ngb", [L, 4096], F32, "ExternalInput")
    wada_in = dram("wada", [L, D, 3 * D], F32, "ExternalInput")
    fg_in = dram("fg", [1, D], F32, "ExternalInput")
    consts_in = dram("consts", [128, NCONST], F32, "ExternalInput")
    out = dram("out", [T, D], F32, "ExternalOutput")
    dbg_o = dram("dbg_o", [128, 20 * TT], BF16, "ExternalOutput") if dbg == "dump" else None
    b_dbg = Buf("dbg")
    dbg_f = dram("dbg_f", [128, 16384], F32, "ExternalOutput") if dbg == "dump" else None
    dstate = {"off": 0, "map": {}, "on": False}
    global DUMP_MAP
    DUMP_MAP = dstate["map"]
    dstage = None

    def dump(name, ap, rd, width):
        if dbg != "dump" or not dstate["on"]:
            return
        o = dstate["off"]
        dstate["map"][name] = (o, width)
        st_ = hn
        op("dve", lambda e: e.tensor_copy(out=st_[:, 0:width], in_=ap), rd, [st_])
        dma("sp", lambda e: e.dma_start(out=dbg_f[:, o:o + width], in_=st_[:, 0:width]), "dbgf", [st_], [b_dbg])
        dstate["off"] += width
    wsc = dram("wsc", [L, NBLKW, 128, 4096], BF16, "Internal")
    xs = dram("xs", [T, D], F32, "Internal")
    b_wsc = [Buf("wsc%d" % l) for l in range(L)]
    b_xs = [Buf("xs%d" % i) for i in range(NT)]
    b_out = [Buf("out%d" % i) for i in range(NT)]

    def tile(name, shape, dt):
        return Tile(nc, name, shape, dt)

    CT = tile("consts", [128, NCONST], F32)

    def cst(name):
        o, n = COFF[name]
        return CT[:, o:o + n]
    identb = tile("identb", [128, 128], BF16)
    onesb = tile("onesb", [128, 128], BF16)
    cbc = tile("cbc", [128, 8, 128], F32)
    cfm = tile("cfm", [128, 8], F32)
    modbc = tile("modbc", [128, 3 * D], F32)
    gbc = tile("gbc", [128, D], F32)
    rows3 = tile("rows3", [128, 3, 512], F32)
    vecs = tile("vecs", [128, NV], F32)
    wsmb = tile("wsmb", [128, 1536], BF16)
    big = tile("big", [128, 8 * TT], F32)
    bada = tile("bada", [128, TT], F32)
    xt = tile("xt", [128, NB, D], F32)
    hT = tile("hT", [128, 8, TT], BF16)
    hn = tile("hn", [128, D], F32)
    hb = tile("hb", [128, D], BF16)
    junk = hb
    wsmf = big
    st4 = tile("st4", [128, 16], F32)
    NSLOT = 3
    wslot = [tile("wslot%d" % i, [128, 4096], BF16) for i in range(NSLOT)]
    FW = TT + 16
    Ft = [tile("F%d" % i, [128, 4, FW], F32) for i in range(4)]
    NRING = 5
    ringt = [tile("ring%d" % i, [128, FW], F32) for i in range(NRING)]
    wkn = "stg dd sig ag cs csx Pin Pex Q kk sqb rn kp prb tb".split()
    wk = {n: tile("wk_" + n, [128, FW], F32) for n in wkn}
    BB = [tile("BB%d" % i, [128, 4, TT], BF16) for i in range(6)]
    artm = [tile("artm%d" % i, [128, 4, NB, 2, 128], BF16) for i in range(2)]
    Vm = [tile("Vm%d" % i, [128, 4, TT], BF16) for i in range(2)]
    opool = tile("opool", [128, 4, TT], BF16)
    oret = tile("oret", [128, 4, TT], BF16)
    orw = tile("orw", [128, 4, TT], BF16)
    puh = tile("puh", [128, 4, 16], F32)
    hal = tile("hal", [128, 16], F32)
    Rst = tile("Rst", [128, 4, 128], F32)
    Rb = tile("Rb", [128, 4, 128], BF16)
    Hst = tile("Hst", [128, 4, 64], F32)
    Hb = tile("Hb", [128, 4, 64], BF16)
    Htmp = tile("Htmp", [128, 4, 64], F32)
    ksF = tile("ksF", [128, 4, TT], F32)

    class _Alias:
        def __init__(self, t, lo, hi, shp=None):
            self.t, self.lo, self.hi, self.b, self.shp = t, lo, hi, t.b, shp

        def __getitem__(self, idx):
            v = self.t[:, self.lo:self.hi]
            if self.shp:
                v = v.rearrange("p (h i) -> p h i", i=self.shp)
            return v[idx]
    Sm = _Alias(hb, 0, 512, 128)
    lor = _Alias(hb, 512, 512 + TT)
    AX1 = tile("AX1", [128, 8, 256], BF16)
    AX2 = tile("AX2", [128, 8, 256], BF16)
    BR = [tile("BR%d" % i, [128, 8, 256], BF16) for i in range(2)]
    Am = [tile("Am%d" % i, [128, 8, 128], BF16) for i in range(2)]
    Gs = tile("Gs", [128, 512], BF16)
    Us = [tile("Us%d" % i, [128, 512], BF16) for i in range(2)]
    WCt = tile("WCt", [128, 4, NCH], F32)
    posi = tile("posi", [128, TT], I32)
    banks = [Tile(nc, "bank%d" % i, [128, 512], F32, psum=True) for i in range(8)]
    ring5t = [tile("ringw%d" % i, [128, 512], F32) for i in range(5)]
    state = {"bank": 0, "ring": 0, "ring5": 0}

    def ring5():
        r = ring5t[state["ring5"] % 5]
        state["ring5"] += 1
        return r

    def pb():
        b = banks[state["bank"] % 8]
        state["bank"] += 1
        return b

    def ring():
        r = ringt[state["ring"] % NRING]
        state["ring"] += 1
        return r

    def op(eng, fn, r, w):
        return S.op(eng, fn, [t.b if hasattr(t, "b") else t for t in r], [t.b if hasattr(t, "b") else t for t in w])

    def dma(eng, fn, key, r, w):
        return S.dma(eng, fn, key, [t.b if hasattr(t, "b") else t for t in r], [t.b if hasattr(t, "b") else t for t in w])

    def cast_layer(l):
        key = "cast%d" % l
        tok = None
        for b in range(NBLKW):
            for hf in range(2):
                tok = S.dma("pool", lambda e, l=l, b=b, hf=hf: e.dma_start(
                    out=wsc[l, b, :, hf * 2048:(hf + 1) * 2048], in_=wblk[l, b, :, hf * 2048:(hf + 1) * 2048]), key)
        b_wsc[l].lw = tok

    order = [0, 1, 2, 3, 4, 5, 6, 7, 11, 8, 9, 10, 12] + list(range(13, 24))
    seq = [(l, b) for l in range(L) for _ in range(NT) for b in order]
    wstate = {"next_load": 0, "next_get": 0}
    wreleased = [False] * len(seq)

    def w_pump():
        while wstate["next_load"] < len(seq):
            n = wstate["next_load"]
            if n >= NSLOT and not wreleased[n - NSLOT]:
                break
            l, b = seq[n]
            s = n % NSLOT
            if b == 11:
                dma("sp", lambda e, l=l, b=b, s=s: e.dma_start(
                    out=wslot[s][:].rearrange("p (k c) -> p k c", c=512)[:, :, 0:128],
                    in_=wsc[l, b].rearrange("p (k c) -> p k c", c=512)[:, :, 0:128]), "ws%d" % s, [b_wsc[l]], [wslot[s]])
            else:
                dma("sp", lambda e, l=l, b=b, s=s: e.dma_start(out=wslot[s][:], in_=wsc[l, b]), "ws%d" % s, [b_wsc[l]], [wslot[s]])
            wstate["next_load"] += 1

    def wget(l, b):
        n = wstate["next_get"]
        assert seq[n] == (l, b), (seq[n], l, b)
        w_pump()
        assert wstate["next_load"] > n, "weight block not loadable (too many held)"
        wstate["next_get"] += 1
        wstate["last"] = n
        t_ = wslot[n % NSLOT]
        t_.widx = n
        return t_

    def wrel(*tiles):
        for t_ in tiles:
            wreleased[t_.widx] = True
        w_pump()

    def w8(wt):
        return wt[:].rearrange("p (k c) -> p k c", c=512)

    dma("sp", lambda e: e.dma_start(out=CT[:], in_=consts_in), "ld_c", [], [CT])
    dma("sp", lambda e: e.dma_start(out=cfm[:], in_=c_in), "ld_c2", [], [cfm])
    cast_layer(0)
    op("act", lambda e: e.activation(out=identb[:], in_=cst("ident"), func=AF.Copy), [CT], [identb])
    op("act", lambda e: e.activation(out=onesb[:], in_=cst("onesbd"), func=AF.Copy), [CT], [onesb])
    op("act", lambda e: e.activation(out=cfm[:], in_=cfm[:], func=AF.Silu), [cfm], [cfm])
    for k in range(8):
        op("dve", lambda e, k=k: e.tensor_scalar(out=cbc[:, k, :], in0=cst("ident"), scalar1=0.0, scalar2=cfm[:, k:k + 1],
                                                 op0=ALU.mult, op1=ALU.add), [CT, cfm], [cbc])
    for t_ in (artm[0], artm[1], Vm[0], Vm[1], Us[0], Us[1], Gs):
        op("pool", lambda e, t_=t_: e.memset(t_[:], 0.0), [], [t_])
    eps_n = cst("eps")[:, 0:1]
    eps_r = cst("eps")[:, 1:2]
    eps_w = cst("eps")[:, 2:3]
    eps_k = cst("eps")[:, 3:4]

    def fm_proj(wt, nch, evac, ncols=128, k_parts=8):
        c = 0
        while c < nch:
            n = min(CPB, nch - c)
            bk = pb()
            for i in range(n):
                for k in range(8):
                    op("pe", lambda e, bk=bk, i=i, k=k, cc=c + i: e.matmul(
                        bk[0:ncols, i * TT:(i + 1) * TT], lhsT=w8(wt)[:, k, cc * 128:cc * 128 + ncols], rhs=hT[:, k, :],
                        start=(k == 0), stop=(k == 7)), [wt, hT], [bk])
            evac(bk, c, n)
            c += n

    def bview(bk, n):
        return bk[:, 0:n * TT].rearrange("p (n t) -> p n t", t=TT)

    for l in range(L):
        S.epoch = l
        if l + 1 < L:
            cast_layer(l + 1)
        dma("sp", lambda e, l=l: e.dma_start(out=vecs[:], in_=vecs_in[l]), "ld_v", [], [vecs])
        dma("sp", lambda e, l=l: e.dma_start(out=wsmf[:, 0:1536], in_=wsm_in[l]), "ld_w", [], [wsmf])
        dma("sp", lambda e, l=l: e.dma_start(out=rows3[:].rearrange("p a c -> p (a c)"),
                                             in_=rows_in[l].rearrange("a c -> (a c)").partition_broadcast(128)), "ld_r", [], [rows3])
        dma("sp", lambda e, l=l: e.dma_start(out=gbc[:], in_=ngb_in[l, 0:D].partition_broadcast(128)), "ld_g", [], [gbc])
        op("act", lambda e: e.activation(out=wsmb[:], in_=wsmf[:, 0:1536], func=AF.Copy), [wsmf], [wsmb])
        NWB = 3 * D // TT
        for nb in range(NWB):
            dma("sp", lambda e, l=l, nb=nb: e.dma_start(
                out=big[:].rearrange("p (k c) -> p k c", c=TT),
                in_=wada_in[l, :, nb * TT:(nb + 1) * TT].rearrange("(k p) c -> p k c", p=128)), "ld_a", [], [big])
            dma("sp", lambda e, l=l, nb=nb: e.dma_start(
                out=bada[:, 0:TT], in_=ngb_in[l, D + nb * TT:D + (nb + 1) * TT].partition_broadcast(128)), "ld_b", [], [bada])
            bk = pb()
            for k in range(8):
                op("pe", lambda e, bk=bk, k=k: e.matmul(bk[:, 0:TT], lhsT=cbc[:, k, :], rhs=big[:, k * TT:(k + 1) * TT],
                                                        start=(k == 0), stop=(k == 7)), [cbc, big], [bk])
            op("dve", lambda e, bk=bk, nb=nb: e.tensor_tensor(out=modbc[:, nb * TT:(nb + 1) * TT], in0=bk[:, 0:TT], in1=bada[:, 0:TT],
                                                              op=ALU.add), [bk, bada], [modbc])
        op("dve", lambda e: e.scalar_tensor_tensor(out=modbc[:, D:2 * D], in0=modbc[:, D:2 * D], scalar=1.0, in1=gbc[:],
                                                   op0=ALU.add, op1=ALU.mult), [modbc, gbc], [modbc])
        B_bc = modbc[:, 0:D]
        A_bc = modbc[:, D:2 * D]
        G_bc = modbc[:, 2 * D:3 * D]
        op("pool", lambda e: e.memset(puh[:], 0.0), [], [puh])
        op("pool", lambda e: e.memset(hal[:], 0.0), [], [hal])
        op("pool", lambda e: e.memset(Rst[:], 0.0), [], [Rst])
        op("pool", lambda e: e.memset(Rb[:], 0.0), [], [Rb])
        op("pool", lambda e: e.memset(Hst[:], 0.0), [], [Hst])
        op("pool", lambda e: e.memset(Hb[:], 0.0), [], [Hb])
        poolw = wsmb[:, 0:512].rearrange("p (g d) -> p g d", d=128)
        lw2 = wsmb[:, 512:1024]
        la2 = wsmb[:, 1024:1536]

        def V(i, n=4):
            return vecs[:, i:i + n]

        for it in range(NT):
            t0 = it * TT
            src = x_in if l == 0 else xs
            rd = [] if l == 0 else [b_xs[it]]
            dma("sp", lambda e, src=src, t0=t0: e.dma_start(out=xt[:], in_=src[t0:t0 + TT, :].rearrange("(j p) d -> p j d", p=128)),
                "ld_x", rd, [xt])
            if dbg == "pro":
                S.emit()
                return nc
            dstate["on"] = (l == 0 and it == NT - 1)
            for j in range(NB):
                op("act", lambda e, j=j: e.activation(out=junk[:], in_=xt[:, j, :], func=AF.Square, accum_out=st4[:, j:j + 1]),
                   [xt], [junk, st4])
            op("act", lambda e: e.activation(out=st4[:, 4:4 + NB], in_=st4[:, 0:NB], func=AF.Sqrt, bias=eps_n, scale=1.0 / D),
               [st4, CT], [st4])
            op("dve", lambda e: e.reciprocal(out=st4[:, 8:8 + NB], in_=st4[:, 4:4 + NB]), [st4], [st4])
            for j in range(NB):
                op("dve", lambda e, j=j: e.scalar_tensor_tensor(out=hn[:], in0=xt[:, j, :], scalar=st4[:, 8 + j:9 + j], in1=A_bc,
                                                                op0=ALU.mult, op1=ALU.mult), [xt, st4, modbc], [hn])
                op("pool", lambda e: e.tensor_tensor(out=hb[:], in0=hn[:], in1=B_bc, op=ALU.add), [hn, modbc], [hb])
                bk = pb()
                bkb = bk[:].bitcast(BF16)
                for k in range(8):
                    op("pe", lambda e, bkb=bkb, k=k: e.transpose(out=bkb[:, k * 128:(k + 1) * 128], in_=hb[:, k * 128:(k + 1) * 128],
                                                                  identity=identb[:]), [hb, identb], [bk])
                op("act", lambda e, bkb=bkb, j=j: e.activation(out=hT[:, :, j * 128:(j + 1) * 128],
                                                               in_=bkb.rearrange("p (k t) -> p k t", t=128), func=AF.Copy), [bk], [hT])
            if dbg == "N":
                S.emit()
                return nc
            pu, sa, sb, sg = Ft
            op("pool", lambda e: e.tensor_copy(out=pu[:, :, 0:16], in_=puh[:]), [puh], [pu])
            wt = wget(l, 0)
            fm_proj(wt, 4, lambda bk, c, n: op("act", lambda e: e.activation(out=pu[:, c:c + n, 16:16 + TT], in_=bview(bk, n), func=AF.Copy),
                                               [bk], [pu]))
            wrel(wt)
            wt = wget(l, 1)
            fm_proj(wt, 4, lambda bk, c, n: op("act", lambda e: e.activation(out=sg[:, c:c + n, 0:TT], in_=bview(bk, n), func=AF.Silu),
                                               [bk], [sg]))
            wrel(wt)
            op("pool", lambda e: e.tensor_copy(out=puh[:], in_=pu[:, :, TT:TT + 16]), [pu], [puh])
            for g in range(4):
                cur = pu
                for m in range(g + 1):
                    sh = 1 << m
                    lo = (1 << (m + 1)) - 1
                    dst = sa if (m % 2 == 0) else sb
                    op("dve", lambda e, g=g, cur=cur, dst=dst, sh=sh, lo=lo: e.tensor_tensor(
                        out=dst[:, g, lo:FW], in0=cur[:, g, lo:FW], in1=cur[:, g, lo - sh:FW - sh], op=ALU.add), [cur], [dst])
                    cur = dst
                wg = float(1 << (g + 1))
                dpool = BB[0]
                op("dve", lambda e, g=g, cur=cur, wg=wg: e.scalar_tensor_tensor(
                    out=dpool[:, g, :], in0=cur[:, g, 16:FW], scalar=1.0 / wg, in1=pu[:, g, 16:FW], op0=ALU.mult, op1=ALU.subtract),
                    [cur, pu], [dpool])
                if it == 0:
                    r1 = ring()
                    op("dve", lambda e, g=g, cur=cur, r1=r1: e.tensor_tensor(out=r1[:, 0:16], in0=cur[:, g, 16:32],
                                                                             in1=cst("invc")[:, g * 16:(g + 1) * 16], op=ALU.mult), [cur, CT], [r1])
                    op("dve", lambda e, g=g, r1=r1: e.tensor_tensor(out=dpool[:, g, 0:16], in0=r1[:, 0:16], in1=pu[:, g, 16:32],
                                                                    op=ALU.subtract), [r1, pu], [dpool])
            for g in range(4):
                bk = pb()
                op("pe", lambda e, bk=bk, g=g: e.matmul(bk[:, 0:TT], lhsT=poolw[:, g, :], rhs=BB[0][:, g, :], start=True, stop=True),
                   [wsmb, BB[0]], [bk])
                op("dve", lambda e, bk=bk, g=g: e.scalar_tensor_tensor(out=opool[:, g, :], in0=bk[:, 0:TT], scalar=V(0)[:, g:g + 1],
                                                                        in1=sg[:, g, 0:TT], op0=ALU.mult, op1=ALU.mult), [bk, vecs, sg], [opool])
            if dbg == "P":
                S.emit()
                return nc
            tabs = Ft[0]
            sret = Ft[1]
            qr, qd, kr, ktm, vtm = BB[0], BB[1], BB[2], BB[3], BB[4]
            dma("sp", lambda e, t0=t0: e.dma_start(out=posi[:], in_=pos_in[0, t0:t0 + TT].partition_broadcast(128)), "ld_p", [], [posi])
            ang, nn, r2 = ring(), ring(), ring()
            op("dve", lambda e: e.tensor_copy(out=ang[:, 0:TT], in_=posi[:]), [posi], [ang])
            op("dve", lambda e: e.tensor_scalar(out=ang[:, 0:TT], in0=ang[:, 0:TT], scalar1=cst("freq"), scalar2=None, op0=ALU.mult),
               [ang, CT], [ang])
            op("dve", lambda e: e.tensor_scalar(out=nn[:, 0:TT], in0=ang[:, 0:TT], scalar1=float(1.0 / (2 * np.pi)), scalar2=None,
                                                op0=ALU.mult), [ang], [nn])
            op("dve", lambda e: e.tensor_copy(out=posi[:], in_=nn[:, 0:TT]), [nn], [posi])
            op("dve", lambda e: e.tensor_copy(out=nn[:, 0:TT], in_=posi[:]), [posi], [nn])
            c1 = 6.28125
            c2 = float(2 * np.pi - 6.28125)
            op("dve", lambda e: e.scalar_tensor_tensor(out=r2[:, 0:TT], in0=nn[:, 0:TT], scalar=-c1, in1=ang[:, 0:TT],
                                                       op0=ALU.mult, op1=ALU.add), [nn, ang], [r2])
            op("dve", lambda e: e.scalar_tensor_tensor(out=r2[:, 0:TT], in0=nn[:, 0:TT], scalar=-c2, in1=r2[:, 0:TT],
                                                       op0=ALU.mult, op1=ALU.add), [nn, r2], [r2])
            ws_, wc_, mk_ = ring(), ring(), ring()
            twopi = float(2 * np.pi)
            op("dve", lambda e: e.tensor_scalar(out=mk_[:, 0:TT], in0=r2[:, 0:TT], scalar1=0.0, scalar2=float(np.pi), op0=ALU.add, op1=ALU.is_gt),
               [r2], [mk_])
            op("dve", lambda e: e.scalar_tensor_tensor(out=ws_[:, 0:TT], in0=mk_[:, 0:TT], scalar=-twopi, in1=r2[:, 0:TT], op0=ALU.mult, op1=ALU.add),
               [mk_, r2], [ws_])
            op("dve", lambda e: e.tensor_scalar(out=mk_[:, 0:TT], in0=r2[:, 0:TT], scalar1=float(np.pi / 2), scalar2=float(np.pi), op0=ALU.add,
                                                op1=ALU.is_gt), [r2], [mk_])
            op("dve", lambda e: e.scalar_tensor_tensor(out=wc_[:, 0:TT], in0=mk_[:, 0:TT], scalar=-twopi, in1=r2[:, 0:TT], op0=ALU.mult, op1=ALU.add),
               [mk_, r2], [wc_])
            op("act", lambda e: e.activation(out=tabs[:, 0, 0:TT], in_=wc_[:, 0:TT], func=AF.Sin, bias=cst("eps")[:, 4:5]), [wc_, CT], [tabs])
            op("act", lambda e: e.activation(out=tabs[:, 1, 0:TT], in_=ws_[:, 0:TT], func=AF.Sin, scale=cst("sgn")), [ws_, CT], [tabs])
            op("pool", lambda e: e.tensor_scalar(out=tabs[:, 2:4, 0:TT], in0=tabs[:, 0:2, 0:TT], scalar1=float(128.0 ** -0.5), scalar2=None,
                                                 op0=ALU.mult), [tabs], [tabs])

            def rope(dst, dq, tc):
                wa = wget(l, 2 if dst is qr else 4)
                wb_ = wget(l, 3 if dst is qr else 5)
                for h in range(4):
                    ba, bb = pb(), pb()
                    for (bk_, wt_) in ((ba, wa), (bb, wb_)):
                        for k in range(8):
                            op("pe", lambda e, bk_=bk_, wt_=wt_, k=k, h=h: e.matmul(
                                bk_[:, 0:TT], lhsT=w8(wt_)[:, k, h * 128:(h + 1) * 128], rhs=hT[:, k, :], start=(k == 0), stop=(k == 7)),
                                [wt_, hT], [bk_])
                    t1, t2 = ring(), ring()
                    op("dve", lambda e, ba=ba, t1=t1: e.tensor_tensor(out=t1[:, 0:TT], in0=ba[:, 0:TT], in1=tabs[:, tc, 0:TT], op=ALU.mult),
                       [ba, tabs], [t1])
                    op("dve", lambda e, bb=bb, t2=t2: e.tensor_tensor(out=t2[:, 0:TT], in0=bb[:, 0:TT], in1=tabs[:, tc + 1, 0:TT], op=ALU.mult),
                       [bb, tabs], [t2])
                    if dq is None:
                        op("pool", lambda e, t1=t1, t2=t2, h=h: e.tensor_tensor(out=dst[:, h, :], in0=t1[:, 0:TT], in1=t2[:, 0:TT], op=ALU.add),
                           [t1, t2], [dst])
                    else:
                        op("pool", lambda e, t1=t1, t2=t2: e.tensor_tensor(out=t1[:, 0:TT], in0=t1[:, 0:TT], in1=t2[:, 0:TT], op=ALU.add),
                           [t1, t2], [t1])
                        op("act", lambda e, t1=t1, h=h: e.activation(out=dst[:, h, :], in_=t1[:, 0:TT], func=AF.Copy), [t1], [dst])
                        op("pool", lambda e, t1=t1, h=h: e.tensor_tensor(
                            out=dq[:, h, :].rearrange("p (j i) -> p j i", i=128), in0=t1[:, 0:TT].rearrange("p (j i) -> p j i", i=128),
                            in1=cst("decq")[:, h * 128:(h + 1) * 128].unsqueeze(1).broadcast_to([128, NB, 128]), op=ALU.mult), [t1, CT], [dq])
                wrel(wa, wb_)
            dump("tabs", tabs[:, :, 0:TT], [tabs], 4 * TT) if False else None
            for i_ in range(4):
                dump("tab%d" % i_, tabs[:, i_, 0:TT], [tabs], TT)
            rope(qr, qd, 0)
            rope(kr, None, 2)
            dump("qr", qr[:].rearrange("p c t -> p (c t)"), [qr], 4 * TT)
            dump("qd", qd[:].rearrange("p c t -> p (c t)"), [qd], 4 * TT)
            dump("kr", kr[:].rearrange("p c t -> p (c t)"), [kr], 4 * TT)
            wt = wget(l, 6)
            for j in range(NB):
                bk = pb()
                for k in range(8):
                    op("pe", lambda e, bk=bk, k=k, j=j, wt=wt: e.matmul(bk[:, :], lhsT=hT[:, k, j * 128:(j + 1) * 128], rhs=w8(wt)[:, k, :],
                                                                      start=(k == 0), stop=(k == 7)), [wt, hT], [bk])
                op("act", lambda e, bk=bk, j=j: e.activation(out=vtm[:, :, j * 128:(j + 1) * 128],
                                                             in_=bk[:, :].rearrange("p (h e) -> p h e", e=128), func=AF.Copy), [bk], [vtm])
            wrel(wt)
            wt = wget(l, 7)
            fm_proj(wt, 4, lambda bk, c, n: op("act", lambda e: e.activation(out=sret[:, c:c + n, 0:TT], in_=bview(bk, n), func=AF.Silu),
                                               [bk], [sret]))
            wrel(wt)
            for j in range(NB):
                js = slice(j * 128, (j + 1) * 128)
                bk = pb()
                bkb = bk[:].bitcast(BF16)
                for h in range(4):
                    op("pe", lambda e, bkb=bkb, h=h, js=js: e.transpose(out=bkb[:, h * 128:(h + 1) * 128], in_=kr[:, h, js], identity=identb[:]),
                       [kr, identb], [bk])
                op("dve", lambda e, bkb=bkb, js=js: e.tensor_tensor(
                    out=ktm[:, :, js], in0=bkb[:, 0:512].rearrange("p (h d) -> p h d", d=128),
                    in1=cst("ktail").unsqueeze(2).broadcast_to([128, 4, 128]), op=ALU.mult), [bk, CT], [ktm])
                bs = pb()
                for h in range(4):
                    op("pe", lambda e, bs=bs, h=h, js=js: e.matmul(bs[:, h * 128:(h + 1) * 128], lhsT=kr[:, h, js], rhs=qr[:, h, js],
                                                                   start=True, stop=True), [kr, qr], [bs])
                op("dve", lambda e, bs=bs: e.tensor_tensor(out=Sm[:].rearrange("p h i -> p (h i)"), in0=bs[:, :], in1=cst("maskT"), op=ALU.mult),
                   [bs, CT], [Sm])
                bo = pb()
                for h in range(4):
                    op("pe", lambda e, bo=bo, h=h, js=js: e.matmul(bo[:, h * 128:(h + 1) * 128], lhsT=Sm[:, h, :], rhs=vtm[:, h, js],
                                                                   start=True, stop=False), [Sm, vtm], [bo])
                    op("pe", lambda e, bo=bo, h=h, js=js: e.matmul(bo[:, h * 128:(h + 1) * 128], lhsT=qd[:, h, js], rhs=Rb[:, h, :],
                                                                   start=False, stop=True), [qd, Rb], [bo])
                bkv = pb()
                for h in range(4):
                    op("pe", lambda e, bkv=bkv, h=h, js=js: e.matmul(bkv[:, h * 128:(h + 1) * 128], lhsT=ktm[:, h, js], rhs=vtm[:, h, js],
                                                                     start=True, stop=True), [ktm, vtm], [bkv])
                for h in range(4):
                    op("dve", lambda e, bkv=bkv, h=h: e.scalar_tensor_tensor(out=Rst[:, h, :], in0=Rst[:, h, :], scalar=float(GAM[h] ** 128),
                                                                              in1=bkv[:, h * 128:(h + 1) * 128], op0=ALU.mult, op1=ALU.add),
                       [Rst, bkv], [Rst])
                op("act", lambda e: e.activation(out=Rb[:], in_=Rst[:], func=AF.Copy), [Rst], [Rb])
                osb, sq, on = ring5(), ring5(), ring5()
                op("act", lambda e, bo=bo, osb=osb: e.activation(out=osb[:, 0:512], in_=bo[:, :], func=AF.Copy), [bo], [osb])
                op("act", lambda e, bo=bo, sq=sq: e.activation(out=sq[:, 0:512], in_=bo[:, :], func=AF.Square), [bo], [sq])
                if j == NB - 1:
                    dump("osb", osb[:, 0:512], [osb], 512)
                    dump("vtm", vtm[:].rearrange("p c t -> p (c t)"), [vtm], 4 * TT)
                    dump("ktm", ktm[:].rearrange("p c t -> p (c t)"), [ktm], 4 * TT)
                    dump("Sm", Sm[:].rearrange("p c t -> p (c t)"), [Sm], 512)
                head_norm(op, st4, osb, sq, on, 4, 128, eps_r, CT)
                if j == NB - 1:
                    dump("on", on[:, 0:512], [on], 512)
                op("dve", lambda e, on=on: e.tensor_tensor(out=on[:, 0:512], in0=on[:, 0:512], in1=rows3[:, 0, :], op=ALU.mult), [on, rows3], [on])
                bt_ = pb()
                for h in range(4):
                    op("pe", lambda e, bt_=bt_, h=h, on=on: e.transpose(out=bt_[:, h * 128:(h + 1) * 128], in_=on[:, h * 128:(h + 1) * 128],
                                                                        identity=cst("ident")), [on, CT], [bt_])
                op("dve", lambda e, bt_=bt_, js=js: e.tensor_tensor(out=oret[:, :, js], in0=bt_[:, :].rearrange("p (h t) -> p h t", t=128),
                                                                    in1=sret[:, :, js], op=ALU.mult), [bt_, sret], [oret])
            if dbg == "R":
                S.emit()
                return nc
            bon = Ft[0]
            srw = Ft[1]
            vsf = Ft[2]
            btt, ktt, vbb, Vtm, btm, ktm2 = BB
            stg_i = [0]

            rsF = Ft[3]

            class CV:
                def __init__(self, t, c):
                    self.t, self.c, self.b = t, c, t.b

                def __getitem__(self, idx):
                    return self.t[idx[0], self.c, idx[1]]

            def shifted(wt_, cc, idx, dst_ap, dst_t):
                bk = pb()
                for k in range(8):
                    op("pe", lambda e, bk=bk, k=k: e.matmul(bk[:, 0:TT], lhsT=w8(wt_)[:, k, cc * 128:(cc + 1) * 128], rhs=hT[:, k, :],
                                                            start=(k == 0), stop=(k == 7)), [wt_, hT], [bk])
                stg, dd = wk["stg"], wk["dd"]
                op("act", lambda e: e.activation(out=stg[:, 1:1 + TT], in_=bk[:, 0:TT], func=AF.Copy), [bk], [stg])
                op("pool", lambda e: e.tensor_copy(out=stg[:, 0:1], in_=hal[:, idx:idx + 1]), [hal], [stg])
                op("pool", lambda e: e.tensor_copy(out=hal[:, idx:idx + 1], in_=stg[:, TT:TT + 1]), [stg], [hal])
                op("dve", lambda e: e.tensor_tensor(out=dd[:, 0:TT], in0=stg[:, 0:TT], in1=stg[:, 1:1 + TT], op=ALU.subtract), [stg], [dd])
                mu_i = (4 + idx) if idx < 12 else 16
                op("dve", lambda e: e.scalar_tensor_tensor(out=dst_ap, in0=dd[:, 0:TT], scalar=vecs[:, mu_i:mu_i + 1], in1=stg[:, 1:1 + TT],
                                                           op0=ALU.mult, op1=ALU.add), [dd, vecs, stg], [dst_t])
            wl_ = wget(l, 11)
            lot = wk["sig"]
            shifted(wl_, 0, 12, lot[:, 0:TT], lot)
            wrel(wl_)
            op("act", lambda e: e.activation(out=lor[0:64, :], in_=lot[0:64, 0:TT], func=AF.Tanh), [lot], [lor])
            op("act", lambda e: e.activation(out=lor[64:128, :], in_=lot[64:128, 0:TT], func=AF.Copy), [lot], [lor])
            wr_ = wget(l, 8)
            for c in range(4):
                shifted(wr_, c, c, rsF[:, c, 0:TT], rsF)
            wrel(wr_)
            wk_ = wget(l, 9)
            for c in range(4):
                shifted(wk_, c, 4 + c, ksF[:, c, 0:TT], ksF)
            wrel(wk_)
            wv_ = wget(l, 10)
            for c in range(4):
                shifted(wv_, c, 8 + c, vsf[:, c, 0:TT], vsf)
                op("pool", lambda e, c=c: e.tensor_copy(out=vbb[:, c, :], in_=vsf[:, c, 0:TT]), [vsf], [vbb])
            wrel(wv_)
            for c in range(4):
                rs_, ks_ = CV(rsF, c), CV(ksF, c)
                bw, ba_ = pb(), pb()
                op("pe", lambda e, bw=bw, c=c: e.matmul(bw[:, 0:TT], lhsT=lw2[:, c * 128:(c + 1) * 128], rhs=lor[:, :], start=True, stop=True),
                   [wsmb, lor], [bw])
                op("pe", lambda e, ba_=ba_, c=c: e.matmul(ba_[:, 0:TT], lhsT=la2[:, c * 128:(c + 1) * 128], rhs=lor[:, :],
                                                          start=True, stop=True), [wsmb, lor], [ba_])
                sig, ag = wk["sig"], wk["ag"]
                op("act", lambda e, bw=bw, sig=sig, c=c: e.activation(out=sig[:, 0:TT], in_=bw[:, 0:TT], func=AF.Sigmoid, bias=V(17)[:, c:c + 1]),
                   [bw, vecs], [sig])
                op("act", lambda e, ba_=ba_, ag=ag, c=c: e.activation(out=ag[:, 0:TT], in_=ba_[:, 0:TT], func=AF.Sigmoid, bias=V(21)[:, c:c + 1]),
                   [ba_, vecs], [ag])
                cs, csx = wk["cs"], wk["csx"]
                op("dve", lambda e, sig=sig, cs=cs: e.tensor_tensor_scan(out=cs[:, 0:TT], data0=cst("scanm")[:, 0:TT], data1=sig[:, 0:TT],
                                                                         initial=0.0, op0=ALU.mult, op1=ALU.add), [sig, CT], [cs])
                op("pool", lambda e, sig=sig, cs=cs, csx=csx: e.tensor_tensor(out=csx[:, 0:TT], in0=cs[:, 0:TT], in1=sig[:, 0:TT], op=ALU.subtract),
                   [cs, sig], [csx])
                Pin, Pex, Q = wk["Pin"], wk["Pex"], wk["Q"]
                op("act", lambda e, cs=cs, Pin=Pin: e.activation(out=Pin[:, 0:TT], in_=cs[:, 0:TT], func=AF.Exp, scale=-C0), [cs], [Pin])
                op("act", lambda e, csx=csx, Pex=Pex: e.activation(out=Pex[:, 0:TT], in_=csx[:, 0:TT], func=AF.Exp, scale=-C0), [csx], [Pex])
                op("act", lambda e, cs=cs, Q=Q: e.activation(out=Q[:, 0:TT], in_=cs[:, 0:TT], func=AF.Exp, scale=C0), [cs], [Q])
                op("pool", lambda e, Pin=Pin, c=c: e.tensor_copy(out=WCt[:, c, :], in_=Pin[:, 63:TT:64]), [Pin], [WCt])
                kk, sqb, rn = wk["kk"], wk["sqb"], wk["rn"]
                op("dve", lambda e, ks_=ks_, kk=kk, c=c: e.tensor_scalar(out=kk[:, 0:TT], in0=ks_[:, 0:TT], scalar1=V(25)[:, c:c + 1], scalar2=None,
                                                                         op0=ALU.mult), [ks_, vecs], [kk])
                sqv = sqb[:, 0:TT // 2].bitcast(BF16)
                op("act", lambda e, kk=kk, sqv=sqv: e.activation(out=sqv, in_=kk[:, 0:TT], func=AF.Square), [kk], [sqb])
                bn_ = pb()
                op("pe", lambda e, bn_=bn_, sqv=sqv: e.matmul(bn_[:, 0:TT], lhsT=onesb[:], rhs=sqv, start=True, stop=True), [onesb, sqb], [bn_])
                op("act", lambda e, bn_=bn_, rn=rn: e.activation(out=rn[:, 0:TT], in_=bn_[:, 0:TT], func=AF.Sqrt, bias=eps_k), [bn_, CT], [rn])
                op("dve", lambda e, rn=rn: e.reciprocal(out=rn[:, 0:TT], in_=rn[:, 0:TT]), [rn], [rn])
                op("dve", lambda e, kk=kk, rn=rn: e.tensor_tensor(out=kk[:, 0:TT], in0=kk[:, 0:TT], in1=rn[:, 0:TT], op=ALU.mult), [kk, rn], [kk])
                kp = wk["kp"]
                op("pool", lambda e, ag=ag, kp=kp, c=c: e.tensor_scalar(out=kp[:, 0:TT], in0=ag[:, 0:TT], scalar1=-1.0, scalar2=V(29)[:, c:c + 1],
                                                                        op0=ALU.add, op1=ALU.mult), [ag, vecs], [kp])
                op("dve", lambda e, kp=kp, ks_=ks_: e.scalar_tensor_tensor(out=kp[:, 0:TT], in0=kp[:, 0:TT], scalar=1.0, in1=ks_[:, 0:TT],
                                                                            op0=ALU.add, op1=ALU.mult), [kp, ks_], [kp])
                prb = wk["prb"]
                prv = prb[:, 0:TT // 2].bitcast(BF16)
                op("dve", lambda e, rs_=rs_, kp=kp, prv=prv, c=c: e.scalar_tensor_tensor(out=prv, in0=rs_[:, 0:TT], scalar=V(33)[:, c:c + 1],
                                                                                        in1=kp[:, 0:TT], op0=ALU.mult, op1=ALU.mult),
                   [rs_, vecs, kp], [prb])
                bb_ = pb()
                op("pe", lambda e, bb_=bb_, prv=prv: e.matmul(bb_[:, 0:TT], lhsT=onesb[:], rhs=prv, start=True, stop=True), [onesb, prb], [bb_])
                op("dve", lambda e, bb_=bb_, c=c: e.tensor_tensor(out=bon[:, c, 0:TT], in0=bb_[:, 0:TT], in1=vsf[:, c, 0:TT], op=ALU.mult),
                   [bb_, vsf], [bon])
                for hp in range(2):
                    rw_ = slice(hp * 64, hp * 64 + 64)
                    op("dve", lambda e, kk=kk, Pex=Pex, hp=hp, rw_=rw_, c=c: e.scalar_tensor_tensor(
                        out=artm[hp][rw_, c, :, 0, :], in0=kk[rw_, 0:TT].rearrange("p (j i) -> p j i", i=128), scalar=-1.0,
                        in1=Pex[rw_, 0:TT].rearrange("p (j i) -> p j i", i=128), op0=ALU.mult, op1=ALU.mult), [kk, Pex], [artm[hp]])
                    op("pool", lambda e, rs_=rs_, Pin=Pin, hp=hp, rw_=rw_, c=c: e.tensor_tensor(
                        out=artm[hp][rw_, c, :, 1, :], in0=rs_[rw_, 0:TT].rearrange("p (j i) -> p j i", i=128),
                        in1=Pin[rw_, 0:TT].rearrange("p (j i) -> p j i", i=128), op=ALU.mult), [rs_, Pin], [artm[hp]])
                tb = wk["tb"]
                op("pool", lambda e, kk=kk, ag=ag, tb=tb: e.tensor_tensor(out=tb[:, 0:TT], in0=kk[:, 0:TT], in1=ag[:, 0:TT], op=ALU.mult),
                   [kk, ag], [tb])
                op("dve", lambda e, tb=tb, Q=Q, c=c: e.tensor_tensor(out=btt[:, c, :], in0=tb[:, 0:TT], in1=Q[:, 0:TT], op=ALU.mult),
                   [tb, Q], [btt])
                op("pool", lambda e, kp=kp, Q=Q, c=c: e.tensor_tensor(out=ktt[:, c, :], in0=kp[:, 0:TT], in1=Q[:, 0:TT], op=ALU.mult),
                   [kp, Q], [ktt])
            wg_ = wget(l, 12)
            fm_proj(wg_, 4, lambda bk, c, n: op("act", lambda e: e.activation(out=srw[:, c:c + n, 0:TT], in_=bview(bk, n), func=AF.Silu),
                                                [bk], [srw]))
            wrel(wg_)
            for j in range(NB):
                js = slice(j * 128, (j + 1) * 128)
                for (srcT, dstT) in ((vbb, Vtm), (btt, btm), (ktt, ktm2)):
                    bk = pb()
                    bkb = bk[:].bitcast(BF16)
                    for c in range(4):
                        op("pe", lambda e, bkb=bkb, c=c, srcT=srcT, js=js: e.transpose(out=bkb[:, c * 128:(c + 1) * 128], in_=srcT[:, c, js],
                                                                                      identity=identb[:]), [srcT, identb], [bk])
                    op("act", lambda e, bkb=bkb, dstT=dstT, js=js: e.activation(out=dstT[:, :, js], in_=bkb[:, 0:512].rearrange("p (c k) -> p c k", k=128),
                                                                                func=AF.Copy), [bk], [dstT])
                    if dstT is Vtm:
                        for cp_ in range(2):
                            rw_ = slice(cp_ * 64, cp_ * 64 + 64)
                            op("act", lambda e, bkb=bkb, js=js, cp_=cp_, rw_=rw_: e.activation(
                                out=Vm[cp_][rw_, :, js], in_=bkb[rw_, 0:512].rearrange("p (c k) -> p c k", k=128), func=AF.Copy), [bk], [Vm[cp_]])
                for c in range(4):
                    b1, b2, b3 = pb(), pb(), pb()
                    for hp in range(2):
                        po = hp * 64
                        arv = artm[hp][:, c, j, :, :].rearrange("p a i -> p (a i)")
                        op("pe", lambda e, b1=b1, hp=hp, c=c, arv=arv, js=js: e.matmul(b1[:, hp * 256:(hp + 1) * 256], lhsT=btt[:, c, js],
                                                                                       rhs=arv, start=True, stop=True), [btt, artm[hp]], [b1])
                        op("pe", lambda e, b2=b2, hp=hp, c=c, arv=arv, js=js: e.matmul(b2[:, hp * 256:(hp + 1) * 256], lhsT=ktt[:, c, js],
                                                                                       rhs=arv, start=True, stop=True), [ktt, artm[hp]], [b2])
                        op("pe", lambda e, b3=b3, hp=hp, c=c, js=js: e.matmul(b3[:, hp * 128:(hp + 1) * 128], lhsT=artm[hp][:, c, j, 0, :],
                                                                              rhs=btt[:, c, js], start=True, stop=True), [btt, artm[hp]], [b3])
                    op("dve", lambda e, b1=b1, c=c: e.tensor_tensor(out=AX1[:, 2 * c:2 * c + 2, :].rearrange("p h x -> p (h x)"), in0=b1[:, :],
                                                                    in1=cst("MU"), op=ALU.mult), [b1, CT], [AX1])
                    op("dve", lambda e, b2=b2, c=c: e.tensor_tensor(out=AX2[:, 2 * c:2 * c + 2, :].rearrange("p h x -> p (h x)"), in0=b2[:, :],
                                                                    in1=cst("MU"), op=ALU.mult), [b2, CT], [AX2])
                    op("dve", lambda e, b3=b3, c=c: e.tensor_tensor(out=Am[0][:, 2 * c:2 * c + 2, :].rearrange("p h x -> p (h x)"), in0=b3[:, 0:256],
                                                                    in1=cst("ML"), op=ALU.mult), [b3, CT], [Am[0]])
                op("pool", lambda e: e.tensor_copy(out=BR[0][:, :, 0:128], in_=AX1[:, :, 0:128]), [AX1], [BR[0]])
                op("pool", lambda e: e.tensor_tensor(out=BR[0][:, :, 128:256], in0=AX1[:, :, 0:128],
                                                     in1=identb[:].unsqueeze(1).broadcast_to([128, 8, 128]), op=ALU.add), [AX1, identb], [BR[0]])
                for m in range(6):
                    cur, nxt = m % 2, (m + 1) % 2
                    last = (m == 5)
                    for c in range(4):
                        bq, ba2 = pb(), (None if last else pb())
                        for hp in range(2):
                            h = 2 * c + hp
                            if m == 0:
                                op("pe", lambda e, bq=bq, hp=hp, h=h: e.matmul(bq[:, hp * 256:hp * 256 + 128], lhsT=Am[0][:, h, :], rhs=BR[0][:, h, 0:128],
                                                                               start=True, stop=True), [Am[0], BR[0]], [bq])
                            elif not last:
                                op("pe", lambda e, bq=bq, hp=hp, h=h, cur=cur: e.matmul(bq[:, hp * 256:(hp + 1) * 256], lhsT=Am[cur][:, h, :], rhs=BR[cur][:, h, :],
                                                                                         start=True, stop=True), [Am[cur], BR[cur]], [bq])
                            else:
                                op("pe", lambda e, bq=bq, hp=hp, h=h, cur=cur: e.matmul(bq[:, hp * 256 + 128:(hp + 1) * 256], lhsT=Am[cur][:, h, :],
                                                                                         rhs=BR[cur][:, h, 128:256], start=True, stop=True), [Am[cur], BR[cur]], [bq])
                            if not last:
                                op("pe", lambda e, ba2=ba2, hp=hp, h=h, cur=cur: e.matmul(ba2[:, hp * 128:(hp + 1) * 128], lhsT=BR[cur][:, h, 0:128],
                                                                                           rhs=Am[cur][:, h, :], start=True, stop=True), [Am[cur], BR[cur]], [ba2])
                        bqv = bq[:, :].rearrange("p (h x) -> p h x", x=256)
                        if not last:
                            op("act", lambda e, bqv=bqv, c=c, nxt=nxt: e.activation(out=BR[nxt][:, 2 * c:2 * c + 2, 0:128], in_=bqv[:, :, 0:128], func=AF.Copy),
                               [bq], [BR[nxt]])
                            op("act", lambda e, ba2=ba2, c=c, nxt=nxt: e.activation(out=Am[nxt][:, 2 * c:2 * c + 2, :],
                                                                                    in_=ba2[:, 0:256].rearrange("p (h x) -> p h x", x=128), func=AF.Copy),
                               [ba2], [Am[nxt]])
                        if m == 0:
                            op("pool", lambda e, c=c, nxt=nxt: e.tensor_copy(out=BR[nxt][:, 2 * c:2 * c + 2, 128:256], in_=BR[0][:, 2 * c:2 * c + 2, 128:256]),
                               [BR[0]], [BR[nxt]])
                        else:
                            op("dve", lambda e, bqv=bqv, c=c, nxt=nxt, cur=cur: e.tensor_tensor(out=BR[nxt][:, 2 * c:2 * c + 2, 128:256], in0=bqv[:, :, 128:256],
                                                                                                in1=BR[cur][:, 2 * c:2 * c + 2, 128:256], op=ALU.add),
                               [bq, BR[cur]], [BR[nxt]])
                TTm = BR[0]
                by = pb()
                for cp in range(2):
                    co = cp * 64
                    cc = 2 * j + cp
                    ts_ = slice(j * 128 + co, j * 128 + co + 64)
                    bg, bu, bh = pb(), pb(), pb()
                    for h in range(8):
                        c, hp, po = h // 2, h % 2, (h % 2) * 64
                        chs = slice(j * 128 + po, j * 128 + po + 64)
                        hs = slice(h * 64, (h + 1) * 64)
                        op("pe", lambda e, bg=bg, c=c, hp=hp, co=co, hs=hs: e.matmul(
                            bg[co:co + 64, hs], lhsT=artm[hp][:, c, j, 0, co:co + 64], rhs=Hb[:, c, :], start=True, stop=False),
                            [artm[hp], Hb], [bg])
                        op("pe", lambda e, bg=bg, h=h, c=c, co=co, hs=hs, chs=chs: e.matmul(
                            bg[co:co + 64, hs], lhsT=AX2[:, h, co:co + 64], rhs=Vtm[:, c, chs], start=False, stop=True),
                            [AX2, Vtm], [bg])
                    op("act", lambda e, bg=bg, co=co: e.activation(out=Gs[co:co + 64, :], in_=bg[co:co + 64, :], func=AF.Copy), [bg], [Gs])
                    for h in range(8):
                        hs = slice(h * 64, (h + 1) * 64)
                        op("pe", lambda e, bu=bu, h=h, co=co, hs=hs: e.matmul(bu[co:co + 64, hs], lhsT=TTm[:, h, 128 + co:128 + co + 64],
                                                                              rhs=Gs[:, hs], start=True, stop=True), [TTm, Gs], [bu])
                    op("act", lambda e, bu=bu, co=co, cp=cp: e.activation(out=Us[cp][co:co + 64, :], in_=bu[co:co + 64, :], func=AF.Copy), [bu], [Us[cp]])
                    for h in range(8):
                        c, hp, po = h // 2, h % 2, (h % 2) * 64
                        chs = slice(j * 128 + po, j * 128 + po + 64)
                        hs = slice(h * 64, (h + 1) * 64)
                        op("pe", lambda e, c=c, hp=hp, co=co, hs=hs: e.matmul(by[co:co + 64, hs], lhsT=artm[hp][:, c, j, 1, co:co + 64],
                                                                              rhs=Hb[:, c, :], start=True, stop=False), [artm[hp], Hb], [by])
                        op("pe", lambda e, h=h, co=co, hs=hs, cp=cp: e.matmul(by[co:co + 64, hs], lhsT=AX1[:, h, 128 + co:128 + co + 64],
                                                                              rhs=Us[cp][:, hs], start=False, stop=False), [AX1, Us[cp]], [by])
                        op("pe", lambda e, h=h, c=c, co=co, hs=hs, chs=chs: e.matmul(by[co:co + 64, hs], lhsT=AX2[:, h, 128 + co:128 + co + 64],
                                                                                     rhs=Vtm[:, c, chs], start=False, stop=True), [AX2, Vtm], [by])
                        op("pe", lambda e, bh=bh, c=c, po=po, hs=hs, chs=chs, cp=cp: e.matmul(bh[po:po + 64, c * 64:(c + 1) * 64], lhsT=btm[:, c, chs],
                                                                                              rhs=Us[cp][:, hs], start=True, stop=False), [btm, Us[cp]], [bh])
                        op("pe", lambda e, bh=bh, c=c, po=po, chs=chs, cp=cp: e.matmul(bh[po:po + 64, c * 64:(c + 1) * 64], lhsT=ktm2[:, c, chs],
                                                                                       rhs=Vm[cp][:, c, chs], start=False, stop=True), [ktm2, Vm[cp]], [bh])
                    op("dve", lambda e, bh=bh: e.tensor_tensor(out=Htmp[:].rearrange("p c v -> p (c v)"), in0=bh[:, 0:256],
                                                               in1=Hst[:].rearrange("p c v -> p (c v)"), op=ALU.add), [bh, Hst], [Htmp])
                    op("dve", lambda e, cc=cc: e.tensor_tensor(out=Hst[:], in0=Htmp[:], in1=WCt[:, :, cc:cc + 1].broadcast_to([128, 4, 64]), op=ALU.mult),
                       [Htmp, WCt], [Hst])
                    op("act", lambda e: e.activation(out=Hb[:], in_=Hst[:], func=AF.Copy), [Hst], [Hb])
                ysb, ysq, yn = ring5(), ring5(), ring5()
                op("act", lambda e, ysb=ysb: e.activation(out=ysb[:, 0:512], in_=by[:, :], func=AF.Copy), [by], [ysb])
                op("act", lambda e, ysq=ysq: e.activation(out=ysq[:, 0:512], in_=by[:, :], func=AF.Square), [by], [ysq])
                head_norm(op, st4, ysb, ysq, yn, 8, 64, eps_w, CT)
                op("dve", lambda e, yn=yn: e.tensor_tensor(out=yn[:, 0:512], in0=yn[:, 0:512], in1=rows3[:, 1, :], op=ALU.mult), [yn, rows3], [yn])
                op("pool", lambda e, yn=yn: e.tensor_tensor(out=yn[:, 0:512], in0=yn[:, 0:512], in1=rows3[:, 2, :], op=ALU.add), [yn, rows3], [yn])
                bt_ = pb()
                for c in range(4):
                    op("pe", lambda e, bt_=bt_, c=c, yn=yn: e.transpose(out=bt_[:, c * 128:(c + 1) * 128], in_=yn[:, c * 128:(c + 1) * 128],
                                                                        identity=cst("ident")), [yn, CT], [bt_])
                yf = ring5()
                op("dve", lambda e, bt_=bt_, yf=yf, js=js: e.tensor_tensor(out=yf[:, 0:512].rearrange("p (c t) -> p c t", t=128),
                                                                           in0=bt_[:, :].rearrange("p (c t) -> p c t", t=128), in1=bon[:, :, js], op=ALU.add),
                   [bt_, bon], [yf])
                op("pool", lambda e, yf=yf, js=js: e.tensor_tensor(out=orw[:, :, js], in0=yf[:, 0:512].rearrange("p (c t) -> p c t", t=128),
                                                                   in1=srw[:, :, js], op=ALU.mult), [yf, srw], [orw])
            if dbg == "W":
                S.emit()
                return nc
            if dbg == "dump" and l == 0 and it == NT - 1:
                for i_, t_ in enumerate((opool, oret, orw)):
                    dma("sp", lambda e, i_=i_, t_=t_: e.dma_start(out=dbg_o[:, i_ * 4 * TT:(i_ + 1) * 4 * TT], in_=t_[:].rearrange("p c t -> p (c t)")),
                        "dbg", [t_], [b_dbg])
                dma("sp", lambda e: e.dma_start(out=dbg_o[:, 12 * TT:20 * TT], in_=hT[:].rearrange("p c t -> p (c t)")), "dbg", [hT], [b_dbg])
            macc = big
            mT0, mT1 = BB[0], BB[1]
            obr = [opool, oret, orw]
            for n in range(3):
                wbn = wget(l, 13 + 3 * n)
                wbv = wbn[:].rearrange("p (k d) -> p k d", d=1024)
                for hf in range(2):
                    wgl = wget(l, 13 + 3 * n + 1 + hf)
                    for dq_ in range(4):
                        dc = hf * 4 + dq_
                        bgl, byn = pb(), pb()
                        for k in range(8):
                            op("pe", lambda e, bgl=bgl, k=k, wgl=wgl, dq_=dq_: e.matmul(bgl[:, 0:TT], lhsT=w8(wgl)[:, k, dq_ * 128:(dq_ + 1) * 128], rhs=hT[:, k, :],
                                                                                        start=(k == 0), stop=(k == 7)), [wgl, hT], [bgl])
                        for k4 in range(4):
                            op("pe", lambda e, byn=byn, k4=k4, wbv=wbv, dc=dc, n=n, wbn=wbn: e.matmul(byn[:, 0:TT], lhsT=wbv[:, k4, dc * 128:(dc + 1) * 128],
                                                                                                      rhs=obr[n][:, k4, :], start=(k4 == 0), stop=(k4 == 3)),
                               [wbn, obr[n]], [byn])
                        sgt = ring()
                        op("act", lambda e, bgl=bgl, sgt=sgt: e.activation(out=sgt[:, 0:TT], in_=bgl[:, 0:TT], func=AF.Sigmoid), [bgl], [sgt])
                        ms = macc[:, dc * TT:(dc + 1) * TT]
                        if n == 0:
                            op("dve", lambda e, byn=byn, sgt=sgt, ms=ms: e.tensor_tensor(out=ms, in0=byn[:, 0:TT], in1=sgt[:, 0:TT], op=ALU.mult),
                               [byn, sgt], [macc])
                        else:
                            tmp = ring()
                            op("dve", lambda e, byn=byn, sgt=sgt, tmp=tmp: e.tensor_tensor(out=tmp[:, 0:TT], in0=byn[:, 0:TT], in1=sgt[:, 0:TT], op=ALU.mult),
                               [byn, sgt], [tmp])
                            if n == 1:
                                op("pool", lambda e, tmp=tmp, ms=ms: e.tensor_tensor(out=ms, in0=ms, in1=tmp[:, 0:TT], op=ALU.add), [tmp, macc], [macc])
                            else:
                                mt_ = (mT0 if dc < 4 else mT1)
                                op("pool", lambda e, tmp=tmp, ms=ms, mt_=mt_, dc=dc: e.tensor_tensor(out=mt_[:, dc % 4, :], in0=ms, in1=tmp[:, 0:TT], op=ALU.add),
                                   [tmp, macc], [mt_])
                    wrel(wgl)
                wrel(wbn)
            for hf in range(2):
                wo_ = wget(l, 22 + hf)
                for j in range(NB):
                    bo = pb()
                    for dc in range(8):
                        mt_ = (mT0 if dc < 4 else mT1)
                        op("pe", lambda e, bo=bo, dc=dc, mt_=mt_, j=j, wo_=wo_: e.matmul(bo[:, :], lhsT=mt_[:, dc % 4, j * 128:(j + 1) * 128], rhs=w8(wo_)[:, dc, :],
                                                                                         start=(dc == 0), stop=(dc == 7)), [mt_, wo_], [bo])
                    tmp = ring5()
                    op("dve", lambda e, bo=bo, tmp=tmp, hf=hf: e.tensor_tensor(out=tmp[:, 0:512], in0=bo[:, :], in1=G_bc[:, hf * 512:(hf + 1) * 512], op=ALU.mult),
                       [bo, modbc], [tmp])
                    op("pool", lambda e, tmp=tmp, hf=hf, j=j: e.tensor_tensor(out=xt[:, j, hf * 512:(hf + 1) * 512], in0=xt[:, j, hf * 512:(hf + 1) * 512],
                                                                              in1=tmp[:, 0:512], op=ALU.add), [tmp, xt], [xt])
                wrel(wo_)
            if l < L - 1:
                dma("sp", lambda e, t0=t0: e.dma_start(out=xs[t0:t0 + TT, :].rearrange("(j p) d -> p j d", p=128), in_=xt[:]), "st_x", [xt], [b_xs[it]])
            else:
                if it == 0:
                    dma("sp", lambda e: e.dma_start(out=gbc[:], in_=fg_in[0, :].partition_broadcast(128)), "ld_g", [], [gbc])
                for j in range(NB):
                    op("act", lambda e, j=j: e.activation(out=junk[:], in_=xt[:, j, :], func=AF.Square, accum_out=st4[:, j:j + 1]), [xt], [junk, st4])
                op("act", lambda e: e.activation(out=st4[:, 4:4 + NB], in_=st4[:, 0:NB], func=AF.Sqrt, bias=eps_n, scale=1.0 / D), [st4, CT], [st4])
                op("dve", lambda e: e.reciprocal(out=st4[:, 8:8 + NB], in_=st4[:, 4:4 + NB]), [st4], [st4])
                for j in range(NB):
                    op("dve", lambda e, j=j: e.scalar_tensor_tensor(out=xt[:, j, :], in0=xt[:, j, :], scalar=st4[:, 8 + j:9 + j], in1=gbc[:],
                                                                    op0=ALU.mult, op1=ALU.mult), [xt, st4, gbc], [xt])
                dma("sp", lambda e, t0=t0: e.dma_start(out=out[t0:t0 + TT, :].rearrange("(j p) d -> p j d", p=128), in_=xt[:]), "st_o", [xt], [b_out[it]])
    S.wait_all("sp", b_out + [b_dbg])
    S.emit()
    return nc


def head_norm(op, st4, xs_, sq, on, nh, hd, eps_ap, CT):
    s1 = st4[:, 0:nh]
    xv = xs_[:, 0:nh * hd].rearrange("p (h d) -> p h d", d=hd)
    qv = sq[:, 0:nh * hd].rearrange("p (h d) -> p h d", d=hd)
    ov = on[:, 0:nh * hd].rearrange("p (h d) -> p h d", d=hd)
    m = st4[:, 0:nh]
    v = st4[:, 8:8 + nh]
    op("dve", lambda e: e.tensor_reduce(out=m, in_=xv, axis=AX.X, op=ALU.add), [xs_], [st4])
    op("dve", lambda e: e.tensor_reduce(out=v, in_=qv, axis=AX.X, op=ALU.add), [sq], [st4])
    op("dve", lambda e: e.tensor_scalar(out=m, in0=m, scalar1=1.0 / hd, scalar2=None, op0=ALU.mult), [st4], [st4])
    msq = sq[:, 0:nh]
    op("dve", lambda e: e.tensor_tensor(out=msq, in0=m, in1=m, op=ALU.mult), [st4], [sq])
    op("dve", lambda e: e.scalar_tensor_tensor(out=v, in0=v, scalar=1.0 / hd, in1=msq, op0=ALU.mult, op1=ALU.subtract), [st4, sq], [st4])
    op("act", lambda e: e.activation(out=v, in_=v, func=AF.Sqrt, bias=eps_ap), [st4, CT], [st4])
    op("dve", lambda e: e.reciprocal(out=v, in_=v), [st4], [st4])
    op("dve", lambda e: e.tensor_tensor(out=ov, in0=xv, in1=m.unsqueeze(2).broadcast_to([128, nh, hd]), op=ALU.subtract), [xs_, st4], [on])
    op("dve", lambda e: e.tensor_tensor(out=ov, in0=ov, in1=v.unsqueeze(2).broadcast_to([128, nh, hd]), op=ALU.mult), [on, st4], [on])


_CACHE = {}


def run(inputs, T, L, NB=2, n_cores=8, dbg=None):
    B = inputs["x"].shape[0]
    wd = prep_weights(inputs)
    key = (T, L, NB, dbg)
    if key not in _CACHE:
        _CACHE[key] = build(T, L, NB, dbg)
    nc = _CACHE[key]
    in_maps = []
    x = np.asarray(inputs["x"], np.float32)
    c = np.asarray(inputs["c"], np.float32)
    pos = np.asarray(inputs["positions"], np.int32)
    for core in range(n_cores):
        b = core % B
        m = dict(wd)
        m["x"] = np.ascontiguousarray(x[b])
        m["c"] = np.ascontiguousarray(c[b].reshape(8, 128).T)
        m["pos"] = np.ascontiguousarray(pos[b].reshape(1, T))
        in_maps.append(m)
    res = run_bass_kernel_spmd(nc, in_maps, core_ids=list(range(n_cores)))
    if dbg == "dump":
        global LAST_DBG
        LAST_DBG = np.asarray(res.results[0]["dbg_o"]).astype(np.float32)
        global LAST_DBGF
        LAST_DBGF = np.asarray(res.results[0]["dbg_f"]).astype(np.float32)
    return np.stack([np.asarray(res.results[b]["out"], np.float32) for b in range(B)], 0)


def kernel(**inputs):
    T = inputs["x"].shape[1]
    L = inputs["w_in"].shape[0]
    return run(inputs, T, L)
```

```python
import contextlib
import numpy as np
import concourse.bass as bass
import concourse.mybir as mybir
from concourse.bass_utils import run_bass_kernel_spmd

F32 = mybir.dt.float32
BF16 = mybir.dt.bfloat16
I32 = mybir.dt.int32
AF = mybir.ActivationFunctionType
ALU = mybir.AluOpType
AX = mybir.AxisListType

D = 1024
W = 512
DIN = 8320
NBLKW = 24
NV = 40
C0 = float(np.exp(-0.5))
GAM = [1.0 - 2.0 ** (-5.0 - h) for h in range(4)]
ENGS = ["pe", "dve", "act", "pool", "sp"]


class Buf:
    __slots__ = ("name", "lw", "rd")

    def __init__(self, name):
        self.name = name
        self.lw = None
        self.rd = []


class _Rec:
    def __getattr__(self, name):
        return lambda *a, **k: (name, a, k)


_REC = _Rec()


class Sched:
    def __init__(self, nc):
        self.nc = nc
        self.ops = {e: [] for e in ENGS}
        self.cnt = {}
        self.keys = []
        self.epoch = 0

    def _key(self, eng):
        return "%s_%d" % (eng, self.epoch)

    def _deps(self, reads, writes, eng=None):
        deps = {}
        pre = None if eng is None else eng + "_"
        for b in reads:
            t = b.lw
            if t is not None and deps.get(t[0], 0) < t[1]:
                deps[t[0]] = t[1]
        for b in writes:
            t = b.lw
            if t is not None and deps.get(t[0], 0) < t[1] and not (pre and t[0].startswith(pre)):
                deps[t[0]] = t[1]
            for t in b.rd:
                if deps.get(t[0], 0) < t[1] and not (pre and t[0].startswith(pre)):
                    deps[t[0]] = t[1]
        return deps

    def _commit(self, tok, reads, writes):
        for b in reads:
            b.rd.append(tok)
        for b in writes:
            b.lw = tok
            b.rd = []

    def _bump(self, key, inc):
        if key not in self.cnt:
            self.cnt[key] = 0
            self.keys.append(key)
        self.cnt[key] += inc
        return (key, self.cnt[key])

    def op(self, eng, fn, reads=(), writes=()):
        deps = self._deps(reads, writes, eng)
        key = self._key(eng)
        tok = self._bump(key, 1)
        self.ops[eng].append((fn(_REC), deps, key, 1))
        self._commit(tok, reads, writes)
        return tok

    def dma(self, eng, fn, semkey, reads=(), writes=()):
        deps = self._deps(reads, writes)
        tok = self._bump(semkey, 16)
        self.ops[eng].append((fn(_REC), deps, semkey, 16))
        self._commit(tok, reads, writes)
        return tok

    def wait_all(self, eng, bufs):
        deps = self._deps(bufs, ())
        self.ops[eng].append((None, deps, None, 0))

    def emit(self):
        nc = self.nc
        with contextlib.ExitStack() as st:
            sems = {}
            for i, k in enumerate(self.keys):
                sems[k] = st.enter_context(nc.semaphore("s%d" % i))
            block = st.enter_context(nc.Block())

            def run(engname, e):
                seen = {}
                for fn, deps, key, inc in self.ops[engname]:
                    for k, v in deps.items():
                        if engname == "pe" and k.startswith("pe_"):
                            continue
                        if seen.get(k, 0) >= v:
                            continue
                        e.wait_ge(sems[k], v)
                        seen[k] = v
                    if fn is not None:
                        getattr(e, fn[0])(*fn[1], **fn[2]).then_inc(sems[key], inc)

            @block.tensor
            def _(e):
                run("pe", e)

            @block.vector
            def _(e):
                run("dve", e)

            @block.scalar
            def _(e):
                run("act", e)

            @block.gpsimd
            def _(e):
                run("pool", e)

            @block.sync
            def _(e):
                run("sp", e)


class Tile:
    def __init__(self, nc, name, shape, dtype, psum=False):
        if psum:
            self.t = nc.alloc_psum_tensor("p_" + name, shape, dtype)
        else:
            self.t = nc.alloc_sbuf_tensor("t_" + name, shape, dtype)
        self.b = Buf(name)

    def __getitem__(self, idx):
        return self.t[idx]


def make_consts():
    c = {}
    p = np.arange(128)
    i = np.arange(128)
    mt = np.zeros((128, 4, 128), np.float64)
    dq = np.zeros((128, 4, 128), np.float64)
    kt = np.zeros((128, 4), np.float64)
    for h in range(4):
        g = GAM[h]
        diff = i[None, :] - p[:, None]
        mt[:, h, :] = np.where(diff >= 0, g ** np.maximum(diff, 0), 0.0)
        dq[:, h, :] = (g ** (i + 1.0))[None, :]
        kt[:, h] = g ** (127.0 - p)
    c["maskT"] = mt.reshape(128, 512)
    c["decq"] = dq.reshape(128, 512)
    c["ktail"] = kt
    same = (p[:, None] // 64) == (i[None, :] // 64)
    mus = (same & (i[None, :] > p[:, None])).astype(np.float64)
    mui = (same & (i[None, :] >= p[:, None])).astype(np.float64)
    mls = (same & (i[None, :] < p[:, None])).astype(np.float64)
    mu2 = np.concatenate([mus, mui], 1)
    c["MU"] = np.concatenate([mu2, mu2], 1)
    c["ML"] = np.concatenate([mls, mls], 1)
    c["onesbd"] = same.astype(np.float64)
    c["ident"] = np.eye(128)
    sm = np.ones((128, 512)); sm[:, ::64] = 0.0
    c["scanm"] = sm
    half = 64
    fr = (np.float32(10000.0) ** (-(np.arange(half, dtype=np.float32)) / np.float32(half))).astype(np.float64)
    c["freq"] = np.concatenate([fr, fr])[:, None]
    c["sgn"] = np.concatenate([-np.ones(64), np.ones(64)])[:, None]
    ic = np.zeros((128, 4, 16))
    for g, w in enumerate((2, 4, 8, 16)):
        ic[:, g, :] = (1.0 / np.minimum(np.arange(16) + 1, w))[None, :]
    c["invc"] = ic.reshape(128, 64)
    c["eps"] = np.array([1e-6, 1e-5, 64e-5, 1e-30, np.pi / 2])[None, :].repeat(128, 0)
    off = {}
    cols = []
    o = 0
    for k, v in c.items():
        off[k] = (o, v.shape[1])
        o += v.shape[1]
        cols.append(v)
    return np.concatenate(cols, 1).astype(np.float32), off


CONSTS, COFF = make_consts()
NCONST = CONSTS.shape[1]


def prep_weights(inp):
    L = inp["w_in"].shape[0]
    w_in = np.asarray(inp["w_in"], np.float32)
    wblk = np.zeros((L, NBLKW, 128, 4096), np.float32)

    def inblk(cols):
        a = w_in[:, :, cols]
        return a.reshape(L, 8, 128, len(cols)).transpose(0, 2, 1, 3)

    def put(b, cols):
        a = inblk(cols)
        n = a.shape[3]
        tmp = np.zeros((L, 128, 8, 512), np.float32)
        tmp[:, :, :, :n] = a
        wblk[:, b] = tmp.reshape(L, 128, 4096)

    ar = np.arange
    sw = np.concatenate([np.concatenate([ar(h * 128 + 64, h * 128 + 128), ar(h * 128, h * 128 + 64)]) for h in range(4)])
    put(0, ar(0, 512)); put(1, ar(512, 1024))
    put(2, ar(1024, 1536)); put(3, 1024 + sw)
    put(4, ar(1536, 2048)); put(5, 1536 + sw)
    put(6, ar(2048, 2560)); put(7, ar(2560, 3072))
    put(8, ar(3072, 3584)); put(9, ar(3648, 4160)); put(10, ar(4160, 4672))
    put(11, np.concatenate([ar(3584, 3648), ar(4672, 4736)]))
    put(12, ar(4736, 5248))
    wb = np.asarray(inp["w_branch"], np.float32)
    for n in range(3):
        wblk[:, 13 + 3 * n] = wb[:, n].reshape(L, 4, 128, 1024).transpose(0, 2, 1, 3).reshape(L, 128, 4096)
        for hf in range(2):
            put(13 + 3 * n + 1 + hf, 5248 + n * 1024 + hf * 512 + ar(512))
    wo = np.asarray(inp["w_out"], np.float32)
    for hf in range(2):
        a = wo[:, :, hf * 512:(hf + 1) * 512].reshape(L, 8, 128, 512).transpose(0, 2, 1, 3)
        wblk[:, 22 + hf] = a.reshape(L, 128, 4096)
    wsm = np.zeros((L, 128, 1536), np.float32)
    pw = np.asarray(inp["pool_w"], np.float32)
    wsm[:, :, 0:512] = pw.transpose(0, 2, 1, 3).reshape(L, 128, 512)
    wsm[:, 0:64, 512:1024] = np.asarray(inp["rwkv_w2"], np.float32)
    wsm[:, 64:128, 1024:1536] = np.asarray(inp["rwkv_a2"], np.float32)

    def fm(v):
        return np.asarray(v, np.float32).reshape(L, 4, 128).transpose(0, 2, 1)
    mu = np.asarray(inp["rwkv_shift_mu"], np.float32)
    vecs = np.zeros((L, 128, NV), np.float32)
    vecs[:, :, 0:4] = fm(inp["pool_scale"])
    vecs[:, :, 4:8] = fm(mu[:, 0:512])
    vecs[:, :, 8:12] = fm(mu[:, 576:1088])
    vecs[:, :, 12:16] = fm(mu[:, 1088:1600])
    vecs[:, 0:64, 16] = mu[:, 512:576]
    vecs[:, 64:128, 16] = mu[:, 1600:1664]
    vecs[:, :, 17:21] = fm(inp["rwkv_w0"])
    vecs[:, :, 21:25] = fm(inp["rwkv_a0"])
    vecs[:, :, 25:29] = fm(inp["rwkv_k_k"])
    vecs[:, :, 29:33] = fm(inp["rwkv_k_a"])
    vecs[:, :, 33:37] = fm(np.asarray(inp["rwkv_r_k"], np.float32).reshape(L, 512))
    rows = np.stack([np.asarray(inp["ret_norm_g"], np.float32), np.asarray(inp["rwkv_ln_w"], np.float32),
                     np.asarray(inp["rwkv_ln_b"], np.float32)], 1)
    ngb = np.concatenate([np.asarray(inp["norm_g"], np.float32), np.asarray(inp["b_ada"], np.float32)], 1)
    return dict(wblk=wblk, wsm=wsm, vecs=vecs, rows=np.ascontiguousarray(rows), ngb=np.ascontiguousarray(ngb),
                wada=np.ascontiguousarray(np.asarray(inp["w_ada"], np.float32)),
                fg=np.asarray(inp["final_g"], np.float32).reshape(1, D), consts=CONSTS)


def build(T, L, NB=2, dbg=None):
    TT = NB * 128
    NT = T // TT
    NCH = NB * 2
    CPB = 512 // TT
    nc = bass.Bass("TRN2", target_bir_lowering=False)
    S = Sched(nc)

    def dram(name, shape, dt, kind):
        return nc.dram_tensor(name, shape, dt, kind=kind).ap()
    x_in = dram("x", [T, D], F32, "ExternalInput")
    c_in = dram("c", [128, 8], F32, "ExternalInput")
    pos_in = dram("pos", [1, T], I32, "ExternalInput")
    wblk = dram("wblk", [L, NBLKW, 128, 4096], F32, "ExternalInput")
    wsm_in = dram("wsm", [L, 128, 1536], F32, "ExternalInput")
    vecs_in = dram("vecs", [L, 128, NV], F32, "ExternalInput")
    rows_in = dram("rows", [L, 3, 512], F32, "ExternalInput")
    ngb_in = dram("ngb", [L, 4096], F32, "ExternalInput")
    wada_in = dram("wada", [L, D, 3 * D], F32, "ExternalInput")
    fg_in = dram("fg", [1, D], F32, "ExternalInput")
    consts_in = dram("consts", [128, NCONST], F32, "ExternalInput")
    out = dram("out", [T, D], F32, "ExternalOutput")
    dbg_o = dram("dbg_o", [128, 20 * TT], BF16, "ExternalOutput") if dbg == "dump" else None
    b_dbg = Buf("dbg")
    dbg_f = dram("dbg_f", [128, 16384], F32, "ExternalOutput") if dbg == "dump" else None
    dstate = {"off": 0, "map": {}, "on": False}
    global DUMP_MAP
    DUMP_MAP = dstate["map"]
    dstage = None

    def dump(name, ap, rd, width):
        if dbg != "dump" or not dstate["on"]:
            return
        o = dstate["off"]
        dstate["map"][name] = (o, width)
        st_ = hn
        op("dve", lambda e: e.tensor_copy(out=st_[:, 0:width], in_=ap), rd, [st_])
        dma("sp", lambda e: e.dma_start(out=dbg_f[:, o:o + width], in_=st_[:, 0:width]), "dbgf", [st_], [b_dbg])
        dstate["off"] += width
    wsc = dram("wsc", [L, NBLKW, 128, 4096], BF16, "Internal")
    xs = dram("xs", [T, D], F32, "Internal")
    b_wsc = [Buf("wsc%d" % l) for l in range(L)]
    b_xs = [Buf("xs%d" % i) for i in range(NT)]
    b_out = [Buf("out%d" % i) for i in range(NT)]

    def tile(name, shape, dt):
        return Tile(nc, name, shape, dt)

    CT = tile("consts", [128, NCONST], F32)

    def cst(name):
        o, n = COFF[name]
        return CT[:, o:o + n]
    identb = tile("identb", [128, 128], BF16)
    onesb = tile("onesb", [128, 128], BF16)
    cbc = tile("cbc", [128, 8, 128], F32)
    cfm = tile("cfm", [128, 8], F32)
    modbc = tile("modbc", [128, 3 * D], F32)
    gbc = tile("gbc", [128, D], F32)
    rows3 = tile("rows3", [128, 3, 512], F32)
    vecs = tile("vecs", [128, NV], F32)
    wsmb = tile("wsmb", [128, 1536], BF16)
    big = tile("big", [128, 8 * TT], F32)
    bada = tile("bada", [128, TT], F32)
    class _Alias:
        def __init__(self, t, lo, hi, shp=None):
            self.t, self.lo, self.hi, self.b, self.shp = t, lo, hi, t.b, shp

        def __getitem__(self, idx):
            v = self.t[:, self.lo:self.hi]
            if self.shp:
                v = v.rearrange("p (h i) -> p h i", i=self.shp)
            return v[idx]
    xtl = [tile("xt%d" % i, [128, NB, D], F32) for i in range(2)]
    hT = tile("hT", [128, 8, TT], BF16)
    hn = _Alias(big, 0, D)
    hb = tile("hb", [128, D], BF16)
    junk = hb
    wsmf = big
    st4 = tile("st4", [128, 16], F32)
    NSLOT = 3
    wslot = [tile("wslot%d" % i, [128, 4096], BF16) for i in range(NSLOT)]
    FW = TT + 16
    Ft = [tile("F%d" % i, [128, 4, FW], F32) for i in range(4)]
    NRING = 5
    ringt = [tile("ring%d" % i, [128, FW], F32) for i in range(NRING)]
    wkn = "stg dd sig ag cs csx Pin Pex Q kk sqb rn kp prb tb".split()
    wk = {n: tile("wk_" + n, [128, FW], F32) for n in wkn}
    BB = [tile("BB%d" % i, [128, 4, TT], BF16) for i in range(6)]
    artm = [tile("artm%d" % i, [128, 4, NB, 2, 128], BF16) for i in range(2)]
    Vm = [tile("Vm%d" % i, [128, 4, TT], BF16) for i in range(2)]
    opool = tile("opool", [128, 4, TT], BF16)
    oret = tile("oret", [128, 4, TT], BF16)
    orw = tile("orw", [128, 4, TT], BF16)
    puh = tile("puh", [128, 4, 16], F32)
    hal = tile("hal", [128, 16], F32)
    Rst = tile("Rst", [128, 4, 128], F32)
    Rb = tile("Rb", [128, 4, 128], BF16)
    Hst = tile("Hst", [128, 4, 64], F32)
    Hb = tile("Hb", [128, 4, 64], BF16)
    ksF = tile("ksF", [128, 4, TT], F32)

    Sm = _Alias(hb, 0, 512, 128)
    lor = _Alias(hb, 512, 512 + TT)
    AX1 = tile("AX1", [128, 8, 256], BF16)
    AX2 = tile("AX2", [128, 8, 256], BF16)
    BR = [tile("BR%d" % i, [128, 8, 256], BF16) for i in range(2)]
    Am = [tile("Am%d" % i, [128, 8, 128], BF16) for i in range(2)]
    Gs = tile("Gs", [128, 512], BF16)
    Us = [tile("Us%d" % i, [128, 512], BF16) for i in range(2)]
    WCt = tile("WCt", [128, 4, NCH], F32)

    class _PosI:
        b = bada.b

        def __getitem__(self, idx):
            return bada[:, 0:TT].bitcast(I32)[idx]
    posi = _PosI()
    psum_all = nc.alloc_psum_tensor("psum_all", [128, 4096], F32)

    class Bank:
        def __init__(self, i):
            self.i = i
            self.b = Buf("bank%d" % i)

        def __getitem__(self, idx):
            return psum_all[:, self.i * 512:(self.i + 1) * 512][idx]
    banks = [Bank(i) for i in range(8)]

    def mbank(i0, n):
        return psum_all[:, i0 * 512:(i0 + n) * 512]
    ring5t = [tile("ringw%d" % i, [128, 512], F32) for i in range(4)]
    state = {"bank": 0, "ring": 0, "ring5": 0}

    def ring5():
        r = ring5t[state["ring5"] % 4]
        state["ring5"] += 1
        return r

    def pb():
        b = banks[state["bank"] % 8]
        state["bank"] += 1
        return b

    def ring():
        r = ringt[state["ring"] % NRING]
        state["ring"] += 1
        return r

    def op(eng, fn, r, w):
        return S.op(eng, fn, [t.b if hasattr(t, "b") else t for t in r], [t.b if hasattr(t, "b") else t for t in w])

    def dma(eng, fn, key, r, w):
        return S.dma(eng, fn, key, [t.b if hasattr(t, "b") else t for t in r], [t.b if hasattr(t, "b") else t for t in w])

    def cast_layer(l):
        key = "cast%d" % l
        tok = None
        for b in range(NBLKW):
            for hf in range(2):
                tok = S.dma("pool", lambda e, l=l, b=b, hf=hf: e.dma_start(
                    out=wsc[l, b, :, hf * 2048:(hf + 1) * 2048], in_=wblk[l, b, :, hf * 2048:(hf + 1) * 2048]), key)
        b_wsc[l].lw = tok

    order = [0, 1, 2, 3, 4, 5, 6, 7, 11, 8, 9, 10, 12] + [14, 15, 13, 17, 18, 16, 20, 21, 19, 22, 23]
    seq = [(l, b) for l in range(L) for _ in range(NT) for b in order]
    wstate = {"next_load": 0, "next_get": 0}
    wreleased = [False] * len(seq)

    def w_pump():
        while wstate["next_load"] < len(seq):
            n = wstate["next_load"]
            if n >= NSLOT and not wreleased[n - NSLOT]:
                break
            l, b = seq[n]
            s = n % NSLOT
            if b == 11:
                dma("sp", lambda e, l=l, b=b, s=s: e.dma_start(
                    out=wslot[s][:].rearrange("p (k c) -> p k c", c=512)[:, :, 0:128],
                    in_=wsc[l, b].rearrange("p (k c) -> p k c", c=512)[:, :, 0:128]), "ws%d" % s, [b_wsc[l]], [wslot[s]])
            else:
                dma("sp", lambda e, l=l, b=b, s=s: e.dma_start(out=wslot[s][:], in_=wsc[l, b]), "ws%d" % s, [b_wsc[l]], [wslot[s]])
            wstate["next_load"] += 1

    def wget(l, b):
        n = wstate["next_get"]
        assert seq[n] == (l, b), (seq[n], l, b)
        w_pump()
        assert wstate["next_load"] > n, "weight block not loadable (too many held)"
        wstate["next_get"] += 1
        wstate["last"] = n
        t_ = wslot[n % NSLOT]
        t_.widx = n
        return t_

    def wrel(*tiles):
        for t_ in tiles:
            wreleased[t_.widx] = True
        w_pump()

    def w8(wt):
        return wt[:].rearrange("p (k c) -> p k c", c=512)

    dma("sp", lambda e: e.dma_start(out=CT[:], in_=consts_in), "ld_c", [], [CT])
    dma("sp", lambda e: e.dma_start(out=cfm[:], in_=c_in), "ld_c2", [], [cfm])
    cast_layer(0)
    op("act", lambda e: e.activation(out=identb[:], in_=cst("ident"), func=AF.Copy), [CT], [identb])
    op("act", lambda e: e.activation(out=onesb[:], in_=cst("onesbd"), func=AF.Copy), [CT], [onesb])
    op("act", lambda e: e.activation(out=cfm[:], in_=cfm[:], func=AF.Silu), [cfm], [cfm])
    for k in range(8):
        op("dve", lambda e, k=k: e.tensor_scalar(out=cbc[:, k, :], in0=cst("ident"), scalar1=0.0, scalar2=cfm[:, k:k + 1],
                                                 op0=ALU.mult, op1=ALU.add), [CT, cfm], [cbc])
    for t_ in (artm[0], artm[1], Vm[0], Vm[1], Us[0], Us[1], Gs):
        op("pool", lambda e, t_=t_: e.memset(t_[:], 0.0), [], [t_])
    eps_n = cst("eps")[:, 0:1]
    eps_r = cst("eps")[:, 1:2]
    eps_w = cst("eps")[:, 2:3]
    eps_k = cst("eps")[:, 3:4]

    def fm_proj(wt, nch, evac, ncols=128, k_parts=8):
        c = 0
        while c < nch:
            n = min(CPB, nch - c)
            bk = pb()
            for i in range(n):
                for k in range(8):
                    op("pe", lambda e, bk=bk, i=i, k=k, cc=c + i: e.matmul(
                        bk[0:ncols, i * TT:(i + 1) * TT], lhsT=w8(wt)[:, k, cc * 128:cc * 128 + ncols], rhs=hT[:, k, :],
                        start=(k == 0), stop=(k == 7)), [wt, hT], [bk])
            evac(bk, c, n)
            c += n

    def bview(bk, n):
        return bk[:, 0:n * TT].rearrange("p (n t) -> p n t", t=TT)

    for l in range(L):
        S.epoch = l
        if l + 1 < L:
            cast_layer(l + 1)
        dma("sp", lambda e, l=l: e.dma_start(out=vecs[:], in_=vecs_in[l]), "ld_v", [], [vecs])
        dma("sp", lambda e, l=l: e.dma_start(out=wsmf[:, 0:1536], in_=wsm_in[l]), "ld_w", [], [wsmf])
        dma("sp", lambda e, l=l: e.dma_start(out=rows3[:].rearrange("p a c -> p (a c)"),
                                             in_=rows_in[l].rearrange("a c -> (a c)").partition_broadcast(128)), "ld_r", [], [rows3])
        dma("sp", lambda e, l=l: e.dma_start(out=gbc[:], in_=ngb_in[l, 0:D].partition_broadcast(128)), "ld_g", [], [gbc])
        op("act", lambda e: e.activation(out=wsmb[:], in_=wsmf[:, 0:1536], func=AF.Copy), [wsmf], [wsmb])
        NWB = 3 * D // TT
        for nb in range(NWB):
            dma("sp", lambda e, l=l, nb=nb: e.dma_start(
                out=big[:].rearrange("p (k c) -> p k c", c=TT),
                in_=wada_in[l, :, nb * TT:(nb + 1) * TT].rearrange("(k p) c -> p k c", p=128)), "ld_a", [], [big])
            dma("sp", lambda e, l=l, nb=nb: e.dma_start(
                out=bada[:, 0:TT], in_=ngb_in[l, D + nb * TT:D + (nb + 1) * TT].partition_broadcast(128)), "ld_b", [], [bada])
            bk = pb()
            for k in range(8):
                op("pe", lambda e, bk=bk, k=k: e.matmul(bk[:, 0:TT], lhsT=cbc[:, k, :], rhs=big[:, k * TT:(k + 1) * TT],
                                                        start=(k == 0), stop=(k == 7)), [cbc, big], [bk])
            op("dve", lambda e, bk=bk, nb=nb: e.tensor_tensor(out=modbc[:, nb * TT:(nb + 1) * TT], in0=bk[:, 0:TT], in1=bada[:, 0:TT],
                                                              op=ALU.add), [bk, bada], [modbc])
        op("dve", lambda e: e.scalar_tensor_tensor(out=modbc[:, D:2 * D], in0=modbc[:, D:2 * D], scalar=1.0, in1=gbc[:],
                                                   op0=ALU.add, op1=ALU.mult), [modbc, gbc], [modbc])
        B_bc = modbc[:, 0:D]
        A_bc = modbc[:, D:2 * D]
        G_bc = modbc[:, 2 * D:3 * D]
        op("pool", lambda e: e.memset(puh[:], 0.0), [], [puh])
        op("pool", lambda e: e.memset(hal[:], 0.0), [], [hal])
        op("pool", lambda e: e.memset(Rst[:], 0.0), [], [Rst])
        op("pool", lambda e: e.memset(Rb[:], 0.0), [], [Rb])
        op("pool", lambda e: e.memset(Hst[:], 0.0), [], [Hst])
        op("pool", lambda e: e.memset(Hb[:], 0.0), [], [Hb])
        poolw = wsmb[:, 0:512].rearrange("p (g d) -> p g d", d=128)
        lw2 = wsmb[:, 512:1024]
        la2 = wsmb[:, 1024:1536]

        def V(i, n=4):
            return vecs[:, i:i + n]

        for it in range(NT):
            t0 = it * TT
            gi = l * NT + it
            xt = xtl[gi % 2]

            def load_x(l_, it_):
                g_ = l_ * NT + it_
                src = x_in if l_ == 0 else xs
                rd = [] if l_ == 0 else [b_xs[it_]]
                dma("sp", lambda e: e.dma_start(out=xtl[g_ % 2][:], in_=src[it_ * TT:(it_ + 1) * TT, :].rearrange("(j p) d -> p j d", p=128)),
                    "ld_x%d" % (g_ % 2), rd, [xtl[g_ % 2]])
            if gi == 0 or NT == 1:
                load_x(l, it)
            if dbg == "pro":
                S.emit()
                return nc
            dstate["on"] = (l == 0 and it == NT - 1)
            for j in range(NB):
                op("act", lambda e, j=j: e.activation(out=junk[:], in_=xt[:, j, :], func=AF.Square, accum_out=st4[:, j:j + 1]),
                   [xt], [junk, st4])
            op("act", lambda e: e.activation(out=st4[:, 4:4 + NB], in_=st4[:, 0:NB], func=AF.Sqrt, bias=eps_n, scale=1.0 / D),
               [st4, CT], [st4])
            op("dve", lambda e: e.reciprocal(out=st4[:, 8:8 + NB], in_=st4[:, 4:4 + NB]), [st4], [st4])
            for j in range(NB):
                op("dve", lambda e, j=j: e.scalar_tensor_tensor(out=hn[:], in0=xt[:, j, :], scalar=st4[:, 8 + j:9 + j], in1=A_bc,
                                                                op0=ALU.mult, op1=ALU.mult), [xt, st4, modbc], [hn])
                op("pool", lambda e: e.tensor_tensor(out=hb[:], in0=hn[:], in1=B_bc, op=ALU.add), [hn, modbc], [hb])
                bk = pb()
                bkb = bk[:].bitcast(BF16)
                for k in range(8):
                    op("pe", lambda e, bkb=bkb, k=k: e.transpose(out=bkb[:, k * 128:(k + 1) * 128], in_=hb[:, k * 128:(k + 1) * 128],
                                                                  identity=identb[:]), [hb, identb], [bk])
                op("act", lambda e, bkb=bkb, j=j: e.activation(out=hT[:, :, j * 128:(j + 1) * 128],
                                                               in_=bkb.rearrange("p (k t) -> p k t", t=128), func=AF.Copy), [bk], [hT])
            if dbg == "N":
                S.emit()
                return nc
            pu, sa, sb, sg = Ft
            op("pool", lambda e: e.tensor_copy(out=pu[:, :, 0:16], in_=puh[:]), [puh], [pu])
            wt = wget(l, 0)
            fm_proj(wt, 4, lambda bk, c, n: op("act", lambda e: e.activation(out=pu[:, c:c + n, 16:16 + TT], in_=bview(bk, n), func=AF.Copy),
                                               [bk], [pu]))
            wrel(wt)
            wt = wget(l, 1)
            fm_proj(wt, 4, lambda bk, c, n: op("act", lambda e: e.activation(out=sg[:, c:c + n, 0:TT], in_=bview(bk, n), func=AF.Silu),
                                               [bk], [sg]))
            wrel(wt)
            op("pool", lambda e: e.tensor_copy(out=puh[:], in_=pu[:, :, TT:TT + 16]), [pu], [puh])
            for g in range(4):
                cur = pu
                for m in range(g + 1):
                    sh = 1 << m
                    lo = (1 << (m + 1)) - 1
                    dst = sa if (m % 2 == 0) else sb
                    op("dve", lambda e, g=g, cur=cur, dst=dst, sh=sh, lo=lo: e.tensor_tensor(
                        out=dst[:, g, lo:FW], in0=cur[:, g, lo:FW], in1=cur[:, g, lo - sh:FW - sh], op=ALU.add), [cur], [dst])
                    cur = dst
                wg = float(1 << (g + 1))
                dpool = BB[0]
                op("dve", lambda e, g=g, cur=cur, wg=wg: e.scalar_tensor_tensor(
                    out=dpool[:, g, :], in0=cur[:, g, 16:FW], scalar=1.0 / wg, in1=pu[:, g, 16:FW], op0=ALU.mult, op1=ALU.subtract),
                    [cur, pu], [dpool])
                if it == 0:
                    r1 = ring()
                    op("dve", lambda e, g=g, cur=cur, r1=r1: e.tensor_tensor(out=r1[:, 0:16], in0=cur[:, g, 16:32],
                                                                             in1=cst("invc")[:, g * 16:(g + 1) * 16], op=ALU.mult), [cur, CT], [r1])
                    op("dve", lambda e, g=g, r1=r1: e.tensor_tensor(out=dpool[:, g, 0:16], in0=r1[:, 0:16], in1=pu[:, g, 16:32],
                                                                    op=ALU.subtract), [r1, pu], [dpool])
            for g in range(4):
                bk = pb()
                op("pe", lambda e, bk=bk, g=g: e.matmul(bk[:, 0:TT], lhsT=poolw[:, g, :], rhs=BB[0][:, g, :], start=True, stop=True),
                   [wsmb, BB[0]], [bk])
                op("dve", lambda e, bk=bk, g=g: e.scalar_tensor_tensor(out=opool[:, g, :], in0=bk[:, 0:TT], scalar=V(0)[:, g:g + 1],
                                                                        in1=sg[:, g, 0:TT], op0=ALU.mult, op1=ALU.mult), [bk, vecs, sg], [opool])
            if dbg == "P":
                S.emit()
                return nc
            if it + 1 < NT:
                load_x(l, it + 1)
            elif l + 1 < L and NT > 1:
                load_x(l + 1, 0)
            tabs = Ft[0]
            sret = Ft[1]
            qr, qd, kr, ktm, vtm = BB[0], BB[1], BB[2], BB[3], BB[4]
            dma("sp", lambda e, t0=t0: e.dma_start(out=posi[:], in_=pos_in[0, t0:t0 + TT].partition_broadcast(128)), "ld_p", [], [posi])
            ang, nn, r2 = ring(), ring(), ring()
            op("dve", lambda e: e.tensor_copy(out=ang[:, 0:TT], in_=posi[:]), [posi], [ang])
            op("dve", lambda e: e.tensor_scalar(out=ang[:, 0:TT], in0=ang[:, 0:TT], scalar1=cst("freq"), scalar2=None, op0=ALU.mult),
               [ang, CT], [ang])
            op("dve", lambda e: e.tensor_scalar(out=nn[:, 0:TT], in0=ang[:, 0:TT], scalar1=float(1.0 / (2 * np.pi)), scalar2=None,
                                                op0=ALU.mult), [ang], [nn])
            op("dve", lambda e: e.tensor_copy(out=posi[:], in_=nn[:, 0:TT]), [nn], [posi])
            op("dve", lambda e: e.tensor_copy(out=nn[:, 0:TT], in_=posi[:]), [posi], [nn])
            c1 = 6.28125
            c2 = float(2 * np.pi - 6.28125)
            op("dve", lambda e: e.scalar_tensor_tensor(out=r2[:, 0:TT], in0=nn[:, 0:TT], scalar=-c1, in1=ang[:, 0:TT],
                                                       op0=ALU.mult, op1=ALU.add), [nn, ang], [r2])
            op("dve", lambda e: e.scalar_tensor_tensor(out=r2[:, 0:TT], in0=nn[:, 0:TT], scalar=-c2, in1=r2[:, 0:TT],
                                                       op0=ALU.mult, op1=ALU.add), [nn, r2], [r2])
            ws_, wc_, mk_ = ring(), ring(), ring()
            twopi = float(2 * np.pi)
            op("dve", lambda e: e.tensor_scalar(out=mk_[:, 0:TT], in0=r2[:, 0:TT], scalar1=0.0, scalar2=float(np.pi), op0=ALU.add, op1=ALU.is_gt),
               [r2], [mk_])
            op("dve", lambda e: e.scalar_tensor_tensor(out=ws_[:, 0:TT], in0=mk_[:, 0:TT], scalar=-twopi, in1=r2[:, 0:TT], op0=ALU.mult, op1=ALU.add),
               [mk_, r2], [ws_])
            op("dve", lambda e: e.tensor_scalar(out=mk_[:, 0:TT], in0=r2[:, 0:TT], scalar1=float(np.pi / 2), scalar2=float(np.pi), op0=ALU.add,
                                                op1=ALU.is_gt), [r2], [mk_])
            op("dve", lambda e: e.scalar_tensor_tensor(out=wc_[:, 0:TT], in0=mk_[:, 0:TT], scalar=-twopi, in1=r2[:, 0:TT], op0=ALU.mult, op1=ALU.add),
               [mk_, r2], [wc_])
            op("act", lambda e: e.activation(out=tabs[:, 0, 0:TT], in_=wc_[:, 0:TT], func=AF.Sin, bias=cst("eps")[:, 4:5]), [wc_, CT], [tabs])
            op("act", lambda e: e.activation(out=tabs[:, 1, 0:TT], in_=ws_[:, 0:TT], func=AF.Sin, scale=cst("sgn")), [ws_, CT], [tabs])
            op("pool", lambda e: e.tensor_scalar(out=tabs[:, 2:4, 0:TT], in0=tabs[:, 0:2, 0:TT], scalar1=float(128.0 ** -0.5), scalar2=None,
                                                 op0=ALU.mult), [tabs], [tabs])

            def rope(dst, dq, tc):
                wa = wget(l, 2 if dst is qr else 4)
                wb_ = wget(l, 3 if dst is qr else 5)
                for h in range(4):
                    ba, bb = pb(), pb()
                    for (bk_, wt_) in ((ba, wa), (bb, wb_)):
                        for k in range(8):
                            op("pe", lambda e, bk_=bk_, wt_=wt_, k=k, h=h: e.matmul(
                                bk_[:, 0:TT], lhsT=w8(wt_)[:, k, h * 128:(h + 1) * 128], rhs=hT[:, k, :], start=(k == 0), stop=(k == 7)),
                                [wt_, hT], [bk_])
                    t1, t2 = ring(), ring()
                    op("dve", lambda e, ba=ba, t1=t1: e.tensor_tensor(out=t1[:, 0:TT], in0=ba[:, 0:TT], in1=tabs[:, tc, 0:TT], op=ALU.mult),
                       [ba, tabs], [t1])
                    op("dve", lambda e, bb=bb, t2=t2: e.tensor_tensor(out=t2[:, 0:TT], in0=bb[:, 0:TT], in1=tabs[:, tc + 1, 0:TT], op=ALU.mult),
                       [bb, tabs], [t2])
                    if dq is None:
                        op("pool", lambda e, t1=t1, t2=t2, h=h: e.tensor_tensor(out=dst[:, h, :], in0=t1[:, 0:TT], in1=t2[:, 0:TT], op=ALU.add),
                           [t1, t2], [dst])
                    else:
                        op("pool", lambda e, t1=t1, t2=t2: e.tensor_tensor(out=t1[:, 0:TT], in0=t1[:, 0:TT], in1=t2[:, 0:TT], op=ALU.add),
                           [t1, t2], [t1])
                        op("act", lambda e, t1=t1, h=h: e.activation(out=dst[:, h, :], in_=t1[:, 0:TT], func=AF.Copy), [t1], [dst])
                        op("pool", lambda e, t1=t1, h=h: e.tensor_tensor(
                            out=dq[:, h, :].rearrange("p (j i) -> p j i", i=128), in0=t1[:, 0:TT].rearrange("p (j i) -> p j i", i=128),
                            in1=cst("decq")[:, h * 128:(h + 1) * 128].unsqueeze(1).broadcast_to([128, NB, 128]), op=ALU.mult), [t1, CT], [dq])
                wrel(wa, wb_)
            dump("tabs", tabs[:, :, 0:TT], [tabs], 4 * TT) if False else None
            for i_ in range(4):
                dump("tab%d" % i_, tabs[:, i_, 0:TT], [tabs], TT)
            rope(qr, qd, 0)
            rope(kr, None, 2)
            dump("qr", qr[:].rearrange("p c t -> p (c t)"), [qr], 4 * TT)
            dump("qd", qd[:].rearrange("p c t -> p (c t)"), [qd], 4 * TT)
            dump("kr", kr[:].rearrange("p c t -> p (c t)"), [kr], 4 * TT)
            wt = wget(l, 6)
            for j in range(NB):
                bk = pb()
                for k in range(8):
                    op("pe", lambda e, bk=bk, k=k, j=j, wt=wt: e.matmul(bk[:, :], lhsT=hT[:, k, j * 128:(j + 1) * 128], rhs=w8(wt)[:, k, :],
                                                                      start=(k == 0), stop=(k == 7)), [wt, hT], [bk])
                op("act", lambda e, bk=bk, j=j: e.activation(out=vtm[:, :, j * 128:(j + 1) * 128],
                                                             in_=bk[:, :].rearrange("p (h e) -> p h e", e=128), func=AF.Copy), [bk], [vtm])
            wrel(wt)
            wt = wget(l, 7)
            fm_proj(wt, 4, lambda bk, c, n: op("act", lambda e: e.activation(out=sret[:, c:c + n, 0:TT], in_=bview(bk, n), func=AF.Silu),
                                               [bk], [sret]))
            wrel(wt)
            for j in range(NB):
                js = slice(j * 128, (j + 1) * 128)
                bk = pb()
                bkb = bk[:].bitcast(BF16)
                for h in range(4):
                    op("pe", lambda e, bkb=bkb, h=h, js=js: e.transpose(out=bkb[:, h * 128:(h + 1) * 128], in_=kr[:, h, js], identity=identb[:]),
                       [kr, identb], [bk])
                op("dve", lambda e, bkb=bkb, js=js: e.tensor_tensor(
                    out=ktm[:, :, js], in0=bkb[:, 0:512].rearrange("p (h d) -> p h d", d=128),
                    in1=cst("ktail").unsqueeze(2).broadcast_to([128, 4, 128]), op=ALU.mult), [bk, CT], [ktm])
                bs = pb()
                for h in range(4):
                    op("pe", lambda e, bs=bs, h=h, js=js: e.matmul(bs[:, h * 128:(h + 1) * 128], lhsT=kr[:, h, js], rhs=qr[:, h, js],
                                                                   start=True, stop=True), [kr, qr], [bs])
                op("dve", lambda e, bs=bs: e.tensor_tensor(out=Sm[:].rearrange("p h i -> p (h i)"), in0=bs[:, :], in1=cst("maskT"), op=ALU.mult),
                   [bs, CT], [Sm])
                bo = pb()
                for h in range(4):
                    op("pe", lambda e, bo=bo, h=h, js=js: e.matmul(bo[:, h * 128:(h + 1) * 128], lhsT=Sm[:, h, :], rhs=vtm[:, h, js],
                                                                   start=True, stop=False), [Sm, vtm], [bo])
                    op("pe", lambda e, bo=bo, h=h, js=js: e.matmul(bo[:, h * 128:(h + 1) * 128], lhsT=qd[:, h, js], rhs=Rb[:, h, :],
                                                                   start=False, stop=True), [qd, Rb], [bo])
                bkv = pb()
                for h in range(4):
                    op("pe", lambda e, bkv=bkv, h=h, js=js: e.matmul(bkv[:, h * 128:(h + 1) * 128], lhsT=ktm[:, h, js], rhs=vtm[:, h, js],
                                                                     start=True, stop=True), [ktm, vtm], [bkv])
                for h in range(4):
                    op("dve", lambda e, bkv=bkv, h=h: e.scalar_tensor_tensor(out=Rst[:, h, :], in0=Rst[:, h, :], scalar=float(GAM[h] ** 128),
                                                                              in1=bkv[:, h * 128:(h + 1) * 128], op0=ALU.mult, op1=ALU.add),
                       [Rst, bkv], [Rst])
                op("act", lambda e: e.activation(out=Rb[:], in_=Rst[:], func=AF.Copy), [Rst], [Rb])
                osb, sq, on = ring5(), ring5(), ring5()
                op("act", lambda e, bo=bo, osb=osb: e.activation(out=osb[:, 0:512], in_=bo[:, :], func=AF.Copy), [bo], [osb])
                op("act", lambda e, bo=bo, sq=sq: e.activation(out=sq[:, 0:512], in_=bo[:, :], func=AF.Square), [bo], [sq])
                if j == NB - 1:
                    dump("osb", osb[:, 0:512], [osb], 512)
                    dump("vtm", vtm[:].rearrange("p c t -> p (c t)"), [vtm], 4 * TT)
                    dump("ktm", ktm[:].rearrange("p c t -> p (c t)"), [ktm], 4 * TT)
                    dump("Sm", Sm[:].rearrange("p c t -> p (c t)"), [Sm], 512)
                head_norm(op, st4, osb, sq, on, 4, 128, eps_r, CT)
                if j == NB - 1:
                    dump("on", on[:, 0:512], [on], 512)
                op("dve", lambda e, on=on: e.tensor_tensor(out=on[:, 0:512], in0=on[:, 0:512], in1=rows3[:, 0, :], op=ALU.mult), [on, rows3], [on])
                bt_ = pb()
                for h in range(4):
                    op("pe", lambda e, bt_=bt_, h=h, on=on: e.transpose(out=bt_[:, h * 128:(h + 1) * 128], in_=on[:, h * 128:(h + 1) * 128],
                                                                        identity=cst("ident")), [on, CT], [bt_])
                op("dve", lambda e, bt_=bt_, js=js: e.tensor_tensor(out=oret[:, :, js], in0=bt_[:, :].rearrange("p (h t) -> p h t", t=128),
                                                                    in1=sret[:, :, js], op=ALU.mult), [bt_, sret], [oret])
            if dbg == "R":
                S.emit()
                return nc
            bon = Ft[0]
            srw = Ft[1]
            vsf = Ft[2]
            btt, ktt, vbb, Vtm, btm, ktm2 = BB
            stg_i = [0]

            rsF = Ft[3]

            class CV:
                def __init__(self, t, c):
                    self.t, self.c, self.b = t, c, t.b

                def __getitem__(self, idx):
                    return self.t[idx[0], self.c, idx[1]]

            def shifted(wt_, cc, idx, dst_ap, dst_t):
                bk = pb()
                for k in range(8):
                    op("pe", lambda e, bk=bk, k=k: e.matmul(bk[:, 0:TT], lhsT=w8(wt_)[:, k, cc * 128:(cc + 1) * 128], rhs=hT[:, k, :],
                                                            start=(k == 0), stop=(k == 7)), [wt_, hT], [bk])
                stg, dd = wk["stg"], wk["dd"]
                op("act", lambda e: e.activation(out=stg[:, 1:1 + TT], in_=bk[:, 0:TT], func=AF.Copy), [bk], [stg])
                op("pool", lambda e: e.tensor_copy(out=stg[:, 0:1], in_=hal[:, idx:idx + 1]), [hal], [stg])
                op("pool", lambda e: e.tensor_copy(out=hal[:, idx:idx + 1], in_=stg[:, TT:TT + 1]), [stg], [hal])
                op("dve", lambda e: e.tensor_tensor(out=dd[:, 0:TT], in0=stg[:, 0:TT], in1=stg[:, 1:1 + TT], op=ALU.subtract), [stg], [dd])
                mu_i = (4 + idx) if idx < 12 else 16
                op("dve", lambda e: e.scalar_tensor_tensor(out=dst_ap, in0=dd[:, 0:TT], scalar=vecs[:, mu_i:mu_i + 1], in1=stg[:, 1:1 + TT],
                                                           op0=ALU.mult, op1=ALU.add), [dd, vecs, stg], [dst_t])
            wl_ = wget(l, 11)
            lot = wk["sig"]
            shifted(wl_, 0, 12, lot[:, 0:TT], lot)
            wrel(wl_)
            op("act", lambda e: e.activation(out=lor[0:64, :], in_=lot[0:64, 0:TT], func=AF.Tanh), [lot], [lor])
            op("act", lambda e: e.activation(out=lor[64:128, :], in_=lot[64:128, 0:TT], func=AF.Copy), [lot], [lor])
            wr_ = wget(l, 8)
            for c in range(4):
                shifted(wr_, c, c, rsF[:, c, 0:TT], rsF)
            wrel(wr_)
            wk_ = wget(l, 9)
            for c in range(4):
                shifted(wk_, c, 4 + c, ksF[:, c, 0:TT], ksF)
            wrel(wk_)
            wv_ = wget(l, 10)
            for c in range(4):
                shifted(wv_, c, 8 + c, vsf[:, c, 0:TT], vsf)
                op("pool", lambda e, c=c: e.tensor_copy(out=vbb[:, c, :], in_=vsf[:, c, 0:TT]), [vsf], [vbb])
            wrel(wv_)
            for c in range(4):
                rs_, ks_ = CV(rsF, c), CV(ksF, c)
                bw, ba_ = pb(), pb()
                op("pe", lambda e, bw=bw, c=c: e.matmul(bw[:, 0:TT], lhsT=lw2[:, c * 128:(c + 1) * 128], rhs=lor[:, :], start=True, stop=True),
                   [wsmb, lor], [bw])
                op("pe", lambda e, ba_=ba_, c=c: e.matmul(ba_[:, 0:TT], lhsT=la2[:, c * 128:(c + 1) * 128], rhs=lor[:, :],
                                                          start=True, stop=True), [wsmb, lor], [ba_])
                sig, ag = wk["sig"], wk["ag"]
                op("act", lambda e, bw=bw, sig=sig, c=c: e.activation(out=sig[:, 0:TT], in_=bw[:, 0:TT], func=AF.Sigmoid, bias=V(17)[:, c:c + 1]),
                   [bw, vecs], [sig])
                op("act", lambda e, ba_=ba_, ag=ag, c=c: e.activation(out=ag[:, 0:TT], in_=ba_[:, 0:TT], func=AF.Sigmoid, bias=V(21)[:, c:c + 1]),
                   [ba_, vecs], [ag])
                cs, csx = wk["cs"], wk["csx"]
                op("dve", lambda e, sig=sig, cs=cs: e.tensor_tensor_scan(out=cs[:, 0:TT], data0=cst("scanm")[:, 0:TT], data1=sig[:, 0:TT],
                                                                         initial=0.0, op0=ALU.mult, op1=ALU.add), [sig, CT], [cs])
                op("pool", lambda e, sig=sig, cs=cs, csx=csx: e.tensor_tensor(out=csx[:, 0:TT], in0=cs[:, 0:TT], in1=sig[:, 0:TT], op=ALU.subtract),
                   [cs, sig], [csx])
                Pin, Pex, Q = wk["Pin"], wk["Pex"], wk["Q"]
                op("act", lambda e, cs=cs, Pin=Pin: e.activation(out=Pin[:, 0:TT], in_=cs[:, 0:TT], func=AF.Exp, scale=-C0), [cs], [Pin])
                op("act", lambda e, csx=csx, Pex=Pex: e.activation(out=Pex[:, 0:TT], in_=csx[:, 0:TT], func=AF.Exp, scale=-C0), [csx], [Pex])
                op("act", lambda e, cs=cs, Q=Q: e.activation(out=Q[:, 0:TT], in_=cs[:, 0:TT], func=AF.Exp, scale=C0), [cs], [Q])
                op("pool", lambda e, Pin=Pin, c=c: e.tensor_copy(out=WCt[:, c, :], in_=Pin[:, 63:TT:64]), [Pin], [WCt])
                kk, sqb, rn = wk["kk"], wk["sqb"], wk["rn"]
                op("dve", lambda e, ks_=ks_, kk=kk, c=c: e.tensor_scalar(out=kk[:, 0:TT], in0=ks_[:, 0:TT], scalar1=V(25)[:, c:c + 1], scalar2=None,
                                                                         op0=ALU.mult), [ks_, vecs], [kk])
                sqv = sqb[:, 0:TT // 2].bitcast(BF16)
                op("act", lambda e, kk=kk, sqv=sqv: e.activation(out=sqv, in_=kk[:, 0:TT], func=AF.Square), [kk], [sqb])
                bn_ = pb()
                op("pe", lambda e, bn_=bn_, sqv=sqv: e.matmul(bn_[:, 0:TT], lhsT=onesb[:], rhs=sqv, start=True, stop=True), [onesb, sqb], [bn_])
                op("act", lambda e, bn_=bn_, rn=rn: e.activation(out=rn[:, 0:TT], in_=bn_[:, 0:TT], func=AF.Sqrt, bias=eps_k), [bn_, CT], [rn])
                op("dve", lambda e, rn=rn: e.reciprocal(out=rn[:, 0:TT], in_=rn[:, 0:TT]), [rn], [rn])
                op("dve", lambda e, kk=kk, rn=rn: e.tensor_tensor(out=kk[:, 0:TT], in0=kk[:, 0:TT], in1=rn[:, 0:TT], op=ALU.mult), [kk, rn], [kk])
                kp = wk["kp"]
                op("pool", lambda e, ag=ag, kp=kp, c=c: e.tensor_scalar(out=kp[:, 0:TT], in0=ag[:, 0:TT], scalar1=-1.0, scalar2=V(29)[:, c:c + 1],
                                                                        op0=ALU.add, op1=ALU.mult), [ag, vecs], [kp])
                op("dve", lambda e, kp=kp, ks_=ks_: e.scalar_tensor_tensor(out=kp[:, 0:TT], in0=kp[:, 0:TT], scalar=1.0, in1=ks_[:, 0:TT],
                                                                            op0=ALU.add, op1=ALU.mult), [kp, ks_], [kp])
                prb = wk["prb"]
                prv = prb[:, 0:TT // 2].bitcast(BF16)
                op("dve", lambda e, rs_=rs_, kp=kp, prv=prv, c=c: e.scalar_tensor_tensor(out=prv, in0=rs_[:, 0:TT], scalar=V(33)[:, c:c + 1],
                                                                                        in1=kp[:, 0:TT], op0=ALU.mult, op1=ALU.mult),
                   [rs_, vecs, kp], [prb])
                bb_ = pb()
                op("pe", lambda e, bb_=bb_, prv=prv: e.matmul(bb_[:, 0:TT], lhsT=onesb[:], rhs=prv, start=True, stop=True), [onesb, prb], [bb_])
                op("dve", lambda e, bb_=bb_, c=c: e.tensor_tensor(out=bon[:, c, 0:TT], in0=bb_[:, 0:TT], in1=vsf[:, c, 0:TT], op=ALU.mult),
                   [bb_, vsf], [bon])
                for hp in range(2):
                    rw_ = slice(hp * 64, hp * 64 + 64)
                    op("dve", lambda e, kk=kk, Pex=Pex, hp=hp, rw_=rw_, c=c: e.scalar_tensor_tensor(
                        out=artm[hp][rw_, c, :, 0, :], in0=kk[rw_, 0:TT].rearrange("p (j i) -> p j i", i=128), scalar=-1.0,
                        in1=Pex[rw_, 0:TT].rearrange("p (j i) -> p j i", i=128), op0=ALU.mult, op1=ALU.mult), [kk, Pex], [artm[hp]])
                    op("pool", lambda e, rs_=rs_, Pin=Pin, hp=hp, rw_=rw_, c=c: e.tensor_tensor(
                        out=artm[hp][rw_, c, :, 1, :], in0=rs_[rw_, 0:TT].rearrange("p (j i) -> p j i", i=128),
                        in1=Pin[rw_, 0:TT].rearrange("p (j i) -> p j i", i=128), op=ALU.mult), [rs_, Pin], [artm[hp]])
                tb = wk["tb"]
                op("pool", lambda e, kk=kk, ag=ag, tb=tb: e.tensor_tensor(out=tb[:, 0:TT], in0=kk[:, 0:TT], in1=ag[:, 0:TT], op=ALU.mult),
                   [kk, ag], [tb])
                op("dve", lambda e, tb=tb, Q=Q, c=c: e.tensor_tensor(out=btt[:, c, :], in0=tb[:, 0:TT], in1=Q[:, 0:TT], op=ALU.mult),
                   [tb, Q], [btt])
                op("pool", lambda e, kp=kp, Q=Q, c=c: e.tensor_tensor(out=ktt[:, c, :], in0=kp[:, 0:TT], in1=Q[:, 0:TT], op=ALU.mult),
                   [kp, Q], [ktt])
            wg_ = wget(l, 12)
            fm_proj(wg_, 4, lambda bk, c, n: op("act", lambda e: e.activation(out=srw[:, c:c + n, 0:TT], in_=bview(bk, n), func=AF.Silu),
                                                [bk], [srw]))
            wrel(wg_)
            for j in range(NB):
                js = slice(j * 128, (j + 1) * 128)
                for (srcT, dstT) in ((vbb, Vtm), (btt, btm), (ktt, ktm2)):
                    bk = pb()
                    bkb = bk[:].bitcast(BF16)
                    for c in range(4):
                        op("pe", lambda e, bkb=bkb, c=c, srcT=srcT, js=js: e.transpose(out=bkb[:, c * 128:(c + 1) * 128], in_=srcT[:, c, js],
                                                                                      identity=identb[:]), [srcT, identb], [bk])
                    op("act", lambda e, bkb=bkb, dstT=dstT, js=js: e.activation(out=dstT[:, :, js], in_=bkb[:, 0:512].rearrange("p (c k) -> p c k", k=128),
                                                                                func=AF.Copy), [bk], [dstT])
                    if dstT is Vtm:
                        for cp_ in range(2):
                            rw_ = slice(cp_ * 64, cp_ * 64 + 64)
                            op("act", lambda e, bkb=bkb, js=js, cp_=cp_, rw_=rw_: e.activation(
                                out=Vm[cp_][rw_, :, js], in_=bkb[rw_, 0:512].rearrange("p (c k) -> p c k", k=128), func=AF.Copy), [bk], [Vm[cp_]])
                for half in range(2):
                    for ci in range(2):
                        c = half * 2 + ci
                        for hp in range(2):
                            arv = artm[hp][:, c, j, :, :].rearrange("p a i -> p (a i)")
                            op("pe", lambda e, ci=ci, hp=hp, c=c, arv=arv, js=js: e.matmul(banks[0 + ci][:, hp * 256:(hp + 1) * 256], lhsT=btt[:, c, js],
                                                                                           rhs=arv, start=True, stop=True), [btt, artm[hp]], [banks[0 + ci]])
                            op("pe", lambda e, ci=ci, hp=hp, c=c, arv=arv, js=js: e.matmul(banks[2 + ci][:, hp * 256:(hp + 1) * 256], lhsT=ktt[:, c, js],
                                                                                           rhs=arv, start=True, stop=True), [ktt, artm[hp]], [banks[2 + ci]])
                            op("pe", lambda e, ci=ci, hp=hp, c=c, js=js: e.matmul(banks[4][:, (ci * 2 + hp) * 128:(ci * 2 + hp + 1) * 128],
                                                                                  lhsT=artm[hp][:, c, j, 0, :], rhs=btt[:, c, js], start=True, stop=True),
                               [btt, artm[hp]], [banks[4]])
                    h0 = 4 * half
                    mu_b = cst("MU").unsqueeze(1).broadcast_to([128, 2, 512])
                    ml_b = cst("ML").unsqueeze(1).broadcast_to([128, 2, 256])
                    op("dve", lambda e, h0=h0, mu_b=mu_b: e.tensor_tensor(out=AX1[:, h0:h0 + 4, :].rearrange("p (a h) x -> p a (h x)", a=2),
                                                                          in0=mbank(0, 2).rearrange("p (a x) -> p a x", a=2), in1=mu_b, op=ALU.mult),
                       [banks[0], banks[1], CT], [AX1])
                    op("dve", lambda e, h0=h0, mu_b=mu_b: e.tensor_tensor(out=AX2[:, h0:h0 + 4, :].rearrange("p (a h) x -> p a (h x)", a=2),
                                                                          in0=mbank(2, 2).rearrange("p (a x) -> p a x", a=2), in1=mu_b, op=ALU.mult),
                       [banks[2], banks[3], CT], [AX2])
                    op("dve", lambda e, h0=h0, ml_b=ml_b: e.tensor_tensor(out=Am[0][:, h0:h0 + 4, :].rearrange("p (a h) x -> p a (h x)", a=2),
                                                                          in0=banks[4][:, :].rearrange("p (a x) -> p a x", a=2), in1=ml_b, op=ALU.mult),
                       [banks[4], CT], [Am[0]])
                op("pool", lambda e: e.tensor_copy(out=BR[0][:, :, 0:128], in_=AX1[:, :, 0:128]), [AX1], [BR[0]])
                op("pool", lambda e: e.tensor_tensor(out=BR[0][:, :, 128:256], in0=AX1[:, :, 0:128],
                                                     in1=identb[:].unsqueeze(1).broadcast_to([128, 8, 128]), op=ALU.add), [AX1, identb], [BR[0]])
                bqs = [banks[0], banks[1], banks[2], banks[3]]
                bas = [banks[4], banks[5]]
                for m in range(6):
                    cur, nxt = m % 2, (m + 1) % 2
                    last = (m == 5)
                    for h in range(8):
                        bq = bqs[h // 2]
                        hp = h % 2
                        if m == 0:
                            op("pe", lambda e, bq=bq, hp=hp, h=h: e.matmul(bq[:, hp * 256:hp * 256 + 128], lhsT=Am[0][:, h, :], rhs=BR[0][:, h, 0:128],
                                                                           start=True, stop=True), [Am[0], BR[0]], [bq])
                        elif not last:
                            op("pe", lambda e, bq=bq, hp=hp, h=h, cur=cur: e.matmul(bq[:, hp * 256:(hp + 1) * 256], lhsT=Am[cur][:, h, :], rhs=BR[cur][:, h, :],
                                                                                     start=True, stop=True), [Am[cur], BR[cur]], [bq])
                        else:
                            op("pe", lambda e, bq=bq, hp=hp, h=h, cur=cur: e.matmul(bq[:, hp * 256 + 128:(hp + 1) * 256], lhsT=Am[cur][:, h, :],
                                                                                     rhs=BR[cur][:, h, 128:256], start=True, stop=True), [Am[cur], BR[cur]], [bq])
                        if not last:
                            ba2 = bas[h // 4]
                            op("pe", lambda e, ba2=ba2, h=h, cur=cur: e.matmul(ba2[:, (h % 4) * 128:(h % 4 + 1) * 128], lhsT=BR[cur][:, h, 0:128],
                                                                               rhs=Am[cur][:, h, :], start=True, stop=True), [Am[cur], BR[cur]], [ba2])
                    bqv = mbank(0, 4).rearrange("p (h x) -> p h x", x=256)
                    if not last:
                        op("act", lambda e, bqv=bqv, nxt=nxt: e.activation(out=BR[nxt][:, :, 0:128], in_=bqv[:, :, 0:128], func=AF.Copy),
                           bqs, [BR[nxt]])
                        op("act", lambda e, nxt=nxt: e.activation(out=Am[nxt][:], in_=mbank(4, 2).rearrange("p (h x) -> p h x", x=128), func=AF.Copy),
                           bas, [Am[nxt]])
                    if m == 0:
                        op("pool", lambda e, nxt=nxt: e.tensor_copy(out=BR[nxt][:, :, 128:256], in_=BR[0][:, :, 128:256]), [BR[0]], [BR[nxt]])
                    else:
                        op("dve", lambda e, bqv=bqv, nxt=nxt, cur=cur: e.tensor_tensor(out=BR[nxt][:, :, 128:256], in0=bqv[:, :, 128:256],
                                                                                       in1=BR[cur][:, :, 128:256], op=ALU.add), bqs + [BR[cur]], [BR[nxt]])
                TTm = BR[0]
                by = pb()
                for cp in range(2):
                    co = cp * 64
                    cc = 2 * j + cp
                    ts_ = slice(j * 128 + co, j * 128 + co + 64)
                    bg, bu, bh = pb(), pb(), pb()
                    for h in range(8):
                        c, hp, po = h // 2, h % 2, (h % 2) * 64
                        chs = slice(j * 128 + po, j * 128 + po + 64)
                        hs = slice(h * 64, (h + 1) * 64)
                        op("pe", lambda e, bg=bg, c=c, hp=hp, co=co, hs=hs: e.matmul(
                            bg[co:co + 64, hs], lhsT=artm[hp][:, c, j, 0, co:co + 64], rhs=Hb[:, c, :], start=True, stop=False),
                            [artm[hp], Hb], [bg])
                        op("pe", lambda e, bg=bg, h=h, c=c, co=co, hs=hs, chs=chs: e.matmul(
                            bg[co:co + 64, hs], lhsT=AX2[:, h, co:co + 64], rhs=Vtm[:, c, chs], start=False, stop=True),
                            [AX2, Vtm], [bg])
                    op("act", lambda e, bg=bg, co=co: e.activation(out=Gs[co:co + 64, :], in_=bg[co:co + 64, :], func=AF.Copy), [bg], [Gs])
                    for h in range(8):
                        hs = slice(h * 64, (h + 1) * 64)
                        op("pe", lambda e, bu=bu, h=h, co=co, hs=hs: e.matmul(bu[co:co + 64, hs], lhsT=TTm[:, h, 128 + co:128 + co + 64],
                                                                              rhs=Gs[:, hs], start=True, stop=True), [TTm, Gs], [bu])
                    op("act", lambda e, bu=bu, co=co, cp=cp: e.activation(out=Us[cp][co:co + 64, :], in_=bu[co:co + 64, :], func=AF.Copy), [bu], [Us[cp]])
                    for h in range(8):
                        c, hp, po = h // 2, h % 2, (h % 2) * 64
                        chs = slice(j * 128 + po, j * 128 + po + 64)
                        hs = slice(h * 64, (h + 1) * 64)
                        op("pe", lambda e, c=c, hp=hp, co=co, hs=hs: e.matmul(by[co:co + 64, hs], lhsT=artm[hp][:, c, j, 1, co:co + 64],
                                                                              rhs=Hb[:, c, :], start=True, stop=False), [artm[hp], Hb], [by])
                        op("pe", lambda e, h=h, co=co, hs=hs, cp=cp: e.matmul(by[co:co + 64, hs], lhsT=AX1[:, h, 128 + co:128 + co + 64],
                                                                              rhs=Us[cp][:, hs], start=False, stop=False), [AX1, Us[cp]], [by])
                        op("pe", lambda e, h=h, c=c, co=co, hs=hs, chs=chs: e.matmul(by[co:co + 64, hs], lhsT=AX2[:, h, 128 + co:128 + co + 64],
                                                                                     rhs=Vtm[:, c, chs], start=False, stop=True), [AX2, Vtm], [by])
                        op("pe", lambda e, bh=bh, c=c, po=po, hs=hs, chs=chs, cp=cp: e.matmul(bh[po:po + 64, c * 64:(c + 1) * 64], lhsT=btm[:, c, chs],
                                                                                              rhs=Us[cp][:, hs], start=True, stop=False), [btm, Us[cp]], [bh])
                        op("pe", lambda e, bh=bh, c=c, po=po, chs=chs, cp=cp: e.matmul(bh[po:po + 64, c * 64:(c + 1) * 64], lhsT=ktm2[:, c, chs],
                                                                                       rhs=Vm[cp][:, c, chs], start=False, stop=True), [ktm2, Vm[cp]], [bh])
                    Htmp = ring5()
                    op("dve", lambda e, bh=bh, Htmp=Htmp: e.tensor_tensor(out=Htmp[:, 0:256], in0=bh[:, 0:256],
                                                                          in1=Hst[:].rearrange("p c v -> p (c v)"), op=ALU.add), [bh, Hst], [Htmp])
                    op("dve", lambda e, cc=cc, Htmp=Htmp: e.tensor_tensor(out=Hst[:], in0=Htmp[:, 0:256].rearrange("p (c v) -> p c v", v=64),
                                                                          in1=WCt[:, :, cc:cc + 1].broadcast_to([128, 4, 64]), op=ALU.mult),
                       [Htmp, WCt], [Hst])
                    op("act", lambda e: e.activation(out=Hb[:], in_=Hst[:], func=AF.Copy), [Hst], [Hb])
                ysb, ysq, yn = ring5(), ring5(), ring5()
                op("act", lambda e, ysb=ysb: e.activation(out=ysb[:, 0:512], in_=by[:, :], func=AF.Copy), [by], [ysb])
                op("act", lambda e, ysq=ysq: e.activation(out=ysq[:, 0:512], in_=by[:, :], func=AF.Square), [by], [ysq])
                head_norm(op, st4, ysb, ysq, yn, 8, 64, eps_w, CT)
                op("dve", lambda e, yn=yn: e.tensor_tensor(out=yn[:, 0:512], in0=yn[:, 0:512], in1=rows3[:, 1, :], op=ALU.mult), [yn, rows3], [yn])
                op("pool", lambda e, yn=yn: e.tensor_tensor(out=yn[:, 0:512], in0=yn[:, 0:512], in1=rows3[:, 2, :], op=ALU.add), [yn, rows3], [yn])
                bt_ = pb()
                for c in range(4):
                    op("pe", lambda e, bt_=bt_, c=c, yn=yn: e.transpose(out=bt_[:, c * 128:(c + 1) * 128], in_=yn[:, c * 128:(c + 1) * 128],
                                                                        identity=cst("ident")), [yn, CT], [bt_])
                yf = ring5()
                op("dve", lambda e, bt_=bt_, yf=yf, js=js: e.tensor_tensor(out=yf[:, 0:512].rearrange("p (c t) -> p c t", t=128),
                                                                           in0=bt_[:, :].rearrange("p (c t) -> p c t", t=128), in1=bon[:, :, js], op=ALU.add),
                   [bt_, bon], [yf])
                op("pool", lambda e, yf=yf, js=js: e.tensor_tensor(out=orw[:, :, js], in0=yf[:, 0:512].rearrange("p (c t) -> p c t", t=128),
                                                                   in1=srw[:, :, js], op=ALU.mult), [yf, srw], [orw])
            if dbg == "W":
                S.emit()
                return nc
            if dbg == "dump" and l == 0 and it == NT - 1:
                for i_, t_ in enumerate((opool, oret, orw)):
                    dma("sp", lambda e, i_=i_, t_=t_: e.dma_start(out=dbg_o[:, i_ * 4 * TT:(i_ + 1) * 4 * TT], in_=t_[:].rearrange("p c t -> p (c t)")),
                        "dbg", [t_], [b_dbg])
                dma("sp", lambda e: e.dma_start(out=dbg_o[:, 12 * TT:20 * TT], in_=hT[:].rearrange("p c t -> p (c t)")), "dbg", [hT], [b_dbg])
            macc = big
            mT0, mT1 = BB[0], BB[1]
            obr = [opool, oret, orw]
            gts = [Ft[2], Ft[3]]
            for n in range(3):
                for hf in range(2):
                    wgl = wget(l, 13 + 3 * n + 1 + hf)
                    c = 0
                    while c < 4:
                        nn_ = min(CPB, 4 - c)
                        bgl = pb()
                        for i in range(nn_):
                            for k in range(8):
                                op("pe", lambda e, bgl=bgl, k=k, wgl=wgl, i=i, cc=c + i: e.matmul(
                                    bgl[:, i * TT:(i + 1) * TT], lhsT=w8(wgl)[:, k, cc * 128:(cc + 1) * 128], rhs=hT[:, k, :],
                                    start=(k == 0), stop=(k == 7)), [wgl, hT], [bgl])
                        op("act", lambda e, bgl=bgl, c=c, nn_=nn_, hf=hf: e.activation(out=gts[hf][:, c:c + nn_, 0:TT], in_=bview(bgl, nn_), func=AF.Sigmoid),
                           [bgl], [gts[hf]])
                        c += nn_
                    wrel(wgl)
                wbn = wget(l, 13 + 3 * n)
                wbv = wbn[:].rearrange("p (k d) -> p k d", d=1024)
                for dp in range(0, 8, CPB):
                    byn = pb()
                    for i in range(CPB):
                        dc = dp + i
                        for k4 in range(4):
                            op("pe", lambda e, byn=byn, k4=k4, wbv=wbv, dc=dc, i=i, n=n, wbn=wbn: e.matmul(
                                byn[:, i * TT:(i + 1) * TT], lhsT=wbv[:, k4, dc * 128:(dc + 1) * 128], rhs=obr[n][:, k4, :],
                                start=(k4 == 0), stop=(k4 == 3)), [wbn, obr[n]], [byn])
                    hf, dq0 = dp // 4, dp % 4
                    gv = gts[hf][:, dq0:dq0 + CPB, 0:TT]
                    ms = macc[:, dp * TT:(dp + CPB) * TT].rearrange("p (i t) -> p i t", t=TT)
                    yv = bview(byn, CPB)
                    if n == 0:
                        op("dve", lambda e, yv=yv, gv=gv, ms=ms: e.tensor_tensor(out=ms, in0=yv, in1=gv, op=ALU.mult), [byn, gts[hf]], [macc])
                    else:
                        tmp = ring5()
                        tv = tmp[:, 0:CPB * TT].rearrange("p (i t) -> p i t", t=TT)
                        op("dve", lambda e, yv=yv, gv=gv, tv=tv: e.tensor_tensor(out=tv, in0=yv, in1=gv, op=ALU.mult), [byn, gts[hf]], [tmp])
                        if n == 1:
                            op("pool", lambda e, tv=tv, ms=ms: e.tensor_tensor(out=ms, in0=ms, in1=tv, op=ALU.add), [tmp, macc], [macc])
                        else:
                            mt_ = (mT0 if dp < 4 else mT1)
                            op("pool", lambda e, tv=tv, ms=ms, mt_=mt_, dq0=dq0: e.tensor_tensor(out=mt_[:, dq0:dq0 + CPB, :], in0=ms, in1=tv, op=ALU.add),
                               [tmp, macc], [mt_])
                wrel(wbn)
            for hf in range(2):
                wo_ = wget(l, 22 + hf)
                for j in range(NB):
                    bo = pb()
                    for dc in range(8):
                        mt_ = (mT0 if dc < 4 else mT1)
                        op("pe", lambda e, bo=bo, dc=dc, mt_=mt_, j=j, wo_=wo_: e.matmul(bo[:, :], lhsT=mt_[:, dc % 4, j * 128:(j + 1) * 128], rhs=w8(wo_)[:, dc, :],
                                                                                         start=(dc == 0), stop=(dc == 7)), [mt_, wo_], [bo])
                    tmp = ring5()
                    op("dve", lambda e, bo=bo, tmp=tmp, hf=hf: e.tensor_tensor(out=tmp[:, 0:512], in0=bo[:, :], in1=G_bc[:, hf * 512:(hf + 1) * 512], op=ALU.mult),
                       [bo, modbc], [tmp])
                    op("pool", lambda e, tmp=tmp, hf=hf, j=j: e.tensor_tensor(out=xt[:, j, hf * 512:(hf + 1) * 512], in0=xt[:, j, hf * 512:(hf + 1) * 512],
                                                                              in1=tmp[:, 0:512], op=ALU.add), [tmp, xt], [xt])
                wrel(wo_)
            if l < L - 1:
                dma("sp", lambda e, t0=t0: e.dma_start(out=xs[t0:t0 + TT, :].rearrange("(j p) d -> p j d", p=128), in_=xt[:]), "st_x", [xt], [b_xs[it]])
            else:
                if it == 0:
                    dma("sp", lambda e: e.dma_start(out=gbc[:], in_=fg_in[0, :].partition_broadcast(128)), "ld_g", [], [gbc])
                for j in range(NB):
                    op("act", lambda e, j=j: e.activation(out=junk[:], in_=xt[:, j, :], func=AF.Square, accum_out=st4[:, j:j + 1]), [xt], [junk, st4])
                op("act", lambda e: e.activation(out=st4[:, 4:4 + NB], in_=st4[:, 0:NB], func=AF.Sqrt, bias=eps_n, scale=1.0 / D), [st4, CT], [st4])
                op("dve", lambda e: e.reciprocal(out=st4[:, 8:8 + NB], in_=st4[:, 4:4 + NB]), [st4], [st4])
                for j in range(NB):
                    op("dve", lambda e, j=j: e.scalar_tensor_tensor(out=xt[:, j, :], in0=xt[:, j, :], scalar=st4[:, 8 + j:9 + j], in1=gbc[:],
                                                                    op0=ALU.mult, op1=ALU.mult), [xt, st4, gbc], [xt])
                dma("sp", lambda e, t0=t0: e.dma_start(out=out[t0:t0 + TT, :].rearrange("(j p) d -> p j d", p=128), in_=xt[:]), "st_o", [xt], [b_out[it]])
    S.wait_all("sp", b_out + [b_dbg])
    S.emit()
    return nc


def head_norm(op, st4, xs_, sq, on, nh, hd, eps_ap, CT):
    s1 = st4[:, 0:nh]
    xv = xs_[:, 0:nh * hd].rearrange("p (h d) -> p h d", d=hd)
    qv = sq[:, 0:nh * hd].rearrange("p (h d) -> p h d", d=hd)
    ov = on[:, 0:nh * hd].rearrange("p (h d) -> p h d", d=hd)
    m = st4[:, 0:nh]
    v = st4[:, 8:8 + nh]
    op("dve", lambda e: e.tensor_reduce(out=m, in_=xv, axis=AX.X, op=ALU.add), [xs_], [st4])
    op("dve", lambda e: e.tensor_reduce(out=v, in_=qv, axis=AX.X, op=ALU.add), [sq], [st4])
    op("dve", lambda e: e.tensor_scalar(out=m, in0=m, scalar1=1.0 / hd, scalar2=None, op0=ALU.mult), [st4], [st4])
    msq = sq[:, 0:nh]
    op("dve", lambda e: e.tensor_tensor(out=msq, in0=m, in1=m, op=ALU.mult), [st4], [sq])
    op("dve", lambda e: e.scalar_tensor_tensor(out=v, in0=v, scalar=1.0 / hd, in1=msq, op0=ALU.mult, op1=ALU.subtract), [st4, sq], [st4])
    op("act", lambda e: e.activation(out=v, in_=v, func=AF.Sqrt, bias=eps_ap), [st4, CT], [st4])
    op("dve", lambda e: e.reciprocal(out=v, in_=v), [st4], [st4])
    op("dve", lambda e: e.tensor_tensor(out=ov, in0=xv, in1=m.unsqueeze(2).broadcast_to([128, nh, hd]), op=ALU.subtract), [xs_, st4], [on])
    op("dve", lambda e: e.tensor_tensor(out=ov, in0=ov, in1=v.unsqueeze(2).broadcast_to([128, nh, hd]), op=ALU.mult), [on, st4], [on])


_CACHE = {}


def run(inputs, T, L, NB=2, n_cores=8, dbg=None):
    B = inputs["x"].shape[0]
    wd = prep_weights(inputs)
    key = (T, L, NB, dbg)
    if key not in _CACHE:
        _CACHE[key] = build(T, L, NB, dbg)
    nc = _CACHE[key]
    in_maps = []
    x = np.asarray(inputs["x"], np.float32)
    c = np.asarray(inputs["c"], np.float32)
    pos = np.asarray(inputs["positions"], np.int32)
    for core in range(n_cores):
        b = core % B
        m = dict(wd)
        m["x"] = np.ascontiguousarray(x[b])
        m["c"] = np.ascontiguousarray(c[b].reshape(8, 128).T)
        m["pos"] = np.ascontiguousarray(pos[b].reshape(1, T))
        in_maps.append(m)
    import os
    res = run_bass_kernel_spmd(nc, in_maps, core_ids=list(range(n_cores)), **({'trace': True} if os.environ.get('KTRACE') else {}))
    if os.environ.get('KTRACE'):
        print('EXEC_TIME_NS', res.exec_time_ns)
    if dbg == "dump":
        global LAST_DBG
        LAST_DBG = np.asarray(res.results[0]["dbg_o"]).astype(np.float32)
        global LAST_DBGF
        LAST_DBGF = np.asarray(res.results[0]["dbg_f"]).astype(np.float32)
    return np.stack([np.asarray(res.results[b]["out"], np.float32) for b in range(B)], 0)


def kernel(**inputs):
    T = inputs["x"].shape[1]
    L = inputs["w_in"].shape[0]
    return run(inputs, T, L)
```

```python
import contextlib
import numpy as np
import concourse.bass as bass
import concourse.mybir as mybir
from concourse.bass_utils import run_bass_kernel_spmd

F32 = mybir.dt.float32
BF16 = mybir.dt.bfloat16
I32 = mybir.dt.int32
AF = mybir.ActivationFunctionType
ALU = mybir.AluOpType
AX = mybir.AxisListType

D = 1024
W = 512
DIN = 8320
NBLKW = 24
NV = 40
C0 = float(np.exp(-0.5))
GAM = [1.0 - 2.0 ** (-5.0 - h) for h in range(4)]
ENGS = ["pe", "dve", "act", "pool", "sp"]


class Buf:
    __slots__ = ("name", "lw", "rd")

    def __init__(self, name):
        self.name = name
        self.lw = None
        self.rd = []


class _Rec:
    def __getattr__(self, name):
        return lambda *a, **k: (name, a, k)


_REC = _Rec()


class Sched:
    def __init__(self, nc):
        self.nc = nc
        self.ops = {e: [] for e in ENGS}
        self.cnt = {}
        self.keys = []
        self.epoch = 0

    def _key(self, eng):
        return "%s_%d" % (eng, self.epoch)

    def _deps(self, reads, writes, eng=None):
        deps = {}
        pre = None if eng is None else eng + "_"
        for b in reads:
            t = b.lw
            if t is not None and deps.get(t[0], 0) < t[1]:
                deps[t[0]] = t[1]
        for b in writes:
            t = b.lw
            if t is not None and deps.get(t[0], 0) < t[1] and not (pre and t[0].startswith(pre)):
                deps[t[0]] = t[1]
            for t in b.rd:
                if deps.get(t[0], 0) < t[1] and not (pre and t[0].startswith(pre)):
                    deps[t[0]] = t[1]
        return deps

    def _commit(self, tok, reads, writes):
        for b in reads:
            b.rd.append(tok)
        for b in writes:
            b.lw = tok
            b.rd = []

    def _bump(self, key, inc):
        if key not in self.cnt:
            self.cnt[key] = 0
            self.keys.append(key)
        self.cnt[key] += inc
        return (key, self.cnt[key])

    def op(self, eng, fn, reads=(), writes=()):
        deps = self._deps(reads, writes, eng)
        key = self._key(eng)
        tok = self._bump(key, 1)
        self.ops[eng].append((fn(_REC), deps, key, 1))
        self._commit(tok, reads, writes)
        return tok

    def dma(self, eng, fn, semkey, reads=(), writes=()):
        deps = self._deps(reads, writes)
        tok = self._bump(semkey, 16)
        self.ops[eng].append((fn(_REC), deps, semkey, 16))
        self._commit(tok, reads, writes)
        return tok

    def wait_all(self, eng, bufs):
        deps = self._deps(bufs, ())
        self.ops[eng].append((None, deps, None, 0))

    def emit(self):
        nc = self.nc
        with contextlib.ExitStack() as st:
            sems = {}
            for i, k in enumerate(self.keys):
                sems[k] = st.enter_context(nc.semaphore("s%d" % i))
            block = st.enter_context(nc.Block())

            def run(engname, e):
                seen = {}
                for fn, deps, key, inc in self.ops[engname]:
                    for k, v in deps.items():
                        if engname == "pe" and k.startswith("pe_"):
                            continue
                        if seen.get(k, 0) >= v:
                            continue
                        e.wait_ge(sems[k], v)
                        seen[k] = v
                    if fn is not None:
                        getattr(e, fn[0])(*fn[1], **fn[2]).then_inc(sems[key], inc)

            @block.tensor
            def _(e):
                run("pe", e)

            @block.vector
            def _(e):
                run("dve", e)

            @block.scalar
            def _(e):
                run("act", e)

            @block.gpsimd
            def _(e):
                run("pool", e)

            @block.sync
            def _(e):
                run("sp", e)


class Tile:
    def __init__(self, nc, name, shape, dtype, psum=False):
        if psum:
            self.t = nc.alloc_psum_tensor("p_" + name, shape, dtype)
        else:
            self.t = nc.alloc_sbuf_tensor("t_" + name, shape, dtype)
        self.b = Buf(name)

    def __getitem__(self, idx):
        return self.t[idx]


def make_consts():
    c = {}
    p = np.arange(128)
    i = np.arange(128)
    mt = np.zeros((128, 4, 128), np.float64)
    dq = np.zeros((128, 4, 128), np.float64)
    kt = np.zeros((128, 4), np.float64)
    for h in range(4):
        g = GAM[h]
        diff = i[None, :] - p[:, None]
        mt[:, h, :] = np.where(diff >= 0, g ** np.maximum(diff, 0), 0.0)
        dq[:, h, :] = (g ** (i + 1.0))[None, :]
        kt[:, h] = g ** (127.0 - p)
    c["maskT"] = mt.reshape(128, 512)
    c["decq"] = dq.reshape(128, 512)
    c["ktail"] = kt
    same = (p[:, None] // 64) == (i[None, :] // 64)
    mus = (same & (i[None, :] > p[:, None])).astype(np.float64)
    mui = (same & (i[None, :] >= p[:, None])).astype(np.float64)
    mls = (same & (i[None, :] < p[:, None])).astype(np.float64)
    mu2 = np.concatenate([mus, mui], 1)
    c["MU"] = np.concatenate([mu2, mu2], 1)
    c["ML"] = np.concatenate([mls, mls], 1)
    c["onesbd"] = same.astype(np.float64)
    c["ident"] = np.eye(128)
    sm = np.ones((128, 512)); sm[:, ::64] = 0.0
    c["scanm"] = sm
    half = 64
    fr = (np.float32(10000.0) ** (-(np.arange(half, dtype=np.float32)) / np.float32(half))).astype(np.float64)
    c["freq"] = np.concatenate([fr, fr])[:, None]
    c["sgn"] = np.concatenate([-np.ones(64), np.ones(64)])[:, None]
    ic = np.zeros((128, 4, 16))
    for g, w in enumerate((2, 4, 8, 16)):
        ic[:, g, :] = (1.0 / np.minimum(np.arange(16) + 1, w))[None, :]
    c["invc"] = ic.reshape(128, 64)
    c["eps"] = np.array([1e-6, 1e-5, 64e-5, 1e-18, np.pi / 2])[None, :].repeat(128, 0)
    off = {}
    cols = []
    o = 0
    for k, v in c.items():
        off[k] = (o, v.shape[1])
        o += v.shape[1]
        cols.append(v)
    return np.concatenate(cols, 1).astype(np.float32), off


CONSTS, COFF = make_consts()
NCONST = CONSTS.shape[1]


def prep_weights(inp):
    L = inp["w_in"].shape[0]
    w_in = np.asarray(inp["w_in"], np.float32)
    wblk = np.zeros((L, NBLKW, 128, 4096), np.float32)

    def inblk(cols):
        a = w_in[:, :, cols]
        return a.reshape(L, 8, 128, len(cols)).transpose(0, 2, 1, 3)

    def put(b, cols):
        a = inblk(cols)
        n = a.shape[3]
        tmp = np.zeros((L, 128, 8, 512), np.float32)
        tmp[:, :, :, :n] = a
        wblk[:, b] = tmp.reshape(L, 128, 4096)

    ar = np.arange
    sw = np.concatenate([np.concatenate([ar(h * 128 + 64, h * 128 + 128), ar(h * 128, h * 128 + 64)]) for h in range(4)])
    put(0, ar(0, 512)); put(1, ar(512, 1024))
    put(2, ar(1024, 1536)); put(3, 1024 + sw)
    put(4, ar(1536, 2048)); put(5, 1536 + sw)
    put(6, ar(2048, 2560)); put(7, ar(2560, 3072))
    put(8, ar(3072, 3584)); put(9, ar(3648, 4160)); put(10, ar(4160, 4672))
    put(11, np.concatenate([ar(3584, 3648), ar(4672, 4736)]))
    put(12, ar(4736, 5248))
    wb = np.asarray(inp["w_branch"], np.float32)
    for n in range(3):
        wblk[:, 13 + 3 * n] = wb[:, n].reshape(L, 4, 128, 1024).transpose(0, 2, 1, 3).reshape(L, 128, 4096)
        for hf in range(2):
            put(13 + 3 * n + 1 + hf, 5248 + n * 1024 + hf * 512 + ar(512))
    wo = np.asarray(inp["w_out"], np.float32)
    for hf in range(2):
        a = wo[:, :, hf * 512:(hf + 1) * 512].reshape(L, 8, 128, 512).transpose(0, 2, 1, 3)
        wblk[:, 22 + hf] = a.reshape(L, 128, 4096)
    wsm = np.zeros((L, 128, 1536), np.float32)
    pw = np.asarray(inp["pool_w"], np.float32)
    wsm[:, :, 0:512] = pw.transpose(0, 2, 1, 3).reshape(L, 128, 512)
    wsm[:, 0:64, 512:1024] = np.asarray(inp["rwkv_w2"], np.float32)
    wsm[:, 64:128, 1024:1536] = np.asarray(inp["rwkv_a2"], np.float32)

    def fm(v):
        return np.asarray(v, np.float32).reshape(L, 4, 128).transpose(0, 2, 1)
    mu = np.asarray(inp["rwkv_shift_mu"], np.float32)
    vecs = np.zeros((L, 128, NV), np.float32)
    vecs[:, :, 0:4] = fm(inp["pool_scale"])
    vecs[:, :, 4:8] = fm(mu[:, 0:512])
    vecs[:, :, 8:12] = fm(mu[:, 576:1088])
    vecs[:, :, 12:16] = fm(mu[:, 1088:1600])
    vecs[:, 0:64, 16] = mu[:, 512:576]
    vecs[:, 64:128, 16] = mu[:, 1600:1664]
    vecs[:, :, 17:21] = fm(inp["rwkv_w0"])
    vecs[:, :, 21:25] = fm(inp["rwkv_a0"])
    vecs[:, :, 25:29] = fm(inp["rwkv_k_k"])
    vecs[:, :, 29:33] = fm(inp["rwkv_k_a"])
    vecs[:, :, 33:37] = fm(np.asarray(inp["rwkv_r_k"], np.float32).reshape(L, 512))
    rows = np.stack([np.asarray(inp["ret_norm_g"], np.float32), np.asarray(inp["rwkv_ln_w"], np.float32),
                     np.asarray(inp["rwkv_ln_b"], np.float32)], 1)
    ngb = np.concatenate([np.asarray(inp["norm_g"], np.float32), np.asarray(inp["b_ada"], np.float32)], 1)
    return dict(wblk=wblk, wsm=wsm, vecs=vecs, rows=np.ascontiguousarray(rows), ngb=np.ascontiguousarray(ngb),
                wada=np.ascontiguousarray(np.asarray(inp["w_ada"], np.float32)),
                fg=np.asarray(inp["final_g"], np.float32).reshape(1, D), consts=CONSTS)


def build(T, L, NB=2, dbg=None):
    TT = NB * 128
    NT = T // TT
    NCH = NB * 2
    CPB = 512 // TT
    nc = bass.Bass("TRN2", target_bir_lowering=False)
    S = Sched(nc)

    def dram(name, shape, dt, kind):
        return nc.dram_tensor(name, shape, dt, kind=kind).ap()
    x_in = dram("x", [T, D], F32, "ExternalInput")
    c_in = dram("c", [128, 8], F32, "ExternalInput")
    pos_in = dram("pos", [1, T], I32, "ExternalInput")
    wblk = dram("wblk", [L, NBLKW, 128, 4096], F32, "ExternalInput")
    wsm_in = dram("wsm", [L, 128, 1536], F32, "ExternalInput")
    vecs_in = dram("vecs", [L, 128, NV], F32, "ExternalInput")
    rows_in = dram("rows", [L, 3, 512], F32, "ExternalInput")
    ngb_in = dram("ngb", [L, 4096], F32, "ExternalInput")
    wada_in = dram("wada", [L, D, 3 * D], F32, "ExternalInput")
    fg_in = dram("fg", [1, D], F32, "ExternalInput")
    consts_in = dram("consts", [128, NCONST], F32, "ExternalInput")
    out = dram("out", [T, D], F32, "ExternalOutput")
    dbg_o = dram("dbg_o", [128, 20 * TT], BF16, "ExternalOutput") if dbg == "dump" else None
    b_dbg = Buf("dbg")
    dbg_f = dram("dbg_f", [128, 16384], F32, "ExternalOutput") if dbg == "dump" else None
    dstate = {"off": 0, "map": {}, "on": False}
    global DUMP_MAP
    DUMP_MAP = dstate["map"]
    dstage = None

    def dump(name, ap, rd, width):
        if dbg != "dump" or not dstate["on"]:
            return
        o = dstate["off"]
        dstate["map"][name] = (o, width)
        st_ = hn
        op("dve", lambda e: e.tensor_copy(out=st_[:, 0:width], in_=ap), rd, [st_])
        dma("sp", lambda e: e.dma_start(out=dbg_f[:, o:o + width], in_=st_[:, 0:width]), "dbgf", [st_], [b_dbg])
        dstate["off"] += width
    wsc = dram("wsc", [L, NBLKW, 128, 4096], BF16, "Internal")
    xs = dram("xs", [T, D], F32, "Internal")
    b_wsc = [Buf("wsc%d" % l) for l in range(L)]
    b_xs = [Buf("xs%d" % i) for i in range(NT)]
    b_out = [Buf("out%d" % i) for i in range(NT)]

    def tile(name, shape, dt):
        return Tile(nc, name, shape, dt)

    CT = tile("consts", [128, NCONST], F32)

    def cst(name):
        o, n = COFF[name]
        return CT[:, o:o + n]
    identb = tile("identb", [128, 128], BF16)
    onesb = tile("onesb", [128, 128], BF16)
    cbc = tile("cbc", [128, 8, 128], F32)
    cfm = tile("cfm", [128, 8], F32)
    modbc = tile("modbc", [128, 3 * D], F32)
    gbc = tile("gbc", [128, D], F32)
    rows3 = tile("rows3", [128, 3, 512], F32)
    vecs = tile("vecs", [128, NV], F32)
    wsmb = tile("wsmb", [128, 1536], BF16)
    big = tile("big", [128, 8 * TT], F32)
    bada = tile("bada", [128, TT], F32)
    class _Alias:
        def __init__(self, t, lo, hi, shp=None):
            self.t, self.lo, self.hi, self.b, self.shp = t, lo, hi, t.b, shp

        def __getitem__(self, idx):
            v = self.t[:, self.lo:self.hi]
            if self.shp:
                v = v.rearrange("p (h i) -> p h i", i=self.shp)
            return v[idx]
    xtl = [tile("xt%d" % i, [128, NB, D], F32) for i in range(2)]
    hT = tile("hT", [128, 8, TT], BF16)
    hn = _Alias(big, 0, D)
    hb = tile("hb", [128, D], BF16)
    junk = hb
    wsmf = big
    st4 = tile("st4", [128, 16], F32)
    NSLOT = 3
    wslot = [tile("wslot%d" % i, [128, 4096], BF16) for i in range(NSLOT)]
    FW = TT + 16
    Ft = [tile("F%d" % i, [128, 4, FW], F32) for i in range(4)]
    NRING = 5
    ringt = [tile("ring%d" % i, [128, FW], F32) for i in range(NRING)]
    wkn = "stg dd sig ag cs csx Pin Pex Q kk sqb rn kp prb tb".split()
    wk = {n: tile("wk_" + n, [128, FW], F32) for n in wkn}
    BB = [tile("BB%d" % i, [128, 4, TT], BF16) for i in range(6)]
    artm = [tile("artm%d" % i, [128, 4, NB, 2, 128], BF16) for i in range(2)]
    Vm = [tile("Vm%d" % i, [128, 4, TT], BF16) for i in range(2)]
    opool = tile("opool", [128, 4, TT], BF16)
    oret = tile("oret", [128, 4, TT], BF16)
    orw = tile("orw", [128, 4, TT], BF16)
    puh = tile("puh", [128, 4, 16], F32)
    hal = tile("hal", [128, 16], F32)
    Rst = tile("Rst", [128, 4, 128], F32)
    Rb = tile("Rb", [128, 4, 128], BF16)
    Hst = [tile("Hst%d" % g, [128, 2, 64], F32) for g in range(2)]
    Hb = [tile("Hb%d" % g, [128, 2, 64], BF16) for g in range(2)]
    ksF = tile("ksF", [128, 4, TT], F32)

    Sm = _Alias(hb, 0, 512, 128)
    lor = _Alias(hb, 512, 512 + TT)
    AX1 = tile("AX1", [128, 8, 256], BF16)
    AX2 = tile("AX2", [128, 8, 256], BF16)
    BR = [[tile("BR%d_%d" % (i, g), [128, 4, 256], BF16) for g in range(2)] for i in range(2)]
    Am = [[tile("Am%d_%d" % (i, g), [128, 4, 128], BF16) for g in range(2)] for i in range(2)]
    Gs = [tile("Gs%d" % g, [128, 256], BF16) for g in range(2)]
    Us = [[tile("Us%d_%d" % (i, g), [128, 256], BF16) for g in range(2)] for i in range(2)]
    WCt = tile("WCt", [128, 4, NCH], F32)

    class _PosI:
        b = bada.b

        def __getitem__(self, idx):
            return bada[:, 0:TT].bitcast(I32)[idx]
    posi = _PosI()
    psum_all = nc.alloc_psum_tensor("psum_all", [128, 4096], F32)

    class Bank:
        def __init__(self, i):
            self.i = i
            self.b = Buf("bank%d" % i)

        def __getitem__(self, idx):
            return psum_all[:, self.i * 512:(self.i + 1) * 512][idx]
    banks = [Bank(i) for i in range(8)]

    def mbank(i0, n):
        return psum_all[:, i0 * 512:(i0 + n) * 512]
    ring5t = [tile("ringw%d" % i, [128, 512], F32) for i in range(4)]
    state = {"bank": 0, "ring": 0, "ring5": 0}

    def ring5():
        r = ring5t[state["ring5"] % 4]
        state["ring5"] += 1
        return r

    def pb():
        b = banks[state["bank"] % 8]
        state["bank"] += 1
        return b

    def ring():
        r = ringt[state["ring"] % NRING]
        state["ring"] += 1
        return r

    def op(eng, fn, r, w):
        return S.op(eng, fn, [t.b if hasattr(t, "b") else t for t in r], [t.b if hasattr(t, "b") else t for t in w])

    def dma(eng, fn, key, r, w):
        return S.dma(eng, fn, key, [t.b if hasattr(t, "b") else t for t in r], [t.b if hasattr(t, "b") else t for t in w])

    def cast_layer(l):
        key = "cast%d" % l
        tok = None
        for b in range(NBLKW):
            for hf in range(2):
                tok = S.dma("pool", lambda e, l=l, b=b, hf=hf: e.dma_start(
                    out=wsc[l, b, :, hf * 2048:(hf + 1) * 2048], in_=wblk[l, b, :, hf * 2048:(hf + 1) * 2048]), key)
        b_wsc[l].lw = tok

    order = [0, 1, 2, 3, 4, 5, 6, 7, 11, 8, 9, 10, 12] + [14, 15, 13, 17, 18, 16, 20, 21, 19, 22, 23]
    seq = [(l, b) for l in range(L) for _ in range(NT) for b in order]
    wstate = {"next_load": 0, "next_get": 0}
    wreleased = [False] * len(seq)

    def w_pump():
        while wstate["next_load"] < len(seq):
            n = wstate["next_load"]
            if n >= NSLOT and not wreleased[n - NSLOT]:
                break
            l, b = seq[n]
            s = n % NSLOT
            if b == 11:
                dma("sp", lambda e, l=l, b=b, s=s: e.dma_start(
                    out=wslot[s][:].rearrange("p (k c) -> p k c", c=512)[:, :, 0:128],
                    in_=wsc[l, b].rearrange("p (k c) -> p k c", c=512)[:, :, 0:128]), "ws%d" % s, [b_wsc[l]], [wslot[s]])
            else:
                dma("sp", lambda e, l=l, b=b, s=s: e.dma_start(out=wslot[s][:], in_=wsc[l, b]), "ws%d" % s, [b_wsc[l]], [wslot[s]])
            wstate["next_load"] += 1

    def wget(l, b):
        n = wstate["next_get"]
        assert seq[n] == (l, b), (seq[n], l, b)
        w_pump()
        assert wstate["next_load"] > n, "weight block not loadable (too many held)"
        wstate["next_get"] += 1
        wstate["last"] = n
        t_ = wslot[n % NSLOT]
        t_.widx = n
        return t_

    def wrel(*tiles):
        for t_ in tiles:
            wreleased[t_.widx] = True
        w_pump()

    def w8(wt):
        return wt[:].rearrange("p (k c) -> p k c", c=512)

    dma("sp", lambda e: e.dma_start(out=CT[:], in_=consts_in), "ld_c", [], [CT])
    dma("sp", lambda e: e.dma_start(out=cfm[:], in_=c_in), "ld_c2", [], [cfm])
    cast_layer(0)
    op("act", lambda e: e.activation(out=identb[:], in_=cst("ident"), func=AF.Copy), [CT], [identb])
    op("act", lambda e: e.activation(out=onesb[:], in_=cst("onesbd"), func=AF.Copy), [CT], [onesb])
    op("act", lambda e: e.activation(out=cfm[:], in_=cfm[:], func=AF.Silu), [cfm], [cfm])
    for k in range(8):
        op("dve", lambda e, k=k: e.tensor_scalar(out=cbc[:, k, :], in0=cst("ident"), scalar1=0.0, scalar2=cfm[:, k:k + 1],
                                                 op0=ALU.mult, op1=ALU.add), [CT, cfm], [cbc])
    for t_ in (artm[0], artm[1], Vm[0], Vm[1], Us[0][0], Us[0][1], Us[1][0], Us[1][1], Gs[0], Gs[1]):
        op("pool", lambda e, t_=t_: e.memset(t_[:], 0.0), [], [t_])
    eps_n = cst("eps")[:, 0:1]
    eps_r = cst("eps")[:, 1:2]
    eps_w = cst("eps")[:, 2:3]
    eps_k = cst("eps")[:, 3:4]

    def fm_proj(wt, nch, evac, ncols=128, k_parts=8):
        c = 0
        while c < nch:
            n = min(CPB, nch - c)
            bk = pb()
            for i in range(n):
                for k in range(8):
                    op("pe", lambda e, bk=bk, i=i, k=k, cc=c + i: e.matmul(
                        bk[0:ncols, i * TT:(i + 1) * TT], lhsT=w8(wt)[:, k, cc * 128:cc * 128 + ncols], rhs=hT[:, k, :],
                        start=(k == 0), stop=(k == 7)), [wt, hT], [bk])
            evac(bk, c, n)
            c += n

    def bview(bk, n):
        return bk[:, 0:n * TT].rearrange("p (n t) -> p n t", t=TT)

    for l in range(L):
        S.epoch = l
        if l + 1 < L:
            cast_layer(l + 1)
        dma("sp", lambda e, l=l: e.dma_start(out=vecs[:], in_=vecs_in[l]), "ld_v", [], [vecs])
        dma("sp", lambda e, l=l: e.dma_start(out=wsmf[:, 0:1536], in_=wsm_in[l]), "ld_w", [], [wsmf])
        dma("sp", lambda e, l=l: e.dma_start(out=rows3[:].rearrange("p a c -> p (a c)"),
                                             in_=rows_in[l].rearrange("a c -> (a c)").partition_broadcast(128)), "ld_r", [], [rows3])
        dma("sp", lambda e, l=l: e.dma_start(out=gbc[:], in_=ngb_in[l, 0:D].partition_broadcast(128)), "ld_g", [], [gbc])
        op("act", lambda e: e.activation(out=wsmb[:], in_=wsmf[:, 0:1536], func=AF.Copy), [wsmf], [wsmb])
        NWB = 3 * D // TT
        for nb in range(NWB):
            dma("sp", lambda e, l=l, nb=nb: e.dma_start(
                out=big[:].rearrange("p (k c) -> p k c", c=TT),
                in_=wada_in[l, :, nb * TT:(nb + 1) * TT].rearrange("(k p) c -> p k c", p=128)), "ld_a", [], [big])
            dma("sp", lambda e, l=l, nb=nb: e.dma_start(
                out=bada[:, 0:TT], in_=ngb_in[l, D + nb * TT:D + (nb + 1) * TT].partition_broadcast(128)), "ld_b", [], [bada])
            bk = pb()
            for k in range(8):
                op("pe", lambda e, bk=bk, k=k: e.matmul(bk[:, 0:TT], lhsT=cbc[:, k, :], rhs=big[:, k * TT:(k + 1) * TT],
                                                        start=(k == 0), stop=(k == 7)), [cbc, big], [bk])
            op("dve", lambda e, bk=bk, nb=nb: e.tensor_tensor(out=modbc[:, nb * TT:(nb + 1) * TT], in0=bk[:, 0:TT], in1=bada[:, 0:TT],
                                                              op=ALU.add), [bk, bada], [modbc])
        op("dve", lambda e: e.scalar_tensor_tensor(out=modbc[:, D:2 * D], in0=modbc[:, D:2 * D], scalar=1.0, in1=gbc[:],
                                                   op0=ALU.add, op1=ALU.mult), [modbc, gbc], [modbc])
        B_bc = modbc[:, 0:D]
        A_bc = modbc[:, D:2 * D]
        G_bc = modbc[:, 2 * D:3 * D]
        op("pool", lambda e: e.memset(puh[:], 0.0), [], [puh])
        op("pool", lambda e: e.memset(hal[:], 0.0), [], [hal])
        op("pool", lambda e: e.memset(Rst[:], 0.0), [], [Rst])
        op("pool", lambda e: e.memset(Rb[:], 0.0), [], [Rb])
        for g_ in range(2):
            op("pool", lambda e, g_=g_: e.memset(Hst[g_][:], 0.0), [], [Hst[g_]])
            op("pool", lambda e, g_=g_: e.memset(Hb[g_][:], 0.0), [], [Hb[g_]])
        poolw = wsmb[:, 0:512].rearrange("p (g d) -> p g d", d=128)
        lw2 = wsmb[:, 512:1024]
        la2 = wsmb[:, 1024:1536]

        def V(i, n=4):
            return vecs[:, i:i + n]

        for it in range(NT):
            t0 = it * TT
            gi = l * NT + it
            xt = xtl[gi % 2]

            def load_x(l_, it_):
                g_ = l_ * NT + it_
                src = x_in if l_ == 0 else xs
                rd = [] if l_ == 0 else [b_xs[it_]]
                dma("sp", lambda e: e.dma_start(out=xtl[g_ % 2][:], in_=src[it_ * TT:(it_ + 1) * TT, :].rearrange("(j p) d -> p j d", p=128)),
                    "ld_x%d" % (g_ % 2), rd, [xtl[g_ % 2]])
            if gi == 0 or NT == 1:
                load_x(l, it)
            if dbg == "pro":
                S.emit()
                return nc
            dstate["on"] = (l == 0 and it == NT - 1)
            for j in range(NB):
                op("act", lambda e, j=j: e.activation(out=junk[:], in_=xt[:, j, :], func=AF.Square, accum_out=st4[:, j:j + 1]),
                   [xt], [junk, st4])
            op("act", lambda e: e.activation(out=st4[:, 4:4 + NB], in_=st4[:, 0:NB], func=AF.Sqrt, bias=eps_n, scale=1.0 / D),
               [st4, CT], [st4])
            op("dve", lambda e: e.reciprocal(out=st4[:, 8:8 + NB], in_=st4[:, 4:4 + NB]), [st4], [st4])
            for j in range(NB):
                op("dve", lambda e, j=j: e.scalar_tensor_tensor(out=hn[:], in0=xt[:, j, :], scalar=st4[:, 8 + j:9 + j], in1=A_bc,
                                                                op0=ALU.mult, op1=ALU.mult), [xt, st4, modbc], [hn])
                op("pool", lambda e: e.tensor_tensor(out=hb[:], in0=hn[:], in1=B_bc, op=ALU.add), [hn, modbc], [hb])
                bk = pb()
                bkb = bk[:].bitcast(BF16)
                for k in range(8):
                    op("pe", lambda e, bkb=bkb, k=k: e.transpose(out=bkb[:, k * 128:(k + 1) * 128], in_=hb[:, k * 128:(k + 1) * 128],
                                                                  identity=identb[:]), [hb, identb], [bk])
                op("act", lambda e, bkb=bkb, j=j: e.activation(out=hT[:, :, j * 128:(j + 1) * 128],
                                                               in_=bkb.rearrange("p (k t) -> p k t", t=128), func=AF.Copy), [bk], [hT])
            if dbg == "N":
                S.emit()
                return nc
            pu, sa, sb, sg = Ft
            op("pool", lambda e: e.tensor_copy(out=pu[:, :, 0:16], in_=puh[:]), [puh], [pu])
            wt = wget(l, 0)
            fm_proj(wt, 4, lambda bk, c, n: op("act", lambda e: e.activation(out=pu[:, c:c + n, 16:16 + TT], in_=bview(bk, n), func=AF.Copy),
                                               [bk], [pu]))
            wrel(wt)
            wt = wget(l, 1)
            fm_proj(wt, 4, lambda bk, c, n: op("act", lambda e: e.activation(out=sg[:, c:c + n, 0:TT], in_=bview(bk, n), func=AF.Silu),
                                               [bk], [sg]))
            wrel(wt)
            op("pool", lambda e: e.tensor_copy(out=puh[:], in_=pu[:, :, TT:TT + 16]), [pu], [puh])
            for g in range(4):
                cur = pu
                for m in range(g + 1):
                    sh = 1 << m
                    lo = (1 << (m + 1)) - 1
                    dst = sa if (m % 2 == 0) else sb
                    op("dve", lambda e, g=g, cur=cur, dst=dst, sh=sh, lo=lo: e.tensor_tensor(
                        out=dst[:, g, lo:FW], in0=cur[:, g, lo:FW], in1=cur[:, g, lo - sh:FW - sh], op=ALU.add), [cur], [dst])
                    cur = dst
                wg = float(1 << (g + 1))
                dpool = BB[0]
                op("dve", lambda e, g=g, cur=cur, wg=wg: e.scalar_tensor_tensor(
                    out=dpool[:, g, :], in0=cur[:, g, 16:FW], scalar=1.0 / wg, in1=pu[:, g, 16:FW], op0=ALU.mult, op1=ALU.subtract),
                    [cur, pu], [dpool])
                if it == 0:
                    r1 = ring()
                    op("dve", lambda e, g=g, cur=cur, r1=r1: e.tensor_tensor(out=r1[:, 0:16], in0=cur[:, g, 16:32],
                                                                             in1=cst("invc")[:, g * 16:(g + 1) * 16], op=ALU.mult), [cur, CT], [r1])
                    op("dve", lambda e, g=g, r1=r1: e.tensor_tensor(out=dpool[:, g, 0:16], in0=r1[:, 0:16], in1=pu[:, g, 16:32],
                                                                    op=ALU.subtract), [r1, pu], [dpool])
            for g in range(4):
                bk = pb()
                op("pe", lambda e, bk=bk, g=g: e.matmul(bk[:, 0:TT], lhsT=poolw[:, g, :], rhs=BB[0][:, g, :], start=True, stop=True),
                   [wsmb, BB[0]], [bk])
                op("dve", lambda e, bk=bk, g=g: e.scalar_tensor_tensor(out=opool[:, g, :], in0=bk[:, 0:TT], scalar=V(0)[:, g:g + 1],
                                                                        in1=sg[:, g, 0:TT], op0=ALU.mult, op1=ALU.mult), [bk, vecs, sg], [opool])
            if dbg == "P":
                S.emit()
                return nc
            if it + 1 < NT:
                load_x(l, it + 1)
            elif l + 1 < L and NT > 1:
                load_x(l + 1, 0)
            tabs = Ft[0]
            sret = Ft[1]
            qr, qd, kr, ktm, vtm = BB[0], BB[1], BB[2], BB[3], BB[4]
            dma("sp", lambda e, t0=t0: e.dma_start(out=posi[:], in_=pos_in[0, t0:t0 + TT].partition_broadcast(128)), "ld_p", [], [posi])
            ang, nn, r2 = ring(), ring(), ring()
            op("dve", lambda e: e.tensor_copy(out=ang[:, 0:TT], in_=posi[:]), [posi], [ang])
            op("dve", lambda e: e.tensor_scalar(out=ang[:, 0:TT], in0=ang[:, 0:TT], scalar1=cst("freq"), scalar2=None, op0=ALU.mult),
               [ang, CT], [ang])
            op("dve", lambda e: e.tensor_scalar(out=nn[:, 0:TT], in0=ang[:, 0:TT], scalar1=float(1.0 / (2 * np.pi)), scalar2=None,
                                                op0=ALU.mult), [ang], [nn])
            op("dve", lambda e: e.tensor_copy(out=posi[:], in_=nn[:, 0:TT]), [nn], [posi])
            op("dve", lambda e: e.tensor_copy(out=nn[:, 0:TT], in_=posi[:]), [posi], [nn])
            c1 = 6.28125
            c2 = float(2 * np.pi - 6.28125)
            op("dve", lambda e: e.scalar_tensor_tensor(out=r2[:, 0:TT], in0=nn[:, 0:TT], scalar=-c1, in1=ang[:, 0:TT],
                                                       op0=ALU.mult, op1=ALU.add), [nn, ang], [r2])
            op("dve", lambda e: e.scalar_tensor_tensor(out=r2[:, 0:TT], in0=nn[:, 0:TT], scalar=-c2, in1=r2[:, 0:TT],
                                                       op0=ALU.mult, op1=ALU.add), [nn, r2], [r2])
            ws_, wc_, mk_ = ring(), ring(), ring()
            twopi = float(2 * np.pi)
            op("dve", lambda e: e.tensor_scalar(out=mk_[:, 0:TT], in0=r2[:, 0:TT], scalar1=0.0, scalar2=float(np.pi), op0=ALU.add, op1=ALU.is_gt),
               [r2], [mk_])
            op("dve", lambda e: e.scalar_tensor_tensor(out=ws_[:, 0:TT], in0=mk_[:, 0:TT], scalar=-twopi, in1=r2[:, 0:TT], op0=ALU.mult, op1=ALU.add),
               [mk_, r2], [ws_])
            op("dve", lambda e: e.tensor_scalar(out=mk_[:, 0:TT], in0=r2[:, 0:TT], scalar1=float(np.pi / 2), scalar2=float(np.pi), op0=ALU.add,
                                                op1=ALU.is_gt), [r2], [mk_])
            op("dve", lambda e: e.scalar_tensor_tensor(out=wc_[:, 0:TT], in0=mk_[:, 0:TT], scalar=-twopi, in1=r2[:, 0:TT], op0=ALU.mult, op1=ALU.add),
               [mk_, r2], [wc_])
            op("act", lambda e: e.activation(out=tabs[:, 0, 0:TT], in_=wc_[:, 0:TT], func=AF.Sin, bias=cst("eps")[:, 4:5]), [wc_, CT], [tabs])
            op("act", lambda e: e.activation(out=tabs[:, 1, 0:TT], in_=ws_[:, 0:TT], func=AF.Sin, scale=cst("sgn")), [ws_, CT], [tabs])
            op("pool", lambda e: e.tensor_scalar(out=tabs[:, 2:4, 0:TT], in0=tabs[:, 0:2, 0:TT], scalar1=float(128.0 ** -0.5), scalar2=None,
                                                 op0=ALU.mult), [tabs], [tabs])

            def rope(dst, dq, tc):
                wa = wget(l, 2 if dst is qr else 4)
                wb_ = wget(l, 3 if dst is qr else 5)
                for h in range(4):
                    ba, bb = pb(), pb()
                    for (bk_, wt_) in ((ba, wa), (bb, wb_)):
                        for k in range(8):
                            op("pe", lambda e, bk_=bk_, wt_=wt_, k=k, h=h: e.matmul(
                                bk_[:, 0:TT], lhsT=w8(wt_)[:, k, h * 128:(h + 1) * 128], rhs=hT[:, k, :], start=(k == 0), stop=(k == 7)),
                                [wt_, hT], [bk_])
                    t1, t2 = ring(), ring()
                    op("dve", lambda e, ba=ba, t1=t1: e.tensor_tensor(out=t1[:, 0:TT], in0=ba[:, 0:TT], in1=tabs[:, tc, 0:TT], op=ALU.mult),
                       [ba, tabs], [t1])
                    op("dve", lambda e, bb=bb, t2=t2: e.tensor_tensor(out=t2[:, 0:TT], in0=bb[:, 0:TT], in1=tabs[:, tc + 1, 0:TT], op=ALU.mult),
                       [bb, tabs], [t2])
                    if dq is None:
                        op("pool", lambda e, t1=t1, t2=t2, h=h: e.tensor_tensor(out=dst[:, h, :], in0=t1[:, 0:TT], in1=t2[:, 0:TT], op=ALU.add),
                           [t1, t2], [dst])
                    else:
                        op("pool", lambda e, t1=t1, t2=t2: e.tensor_tensor(out=t1[:, 0:TT], in0=t1[:, 0:TT], in1=t2[:, 0:TT], op=ALU.add),
                           [t1, t2], [t1])
                        op("act", lambda e, t1=t1, h=h: e.activation(out=dst[:, h, :], in_=t1[:, 0:TT], func=AF.Copy), [t1], [dst])
                        op("pool", lambda e, t1=t1, h=h: e.tensor_tensor(
                            out=dq[:, h, :].rearrange("p (j i) -> p j i", i=128), in0=t1[:, 0:TT].rearrange("p (j i) -> p j i", i=128),
                            in1=cst("decq")[:, h * 128:(h + 1) * 128].unsqueeze(1).broadcast_to([128, NB, 128]), op=ALU.mult), [t1, CT], [dq])
                wrel(wa, wb_)
            dump("tabs", tabs[:, :, 0:TT], [tabs], 4 * TT) if False else None
            for i_ in range(4):
                dump("tab%d" % i_, tabs[:, i_, 0:TT], [tabs], TT)
            rope(qr, qd, 0)
            rope(kr, None, 2)
            dump("qr", qr[:].rearrange("p c t -> p (c t)"), [qr], 4 * TT)
            dump("qd", qd[:].rearrange("p c t -> p (c t)"), [qd], 4 * TT)
            dump("kr", kr[:].rearrange("p c t -> p (c t)"), [kr], 4 * TT)
            wt = wget(l, 6)
            for j in range(NB):
                bk = pb()
                for k in range(8):
                    op("pe", lambda e, bk=bk, k=k, j=j, wt=wt: e.matmul(bk[:, :], lhsT=hT[:, k, j * 128:(j + 1) * 128], rhs=w8(wt)[:, k, :],
                                                                      start=(k == 0), stop=(k == 7)), [wt, hT], [bk])
                op("act", lambda e, bk=bk, j=j: e.activation(out=vtm[:, :, j * 128:(j + 1) * 128],
                                                             in_=bk[:, :].rearrange("p (h e) -> p h e", e=128), func=AF.Copy), [bk], [vtm])
            wrel(wt)
            wt = wget(l, 7)
            fm_proj(wt, 4, lambda bk, c, n: op("act", lambda e: e.activation(out=sret[:, c:c + n, 0:TT], in_=bview(bk, n), func=AF.Silu),
                                               [bk], [sret]))
            wrel(wt)
            for j in range(NB):
                js = slice(j * 128, (j + 1) * 128)
                bk = pb()
                bkb = bk[:].bitcast(BF16)
                for h in range(4):
                    op("pe", lambda e, bkb=bkb, h=h, js=js: e.transpose(out=bkb[:, h * 128:(h + 1) * 128], in_=kr[:, h, js], identity=identb[:]),
                       [kr, identb], [bk])
                op("dve", lambda e, bkb=bkb, js=js: e.tensor_tensor(
                    out=ktm[:, :, js], in0=bkb[:, 0:512].rearrange("p (h d) -> p h d", d=128),
                    in1=cst("ktail").unsqueeze(2).broadcast_to([128, 4, 128]), op=ALU.mult), [bk, CT], [ktm])
                bs = pb()
                for h in range(4):
                    op("pe", lambda e, bs=bs, h=h, js=js: e.matmul(bs[:, h * 128:(h + 1) * 128], lhsT=kr[:, h, js], rhs=qr[:, h, js],
                                                                   start=True, stop=True), [kr, qr], [bs])
                op("dve", lambda e, bs=bs: e.tensor_tensor(out=Sm[:].rearrange("p h i -> p (h i)"), in0=bs[:, :], in1=cst("maskT"), op=ALU.mult),
                   [bs, CT], [Sm])
                bo = pb()
                for h in range(4):
                    op("pe", lambda e, bo=bo, h=h, js=js: e.matmul(bo[:, h * 128:(h + 1) * 128], lhsT=Sm[:, h, :], rhs=vtm[:, h, js],
                                                                   start=True, stop=False), [Sm, vtm], [bo])
                    op("pe", lambda e, bo=bo, h=h, js=js: e.matmul(bo[:, h * 128:(h + 1) * 128], lhsT=qd[:, h, js], rhs=Rb[:, h, :],
                                                                   start=False, stop=True), [qd, Rb], [bo])
                bkv = pb()
                for h in range(4):
                    op("pe", lambda e, bkv=bkv, h=h, js=js: e.matmul(bkv[:, h * 128:(h + 1) * 128], lhsT=ktm[:, h, js], rhs=vtm[:, h, js],
                                                                     start=True, stop=True), [ktm, vtm], [bkv])
                for h in range(4):
                    op("dve", lambda e, bkv=bkv, h=h: e.scalar_tensor_tensor(out=Rst[:, h, :], in0=Rst[:, h, :], scalar=float(GAM[h] ** 128),
                                                                              in1=bkv[:, h * 128:(h + 1) * 128], op0=ALU.mult, op1=ALU.add),
                       [Rst, bkv], [Rst])
                op("act", lambda e: e.activation(out=Rb[:], in_=Rst[:], func=AF.Copy), [Rst], [Rb])
                osb, sq, on = ring5(), ring5(), ring5()
                op("act", lambda e, bo=bo, osb=osb: e.activation(out=osb[:, 0:512], in_=bo[:, :], func=AF.Copy), [bo], [osb])
                op("act", lambda e, bo=bo, sq=sq: e.activation(out=sq[:, 0:512], in_=bo[:, :], func=AF.Square), [bo], [sq])
                if j == NB - 1:
                    dump("osb", osb[:, 0:512], [osb], 512)
                    dump("vtm", vtm[:].rearrange("p c t -> p (c t)"), [vtm], 4 * TT)
                    dump("ktm", ktm[:].rearrange("p c t -> p (c t)"), [ktm], 4 * TT)
                    dump("Sm", Sm[:].rearrange("p c t -> p (c t)"), [Sm], 512)
                head_norm(op, st4, osb, sq, on, 4, 128, eps_r, CT)
                if j == NB - 1:
                    dump("on", on[:, 0:512], [on], 512)
                op("dve", lambda e, on=on: e.tensor_tensor(out=on[:, 0:512], in0=on[:, 0:512], in1=rows3[:, 0, :], op=ALU.mult), [on, rows3], [on])
                bt_ = pb()
                for h in range(4):
                    op("pe", lambda e, bt_=bt_, h=h, on=on: e.transpose(out=bt_[:, h * 128:(h + 1) * 128], in_=on[:, h * 128:(h + 1) * 128],
                                                                        identity=cst("ident")), [on, CT], [bt_])
                op("dve", lambda e, bt_=bt_, js=js: e.tensor_tensor(out=oret[:, :, js], in0=bt_[:, :].rearrange("p (h t) -> p h t", t=128),
                                                                    in1=sret[:, :, js], op=ALU.mult), [bt_, sret], [oret])
            if dbg == "R":
                S.emit()
                return nc
            bon = Ft[0]
            srw = Ft[1]
            vsf = Ft[2]
            btt, ktt, vbb, Vtm, btm, ktm2 = BB
            stg_i = [0]

            rsF = Ft[3]

            class CV:
                def __init__(self, t, c):
                    self.t, self.c, self.b = t, c, t.b

                def __getitem__(self, idx):
                    return self.t[idx[0], self.c, idx[1]]

            def shifted(wt_, cc, idx, dst_ap, dst_t):
                bk = pb()
                for k in range(8):
                    op("pe", lambda e, bk=bk, k=k: e.matmul(bk[:, 0:TT], lhsT=w8(wt_)[:, k, cc * 128:(cc + 1) * 128], rhs=hT[:, k, :],
                                                            start=(k == 0), stop=(k == 7)), [wt_, hT], [bk])
                stg, dd = wk["stg"], wk["dd"]
                op("act", lambda e: e.activation(out=stg[:, 1:1 + TT], in_=bk[:, 0:TT], func=AF.Copy), [bk], [stg])
                op("pool", lambda e: e.tensor_copy(out=stg[:, 0:1], in_=hal[:, idx:idx + 1]), [hal], [stg])
                op("pool", lambda e: e.tensor_copy(out=hal[:, idx:idx + 1], in_=stg[:, TT:TT + 1]), [stg], [hal])
                op("dve", lambda e: e.tensor_tensor(out=dd[:, 0:TT], in0=stg[:, 0:TT], in1=stg[:, 1:1 + TT], op=ALU.subtract), [stg], [dd])
                mu_i = (4 + idx) if idx < 12 else 16
                op("dve", lambda e: e.scalar_tensor_tensor(out=dst_ap, in0=dd[:, 0:TT], scalar=vecs[:, mu_i:mu_i + 1], in1=stg[:, 1:1 + TT],
                                                           op0=ALU.mult, op1=ALU.add), [dd, vecs, stg], [dst_t])
            wl_ = wget(l, 11)
            lot = wk["sig"]
            shifted(wl_, 0, 12, lot[:, 0:TT], lot)
            wrel(wl_)
            op("act", lambda e: e.activation(out=lor[0:64, :], in_=lot[0:64, 0:TT], func=AF.Tanh), [lot], [lor])
            op("act", lambda e: e.activation(out=lor[64:128, :], in_=lot[64:128, 0:TT], func=AF.Copy), [lot], [lor])
            wr_ = wget(l, 8)
            for c in range(4):
                shifted(wr_, c, c, rsF[:, c, 0:TT], rsF)
            wrel(wr_)
            wk_ = wget(l, 9)
            for c in range(4):
                shifted(wk_, c, 4 + c, ksF[:, c, 0:TT], ksF)
            wrel(wk_)
            wv_ = wget(l, 10)
            for c in range(4):
                shifted(wv_, c, 8 + c, vsf[:, c, 0:TT], vsf)
                op("pool", lambda e, c=c: e.tensor_copy(out=vbb[:, c, :], in_=vsf[:, c, 0:TT]), [vsf], [vbb])
            wrel(wv_)
            for c in range(4):
                rs_, ks_ = CV(rsF, c), CV(ksF, c)
                bw, ba_ = pb(), pb()
                op("pe", lambda e, bw=bw, c=c: e.matmul(bw[:, 0:TT], lhsT=lw2[:, c * 128:(c + 1) * 128], rhs=lor[:, :], start=True, stop=True),
                   [wsmb, lor], [bw])
                op("pe", lambda e, ba_=ba_, c=c: e.matmul(ba_[:, 0:TT], lhsT=la2[:, c * 128:(c + 1) * 128], rhs=lor[:, :],
                                                          start=True, stop=True), [wsmb, lor], [ba_])
                sig, ag = wk["sig"], wk["ag"]
                op("act", lambda e, bw=bw, sig=sig, c=c: e.activation(out=sig[:, 0:TT], in_=bw[:, 0:TT], func=AF.Sigmoid, bias=V(17)[:, c:c + 1]),
                   [bw, vecs], [sig])
                op("act", lambda e, ba_=ba_, ag=ag, c=c: e.activation(out=ag[:, 0:TT], in_=ba_[:, 0:TT], func=AF.Sigmoid, bias=V(21)[:, c:c + 1]),
                   [ba_, vecs], [ag])
                cs, csx = wk["cs"], wk["csx"]
                op("dve", lambda e, sig=sig, cs=cs: e.tensor_tensor_scan(out=cs[:, 0:TT], data0=cst("scanm")[:, 0:TT], data1=sig[:, 0:TT],
                                                                         initial=0.0, op0=ALU.mult, op1=ALU.add), [sig, CT], [cs])
                op("pool", lambda e, sig=sig, cs=cs, csx=csx: e.tensor_tensor(out=csx[:, 0:TT], in0=cs[:, 0:TT], in1=sig[:, 0:TT], op=ALU.subtract),
                   [cs, sig], [csx])
                Pin, Pex, Q = wk["Pin"], wk["Pex"], wk["Q"]
                op("act", lambda e, cs=cs, Pin=Pin: e.activation(out=Pin[:, 0:TT], in_=cs[:, 0:TT], func=AF.Exp, scale=-C0), [cs], [Pin])
                op("act", lambda e, csx=csx, Pex=Pex: e.activation(out=Pex[:, 0:TT], in_=csx[:, 0:TT], func=AF.Exp, scale=-C0), [csx], [Pex])
                op("act", lambda e, cs=cs, Q=Q: e.activation(out=Q[:, 0:TT], in_=cs[:, 0:TT], func=AF.Exp, scale=C0), [cs], [Q])
                op("pool", lambda e, Pin=Pin, c=c: e.tensor_copy(out=WCt[:, c, :], in_=Pin[:, 63:TT:64]), [Pin], [WCt])
                kk, sqb, rn = wk["kk"], wk["sqb"], wk["rn"]
                op("dve", lambda e, ks_=ks_, kk=kk, c=c: e.tensor_scalar(out=kk[:, 0:TT], in0=ks_[:, 0:TT], scalar1=V(25)[:, c:c + 1], scalar2=None,
                                                                         op0=ALU.mult), [ks_, vecs], [kk])
                sqv = sqb[:, 0:TT // 2].bitcast(BF16)
                op("pool", lambda e, kk=kk, sqv=sqv: e.tensor_tensor(out=sqv, in0=kk[:, 0:TT], in1=kk[:, 0:TT], op=ALU.mult), [kk], [sqb])
                bn_ = pb()
                op("pe", lambda e, bn_=bn_, sqv=sqv: e.matmul(bn_[:, 0:TT], lhsT=onesb[:], rhs=sqv, start=True, stop=True), [onesb, sqb], [bn_])
                op("act", lambda e, bn_=bn_, rn=rn: e.activation(out=rn[:, 0:TT], in_=bn_[:, 0:TT], func=AF.Ln, bias=eps_k), [bn_, CT], [rn])
                op("act", lambda e, rn=rn: e.activation(out=rn[:, 0:TT], in_=rn[:, 0:TT], func=AF.Exp, scale=-0.5), [rn], [rn])
                op("dve", lambda e, kk=kk, rn=rn: e.tensor_tensor(out=kk[:, 0:TT], in0=kk[:, 0:TT], in1=rn[:, 0:TT], op=ALU.mult), [kk, rn], [kk])
                kp = wk["kp"]
                op("pool", lambda e, ag=ag, kp=kp, c=c: e.tensor_scalar(out=kp[:, 0:TT], in0=ag[:, 0:TT], scalar1=-1.0, scalar2=V(29)[:, c:c + 1],
                                                                        op0=ALU.add, op1=ALU.mult), [ag, vecs], [kp])
                op("dve", lambda e, kp=kp, ks_=ks_: e.scalar_tensor_tensor(out=kp[:, 0:TT], in0=kp[:, 0:TT], scalar=1.0, in1=ks_[:, 0:TT],
                                                                            op0=ALU.add, op1=ALU.mult), [kp, ks_], [kp])
                prb = wk["prb"]
                prv = prb[:, 0:TT // 2].bitcast(BF16)
                op("dve", lambda e, rs_=rs_, kp=kp, prv=prv, c=c: e.scalar_tensor_tensor(out=prv, in0=rs_[:, 0:TT], scalar=V(33)[:, c:c + 1],
                                                                                        in1=kp[:, 0:TT], op0=ALU.mult, op1=ALU.mult),
                   [rs_, vecs, kp], [prb])
                bb_ = pb()
                op("pe", lambda e, bb_=bb_, prv=prv: e.matmul(bb_[:, 0:TT], lhsT=onesb[:], rhs=prv, start=True, stop=True), [onesb, prb], [bb_])
                op("dve", lambda e, bb_=bb_, c=c: e.tensor_tensor(out=bon[:, c, 0:TT], in0=bb_[:, 0:TT], in1=vsf[:, c, 0:TT], op=ALU.mult),
                   [bb_, vsf], [bon])
                for hp in range(2):
                    rw_ = slice(hp * 64, hp * 64 + 64)
                    op("dve", lambda e, kk=kk, Pex=Pex, hp=hp, rw_=rw_, c=c: e.scalar_tensor_tensor(
                        out=artm[hp][rw_, c, :, 0, :], in0=kk[rw_, 0:TT].rearrange("p (j i) -> p j i", i=128), scalar=-1.0,
                        in1=Pex[rw_, 0:TT].rearrange("p (j i) -> p j i", i=128), op0=ALU.mult, op1=ALU.mult), [kk, Pex], [artm[hp]])
                    op("pool", lambda e, rs_=rs_, Pin=Pin, hp=hp, rw_=rw_, c=c: e.tensor_tensor(
                        out=artm[hp][rw_, c, :, 1, :], in0=rs_[rw_, 0:TT].rearrange("p (j i) -> p j i", i=128),
                        in1=Pin[rw_, 0:TT].rearrange("p (j i) -> p j i", i=128), op=ALU.mult), [rs_, Pin], [artm[hp]])
                tb = wk["tb"]
                op("pool", lambda e, kk=kk, ag=ag, tb=tb: e.tensor_tensor(out=tb[:, 0:TT], in0=kk[:, 0:TT], in1=ag[:, 0:TT], op=ALU.mult),
                   [kk, ag], [tb])
                op("dve", lambda e, tb=tb, Q=Q, c=c: e.tensor_tensor(out=btt[:, c, :], in0=tb[:, 0:TT], in1=Q[:, 0:TT], op=ALU.mult),
                   [tb, Q], [btt])
                op("pool", lambda e, kp=kp, Q=Q, c=c: e.tensor_tensor(out=ktt[:, c, :], in0=kp[:, 0:TT], in1=Q[:, 0:TT], op=ALU.mult),
                   [kp, Q], [ktt])
            wg_ = wget(l, 12)
            fm_proj(wg_, 4, lambda bk, c, n: op("act", lambda e: e.activation(out=srw[:, c:c + n, 0:TT], in_=bview(bk, n), func=AF.Silu),
                                                [bk], [srw]))
            wrel(wg_)
            for j in range(NB):
                js = slice(j * 128, (j + 1) * 128)
                for (srcT, dstT) in ((vbb, Vtm), (btt, btm), (ktt, ktm2)):
                    bk = pb()
                    bkb = bk[:].bitcast(BF16)
                    for c in range(4):
                        op("pe", lambda e, bkb=bkb, c=c, srcT=srcT, js=js: e.transpose(out=bkb[:, c * 128:(c + 1) * 128], in_=srcT[:, c, js],
                                                                                      identity=identb[:]), [srcT, identb], [bk])
                    op("act", lambda e, bkb=bkb, dstT=dstT, js=js: e.activation(out=dstT[:, :, js], in_=bkb[:, 0:512].rearrange("p (c k) -> p c k", k=128),
                                                                                func=AF.Copy), [bk], [dstT])
                    if dstT is Vtm:
                        for cp_ in range(2):
                            rw_ = slice(cp_ * 64, cp_ * 64 + 64)
                            op("act", lambda e, bkb=bkb, js=js, cp_=cp_, rw_=rw_: e.activation(
                                out=Vm[cp_][rw_, :, js], in_=bkb[rw_, 0:512].rearrange("p (c k) -> p c k", k=128), func=AF.Copy), [bk], [Vm[cp_]])
                for half in range(2):
                    for ci in range(2):
                        c = half * 2 + ci
                        for hp in range(2):
                            arv = artm[hp][:, c, j, :, :].rearrange("p a i -> p (a i)")
                            op("pe", lambda e, ci=ci, hp=hp, c=c, arv=arv, js=js: e.matmul(banks[0 + ci][:, hp * 256:(hp + 1) * 256], lhsT=btt[:, c, js],
                                                                                           rhs=arv, start=True, stop=True), [btt, artm[hp]], [banks[0 + ci]])
                            op("pe", lambda e, ci=ci, hp=hp, c=c, arv=arv, js=js: e.matmul(banks[2 + ci][:, hp * 256:(hp + 1) * 256], lhsT=ktt[:, c, js],
                                                                                           rhs=arv, start=True, stop=True), [ktt, artm[hp]], [banks[2 + ci]])
                            op("pe", lambda e, ci=ci, hp=hp, c=c, js=js: e.matmul(banks[4][:, (ci * 2 + hp) * 128:(ci * 2 + hp + 1) * 128],
                                                                                  lhsT=artm[hp][:, c, j, 0, :], rhs=btt[:, c, js], start=True, stop=True),
                               [btt, artm[hp]], [banks[4]])
                    h0 = 4 * half
                    mu_b = cst("MU").unsqueeze(1).broadcast_to([128, 2, 512])
                    ml_b = cst("ML").unsqueeze(1).broadcast_to([128, 2, 256])
                    op("dve", lambda e, h0=h0, mu_b=mu_b: e.tensor_tensor(out=AX1[:, h0:h0 + 4, :].rearrange("p (a h) x -> p a (h x)", a=2),
                                                                          in0=mbank(0, 2).rearrange("p (a x) -> p a x", a=2), in1=mu_b, op=ALU.mult),
                       [banks[0], banks[1], CT], [AX1])
                    op("dve", lambda e, h0=h0, mu_b=mu_b: e.tensor_tensor(out=AX2[:, h0:h0 + 4, :].rearrange("p (a h) x -> p a (h x)", a=2),
                                                                          in0=mbank(2, 2).rearrange("p (a x) -> p a x", a=2), in1=mu_b, op=ALU.mult),
                       [banks[2], banks[3], CT], [AX2])
                    op("dve", lambda e, half=half, ml_b=ml_b: e.tensor_tensor(out=Am[0][half][:].rearrange("p (a h) x -> p a (h x)", a=2),
                                                                              in0=banks[4][:, :].rearrange("p (a x) -> p a x", a=2), in1=ml_b, op=ALU.mult),
                       [banks[4], CT], [Am[0][half]])
                for g in range(2):
                    hsl = slice(4 * g, 4 * g + 4)
                    op("pool", lambda e, g=g, hsl=hsl: e.tensor_copy(out=BR[0][g][:, :, 0:128], in_=AX1[:, hsl, 0:128]), [AX1], [BR[0][g]])
                    op("pool", lambda e, g=g, hsl=hsl: e.tensor_tensor(out=BR[0][g][:, :, 128:256], in0=AX1[:, hsl, 0:128],
                                                                       in1=identb[:].unsqueeze(1).broadcast_to([128, 4, 128]), op=ALU.add),
                       [AX1, identb], [BR[0][g]])
                for m in range(6):
                    cur, nxt = m % 2, (m + 1) % 2
                    last = (m == 5)
                    for g in range(2):
                        bqg = [banks[2 * g], banks[2 * g + 1]]
                        bag = banks[4 + g]
                        for hh in range(4):
                            bq = bqg[hh // 2]
                            hp = hh % 2
                            if m == 0:
                                op("pe", lambda e, bq=bq, hp=hp, hh=hh, g=g: e.matmul(bq[:, hp * 256:hp * 256 + 128], lhsT=Am[0][g][:, hh, :],
                                                                                      rhs=BR[0][g][:, hh, 0:128], start=True, stop=True),
                                   [Am[0][g], BR[0][g]], [bq])
                            elif not last:
                                op("pe", lambda e, bq=bq, hp=hp, hh=hh, g=g, cur=cur: e.matmul(bq[:, hp * 256:(hp + 1) * 256], lhsT=Am[cur][g][:, hh, :],
                                                                                               rhs=BR[cur][g][:, hh, :], start=True, stop=True),
                                   [Am[cur][g], BR[cur][g]], [bq])
                            else:
                                op("pe", lambda e, bq=bq, hp=hp, hh=hh, g=g, cur=cur: e.matmul(bq[:, hp * 256 + 128:(hp + 1) * 256], lhsT=Am[cur][g][:, hh, :],
                                                                                               rhs=BR[cur][g][:, hh, 128:256], start=True, stop=True),
                                   [Am[cur][g], BR[cur][g]], [bq])
                            if not last:
                                op("pe", lambda e, bag=bag, hh=hh, g=g, cur=cur: e.matmul(bag[:, hh * 128:(hh + 1) * 128], lhsT=BR[cur][g][:, hh, 0:128],
                                                                                          rhs=Am[cur][g][:, hh, :], start=True, stop=True),
                                   [Am[cur][g], BR[cur][g]], [bag])
                        bqv = mbank(2 * g, 2).rearrange("p (h x) -> p h x", x=256)
                        if not last:
                            op("act", lambda e, bqv=bqv, nxt=nxt, g=g: e.activation(out=BR[nxt][g][:, :, 0:128], in_=bqv[:, :, 0:128], func=AF.Copy),
                               bqg, [BR[nxt][g]])
                            op("act", lambda e, nxt=nxt, g=g, bag=bag: e.activation(out=Am[nxt][g][:], in_=bag[:, :].rearrange("p (h x) -> p h x", x=128),
                                                                                    func=AF.Copy), [bag], [Am[nxt][g]])
                        if m == 0:
                            op("pool", lambda e, nxt=nxt, g=g: e.tensor_copy(out=BR[nxt][g][:, :, 128:256], in_=BR[0][g][:, :, 128:256]),
                               [BR[0][g]], [BR[nxt][g]])
                        else:
                            op("dve", lambda e, bqv=bqv, nxt=nxt, cur=cur, g=g: e.tensor_tensor(out=BR[nxt][g][:, :, 128:256], in0=bqv[:, :, 128:256],
                                                                                                in1=BR[cur][g][:, :, 128:256], op=ALU.add),
                               bqg + [BR[cur][g]], [BR[nxt][g]])
                TTm = BR[0]
                by = banks[6]
                for cp in range(2):
                    co = cp * 64
                    cc = 2 * j + cp
                    bgs, bus, bhs = [banks[0], banks[1]], [banks[2], banks[3]], [banks[4], banks[5]]

                    def hinfo(g, hh):
                        h = 4 * g + hh
                        c, hp, po = h // 2, h % 2, (h % 2) * 64
                        chs = slice(j * 128 + po, j * 128 + po + 64)
                        return h, c, hp, po, chs, slice(hh * 64, (hh + 1) * 64), slice(h * 64, (h + 1) * 64)
                    for g in range(2):
                        for hh in range(4):
                            h, c, hp, po, chs, ls, hs = hinfo(g, hh)
                            op("pe", lambda e, g=g, c=c, hp=hp, ls=ls: e.matmul(
                                bgs[g][co:co + 64, ls], lhsT=artm[hp][:, c, j, 0, co:co + 64], rhs=Hb[g][:, c % 2, :], start=True, stop=False),
                                [artm[hp], Hb[g]], [bgs[g]])
                            op("pe", lambda e, g=g, h=h, c=c, ls=ls, chs=chs: e.matmul(
                                bgs[g][co:co + 64, ls], lhsT=AX2[:, h, co:co + 64], rhs=Vtm[:, c, chs], start=False, stop=True),
                                [AX2, Vtm], [bgs[g]])
                    for g in range(2):
                        op("act", lambda e, g=g: e.activation(out=Gs[g][co:co + 64, :], in_=bgs[g][co:co + 64, 0:256], func=AF.Copy), [bgs[g]], [Gs[g]])
                    for g in range(2):
                        for hh in range(4):
                            h, c, hp, po, chs, ls, hs = hinfo(g, hh)
                            op("pe", lambda e, g=g, hh=hh, ls=ls: e.matmul(bus[g][co:co + 64, ls], lhsT=TTm[g][:, hh, 128 + co:128 + co + 64],
                                                                           rhs=Gs[g][:, ls], start=True, stop=True), [TTm[g], Gs[g]], [bus[g]])
                    for g in range(2):
                        op("act", lambda e, g=g: e.activation(out=Us[cp][g][co:co + 64, :], in_=bus[g][co:co + 64, 0:256], func=AF.Copy),
                           [bus[g]], [Us[cp][g]])
                    for g in range(2):
                        for hh in range(4):
                            h, c, hp, po, chs, ls, hs = hinfo(g, hh)
                            op("pe", lambda e, g=g, c=c, hp=hp, hs=hs: e.matmul(by[co:co + 64, hs], lhsT=artm[hp][:, c, j, 1, co:co + 64],
                                                                                rhs=Hb[g][:, c % 2, :], start=True, stop=False), [artm[hp], Hb[g]], [by])
                            op("pe", lambda e, g=g, h=h, hs=hs, ls=ls: e.matmul(by[co:co + 64, hs], lhsT=AX1[:, h, 128 + co:128 + co + 64],
                                                                                rhs=Us[cp][g][:, ls], start=False, stop=False), [AX1, Us[cp][g]], [by])
                            op("pe", lambda e, h=h, c=c, hs=hs, chs=chs: e.matmul(by[co:co + 64, hs], lhsT=AX2[:, h, 128 + co:128 + co + 64],
                                                                                  rhs=Vtm[:, c, chs], start=False, stop=True), [AX2, Vtm], [by])
                            op("pe", lambda e, g=g, c=c, po=po, ls=ls, chs=chs: e.matmul(bhs[g][po:po + 64, (c % 2) * 64:(c % 2 + 1) * 64], lhsT=btm[:, c, chs],
                                                                                         rhs=Us[cp][g][:, ls], start=True, stop=False), [btm, Us[cp][g]], [bhs[g]])
                            op("pe", lambda e, g=g, c=c, po=po, chs=chs: e.matmul(bhs[g][po:po + 64, (c % 2) * 64:(c % 2 + 1) * 64], lhsT=ktm2[:, c, chs],
                                                                                  rhs=Vm[cp][:, c, chs], start=False, stop=True), [ktm2, Vm[cp]], [bhs[g]])
                        Htmp = ring5()
                        op("dve", lambda e, g=g, Htmp=Htmp: e.tensor_tensor(out=Htmp[:, 0:128], in0=bhs[g][:, 0:128],
                                                                            in1=Hst[g][:].rearrange("p c v -> p (c v)"), op=ALU.add), [bhs[g], Hst[g]], [Htmp])
                        op("dve", lambda e, g=g, Htmp=Htmp: e.tensor_tensor(out=Hst[g][:], in0=Htmp[:, 0:128].rearrange("p (c v) -> p c v", v=64),
                                                                            in1=WCt[:, 2 * g:2 * g + 2, cc:cc + 1].broadcast_to([128, 2, 64]), op=ALU.mult),
                           [Htmp, WCt], [Hst[g]])
                        op("act", lambda e, g=g: e.activation(out=Hb[g][:], in_=Hst[g][:], func=AF.Copy), [Hst[g]], [Hb[g]])
                ysb, ysq, yn = ring5(), ring5(), ring5()
                op("act", lambda e, ysb=ysb: e.activation(out=ysb[:, 0:512], in_=by[:, :], func=AF.Copy), [by], [ysb])
                op("act", lambda e, ysq=ysq: e.activation(out=ysq[:, 0:512], in_=by[:, :], func=AF.Square), [by], [ysq])
                head_norm(op, st4, ysb, ysq, yn, 8, 64, eps_w, CT)
                op("dve", lambda e, yn=yn: e.tensor_tensor(out=yn[:, 0:512], in0=yn[:, 0:512], in1=rows3[:, 1, :], op=ALU.mult), [yn, rows3], [yn])
                op("pool", lambda e, yn=yn: e.tensor_tensor(out=yn[:, 0:512], in0=yn[:, 0:512], in1=rows3[:, 2, :], op=ALU.add), [yn, rows3], [yn])
                bt_ = pb()
                for c in range(4):
                    op("pe", lambda e, bt_=bt_, c=c, yn=yn: e.transpose(out=bt_[:, c * 128:(c + 1) * 128], in_=yn[:, c * 128:(c + 1) * 128],
                                                                        identity=cst("ident")), [yn, CT], [bt_])
                yf = ring5()
                op("dve", lambda e, bt_=bt_, yf=yf, js=js: e.tensor_tensor(out=yf[:, 0:512].rearrange("p (c t) -> p c t", t=128),
                                                                           in0=bt_[:, :].rearrange("p (c t) -> p c t", t=128), in1=bon[:, :, js], op=ALU.add),
                   [bt_, bon], [yf])
                op("pool", lambda e, yf=yf, js=js: e.tensor_tensor(out=orw[:, :, js], in0=yf[:, 0:512].rearrange("p (c t) -> p c t", t=128),
                                                                   in1=srw[:, :, js], op=ALU.mult), [yf, srw], [orw])
            if dbg == "W":
                S.emit()
                return nc
            if dbg == "dump" and l == 0 and it == NT - 1:
                for i_, t_ in enumerate((opool, oret, orw)):
                    dma("sp", lambda e, i_=i_, t_=t_: e.dma_start(out=dbg_o[:, i_ * 4 * TT:(i_ + 1) * 4 * TT], in_=t_[:].rearrange("p c t -> p (c t)")),
                        "dbg", [t_], [b_dbg])
                dma("sp", lambda e: e.dma_start(out=dbg_o[:, 12 * TT:20 * TT], in_=hT[:].rearrange("p c t -> p (c t)")), "dbg", [hT], [b_dbg])
            macc = big
            mT0, mT1 = BB[0], BB[1]
            obr = [opool, oret, orw]
            gts = [Ft[2], Ft[3]]
            for n in range(3):
                for hf in range(2):
                    wgl = wget(l, 13 + 3 * n + 1 + hf)
                    c = 0
                    while c < 4:
                        nn_ = min(CPB, 4 - c)
                        bgl = pb()
                        for i in range(nn_):
                            for k in range(8):
                                op("pe", lambda e, bgl=bgl, k=k, wgl=wgl, i=i, cc=c + i: e.matmul(
                                    bgl[:, i * TT:(i + 1) * TT], lhsT=w8(wgl)[:, k, cc * 128:(cc + 1) * 128], rhs=hT[:, k, :],
                                    start=(k == 0), stop=(k == 7)), [wgl, hT], [bgl])
                        op("act", lambda e, bgl=bgl, c=c, nn_=nn_, hf=hf: e.activation(out=gts[hf][:, c:c + nn_, 0:TT], in_=bview(bgl, nn_), func=AF.Sigmoid),
                           [bgl], [gts[hf]])
                        c += nn_
                    wrel(wgl)
                wbn = wget(l, 13 + 3 * n)
                wbv = wbn[:].rearrange("p (k d) -> p k d", d=1024)
                for dp in range(0, 8, CPB):
                    byn = pb()
                    for i in range(CPB):
                        dc = dp + i
                        for k4 in range(4):
                            op("pe", lambda e, byn=byn, k4=k4, wbv=wbv, dc=dc, i=i, n=n, wbn=wbn: e.matmul(
                                byn[:, i * TT:(i + 1) * TT], lhsT=wbv[:, k4, dc * 128:(dc + 1) * 128], rhs=obr[n][:, k4, :],
                                start=(k4 == 0), stop=(k4 == 3)), [wbn, obr[n]], [byn])
                    hf, dq0 = dp // 4, dp % 4
                    gv = gts[hf][:, dq0:dq0 + CPB, 0:TT]
                    ms = macc[:, dp * TT:(dp + CPB) * TT].rearrange("p (i t) -> p i t", t=TT)
                    yv = bview(byn, CPB)
                    if n == 0:
                        op("dve", lambda e, yv=yv, gv=gv, ms=ms: e.tensor_tensor(out=ms, in0=yv, in1=gv, op=ALU.mult), [byn, gts[hf]], [macc])
                    else:
                        tmp = ring5()
                        tv = tmp[:, 0:CPB * TT].rearrange("p (i t) -> p i t", t=TT)
                        op("dve", lambda e, yv=yv, gv=gv, tv=tv: e.tensor_tensor(out=tv, in0=yv, in1=gv, op=ALU.mult), [byn, gts[hf]], [tmp])
                        if n == 1:
                            op("pool", lambda e, tv=tv, ms=ms: e.tensor_tensor(out=ms, in0=ms, in1=tv, op=ALU.add), [tmp, macc], [macc])
                        else:
                            mt_ = (mT0 if dp < 4 else mT1)
                            op("pool", lambda e, tv=tv, ms=ms, mt_=mt_, dq0=dq0: e.tensor_tensor(out=mt_[:, dq0:dq0 + CPB, :], in0=ms, in1=tv, op=ALU.add),
                               [tmp, macc], [mt_])
                wrel(wbn)
            for hf in range(2):
                wo_ = wget(l, 22 + hf)
                for j in range(NB):
                    bo = pb()
                    for dc in range(8):
                        mt_ = (mT0 if dc < 4 else mT1)
                        op("pe", lambda e, bo=bo, dc=dc, mt_=mt_, j=j, wo_=wo_: e.matmul(bo[:, :], lhsT=mt_[:, dc % 4, j * 128:(j + 1) * 128], rhs=w8(wo_)[:, dc, :],
                                                                                         start=(dc == 0), stop=(dc == 7)), [mt_, wo_], [bo])
                    tmp = ring5()
                    op("dve", lambda e, bo=bo, tmp=tmp, hf=hf: e.tensor_tensor(out=tmp[:, 0:512], in0=bo[:, :], in1=G_bc[:, hf * 512:(hf + 1) * 512], op=ALU.mult),
                       [bo, modbc], [tmp])
                    op("pool", lambda e, tmp=tmp, hf=hf, j=j: e.tensor_tensor(out=xt[:, j, hf * 512:(hf + 1) * 512], in0=xt[:, j, hf * 512:(hf + 1) * 512],
                                                                              in1=tmp[:, 0:512], op=ALU.add), [tmp, xt], [xt])
                wrel(wo_)
            if l < L - 1:
                dma("sp", lambda e, t0=t0: e.dma_start(out=xs[t0:t0 + TT, :].rearrange("(j p) d -> p j d", p=128), in_=xt[:]), "st_x", [xt], [b_xs[it]])
            else:
                if it == 0:
                    dma("sp", lambda e: e.dma_start(out=gbc[:], in_=fg_in[0, :].partition_broadcast(128)), "ld_g", [], [gbc])
                for j in range(NB):
                    op("act", lambda e, j=j: e.activation(out=junk[:], in_=xt[:, j, :], func=AF.Square, accum_out=st4[:, j:j + 1]), [xt], [junk, st4])
                op("act", lambda e: e.activation(out=st4[:, 4:4 + NB], in_=st4[:, 0:NB], func=AF.Sqrt, bias=eps_n, scale=1.0 / D), [st4, CT], [st4])
                op("dve", lambda e: e.reciprocal(out=st4[:, 8:8 + NB], in_=st4[:, 4:4 + NB]), [st4], [st4])
                for j in range(NB):
                    op("dve", lambda e, j=j: e.scalar_tensor_tensor(out=xt[:, j, :], in0=xt[:, j, :], scalar=st4[:, 8 + j:9 + j], in1=gbc[:],
                                                                    op0=ALU.mult, op1=ALU.mult), [xt, st4, gbc], [xt])
                dma("sp", lambda e, t0=t0: e.dma_start(out=out[t0:t0 + TT, :].rearrange("(j p) d -> p j d", p=128), in_=xt[:]), "st_o", [xt], [b_out[it]])
    S.wait_all("sp", b_out + [b_dbg])
    S.emit()
    return nc


def head_norm(op, st4, xs_, sq, on, nh, hd, eps_ap, CT):
    s1 = st4[:, 0:nh]
    xv = xs_[:, 0:nh * hd].rearrange("p (h d) -> p h d", d=hd)
    qv = sq[:, 0:nh * hd].rearrange("p (h d) -> p h d", d=hd)
    ov = on[:, 0:nh * hd].rearrange("p (h d) -> p h d", d=hd)
    m = st4[:, 0:nh]
    v = st4[:, 8:8 + nh]
    op("dve", lambda e: e.tensor_reduce(out=m, in_=xv, axis=AX.X, op=ALU.add), [xs_], [st4])
    op("dve", lambda e: e.tensor_reduce(out=v, in_=qv, axis=AX.X, op=ALU.add), [sq], [st4])
    op("dve", lambda e: e.tensor_scalar(out=m, in0=m, scalar1=1.0 / hd, scalar2=None, op0=ALU.mult), [st4], [st4])
    msq = sq[:, 0:nh]
    op("dve", lambda e: e.tensor_tensor(out=msq, in0=m, in1=m, op=ALU.mult), [st4], [sq])
    op("dve", lambda e: e.scalar_tensor_tensor(out=v, in0=v, scalar=1.0 / hd, in1=msq, op0=ALU.mult, op1=ALU.subtract), [st4, sq], [st4])
    op("act", lambda e: e.activation(out=v, in_=v, func=AF.Sqrt, bias=eps_ap), [st4, CT], [st4])
    op("dve", lambda e: e.reciprocal(out=v, in_=v), [st4], [st4])
    op("dve", lambda e: e.tensor_tensor(out=ov, in0=xv, in1=m.unsqueeze(2).broadcast_to([128, nh, hd]), op=ALU.subtract), [xs_, st4], [on])
    op("dve", lambda e: e.tensor_tensor(out=ov, in0=ov, in1=v.unsqueeze(2).broadcast_to([128, nh, hd]), op=ALU.mult), [on, st4], [on])


_CACHE = {}


def run(inputs, T, L, NB=2, n_cores=8, dbg=None):
    B = inputs["x"].shape[0]
    wd = prep_weights(inputs)
    key = (T, L, NB, dbg)
    if key not in _CACHE:
        _CACHE[key] = build(T, L, NB, dbg)
    nc = _CACHE[key]
    in_maps = []
    x = np.asarray(inputs["x"], np.float32)
    c = np.asarray(inputs["c"], np.float32)
    pos = np.asarray(inputs["positions"], np.int32)
    for core in range(n_cores):
        b = core % B
        m = dict(wd)
        m["x"] = np.ascontiguousarray(x[b])
        m["c"] = np.ascontiguousarray(c[b].reshape(8, 128).T)
        m["pos"] = np.ascontiguousarray(pos[b].reshape(1, T))
        in_maps.append(m)
    import os
    res = run_bass_kernel_spmd(nc, in_maps, core_ids=list(range(n_cores)), **({'trace': True} if os.environ.get('KTRACE') else {}))
    if os.environ.get('KTRACE'):
        print('EXEC_TIME_NS', res.exec_time_ns)
    if dbg == "dump":
        global LAST_DBG
        LAST_DBG = np.asarray(res.results[0]["dbg_o"]).astype(np.float32)
        global LAST_DBGF
        LAST_DBGF = np.asarray(res.results[0]["dbg_f"]).astype(np.float32)
    return np.stack([np.asarray(res.results[b]["out"], np.float32) for b in range(B)], 0)


def kernel(**inputs):
    T = inputs["x"].shape[1]
    L = inputs["w_in"].shape[0]
    return run(inputs, T, L)
```

```python
import contextlib
import numpy as np
import concourse.bass as bass
import concourse.mybir as mybir
from concourse.bass_utils import run_bass_kernel_spmd

F32 = mybir.dt.float32
BF16 = mybir.dt.bfloat16
I32 = mybir.dt.int32
AF = mybir.ActivationFunctionType
ALU = mybir.AluOpType
AX = mybir.AxisListType

D = 1024
W = 512
DIN = 8320
NBLKW = 24
NV = 40
C0 = float(np.exp(-0.5))
GAM = [1.0 - 2.0 ** (-5.0 - h) for h in range(4)]
ENGS = ["pe", "dve", "act", "pool", "sp"]


class Buf:
    __slots__ = ("name", "lw", "rd")

    def __init__(self, name):
        self.name = name
        self.lw = None
        self.rd = []


class _Rec:
    def __getattr__(self, name):
        return lambda *a, **k: (name, a, k)


_REC = _Rec()


class Sched:
    def __init__(self, nc):
        self.nc = nc
        self.ops = {e: [] for e in ENGS}
        self.cnt = {}
        self.keys = []
        self.epoch = 0

    def _key(self, eng):
        return "%s_%d" % (eng, self.epoch)

    def _deps(self, reads, writes, eng=None):
        deps = {}
        pre = None if eng is None else eng + "_"
        for b in reads:
            t = b.lw
            if t is not None and deps.get(t[0], 0) < t[1]:
                deps[t[0]] = t[1]
        for b in writes:
            t = b.lw
            if t is not None and deps.get(t[0], 0) < t[1] and not (pre and t[0].startswith(pre)):
                deps[t[0]] = t[1]
            for t in b.rd:
                if deps.get(t[0], 0) < t[1] and not (pre and t[0].startswith(pre)):
                    deps[t[0]] = t[1]
        return deps

    def _commit(self, tok, reads, writes):
        for b in reads:
            b.rd.append(tok)
        for b in writes:
            b.lw = tok
            b.rd = []

    def _bump(self, key, inc):
        if key not in self.cnt:
            self.cnt[key] = 0
            self.keys.append(key)
        self.cnt[key] += inc
        return (key, self.cnt[key])

    def op(self, eng, fn, reads=(), writes=()):
        deps = self._deps(reads, writes, eng)
        key = self._key(eng)
        tok = self._bump(key, 1)
        self.ops[eng].append((fn(_REC), deps, key, 1))
        self._commit(tok, reads, writes)
        return tok

    def dma(self, eng, fn, semkey, reads=(), writes=()):
        deps = self._deps(reads, writes)
        tok = self._bump(semkey, 16)
        self.ops[eng].append((fn(_REC), deps, semkey, 16))
        self._commit(tok, reads, writes)
        return tok

    def wait_all(self, eng, bufs):
        deps = self._deps(bufs, ())
        self.ops[eng].append((None, deps, None, 0))

    def emit(self):
        nc = self.nc
        with contextlib.ExitStack() as st:
            sems = {}
            for i, k in enumerate(self.keys):
                sems[k] = st.enter_context(nc.semaphore("s%d" % i))
            block = st.enter_context(nc.Block())

            def run(engname, e):
                seen = {}
                for fn, deps, key, inc in self.ops[engname]:
                    for k, v in deps.items():
                        if engname == "pe" and k.startswith("pe_"):
                            continue
                        if seen.get(k, 0) >= v:
                            continue
                        e.wait_ge(sems[k], v)
                        seen[k] = v
                    if fn is not None:
                        getattr(e, fn[0])(*fn[1], **fn[2]).then_inc(sems[key], inc)

            @block.tensor
            def _(e):
                run("pe", e)

            @block.vector
            def _(e):
                run("dve", e)

            @block.scalar
            def _(e):
                run("act", e)

            @block.gpsimd
            def _(e):
                run("pool", e)

            @block.sync
            def _(e):
                run("sp", e)


class Tile:
    def __init__(self, nc, name, shape, dtype, psum=False):
        if psum:
            self.t = nc.alloc_psum_tensor("p_" + name, shape, dtype)
        else:
            self.t = nc.alloc_sbuf_tensor("t_" + name, shape, dtype)
        self.b = Buf(name)

    def __getitem__(self, idx):
        return self.t[idx]


def make_consts():
    c = {}
    p = np.arange(128)
    i = np.arange(128)
    mt = np.zeros((128, 4, 128), np.float64)
    dq = np.zeros((128, 4, 128), np.float64)
    kt = np.zeros((128, 4), np.float64)
    for h in range(4):
        g = GAM[h]
        diff = i[None, :] - p[:, None]
        mt[:, h, :] = np.where(diff >= 0, g ** np.maximum(diff, 0), 0.0)
        dq[:, h, :] = (g ** (i + 1.0))[None, :]
        kt[:, h] = g ** (127.0 - p)
    c["maskT"] = mt.reshape(128, 512)
    c["decq"] = dq.reshape(128, 512)
    c["ktail"] = kt
    same = (p[:, None] // 64) == (i[None, :] // 64)
    mus = (same & (i[None, :] > p[:, None])).astype(np.float64)
    mui = (same & (i[None, :] >= p[:, None])).astype(np.float64)
    mls = (same & (i[None, :] < p[:, None])).astype(np.float64)
    mu2 = np.concatenate([mus, mui], 1)
    c["MU"] = np.concatenate([mu2, mu2], 1)
    c["ML"] = np.concatenate([mls, mls], 1)
    c["onesbd"] = same.astype(np.float64)
    c["ident"] = np.eye(128)
    sm = np.ones((128, 512)); sm[:, ::64] = 0.0
    c["scanm"] = sm
    half = 64
    fr = (np.float32(10000.0) ** (-(np.arange(half, dtype=np.float32)) / np.float32(half))).astype(np.float64)
    c["freq"] = np.concatenate([fr, fr])[:, None]
    c["sgn"] = np.concatenate([-np.ones(64), np.ones(64)])[:, None]
    ic = np.zeros((128, 4, 16))
    for g, w in enumerate((2, 4, 8, 16)):
        ic[:, g, :] = (1.0 / np.minimum(np.arange(16) + 1, w))[None, :]
    c["invc"] = ic.reshape(128, 64)
    c["eps"] = np.array([1e-6, 1e-5, 64e-5, 1e-18, np.pi / 2])[None, :].repeat(128, 0)
    off = {}
    cols = []
    o = 0
    for k, v in c.items():
        off[k] = (o, v.shape[1])
        o += v.shape[1]
        cols.append(v)
    return np.concatenate(cols, 1).astype(np.float32), off


CONSTS, COFF = make_consts()
NCONST = CONSTS.shape[1]


def prep_weights(inp):
    L = inp["w_in"].shape[0]
    w_in = np.asarray(inp["w_in"], np.float32)
    wblk = np.zeros((L, NBLKW, 128, 4096), np.float32)

    def inblk(cols):
        a = w_in[:, :, cols]
        return a.reshape(L, 8, 128, len(cols)).transpose(0, 2, 1, 3)

    def put(b, cols):
        a = inblk(cols)
        n = a.shape[3]
        tmp = np.zeros((L, 128, 8, 512), np.float32)
        tmp[:, :, :, :n] = a
        wblk[:, b] = tmp.reshape(L, 128, 4096)

    ar = np.arange
    sw = np.concatenate([np.concatenate([ar(h * 128 + 64, h * 128 + 128), ar(h * 128, h * 128 + 64)]) for h in range(4)])
    put(0, ar(0, 512)); put(1, ar(512, 1024))
    put(2, ar(1024, 1536)); put(3, 1024 + sw)
    put(4, ar(1536, 2048)); put(5, 1536 + sw)
    put(6, ar(2048, 2560)); put(7, ar(2560, 3072))
    put(8, ar(3072, 3584)); put(9, ar(3648, 4160)); put(10, ar(4160, 4672))
    put(11, np.concatenate([ar(3584, 3648), ar(4672, 4736)]))
    put(12, ar(4736, 5248))
    wb = np.asarray(inp["w_branch"], np.float32)
    for n in range(3):
        wblk[:, 13 + 3 * n] = wb[:, n].reshape(L, 4, 128, 1024).transpose(0, 2, 1, 3).reshape(L, 128, 4096)
        for hf in range(2):
            put(13 + 3 * n + 1 + hf, 5248 + n * 1024 + hf * 512 + ar(512))
    wo = np.asarray(inp["w_out"], np.float32)
    for hf in range(2):
        a = wo[:, :, hf * 512:(hf + 1) * 512].reshape(L, 8, 128, 512).transpose(0, 2, 1, 3)
        wblk[:, 22 + hf] = a.reshape(L, 128, 4096)
    wsm = np.zeros((L, 128, 1536), np.float32)
    pw = np.asarray(inp["pool_w"], np.float32)
    wsm[:, :, 0:512] = pw.transpose(0, 2, 1, 3).reshape(L, 128, 512)
    wsm[:, 0:64, 512:1024] = np.asarray(inp["rwkv_w2"], np.float32)
    wsm[:, 64:128, 1024:1536] = np.asarray(inp["rwkv_a2"], np.float32)

    def fm(v):
        return np.asarray(v, np.float32).reshape(L, 4, 128).transpose(0, 2, 1)
    mu = np.asarray(inp["rwkv_shift_mu"], np.float32)
    vecs = np.zeros((L, 128, NV), np.float32)
    vecs[:, :, 0:4] = fm(inp["pool_scale"])
    vecs[:, :, 4:8] = fm(mu[:, 0:512])
    vecs[:, :, 8:12] = fm(mu[:, 576:1088])
    vecs[:, :, 12:16] = fm(mu[:, 1088:1600])
    vecs[:, 0:64, 16] = mu[:, 512:576]
    vecs[:, 64:128, 16] = mu[:, 1600:1664]
    vecs[:, :, 17:21] = fm(inp["rwkv_w0"])
    vecs[:, :, 21:25] = fm(inp["rwkv_a0"])
    vecs[:, :, 25:29] = fm(inp["rwkv_k_k"])
    vecs[:, :, 29:33] = fm(inp["rwkv_k_a"])
    vecs[:, :, 33:37] = fm(np.asarray(inp["rwkv_r_k"], np.float32).reshape(L, 512))
    rows = np.stack([np.asarray(inp["ret_norm_g"], np.float32), np.asarray(inp["rwkv_ln_w"], np.float32),
                     np.asarray(inp["rwkv_ln_b"], np.float32)], 1)
    ngb = np.concatenate([np.asarray(inp["norm_g"], np.float32), np.asarray(inp["b_ada"], np.float32)], 1)
    return dict(wblk=wblk, wsm=wsm, vecs=vecs, rows=np.ascontiguousarray(rows), ngb=np.ascontiguousarray(ngb),
                wada=np.ascontiguousarray(np.asarray(inp["w_ada"], np.float32)),
                fg=np.asarray(inp["final_g"], np.float32).reshape(1, D), consts=CONSTS)


def build(T, L, NB=2, dbg=None):
    TT = NB * 128
    NT = T // TT
    NCH = NB * 2
    CPB = 512 // TT
    nc = bass.Bass("TRN2", target_bir_lowering=False)
    S = Sched(nc)

    def dram(name, shape, dt, kind):
        return nc.dram_tensor(name, shape, dt, kind=kind).ap()
    x_in = dram("x", [T, D], F32, "ExternalInput")
    c_in = dram("c", [128, 8], F32, "ExternalInput")
    pos_in = dram("pos", [1, T], I32, "ExternalInput")
    wblk = dram("wblk", [L, NBLKW, 128, 4096], F32, "ExternalInput")
    wsm_in = dram("wsm", [L, 128, 1536], F32, "ExternalInput")
    vecs_in = dram("vecs", [L, 128, NV], F32, "ExternalInput")
    rows_in = dram("rows", [L, 3, 512], F32, "ExternalInput")
    ngb_in = dram("ngb", [L, 4096], F32, "ExternalInput")
    wada_in = dram("wada", [L, D, 3 * D], F32, "ExternalInput")
    fg_in = dram("fg", [1, D], F32, "ExternalInput")
    consts_in = dram("consts", [128, NCONST], F32, "ExternalInput")
    out = dram("out", [T, D], F32, "ExternalOutput")
    dbg_o = dram("dbg_o", [128, 20 * TT], BF16, "ExternalOutput") if dbg == "dump" else None
    b_dbg = Buf("dbg")
    dbg_f = dram("dbg_f", [128, 16384], F32, "ExternalOutput") if dbg == "dump" else None
    dstate = {"off": 0, "map": {}, "on": False}
    global DUMP_MAP
    DUMP_MAP = dstate["map"]
    dstage = None

    def dump(name, ap, rd, width):
        if dbg != "dump" or not dstate["on"]:
            return
        o = dstate["off"]
        dstate["map"][name] = (o, width)
        st_ = hn
        op("dve", lambda e: e.tensor_copy(out=st_[:, 0:width], in_=ap), rd, [st_])
        dma("sp", lambda e: e.dma_start(out=dbg_f[:, o:o + width], in_=st_[:, 0:width]), "dbgf", [st_], [b_dbg])
        dstate["off"] += width
    wsc = dram("wsc", [L, NBLKW, 128, 4096], BF16, "Internal")
    xs = dram("xs", [T, D], F32, "Internal")
    b_wsc = [Buf("wsc%d" % l) for l in range(L)]
    b_xs = [Buf("xs%d" % i) for i in range(NT)]
    b_out = [Buf("out%d" % i) for i in range(NT)]

    def tile(name, shape, dt):
        return Tile(nc, name, shape, dt)

    CT = tile("consts", [128, NCONST], F32)

    def cst(name):
        o, n = COFF[name]
        return CT[:, o:o + n]
    identb = tile("identb", [128, 128], BF16)
    onesb = tile("onesb", [128, 128], BF16)
    cbc = tile("cbc", [128, 8, 128], F32)
    cfm = tile("cfm", [128, 8], F32)
    modbc = tile("modbc", [128, 3 * D], F32)
    gbc = tile("gbc", [128, D], F32)
    rows3 = tile("rows3", [128, 3, 512], F32)
    vecs = tile("vecs", [128, NV], F32)
    wsmb = tile("wsmb", [128, 1536], BF16)
    big = tile("big", [128, 8 * TT], F32)
    bada = tile("bada", [128, TT], F32)
    class _Alias:
        def __init__(self, t, lo, hi, shp=None):
            self.t, self.lo, self.hi, self.b, self.shp = t, lo, hi, t.b, shp

        def __getitem__(self, idx):
            v = self.t[:, self.lo:self.hi]
            if self.shp:
                v = v.rearrange("p (h i) -> p h i", i=self.shp)
            return v[idx]
    xtl = [tile("xt%d" % i, [128, NB, D], F32) for i in range(2)]
    hT = tile("hT", [128, 8, TT], BF16)
    hn = _Alias(big, 0, D)
    hb = tile("hb", [128, D], BF16)
    junk = hb
    wsmf = big
    st4 = tile("st4", [128, 16], F32)
    NSLOT = 3
    wslot = [tile("wslot%d" % i, [128, 4096], BF16) for i in range(NSLOT)]
    FW = TT + 16
    Ft = [tile("F%d" % i, [128, 4, FW], F32) for i in range(4)]
    NRING = 5
    ringt = [tile("ring%d" % i, [128, FW], F32) for i in range(NRING)]
    wkn = "stg dd sig ag cs csx Pin Pex Q kk sqb rn kp prb tb".split()
    wk = {n: tile("wk_" + n, [128, FW], F32) for n in wkn}
    WK2_PENDING = True
    BB = [tile("BB%d" % i, [128, 4, TT], BF16) for i in range(6)]
    artm = [tile("artm%d" % i, [128, 4, NB, 2, 128], BF16) for i in range(2)]
    Vm = [tile("Vm%d" % i, [128, 4, TT], BF16) for i in range(2)]
    opool = tile("opool", [128, 4, TT], BF16)
    oret = tile("oret", [128, 4, TT], BF16)
    orw = tile("orw", [128, 4, TT], BF16)
    puh = tile("puh", [128, 4, 16], F32)
    hal = tile("hal", [128, 16], F32)
    Rst = tile("Rst", [128, 4, 128], F32)
    Rb = tile("Rb", [128, 4, 128], BF16)
    Hst = [tile("Hst%d" % g, [128, 2, 64], F32) for g in range(2)]
    Hb = [tile("Hb%d" % g, [128, 2, 64], BF16) for g in range(2)]
    ksF = tile("ksF", [128, 4, TT], F32)

    Sm = _Alias(hb, 0, 512, 128)
    lor = _Alias(hb, 512, 512 + TT)
    AX1 = tile("AX1", [128, 8, 256], BF16)
    AX2 = tile("AX2", [128, 8, 256], BF16)
    BR = [[tile("BR%d_%d" % (i, g), [128, 4, 256], BF16) for g in range(2)] for i in range(2)]
    Am = [[tile("Am%d_%d" % (i, g), [128, 4, 128], BF16) for g in range(2)] for i in range(2)]
    Gs = [tile("Gs%d" % g, [128, 256], BF16) for g in range(2)]
    fdummy = tile("fdummy", [128, 2], F32)

    class ATile:
        def __init__(self, name, base, lo):
            self.b = Buf(name)
            self.base, self.lo = base, lo

        def __getitem__(self, idx):
            v = self.base[:]
            if len(v.shape) == 3:
                v = v.rearrange("p h x -> p (h x)")
            return v.bitcast(F32)[:, self.lo:self.lo + TT][idx]
    wk2 = {}
    _regions = [(AX1, 4), (AX2, 4), (BR[0][0], 2), (BR[0][1], 2), (BR[1][0], 2), (BR[1][1], 2)]
    _names = "sig ag cs csx Pin Pex Q kk sqb rn kp prb tb".split()
    _ri = 0
    for base_, n_ in _regions:
        for i_ in range(n_):
            if _ri < len(_names):
                wk2[_names[_ri]] = ATile("wk2_" + _names[_ri], base_, i_ * TT)
                _ri += 1
    assert _ri == len(_names)
    Us = [[tile("Us%d_%d" % (i, g), [128, 256], BF16) for g in range(2)] for i in range(2)]
    WCt = tile("WCt", [128, 4, NCH], F32)

    class _PosI:
        b = bada.b

        def __getitem__(self, idx):
            return bada[:, 0:TT].bitcast(I32)[idx]
    posi = _PosI()
    psum_all = nc.alloc_psum_tensor("psum_all", [128, 4096], F32)

    class Bank:
        def __init__(self, i):
            self.i = i
            self.b = Buf("bank%d" % i)

        def __getitem__(self, idx):
            return psum_all[:, self.i * 512:(self.i + 1) * 512][idx]
    banks = [Bank(i) for i in range(8)]

    def mbank(i0, n):
        return psum_all[:, i0 * 512:(i0 + n) * 512]
    ring5t = [tile("ringw%d" % i, [128, 512], F32) for i in range(4)]
    state = {"bank": 0, "ring": 0, "ring5": 0}

    def ring5():
        r = ring5t[state["ring5"] % 4]
        state["ring5"] += 1
        return r

    def pb():
        b = banks[state["bank"] % 8]
        state["bank"] += 1
        return b

    def ring():
        r = ringt[state["ring"] % NRING]
        state["ring"] += 1
        return r

    def op(eng, fn, r, w):
        return S.op(eng, fn, [t.b if hasattr(t, "b") else t for t in r], [t.b if hasattr(t, "b") else t for t in w])

    def dma(eng, fn, key, r, w):
        return S.dma(eng, fn, key, [t.b if hasattr(t, "b") else t for t in r], [t.b if hasattr(t, "b") else t for t in w])

    def cast_layer(l):
        key = "cast%d" % l
        tok = None
        for b in range(NBLKW):
            for hf in range(2):
                tok = S.dma("pool", lambda e, l=l, b=b, hf=hf: e.dma_start(
                    out=wsc[l, b, :, hf * 2048:(hf + 1) * 2048], in_=wblk[l, b, :, hf * 2048:(hf + 1) * 2048]), key)
        b_wsc[l].lw = tok

    order = [0, 1, 2, 3, 4, 5, 6, 7, 11, 8, 9, 10, 12] + [14, 15, 13, 17, 18, 16, 20, 21, 19, 22, 23]
    seq = [(l, b) for l in range(L) for _ in range(NT) for b in order]
    wstate = {"next_load": 0, "next_get": 0}
    wreleased = [False] * len(seq)

    def w_pump():
        while wstate["next_load"] < len(seq):
            n = wstate["next_load"]
            if n >= NSLOT and not wreleased[n - NSLOT]:
                break
            l, b = seq[n]
            s = n % NSLOT
            if b == 11:
                dma("sp", lambda e, l=l, b=b, s=s: e.dma_start(
                    out=wslot[s][:].rearrange("p (k c) -> p k c", c=512)[:, :, 0:128],
                    in_=wsc[l, b].rearrange("p (k c) -> p k c", c=512)[:, :, 0:128]), "ws%d" % s, [b_wsc[l]], [wslot[s]])
            else:
                dma("sp", lambda e, l=l, b=b, s=s: e.dma_start(out=wslot[s][:], in_=wsc[l, b]), "ws%d" % s, [b_wsc[l]], [wslot[s]])
            wstate["next_load"] += 1

    def wget(l, b):
        n = wstate["next_get"]
        assert seq[n] == (l, b), (seq[n], l, b)
        w_pump()
        assert wstate["next_load"] > n, "weight block not loadable (too many held)"
        wstate["next_get"] += 1
        wstate["last"] = n
        t_ = wslot[n % NSLOT]
        t_.widx = n
        return t_

    def wrel(*tiles):
        for t_ in tiles:
            wreleased[t_.widx] = True
        w_pump()

    def w8(wt):
        return wt[:].rearrange("p (k c) -> p k c", c=512)

    dma("sp", lambda e: e.dma_start(out=CT[:], in_=consts_in), "ld_c", [], [CT])
    dma("sp", lambda e: e.dma_start(out=cfm[:], in_=c_in), "ld_c2", [], [cfm])
    cast_layer(0)
    op("act", lambda e: e.activation(out=identb[:], in_=cst("ident"), func=AF.Copy), [CT], [identb])
    op("act", lambda e: e.activation(out=onesb[:], in_=cst("onesbd"), func=AF.Copy), [CT], [onesb])
    op("act", lambda e: e.activation(out=cfm[:], in_=cfm[:], func=AF.Silu), [cfm], [cfm])
    for k in range(8):
        op("dve", lambda e, k=k: e.tensor_scalar(out=cbc[:, k, :], in0=cst("ident"), scalar1=0.0, scalar2=cfm[:, k:k + 1],
                                                 op0=ALU.mult, op1=ALU.add), [CT, cfm], [cbc])
    for t_ in (artm[0], artm[1], Vm[0], Vm[1], Us[0][0], Us[0][1], Us[1][0], Us[1][1], Gs[0], Gs[1]):
        op("pool", lambda e, t_=t_: e.memset(t_[:], 0.0), [], [t_])
    eps_n = cst("eps")[:, 0:1]
    eps_r = cst("eps")[:, 1:2]
    eps_w = cst("eps")[:, 2:3]
    eps_k = cst("eps")[:, 3:4]

    def fm_proj(wt, nch, evac, ncols=128, k_parts=8):
        c = 0
        while c < nch:
            n = min(CPB, nch - c)
            bk = pb()
            for i in range(n):
                for k in range(8):
                    op("pe", lambda e, bk=bk, i=i, k=k, cc=c + i: e.matmul(
                        bk[0:ncols, i * TT:(i + 1) * TT], lhsT=w8(wt)[:, k, cc * 128:cc * 128 + ncols], rhs=hT[:, k, :],
                        start=(k == 0), stop=(k == 7)), [wt, hT], [bk])
            evac(bk, c, n)
            c += n

    def bview(bk, n):
        return bk[:, 0:n * TT].rearrange("p (n t) -> p n t", t=TT)

    for l in range(L):
        S.epoch = l
        if l + 1 < L:
            cast_layer(l + 1)
        dma("sp", lambda e, l=l: e.dma_start(out=vecs[:], in_=vecs_in[l]), "ld_v", [], [vecs])
        dma("sp", lambda e, l=l: e.dma_start(out=wsmf[:, 0:1536], in_=wsm_in[l]), "ld_w", [], [wsmf])
        dma("sp", lambda e, l=l: e.dma_start(out=rows3[:].rearrange("p a c -> p (a c)"),
                                             in_=rows_in[l].rearrange("a c -> (a c)").partition_broadcast(128)), "ld_r", [], [rows3])
        dma("sp", lambda e, l=l: e.dma_start(out=gbc[:], in_=ngb_in[l, 0:D].partition_broadcast(128)), "ld_g", [], [gbc])
        op("act", lambda e: e.activation(out=wsmb[:], in_=wsmf[:, 0:1536], func=AF.Copy), [wsmf], [wsmb])
        NWB = 3 * D // TT
        for nb in range(NWB):
            dma("sp", lambda e, l=l, nb=nb: e.dma_start(
                out=big[:].rearrange("p (k c) -> p k c", c=TT),
                in_=wada_in[l, :, nb * TT:(nb + 1) * TT].rearrange("(k p) c -> p k c", p=128)), "ld_a", [], [big])
            dma("sp", lambda e, l=l, nb=nb: e.dma_start(
                out=bada[:, 0:TT], in_=ngb_in[l, D + nb * TT:D + (nb + 1) * TT].partition_broadcast(128)), "ld_b", [], [bada])
            bk = pb()
            for k in range(8):
                op("pe", lambda e, bk=bk, k=k: e.matmul(bk[:, 0:TT], lhsT=cbc[:, k, :], rhs=big[:, k * TT:(k + 1) * TT],
                                                        start=(k == 0), stop=(k == 7)), [cbc, big], [bk])
            op("dve", lambda e, bk=bk, nb=nb: e.tensor_tensor(out=modbc[:, nb * TT:(nb + 1) * TT], in0=bk[:, 0:TT], in1=bada[:, 0:TT],
                                                              op=ALU.add), [bk, bada], [modbc])
        op("dve", lambda e: e.scalar_tensor_tensor(out=modbc[:, D:2 * D], in0=modbc[:, D:2 * D], scalar=1.0, in1=gbc[:],
                                                   op0=ALU.add, op1=ALU.mult), [modbc, gbc], [modbc])
        B_bc = modbc[:, 0:D]
        A_bc = modbc[:, D:2 * D]
        G_bc = modbc[:, 2 * D:3 * D]
        op("pool", lambda e: e.memset(puh[:], 0.0), [], [puh])
        op("pool", lambda e: e.memset(hal[:], 0.0), [], [hal])
        op("pool", lambda e: e.memset(Rst[:], 0.0), [], [Rst])
        op("pool", lambda e: e.memset(Rb[:], 0.0), [], [Rb])
        for g_ in range(2):
            op("pool", lambda e, g_=g_: e.memset(Hst[g_][:], 0.0), [], [Hst[g_]])
            op("pool", lambda e, g_=g_: e.memset(Hb[g_][:], 0.0), [], [Hb[g_]])
        poolw = wsmb[:, 0:512].rearrange("p (g d) -> p g d", d=128)
        lw2 = wsmb[:, 512:1024]
        la2 = wsmb[:, 1024:1536]

        def V(i, n=4):
            return vecs[:, i:i + n]

        for it in range(NT):
            t0 = it * TT
            gi = l * NT + it
            xt = xtl[gi % 2]

            def load_x(l_, it_):
                g_ = l_ * NT + it_
                src = x_in if l_ == 0 else xs
                rd = [] if l_ == 0 else [b_xs[it_]]
                dma("sp", lambda e: e.dma_start(out=xtl[g_ % 2][:], in_=src[it_ * TT:(it_ + 1) * TT, :].rearrange("(j p) d -> p j d", p=128)),
                    "ld_x%d" % (g_ % 2), rd, [xtl[g_ % 2]])
            if gi == 0 or NT == 1:
                load_x(l, it)
            if dbg == "pro":
                S.emit()
                return nc
            dstate["on"] = (l == 0 and it == NT - 1)
            for j in range(NB):
                op("act", lambda e, j=j: e.activation(out=junk[:], in_=xt[:, j, :], func=AF.Square, accum_out=st4[:, j:j + 1]),
                   [xt], [junk, st4])
            op("act", lambda e: e.activation(out=st4[:, 4:4 + NB], in_=st4[:, 0:NB], func=AF.Sqrt, bias=eps_n, scale=1.0 / D),
               [st4, CT], [st4])
            op("dve", lambda e: e.reciprocal(out=st4[:, 8:8 + NB], in_=st4[:, 4:4 + NB]), [st4], [st4])
            for j in range(NB):
                op("dve", lambda e, j=j: e.scalar_tensor_tensor(out=hn[:], in0=xt[:, j, :], scalar=st4[:, 8 + j:9 + j], in1=A_bc,
                                                                op0=ALU.mult, op1=ALU.mult), [xt, st4, modbc], [hn])
                op("pool", lambda e: e.tensor_tensor(out=hb[:], in0=hn[:], in1=B_bc, op=ALU.add), [hn, modbc], [hb])
                bk = pb()
                bkb = bk[:].bitcast(BF16)
                for k in range(8):
                    op("pe", lambda e, bkb=bkb, k=k: e.transpose(out=bkb[:, k * 128:(k + 1) * 128], in_=hb[:, k * 128:(k + 1) * 128],
                                                                  identity=identb[:]), [hb, identb], [bk])
                op("act", lambda e, bkb=bkb, j=j: e.activation(out=hT[:, :, j * 128:(j + 1) * 128],
                                                               in_=bkb.rearrange("p (k t) -> p k t", t=128), func=AF.Copy), [bk], [hT])
            if dbg == "N":
                S.emit()
                return nc
            pu, sa, sb, sg = Ft
            op("pool", lambda e: e.tensor_copy(out=pu[:, :, 0:16], in_=puh[:]), [puh], [pu])
            wt = wget(l, 0)
            fm_proj(wt, 4, lambda bk, c, n: op("act", lambda e: e.activation(out=pu[:, c:c + n, 16:16 + TT], in_=bview(bk, n), func=AF.Copy),
                                               [bk], [pu]))
            wrel(wt)
            wt = wget(l, 1)
            fm_proj(wt, 4, lambda bk, c, n: op("act", lambda e: e.activation(out=sg[:, c:c + n, 0:TT], in_=bview(bk, n), func=AF.Silu),
                                               [bk], [sg]))
            wrel(wt)
            op("pool", lambda e: e.tensor_copy(out=puh[:], in_=pu[:, :, TT:TT + 16]), [pu], [puh])
            for g in range(4):
                cur = pu
                for m in range(g + 1):
                    sh = 1 << m
                    lo = (1 << (m + 1)) - 1
                    dst = sa if (m % 2 == 0) else sb
                    op("dve", lambda e, g=g, cur=cur, dst=dst, sh=sh, lo=lo: e.tensor_tensor(
                        out=dst[:, g, lo:FW], in0=cur[:, g, lo:FW], in1=cur[:, g, lo - sh:FW - sh], op=ALU.add), [cur], [dst])
                    cur = dst
                wg = float(1 << (g + 1))
                dpool = BB[0]
                op("dve", lambda e, g=g, cur=cur, wg=wg: e.scalar_tensor_tensor(
                    out=dpool[:, g, :], in0=cur[:, g, 16:FW], scalar=1.0 / wg, in1=pu[:, g, 16:FW], op0=ALU.mult, op1=ALU.subtract),
                    [cur, pu], [dpool])
                if it == 0:
                    r1 = ring()
                    op("dve", lambda e, g=g, cur=cur, r1=r1: e.tensor_tensor(out=r1[:, 0:16], in0=cur[:, g, 16:32],
                                                                             in1=cst("invc")[:, g * 16:(g + 1) * 16], op=ALU.mult), [cur, CT], [r1])
                    op("dve", lambda e, g=g, r1=r1: e.tensor_tensor(out=dpool[:, g, 0:16], in0=r1[:, 0:16], in1=pu[:, g, 16:32],
                                                                    op=ALU.subtract), [r1, pu], [dpool])
            for g in range(4):
                bk = pb()
                op("pe", lambda e, bk=bk, g=g: e.matmul(bk[:, 0:TT], lhsT=poolw[:, g, :], rhs=BB[0][:, g, :], start=True, stop=True),
                   [wsmb, BB[0]], [bk])
                op("dve", lambda e, bk=bk, g=g: e.scalar_tensor_tensor(out=opool[:, g, :], in0=bk[:, 0:TT], scalar=V(0)[:, g:g + 1],
                                                                        in1=sg[:, g, 0:TT], op0=ALU.mult, op1=ALU.mult), [bk, vecs, sg], [opool])
            if dbg == "P":
                S.emit()
                return nc
            if it + 1 < NT:
                load_x(l, it + 1)
            elif l + 1 < L and NT > 1:
                load_x(l + 1, 0)
            tabs = Ft[0]
            sret = Ft[1]
            qr, qd, kr, ktm, vtm = BB[0], BB[1], BB[2], BB[3], BB[4]
            dma("sp", lambda e, t0=t0: e.dma_start(out=posi[:], in_=pos_in[0, t0:t0 + TT].partition_broadcast(128)), "ld_p", [], [posi])
            ang, nn, r2 = ring(), ring(), ring()
            op("dve", lambda e: e.tensor_copy(out=ang[:, 0:TT], in_=posi[:]), [posi], [ang])
            op("dve", lambda e: e.tensor_scalar(out=ang[:, 0:TT], in0=ang[:, 0:TT], scalar1=cst("freq"), scalar2=None, op0=ALU.mult),
               [ang, CT], [ang])
            op("dve", lambda e: e.tensor_scalar(out=nn[:, 0:TT], in0=ang[:, 0:TT], scalar1=float(1.0 / (2 * np.pi)), scalar2=None,
                                                op0=ALU.mult), [ang], [nn])
            op("dve", lambda e: e.tensor_copy(out=posi[:], in_=nn[:, 0:TT]), [nn], [posi])
            op("dve", lambda e: e.tensor_copy(out=nn[:, 0:TT], in_=posi[:]), [posi], [nn])
            c1 = 6.28125
            c2 = float(2 * np.pi - 6.28125)
            op("dve", lambda e: e.scalar_tensor_tensor(out=r2[:, 0:TT], in0=nn[:, 0:TT], scalar=-c1, in1=ang[:, 0:TT],
                                                       op0=ALU.mult, op1=ALU.add), [nn, ang], [r2])
            op("dve", lambda e: e.scalar_tensor_tensor(out=r2[:, 0:TT], in0=nn[:, 0:TT], scalar=-c2, in1=r2[:, 0:TT],
                                                       op0=ALU.mult, op1=ALU.add), [nn, r2], [r2])
            ws_, wc_, mk_ = ring(), ring(), ring()
            twopi = float(2 * np.pi)
            op("dve", lambda e: e.tensor_scalar(out=mk_[:, 0:TT], in0=r2[:, 0:TT], scalar1=0.0, scalar2=float(np.pi), op0=ALU.add, op1=ALU.is_gt),
               [r2], [mk_])
            op("dve", lambda e: e.scalar_tensor_tensor(out=ws_[:, 0:TT], in0=mk_[:, 0:TT], scalar=-twopi, in1=r2[:, 0:TT], op0=ALU.mult, op1=ALU.add),
               [mk_, r2], [ws_])
            op("dve", lambda e: e.tensor_scalar(out=mk_[:, 0:TT], in0=r2[:, 0:TT], scalar1=float(np.pi / 2), scalar2=float(np.pi), op0=ALU.add,
                                                op1=ALU.is_gt), [r2], [mk_])
            op("dve", lambda e: e.scalar_tensor_tensor(out=wc_[:, 0:TT], in0=mk_[:, 0:TT], scalar=-twopi, in1=r2[:, 0:TT], op0=ALU.mult, op1=ALU.add),
               [mk_, r2], [wc_])
            op("act", lambda e: e.activation(out=tabs[:, 0, 0:TT], in_=wc_[:, 0:TT], func=AF.Sin, bias=cst("eps")[:, 4:5]), [wc_, CT], [tabs])
            op("act", lambda e: e.activation(out=tabs[:, 1, 0:TT], in_=ws_[:, 0:TT], func=AF.Sin, scale=cst("sgn")), [ws_, CT], [tabs])
            op("pool", lambda e: e.tensor_scalar(out=tabs[:, 2:4, 0:TT], in0=tabs[:, 0:2, 0:TT], scalar1=float(128.0 ** -0.5), scalar2=None,
                                                 op0=ALU.mult), [tabs], [tabs])

            def rope(dst, dq, tc):
                wa = wget(l, 2 if dst is qr else 4)
                wb_ = wget(l, 3 if dst is qr else 5)
                for h in range(4):
                    ba, bb = pb(), pb()
                    for (bk_, wt_) in ((ba, wa), (bb, wb_)):
                        for k in range(8):
                            op("pe", lambda e, bk_=bk_, wt_=wt_, k=k, h=h: e.matmul(
                                bk_[:, 0:TT], lhsT=w8(wt_)[:, k, h * 128:(h + 1) * 128], rhs=hT[:, k, :], start=(k == 0), stop=(k == 7)),
                                [wt_, hT], [bk_])
                    t1, t2 = ring(), ring()
                    op("dve", lambda e, ba=ba, t1=t1: e.tensor_tensor(out=t1[:, 0:TT], in0=ba[:, 0:TT], in1=tabs[:, tc, 0:TT], op=ALU.mult),
                       [ba, tabs], [t1])
                    op("dve", lambda e, bb=bb, t2=t2: e.tensor_tensor(out=t2[:, 0:TT], in0=bb[:, 0:TT], in1=tabs[:, tc + 1, 0:TT], op=ALU.mult),
                       [bb, tabs], [t2])
                    if dq is None:
                        op("pool", lambda e, t1=t1, t2=t2, h=h: e.tensor_tensor(out=dst[:, h, :], in0=t1[:, 0:TT], in1=t2[:, 0:TT], op=ALU.add),
                           [t1, t2], [dst])
                    else:
                        op("pool", lambda e, t1=t1, t2=t2: e.tensor_tensor(out=t1[:, 0:TT], in0=t1[:, 0:TT], in1=t2[:, 0:TT], op=ALU.add),
                           [t1, t2], [t1])
                        op("act", lambda e, t1=t1, h=h: e.activation(out=dst[:, h, :], in_=t1[:, 0:TT], func=AF.Copy), [t1], [dst])
                        op("pool", lambda e, t1=t1, h=h: e.tensor_tensor(
                            out=dq[:, h, :].rearrange("p (j i) -> p j i", i=128), in0=t1[:, 0:TT].rearrange("p (j i) -> p j i", i=128),
                            in1=cst("decq")[:, h * 128:(h + 1) * 128].unsqueeze(1).broadcast_to([128, NB, 128]), op=ALU.mult), [t1, CT], [dq])
                wrel(wa, wb_)
            dump("tabs", tabs[:, :, 0:TT], [tabs], 4 * TT) if False else None
            for i_ in range(4):
                dump("tab%d" % i_, tabs[:, i_, 0:TT], [tabs], TT)
            rope(qr, qd, 0)
            rope(kr, None, 2)
            dump("qr", qr[:].rearrange("p c t -> p (c t)"), [qr], 4 * TT)
            dump("qd", qd[:].rearrange("p c t -> p (c t)"), [qd], 4 * TT)
            dump("kr", kr[:].rearrange("p c t -> p (c t)"), [kr], 4 * TT)
            wt = wget(l, 6)
            for j in range(NB):
                bk = pb()
                for k in range(8):
                    op("pe", lambda e, bk=bk, k=k, j=j, wt=wt: e.matmul(bk[:, :], lhsT=hT[:, k, j * 128:(j + 1) * 128], rhs=w8(wt)[:, k, :],
                                                                      start=(k == 0), stop=(k == 7)), [wt, hT], [bk])
                op("act", lambda e, bk=bk, j=j: e.activation(out=vtm[:, :, j * 128:(j + 1) * 128],
                                                             in_=bk[:, :].rearrange("p (h e) -> p h e", e=128), func=AF.Copy), [bk], [vtm])
            wrel(wt)
            wt = wget(l, 7)
            fm_proj(wt, 4, lambda bk, c, n: op("act", lambda e: e.activation(out=sret[:, c:c + n, 0:TT], in_=bview(bk, n), func=AF.Silu),
                                               [bk], [sret]))
            wrel(wt)
            for j in range(NB):
                js = slice(j * 128, (j + 1) * 128)
                bk = pb()
                bkb = bk[:].bitcast(BF16)
                for h in range(4):
                    op("pe", lambda e, bkb=bkb, h=h, js=js: e.transpose(out=bkb[:, h * 128:(h + 1) * 128], in_=kr[:, h, js], identity=identb[:]),
                       [kr, identb], [bk])
                op("dve", lambda e, bkb=bkb, js=js: e.tensor_tensor(
                    out=ktm[:, :, js], in0=bkb[:, 0:512].rearrange("p (h d) -> p h d", d=128),
                    in1=cst("ktail").unsqueeze(2).broadcast_to([128, 4, 128]), op=ALU.mult), [bk, CT], [ktm])
                bs = pb()
                for h in range(4):
                    op("pe", lambda e, bs=bs, h=h, js=js: e.matmul(bs[:, h * 128:(h + 1) * 128], lhsT=kr[:, h, js], rhs=qr[:, h, js],
                                                                   start=True, stop=True), [kr, qr], [bs])
                op("dve", lambda e, bs=bs: e.tensor_tensor(out=Sm[:].rearrange("p h i -> p (h i)"), in0=bs[:, :], in1=cst("maskT"), op=ALU.mult),
                   [bs, CT], [Sm])
                bo = pb()
                for h in range(4):
                    op("pe", lambda e, bo=bo, h=h, js=js: e.matmul(bo[:, h * 128:(h + 1) * 128], lhsT=Sm[:, h, :], rhs=vtm[:, h, js],
                                                                   start=True, stop=False), [Sm, vtm], [bo])
                    op("pe", lambda e, bo=bo, h=h, js=js: e.matmul(bo[:, h * 128:(h + 1) * 128], lhsT=qd[:, h, js], rhs=Rb[:, h, :],
                                                                   start=False, stop=True), [qd, Rb], [bo])
                bkv = pb()
                for h in range(4):
                    op("pe", lambda e, bkv=bkv, h=h, js=js: e.matmul(bkv[:, h * 128:(h + 1) * 128], lhsT=ktm[:, h, js], rhs=vtm[:, h, js],
                                                                     start=True, stop=True), [ktm, vtm], [bkv])
                for h in range(4):
                    op("dve", lambda e, bkv=bkv, h=h: e.scalar_tensor_tensor(out=Rst[:, h, :], in0=Rst[:, h, :], scalar=float(GAM[h] ** 128),
                                                                              in1=bkv[:, h * 128:(h + 1) * 128], op0=ALU.mult, op1=ALU.add),
                       [Rst, bkv], [Rst])
                op("act", lambda e: e.activation(out=Rb[:], in_=Rst[:], func=AF.Copy), [Rst], [Rb])
                osb, sq, on = ring5(), ring5(), ring5()
                op("act", lambda e, bo=bo, osb=osb: e.activation(out=osb[:, 0:512], in_=bo[:, :], func=AF.Copy), [bo], [osb])
                op("act", lambda e, bo=bo, sq=sq: e.activation(out=sq[:, 0:512], in_=bo[:, :], func=AF.Square), [bo], [sq])
                if j == NB - 1:
                    dump("osb", osb[:, 0:512], [osb], 512)
                    dump("vtm", vtm[:].rearrange("p c t -> p (c t)"), [vtm], 4 * TT)
                    dump("ktm", ktm[:].rearrange("p c t -> p (c t)"), [ktm], 4 * TT)
                    dump("Sm", Sm[:].rearrange("p c t -> p (c t)"), [Sm], 512)
                head_norm(op, st4, osb, sq, on, 4, 128, eps_r, CT)
                if j == NB - 1:
                    dump("on", on[:, 0:512], [on], 512)
                op("dve", lambda e, on=on: e.tensor_tensor(out=on[:, 0:512], in0=on[:, 0:512], in1=rows3[:, 0, :], op=ALU.mult), [on, rows3], [on])
                bt_ = pb()
                for h in range(4):
                    op("pe", lambda e, bt_=bt_, h=h, on=on: e.transpose(out=bt_[:, h * 128:(h + 1) * 128], in_=on[:, h * 128:(h + 1) * 128],
                                                                        identity=cst("ident")), [on, CT], [bt_])
                op("dve", lambda e, bt_=bt_, js=js: e.tensor_tensor(out=oret[:, :, js], in0=bt_[:, :].rearrange("p (h t) -> p h t", t=128),
                                                                    in1=sret[:, :, js], op=ALU.mult), [bt_, sret], [oret])
            if dbg == "R":
                S.emit()
                return nc
            bon = Ft[0]
            srw = Ft[1]
            vsf = Ft[2]
            btt, ktt, vbb, Vtm, btm, ktm2 = BB
            stg_i = [0]

            rsF = Ft[3]

            class CV:
                def __init__(self, t, c):
                    self.t, self.c, self.b = t, c, t.b

                def __getitem__(self, idx):
                    return self.t[idx[0], self.c, idx[1]]

            def shifted(wt_, cc, idx, dst_ap, dst_t):
                bk = pb()
                for k in range(8):
                    op("pe", lambda e, bk=bk, k=k: e.matmul(bk[:, 0:TT], lhsT=w8(wt_)[:, k, cc * 128:(cc + 1) * 128], rhs=hT[:, k, :],
                                                            start=(k == 0), stop=(k == 7)), [wt_, hT], [bk])
                stg, dd = wk["stg"], wk["dd"]
                op("act", lambda e: e.activation(out=stg[:, 1:1 + TT], in_=bk[:, 0:TT], func=AF.Copy), [bk], [stg])
                op("pool", lambda e: e.tensor_copy(out=stg[:, 0:1], in_=hal[:, idx:idx + 1]), [hal], [stg])
                op("pool", lambda e: e.tensor_copy(out=hal[:, idx:idx + 1], in_=stg[:, TT:TT + 1]), [stg], [hal])
                op("dve", lambda e: e.tensor_tensor(out=dd[:, 0:TT], in0=stg[:, 0:TT], in1=stg[:, 1:1 + TT], op=ALU.subtract), [stg], [dd])
                mu_i = (4 + idx) if idx < 12 else 16
                op("dve", lambda e: e.scalar_tensor_tensor(out=dst_ap, in0=dd[:, 0:TT], scalar=vecs[:, mu_i:mu_i + 1], in1=stg[:, 1:1 + TT],
                                                           op0=ALU.mult, op1=ALU.add), [dd, vecs, stg], [dst_t])
            wl_ = wget(l, 11)
            lot = wk["sig"]
            shifted(wl_, 0, 12, lot[:, 0:TT], lot)
            wrel(wl_)
            op("act", lambda e: e.activation(out=lor[0:64, :], in_=lot[0:64, 0:TT], func=AF.Tanh), [lot], [lor])
            op("act", lambda e: e.activation(out=lor[64:128, :], in_=lot[64:128, 0:TT], func=AF.Copy), [lot], [lor])
            wr_ = wget(l, 8)
            for c in range(4):
                shifted(wr_, c, c, rsF[:, c, 0:TT], rsF)
            wrel(wr_)
            wk_ = wget(l, 9)
            for c in range(4):
                shifted(wk_, c, 4 + c, ksF[:, c, 0:TT], ksF)
            wrel(wk_)
            wv_ = wget(l, 10)
            for c in range(4):
                shifted(wv_, c, 8 + c, vsf[:, c, 0:TT], vsf)
                op("pool", lambda e, c=c: e.tensor_copy(out=vbb[:, c, :], in_=vsf[:, c, 0:TT]), [vsf], [vbb])
            wrel(wv_)
            fence_r = [AX1, AX2] + [BR[i_][g_] for i_ in range(2) for g_ in range(2)]
            op("pool", lambda e: e.memset(fdummy[:], 0.0), fence_r, list(wk2.values()))
            clists = []
            for c in range(4):
                cur_ = []

                def q(eng, fn, r, w, cur_=cur_):
                    cur_.append((eng, fn, r, w))
                clists.append(cur_)
                WK = wk if c % 2 == 0 else wk2
                rs_, ks_ = CV(rsF, c), CV(ksF, c)
                bw, ba_ = pb(), pb()
                q("pe", lambda e, bw=bw, c=c: e.matmul(bw[:, 0:TT], lhsT=lw2[:, c * 128:(c + 1) * 128], rhs=lor[:, :], start=True, stop=True),
                   [wsmb, lor], [bw])
                q("pe", lambda e, ba_=ba_, c=c: e.matmul(ba_[:, 0:TT], lhsT=la2[:, c * 128:(c + 1) * 128], rhs=lor[:, :],
                                                          start=True, stop=True), [wsmb, lor], [ba_])
                sig, ag = WK["sig"], WK["ag"]
                q("act", lambda e, bw=bw, sig=sig, c=c: e.activation(out=sig[:, 0:TT], in_=bw[:, 0:TT], func=AF.Sigmoid, bias=V(17)[:, c:c + 1]),
                   [bw, vecs], [sig])
                q("act", lambda e, ba_=ba_, ag=ag, c=c: e.activation(out=ag[:, 0:TT], in_=ba_[:, 0:TT], func=AF.Sigmoid, bias=V(21)[:, c:c + 1]),
                   [ba_, vecs], [ag])
                cs, csx = WK["cs"], WK["csx"]
                q("dve", lambda e, sig=sig, cs=cs: e.tensor_tensor_scan(out=cs[:, 0:TT], data0=cst("scanm")[:, 0:TT], data1=sig[:, 0:TT],
                                                                         initial=0.0, op0=ALU.mult, op1=ALU.add), [sig, CT], [cs])
                q("pool", lambda e, sig=sig, cs=cs, csx=csx: e.tensor_tensor(out=csx[:, 0:TT], in0=cs[:, 0:TT], in1=sig[:, 0:TT], op=ALU.subtract),
                   [cs, sig], [csx])
                Pin, Pex, Q = WK["Pin"], WK["Pex"], WK["Q"]
                q("act", lambda e, cs=cs, Pin=Pin: e.activation(out=Pin[:, 0:TT], in_=cs[:, 0:TT], func=AF.Exp, scale=-C0), [cs], [Pin])
                q("act", lambda e, csx=csx, Pex=Pex: e.activation(out=Pex[:, 0:TT], in_=csx[:, 0:TT], func=AF.Exp, scale=-C0), [csx], [Pex])
                q("act", lambda e, cs=cs, Q=Q: e.activation(out=Q[:, 0:TT], in_=cs[:, 0:TT], func=AF.Exp, scale=C0), [cs], [Q])
                q("pool", lambda e, Pin=Pin, c=c: e.tensor_copy(out=WCt[:, c, :], in_=Pin[:, 63:TT:64]), [Pin], [WCt])
                kk, sqb, rn = WK["kk"], WK["sqb"], WK["rn"]
                q("dve", lambda e, ks_=ks_, kk=kk, c=c: e.tensor_scalar(out=kk[:, 0:TT], in0=ks_[:, 0:TT], scalar1=V(25)[:, c:c + 1], scalar2=None,
                                                                         op0=ALU.mult), [ks_, vecs], [kk])
                sqv = sqb[:, 0:TT // 2].bitcast(BF16)
                q("pool", lambda e, kk=kk, sqv=sqv: e.tensor_tensor(out=sqv, in0=kk[:, 0:TT], in1=kk[:, 0:TT], op=ALU.mult), [kk], [sqb])
                bn_ = pb()
                q("pe", lambda e, bn_=bn_, sqv=sqv: e.matmul(bn_[:, 0:TT], lhsT=onesb[:], rhs=sqv, start=True, stop=True), [onesb, sqb], [bn_])
                q("act", lambda e, bn_=bn_, rn=rn: e.activation(out=rn[:, 0:TT], in_=bn_[:, 0:TT], func=AF.Ln, bias=eps_k), [bn_, CT], [rn])
                q("act", lambda e, rn=rn: e.activation(out=rn[:, 0:TT], in_=rn[:, 0:TT], func=AF.Exp, scale=-0.5), [rn], [rn])
                q("dve", lambda e, kk=kk, rn=rn: e.tensor_tensor(out=kk[:, 0:TT], in0=kk[:, 0:TT], in1=rn[:, 0:TT], op=ALU.mult), [kk, rn], [kk])
                kp = WK["kp"]
                q("pool", lambda e, ag=ag, kp=kp, c=c: e.tensor_scalar(out=kp[:, 0:TT], in0=ag[:, 0:TT], scalar1=-1.0, scalar2=V(29)[:, c:c + 1],
                                                                        op0=ALU.add, op1=ALU.mult), [ag, vecs], [kp])
                q("dve", lambda e, kp=kp, ks_=ks_: e.scalar_tensor_tensor(out=kp[:, 0:TT], in0=kp[:, 0:TT], scalar=1.0, in1=ks_[:, 0:TT],
                                                                            op0=ALU.add, op1=ALU.mult), [kp, ks_], [kp])
                prb = WK["prb"]
                prv = prb[:, 0:TT // 2].bitcast(BF16)
                q("dve", lambda e, rs_=rs_, kp=kp, prv=prv, c=c: e.scalar_tensor_tensor(out=prv, in0=rs_[:, 0:TT], scalar=V(33)[:, c:c + 1],
                                                                                        in1=kp[:, 0:TT], op0=ALU.mult, op1=ALU.mult),
                   [rs_, vecs, kp], [prb])
                bb_ = pb()
                q("pe", lambda e, bb_=bb_, prv=prv: e.matmul(bb_[:, 0:TT], lhsT=onesb[:], rhs=prv, start=True, stop=True), [onesb, prb], [bb_])
                q("dve", lambda e, bb_=bb_, c=c: e.tensor_tensor(out=bon[:, c, 0:TT], in0=bb_[:, 0:TT], in1=vsf[:, c, 0:TT], op=ALU.mult),
                   [bb_, vsf], [bon])
                for hp in range(2):
                    rw_ = slice(hp * 64, hp * 64 + 64)
                    q("dve", lambda e, kk=kk, Pex=Pex, hp=hp, rw_=rw_, c=c: e.scalar_tensor_tensor(
                        out=artm[hp][rw_, c, :, 0, :], in0=kk[rw_, 0:TT].rearrange("p (j i) -> p j i", i=128), scalar=-1.0,
                        in1=Pex[rw_, 0:TT].rearrange("p (j i) -> p j i", i=128), op0=ALU.mult, op1=ALU.mult), [kk, Pex], [artm[hp]])
                    q("pool", lambda e, rs_=rs_, Pin=Pin, hp=hp, rw_=rw_, c=c: e.tensor_tensor(
                        out=artm[hp][rw_, c, :, 1, :], in0=rs_[rw_, 0:TT].rearrange("p (j i) -> p j i", i=128),
                        in1=Pin[rw_, 0:TT].rearrange("p (j i) -> p j i", i=128), op=ALU.mult), [rs_, Pin], [artm[hp]])
                tb = WK["tb"]
                q("pool", lambda e, kk=kk, ag=ag, tb=tb: e.tensor_tensor(out=tb[:, 0:TT], in0=kk[:, 0:TT], in1=ag[:, 0:TT], op=ALU.mult),
                   [kk, ag], [tb])
                q("dve", lambda e, tb=tb, Q=Q, c=c: e.tensor_tensor(out=btt[:, c, :], in0=tb[:, 0:TT], in1=Q[:, 0:TT], op=ALU.mult),
                   [tb, Q], [btt])
                q("pool", lambda e, kp=kp, Q=Q, c=c: e.tensor_tensor(out=ktt[:, c, :], in0=kp[:, 0:TT], in1=Q[:, 0:TT], op=ALU.mult),
                   [kp, Q], [ktt])
            sa_, sb_ = clists[0] + clists[2], clists[1] + clists[3]
            k_ = len(clists[0]) // 2
            merged = list(sa_[:k_])
            ia_, ib_ = k_, 0
            while ia_ < len(sa_) or ib_ < len(sb_):
                if ib_ < len(sb_):
                    merged.append(sb_[ib_])
                    ib_ += 1
                if ia_ < len(sa_):
                    merged.append(sa_[ia_])
                    ia_ += 1
            for (eng_, fn_, r_, w_) in merged:
                op(eng_, fn_, r_, w_)
            op("pool", lambda e: e.memset(fdummy[:], 0.0), list(wk2.values()), fence_r)
            wg_ = wget(l, 12)
            fm_proj(wg_, 4, lambda bk, c, n: op("act", lambda e: e.activation(out=srw[:, c:c + n, 0:TT], in_=bview(bk, n), func=AF.Silu),
                                                [bk], [srw]))
            wrel(wg_)
            def part1(j):
                js = slice(j * 128, (j + 1) * 128)
                for (srcT, dstT) in ((vbb, Vtm), (btt, btm), (ktt, ktm2)):
                    bk = pb()
                    bkb = bk[:].bitcast(BF16)
                    for c in range(4):
                        op("pe", lambda e, bkb=bkb, c=c, srcT=srcT, js=js: e.transpose(out=bkb[:, c * 128:(c + 1) * 128], in_=srcT[:, c, js],
                                                                                      identity=identb[:]), [srcT, identb], [bk])
                    op("act", lambda e, bkb=bkb, dstT=dstT, js=js: e.activation(out=dstT[:, :, js], in_=bkb[:, 0:512].rearrange("p (c k) -> p c k", k=128),
                                                                                func=AF.Copy), [bk], [dstT])
                    if dstT is Vtm:
                        for cp_ in range(2):
                            rw_ = slice(cp_ * 64, cp_ * 64 + 64)
                            op("act", lambda e, bkb=bkb, js=js, cp_=cp_, rw_=rw_: e.activation(
                                out=Vm[cp_][rw_, :, js], in_=bkb[rw_, 0:512].rearrange("p (c k) -> p c k", k=128), func=AF.Copy), [bk], [Vm[cp_]])
                for half in range(2):
                    for ci in range(2):
                        c = half * 2 + ci
                        for hp in range(2):
                            arv = artm[hp][:, c, j, :, :].rearrange("p a i -> p (a i)")
                            op("pe", lambda e, ci=ci, hp=hp, c=c, arv=arv, js=js: e.matmul(banks[0 + ci][:, hp * 256:(hp + 1) * 256], lhsT=btt[:, c, js],
                                                                                           rhs=arv, start=True, stop=True), [btt, artm[hp]], [banks[0 + ci]])
                            op("pe", lambda e, ci=ci, hp=hp, c=c, arv=arv, js=js: e.matmul(banks[2 + ci][:, hp * 256:(hp + 1) * 256], lhsT=ktt[:, c, js],
                                                                                           rhs=arv, start=True, stop=True), [ktt, artm[hp]], [banks[2 + ci]])
                            op("pe", lambda e, ci=ci, hp=hp, c=c, js=js: e.matmul(banks[4][:, (ci * 2 + hp) * 128:(ci * 2 + hp + 1) * 128],
                                                                                  lhsT=artm[hp][:, c, j, 0, :], rhs=btt[:, c, js], start=True, stop=True),
                               [btt, artm[hp]], [banks[4]])
                    h0 = 4 * half
                    mu_b = cst("MU").unsqueeze(1).broadcast_to([128, 2, 512])
                    ml_b = cst("ML").unsqueeze(1).broadcast_to([128, 2, 256])
                    op("dve", lambda e, h0=h0, mu_b=mu_b: e.tensor_tensor(out=AX1[:, h0:h0 + 4, :].rearrange("p (a h) x -> p a (h x)", a=2),
                                                                          in0=mbank(0, 2).rearrange("p (a x) -> p a x", a=2), in1=mu_b, op=ALU.mult),
                       [banks[0], banks[1], CT], [AX1])
                    op("dve", lambda e, h0=h0, mu_b=mu_b: e.tensor_tensor(out=AX2[:, h0:h0 + 4, :].rearrange("p (a h) x -> p a (h x)", a=2),
                                                                          in0=mbank(2, 2).rearrange("p (a x) -> p a x", a=2), in1=mu_b, op=ALU.mult),
                       [banks[2], banks[3], CT], [AX2])
                    op("dve", lambda e, half=half, ml_b=ml_b: e.tensor_tensor(out=Am[0][half][:].rearrange("p (a h) x -> p a (h x)", a=2),
                                                                              in0=banks[4][:, :].rearrange("p (a x) -> p a x", a=2), in1=ml_b, op=ALU.mult),
                       [banks[4], CT], [Am[0][half]])
            def part2(j):
                js = slice(j * 128, (j + 1) * 128)
                for g in range(2):
                    hsl = slice(4 * g, 4 * g + 4)
                    op("pool", lambda e, g=g, hsl=hsl: e.tensor_copy(out=BR[0][g][:, :, 0:128], in_=AX1[:, hsl, 0:128]), [AX1], [BR[0][g]])
                    op("pool", lambda e, g=g, hsl=hsl: e.tensor_tensor(out=BR[0][g][:, :, 128:256], in0=AX1[:, hsl, 0:128],
                                                                       in1=identb[:].unsqueeze(1).broadcast_to([128, 4, 128]), op=ALU.add),
                       [AX1, identb], [BR[0][g]])
                for m in range(6):
                    cur, nxt = m % 2, (m + 1) % 2
                    last = (m == 5)
                    for g in range(2):
                        bqg = [banks[2 * g], banks[2 * g + 1]]
                        bag = banks[4 + g]
                        for hh in range(4):
                            bq = bqg[hh // 2]
                            hp = hh % 2
                            if m == 0:
                                op("pe", lambda e, bq=bq, hp=hp, hh=hh, g=g: e.matmul(bq[:, hp * 256:hp * 256 + 128], lhsT=Am[0][g][:, hh, :],
                                                                                      rhs=BR[0][g][:, hh, 0:128], start=True, stop=True),
                                   [Am[0][g], BR[0][g]], [bq])
                            elif not last:
                                op("pe", lambda e, bq=bq, hp=hp, hh=hh, g=g, cur=cur: e.matmul(bq[:, hp * 256:(hp + 1) * 256], lhsT=Am[cur][g][:, hh, :],
                                                                                               rhs=BR[cur][g][:, hh, :], start=True, stop=True),
                                   [Am[cur][g], BR[cur][g]], [bq])
                            else:
                                op("pe", lambda e, bq=bq, hp=hp, hh=hh, g=g, cur=cur: e.matmul(bq[:, hp * 256 + 128:(hp + 1) * 256], lhsT=Am[cur][g][:, hh, :],
                                                                                               rhs=BR[cur][g][:, hh, 128:256], start=True, stop=True),
                                   [Am[cur][g], BR[cur][g]], [bq])
                            if not last:
                                op("pe", lambda e, bag=bag, hh=hh, g=g, cur=cur: e.matmul(bag[:, hh * 128:(hh + 1) * 128], lhsT=BR[cur][g][:, hh, 0:128],
                                                                                          rhs=Am[cur][g][:, hh, :], start=True, stop=True),
                                   [Am[cur][g], BR[cur][g]], [bag])
                        bqv = mbank(2 * g, 2).rearrange("p (h x) -> p h x", x=256)
                        if not last:
                            op("act", lambda e, bqv=bqv, nxt=nxt, g=g: e.activation(out=BR[nxt][g][:, :, 0:128], in_=bqv[:, :, 0:128], func=AF.Copy),
                               bqg, [BR[nxt][g]])
                            op("act", lambda e, nxt=nxt, g=g, bag=bag: e.activation(out=Am[nxt][g][:], in_=bag[:, :].rearrange("p (h x) -> p h x", x=128),
                                                                                    func=AF.Copy), [bag], [Am[nxt][g]])
                        if m == 0:
                            op("pool", lambda e, nxt=nxt, g=g: e.tensor_copy(out=BR[nxt][g][:, :, 128:256], in_=BR[0][g][:, :, 128:256]),
                               [BR[0][g]], [BR[nxt][g]])
                        else:
                            op("dve", lambda e, bqv=bqv, nxt=nxt, cur=cur, g=g: e.tensor_tensor(out=BR[nxt][g][:, :, 128:256], in0=bqv[:, :, 128:256],
                                                                                                in1=BR[cur][g][:, :, 128:256], op=ALU.add),
                               bqg + [BR[cur][g]], [BR[nxt][g]])
                TTm = BR[0]
                by = banks[6]
                for cp in range(2):
                    co = cp * 64
                    cc = 2 * j + cp
                    bgs, bus, bhs = [banks[0], banks[1]], [banks[2], banks[3]], [banks[4], banks[5]]

                    def hinfo(g, hh):
                        h = 4 * g + hh
                        c, hp, po = h // 2, h % 2, (h % 2) * 64
                        chs = slice(j * 128 + po, j * 128 + po + 64)
                        return h, c, hp, po, chs, slice(hh * 64, (hh + 1) * 64), slice(h * 64, (h + 1) * 64)
                    for g in range(2):
                        for hh in range(4):
                            h, c, hp, po, chs, ls, hs = hinfo(g, hh)
                            op("pe", lambda e, g=g, c=c, hp=hp, ls=ls: e.matmul(
                                bgs[g][co:co + 64, ls], lhsT=artm[hp][:, c, j, 0, co:co + 64], rhs=Hb[g][:, c % 2, :], start=True, stop=False),
                                [artm[hp], Hb[g]], [bgs[g]])
                            op("pe", lambda e, g=g, h=h, c=c, ls=ls, chs=chs: e.matmul(
                                bgs[g][co:co + 64, ls], lhsT=AX2[:, h, co:co + 64], rhs=Vtm[:, c, chs], start=False, stop=True),
                                [AX2, Vtm], [bgs[g]])
                    for g in range(2):
                        op("act", lambda e, g=g: e.activation(out=Gs[g][co:co + 64, :], in_=bgs[g][co:co + 64, 0:256], func=AF.Copy), [bgs[g]], [Gs[g]])
                    for g in range(2):
                        for hh in range(4):
                            h, c, hp, po, chs, ls, hs = hinfo(g, hh)
                            op("pe", lambda e, g=g, hh=hh, ls=ls: e.matmul(bus[g][co:co + 64, ls], lhsT=TTm[g][:, hh, 128 + co:128 + co + 64],
                                                                           rhs=Gs[g][:, ls], start=True, stop=True), [TTm[g], Gs[g]], [bus[g]])
                    for g in range(2):
                        op("act", lambda e, g=g: e.activation(out=Us[cp][g][co:co + 64, :], in_=bus[g][co:co + 64, 0:256], func=AF.Copy),
                           [bus[g]], [Us[cp][g]])
                    for g in range(2):
                        for hh in range(4):
                            h, c, hp, po, chs, ls, hs = hinfo(g, hh)
                            op("pe", lambda e, g=g, c=c, hp=hp, hs=hs: e.matmul(by[co:co + 64, hs], lhsT=artm[hp][:, c, j, 1, co:co + 64],
                                                                                rhs=Hb[g][:, c % 2, :], start=True, stop=False), [artm[hp], Hb[g]], [by])
                            op("pe", lambda e, g=g, h=h, hs=hs, ls=ls: e.matmul(by[co:co + 64, hs], lhsT=AX1[:, h, 128 + co:128 + co + 64],
                                                                                rhs=Us[cp][g][:, ls], start=False, stop=False), [AX1, Us[cp][g]], [by])
                            op("pe", lambda e, h=h, c=c, hs=hs, chs=chs: e.matmul(by[co:co + 64, hs], lhsT=AX2[:, h, 128 + co:128 + co + 64],
                                                                                  rhs=Vtm[:, c, chs], start=False, stop=True), [AX2, Vtm], [by])
                            op("pe", lambda e, g=g, c=c, po=po, ls=ls, chs=chs: e.matmul(bhs[g][po:po + 64, (c % 2) * 64:(c % 2 + 1) * 64], lhsT=btm[:, c, chs],
                                                                                         rhs=Us[cp][g][:, ls], start=True, stop=False), [btm, Us[cp][g]], [bhs[g]])
                            op("pe", lambda e, g=g, c=c, po=po, chs=chs: e.matmul(bhs[g][po:po + 64, (c % 2) * 64:(c % 2 + 1) * 64], lhsT=ktm2[:, c, chs],
                                                                                  rhs=Vm[cp][:, c, chs], start=False, stop=True), [ktm2, Vm[cp]], [bhs[g]])
                        Htmp = ring5()
                        op("dve", lambda e, g=g, Htmp=Htmp: e.tensor_tensor(out=Htmp[:, 0:128], in0=bhs[g][:, 0:128],
                                                                            in1=Hst[g][:].rearrange("p c v -> p (c v)"), op=ALU.add), [bhs[g], Hst[g]], [Htmp])
                        op("dve", lambda e, g=g, Htmp=Htmp: e.tensor_tensor(out=Hst[g][:], in0=Htmp[:, 0:128].rearrange("p (c v) -> p c v", v=64),
                                                                            in1=WCt[:, 2 * g:2 * g + 2, cc:cc + 1].broadcast_to([128, 2, 64]), op=ALU.mult),
                           [Htmp, WCt], [Hst[g]])
                        op("act", lambda e, g=g: e.activation(out=Hb[g][:], in_=Hst[g][:], func=AF.Copy), [Hst[g]], [Hb[g]])
            def part3(j):
                js = slice(j * 128, (j + 1) * 128)
                by = banks[6]
                ysb, ysq, yn = ring5(), ring5(), ring5()
                op("act", lambda e, ysb=ysb: e.activation(out=ysb[:, 0:512], in_=by[:, :], func=AF.Copy), [by], [ysb])
                op("act", lambda e, ysq=ysq: e.activation(out=ysq[:, 0:512], in_=by[:, :], func=AF.Square), [by], [ysq])
                head_norm(op, st4, ysb, ysq, yn, 8, 64, eps_w, CT)
                op("dve", lambda e, yn=yn: e.tensor_tensor(out=yn[:, 0:512], in0=yn[:, 0:512], in1=rows3[:, 1, :], op=ALU.mult), [yn, rows3], [yn])
                op("pool", lambda e, yn=yn: e.tensor_tensor(out=yn[:, 0:512], in0=yn[:, 0:512], in1=rows3[:, 2, :], op=ALU.add), [yn, rows3], [yn])
                bt_ = pb()
                for c in range(4):
                    op("pe", lambda e, bt_=bt_, c=c, yn=yn: e.transpose(out=bt_[:, c * 128:(c + 1) * 128], in_=yn[:, c * 128:(c + 1) * 128],
                                                                        identity=cst("ident")), [yn, CT], [bt_])
                yf = ring5()
                op("dve", lambda e, bt_=bt_, yf=yf, js=js: e.tensor_tensor(out=yf[:, 0:512].rearrange("p (c t) -> p c t", t=128),
                                                                           in0=bt_[:, :].rearrange("p (c t) -> p c t", t=128), in1=bon[:, :, js], op=ALU.add),
                   [bt_, bon], [yf])
                op("pool", lambda e, yf=yf, js=js: e.tensor_tensor(out=orw[:, :, js], in0=yf[:, 0:512].rearrange("p (c t) -> p c t", t=128),
                                                                   in1=srw[:, :, js], op=ALU.mult), [yf, srw], [orw])
            for j in range(NB):
                part1(j)
                part2(j)
                part3(j)
            if dbg == "W":
                S.emit()
                return nc
            if dbg == "dump" and l == 0 and it == NT - 1:
                for i_, t_ in enumerate((opool, oret, orw)):
                    dma("sp", lambda e, i_=i_, t_=t_: e.dma_start(out=dbg_o[:, i_ * 4 * TT:(i_ + 1) * 4 * TT], in_=t_[:].rearrange("p c t -> p (c t)")),
                        "dbg", [t_], [b_dbg])
                dma("sp", lambda e: e.dma_start(out=dbg_o[:, 12 * TT:20 * TT], in_=hT[:].rearrange("p c t -> p (c t)")), "dbg", [hT], [b_dbg])
            macc = big
            mT0, mT1 = BB[0], BB[1]
            obr = [opool, oret, orw]
            gts = [Ft[2], Ft[3]]
            for n in range(3):
                for hf in range(2):
                    wgl = wget(l, 13 + 3 * n + 1 + hf)
                    c = 0
                    while c < 4:
                        nn_ = min(CPB, 4 - c)
                        bgl = pb()
                        for i in range(nn_):
                            for k in range(8):
                                op("pe", lambda e, bgl=bgl, k=k, wgl=wgl, i=i, cc=c + i: e.matmul(
                                    bgl[:, i * TT:(i + 1) * TT], lhsT=w8(wgl)[:, k, cc * 128:(cc + 1) * 128], rhs=hT[:, k, :],
                                    start=(k == 0), stop=(k == 7)), [wgl, hT], [bgl])
                        op("act", lambda e, bgl=bgl, c=c, nn_=nn_, hf=hf: e.activation(out=gts[hf][:, c:c + nn_, 0:TT], in_=bview(bgl, nn_), func=AF.Sigmoid),
                           [bgl], [gts[hf]])
                        c += nn_
                    wrel(wgl)
                wbn = wget(l, 13 + 3 * n)
                wbv = wbn[:].rearrange("p (k d) -> p k d", d=1024)
                for dp in range(0, 8, CPB):
                    byn = pb()
                    for i in range(CPB):
                        dc = dp + i
                        for k4 in range(4):
                            op("pe", lambda e, byn=byn, k4=k4, wbv=wbv, dc=dc, i=i, n=n, wbn=wbn: e.matmul(
                                byn[:, i * TT:(i + 1) * TT], lhsT=wbv[:, k4, dc * 128:(dc + 1) * 128], rhs=obr[n][:, k4, :],
                                start=(k4 == 0), stop=(k4 == 3)), [wbn, obr[n]], [byn])
                    hf, dq0 = dp // 4, dp % 4
                    gv = gts[hf][:, dq0:dq0 + CPB, 0:TT]
                    ms = macc[:, dp * TT:(dp + CPB) * TT].rearrange("p (i t) -> p i t", t=TT)
                    yv = bview(byn, CPB)
                    if n == 0:
                        op("dve", lambda e, yv=yv, gv=gv, ms=ms: e.tensor_tensor(out=ms, in0=yv, in1=gv, op=ALU.mult), [byn, gts[hf]], [macc])
                    else:
                        tmp = ring5()
                        tv = tmp[:, 0:CPB * TT].rearrange("p (i t) -> p i t", t=TT)
                        op("dve", lambda e, yv=yv, gv=gv, tv=tv: e.tensor_tensor(out=tv, in0=yv, in1=gv, op=ALU.mult), [byn, gts[hf]], [tmp])
                        if n == 1:
                            op("pool", lambda e, tv=tv, ms=ms: e.tensor_tensor(out=ms, in0=ms, in1=tv, op=ALU.add), [tmp, macc], [macc])
                        else:
                            mt_ = (mT0 if dp < 4 else mT1)
                            op("pool", lambda e, tv=tv, ms=ms, mt_=mt_, dq0=dq0: e.tensor_tensor(out=mt_[:, dq0:dq0 + CPB, :], in0=ms, in1=tv, op=ALU.add),
                               [tmp, macc], [mt_])
                wrel(wbn)
            for hf in range(2):
                wo_ = wget(l, 22 + hf)
                for j in range(NB):
                    bo = pb()
                    for dc in range(8):
                        mt_ = (mT0 if dc < 4 else mT1)
                        op("pe", lambda e, bo=bo, dc=dc, mt_=mt_, j=j, wo_=wo_: e.matmul(bo[:, :], lhsT=mt_[:, dc % 4, j * 128:(j + 1) * 128], rhs=w8(wo_)[:, dc, :],
                                                                                         start=(dc == 0), stop=(dc == 7)), [mt_, wo_], [bo])
                    tmp = ring5()
                    op("dve", lambda e, bo=bo, tmp=tmp, hf=hf: e.tensor_tensor(out=tmp[:, 0:512], in0=bo[:, :], in1=G_bc[:, hf * 512:(hf + 1) * 512], op=ALU.mult),
                       [bo, modbc], [tmp])
                    op("pool", lambda e, tmp=tmp, hf=hf, j=j: e.tensor_tensor(out=xt[:, j, hf * 512:(hf + 1) * 512], in0=xt[:, j, hf * 512:(hf + 1) * 512],
                                                                              in1=tmp[:, 0:512], op=ALU.add), [tmp, xt], [xt])
                wrel(wo_)
            if l < L - 1:
                dma("sp", lambda e, t0=t0: e.dma_start(out=xs[t0:t0 + TT, :].rearrange("(j p) d -> p j d", p=128), in_=xt[:]), "st_x", [xt], [b_xs[it]])
            else:
                if it == 0:
                    dma("sp", lambda e: e.dma_start(out=gbc[:], in_=fg_in[0, :].partition_broadcast(128)), "ld_g", [], [gbc])
                for j in range(NB):
                    op("act", lambda e, j=j: e.activation(out=junk[:], in_=xt[:, j, :], func=AF.Square, accum_out=st4[:, j:j + 1]), [xt], [junk, st4])
                op("act", lambda e: e.activation(out=st4[:, 4:4 + NB], in_=st4[:, 0:NB], func=AF.Sqrt, bias=eps_n, scale=1.0 / D), [st4, CT], [st4])
                op("dve", lambda e: e.reciprocal(out=st4[:, 8:8 + NB], in_=st4[:, 4:4 + NB]), [st4], [st4])
                for j in range(NB):
                    op("dve", lambda e, j=j: e.scalar_tensor_tensor(out=xt[:, j, :], in0=xt[:, j, :], scalar=st4[:, 8 + j:9 + j], in1=gbc[:],
                                                                    op0=ALU.mult, op1=ALU.mult), [xt, st4, gbc], [xt])
                dma("sp", lambda e, t0=t0: e.dma_start(out=out[t0:t0 + TT, :].rearrange("(j p) d -> p j d", p=128), in_=xt[:]), "st_o", [xt], [b_out[it]])
    S.wait_all("sp", b_out + [b_dbg])
    S.emit()
    return nc


def head_norm(op, st4, xs_, sq, on, nh, hd, eps_ap, CT):
    s1 = st4[:, 0:nh]
    xv = xs_[:, 0:nh * hd].rearrange("p (h d) -> p h d", d=hd)
    qv = sq[:, 0:nh * hd].rearrange("p (h d) -> p h d", d=hd)
    ov = on[:, 0:nh * hd].rearrange("p (h d) -> p h d", d=hd)
    m = st4[:, 0:nh]
    v = st4[:, 8:8 + nh]
    op("dve", lambda e: e.tensor_reduce(out=m, in_=xv, axis=AX.X, op=ALU.add), [xs_], [st4])
    op("dve", lambda e: e.tensor_reduce(out=v, in_=qv, axis=AX.X, op=ALU.add), [sq], [st4])
    op("dve", lambda e: e.tensor_scalar(out=m, in0=m, scalar1=1.0 / hd, scalar2=None, op0=ALU.mult), [st4], [st4])
    msq = sq[:, 0:nh]
    op("dve", lambda e: e.tensor_tensor(out=msq, in0=m, in1=m, op=ALU.mult), [st4], [sq])
    op("dve", lambda e: e.scalar_tensor_tensor(out=v, in0=v, scalar=1.0 / hd, in1=msq, op0=ALU.mult, op1=ALU.subtract), [st4, sq], [st4])
    op("act", lambda e: e.activation(out=v, in_=v, func=AF.Sqrt, bias=eps_ap), [st4, CT], [st4])
    op("dve", lambda e: e.reciprocal(out=v, in_=v), [st4], [st4])
    op("dve", lambda e: e.tensor_tensor(out=ov, in0=xv, in1=m.unsqueeze(2).broadcast_to([128, nh, hd]), op=ALU.subtract), [xs_, st4], [on])
    op("dve", lambda e: e.tensor_tensor(out=ov, in0=ov, in1=v.unsqueeze(2).broadcast_to([128, nh, hd]), op=ALU.mult), [on, st4], [on])


_CACHE = {}


def run(inputs, T, L, NB=2, n_cores=8, dbg=None):
    B = inputs["x"].shape[0]
    wd = prep_weights(inputs)
    key = (T, L, NB, dbg)
    if key not in _CACHE:
        _CACHE[key] = build(T, L, NB, dbg)
    nc = _CACHE[key]
    in_maps = []
    x = np.asarray(inputs["x"], np.float32)
    c = np.asarray(inputs["c"], np.float32)
    pos = np.asarray(inputs["positions"], np.int32)
    for core in range(n_cores):
        b = core % B
        m = dict(wd)
        m["x"] = np.ascontiguousarray(x[b])
        m["c"] = np.ascontiguousarray(c[b].reshape(8, 128).T)
        m["pos"] = np.ascontiguousarray(pos[b].reshape(1, T))
        in_maps.append(m)
    import os
    res = run_bass_kernel_spmd(nc, in_maps, core_ids=list(range(n_cores)), **({'trace': True} if os.environ.get('KTRACE') else {}))
    if os.environ.get('KTRACE'):
        print('EXEC_TIME_NS', res.exec_time_ns)
    if dbg == "dump":
        global LAST_DBG
        LAST_DBG = np.asarray(res.results[0]["dbg_o"]).astype(np.float32)
        global LAST_DBGF
        LAST_DBGF = np.asarray(res.results[0]["dbg_f"]).astype(np.float32)
    return np.stack([np.asarray(res.results[b]["out"], np.float32) for b in range(B)], 0)


def kernel(**inputs):
    T = inputs["x"].shape[1]
    L = inputs["w_in"].shape[0]
    return run(inputs, T, L)
```

```python
import contextlib
import numpy as np
import concourse.bass as bass
import concourse.mybir as mybir
from concourse.bass_utils import run_bass_kernel_spmd

F32 = mybir.dt.float32
BF16 = mybir.dt.bfloat16
I32 = mybir.dt.int32
AF = mybir.ActivationFunctionType
ALU = mybir.AluOpType
AX = mybir.AxisListType

D = 1024
W = 512
DIN = 8320
NBLKW = 24
NV = 40
C0 = float(np.exp(-0.5))
GAM = [1.0 - 2.0 ** (-5.0 - h) for h in range(4)]
ENGS = ["pe", "dve", "act", "pool", "sp"]


class Buf:
    __slots__ = ("name", "lw", "rd")

    def __init__(self, name):
        self.name = name
        self.lw = None
        self.rd = []


class _Rec:
    def __getattr__(self, name):
        return lambda *a, **k: (name, a, k)


_REC = _Rec()


class Sched:
    def __init__(self, nc):
        self.nc = nc
        self.ops = {e: [] for e in ENGS}
        self.cnt = {}
        self.keys = []
        self.epoch = 0

    def _key(self, eng):
        return "%s_%d" % (eng, self.epoch)

    def _deps(self, reads, writes, eng=None):
        deps = {}
        pre = None if eng is None else eng + "_"
        for b in reads:
            t = b.lw
            if t is not None and deps.get(t[0], 0) < t[1]:
                deps[t[0]] = t[1]
        for b in writes:
            t = b.lw
            if t is not None and deps.get(t[0], 0) < t[1] and not (pre and t[0].startswith(pre)):
                deps[t[0]] = t[1]
            for t in b.rd:
                if deps.get(t[0], 0) < t[1] and not (pre and t[0].startswith(pre)):
                    deps[t[0]] = t[1]
        return deps

    def _commit(self, tok, reads, writes):
        for b in reads:
            b.rd.append(tok)
        for b in writes:
            b.lw = tok
            b.rd = []

    def _bump(self, key, inc):
        if key not in self.cnt:
            self.cnt[key] = 0
            self.keys.append(key)
        self.cnt[key] += inc
        return (key, self.cnt[key])

    def op(self, eng, fn, reads=(), writes=()):
        deps = self._deps(reads, writes, eng)
        key = self._key(eng)
        tok = self._bump(key, 1)
        self.ops[eng].append((fn(_REC), deps, key, 1))
        self._commit(tok, reads, writes)
        return tok

    def dma(self, eng, fn, semkey, reads=(), writes=()):
        deps = self._deps(reads, writes)
        tok = self._bump(semkey, 16)
        self.ops[eng].append((fn(_REC), deps, semkey, 16))
        self._commit(tok, reads, writes)
        return tok

    def wait_all(self, eng, bufs):
        deps = self._deps(bufs, ())
        self.ops[eng].append((None, deps, None, 0))

    def emit(self):
        nc = self.nc
        with contextlib.ExitStack() as st:
            sems = {}
            for i, k in enumerate(self.keys):
                sems[k] = st.enter_context(nc.semaphore("s%d" % i))
            block = st.enter_context(nc.Block())

            def run(engname, e):
                seen = {}
                for fn, deps, key, inc in self.ops[engname]:
                    for k, v in deps.items():
                        if engname == "pe" and k.startswith("pe_"):
                            continue
                        if seen.get(k, 0) >= v:
                            continue
                        e.wait_ge(sems[k], v)
                        seen[k] = v
                    if fn is not None:
                        getattr(e, fn[0])(*fn[1], **fn[2]).then_inc(sems[key], inc)

            @block.tensor
            def _(e):
                run("pe", e)

            @block.vector
            def _(e):
                run("dve", e)

            @block.scalar
            def _(e):
                run("act", e)

            @block.gpsimd
            def _(e):
                run("pool", e)

            @block.sync
            def _(e):
                run("sp", e)


class Tile:
    def __init__(self, nc, name, shape, dtype, psum=False):
        if psum:
            self.t = nc.alloc_psum_tensor("p_" + name, shape, dtype)
        else:
            self.t = nc.alloc_sbuf_tensor("t_" + name, shape, dtype)
        self.b = Buf(name)

    def __getitem__(self, idx):
        return self.t[idx]


def make_consts():
    c = {}
    p = np.arange(128)
    i = np.arange(128)
    mt = np.zeros((128, 4, 128), np.float64)
    dq = np.zeros((128, 4, 128), np.float64)
    kt = np.zeros((128, 4), np.float64)
    for h in range(4):
        g = GAM[h]
        diff = i[None, :] - p[:, None]
        mt[:, h, :] = np.where(diff >= 0, g ** np.maximum(diff, 0), 0.0)
        dq[:, h, :] = (g ** (i + 1.0))[None, :]
        kt[:, h] = g ** (127.0 - p)
    c["maskT"] = mt.reshape(128, 512)
    c["decq"] = dq.reshape(128, 512)
    c["ktail"] = kt
    same = (p[:, None] // 64) == (i[None, :] // 64)
    mus = (same & (i[None, :] > p[:, None])).astype(np.float64)
    mui = (same & (i[None, :] >= p[:, None])).astype(np.float64)
    mls = (same & (i[None, :] < p[:, None])).astype(np.float64)
    mu2 = np.concatenate([mus, mui], 1)
    c["MU"] = np.concatenate([mu2, mu2], 1)
    c["ML"] = np.concatenate([mls, mls], 1)
    c["onesbd"] = same.astype(np.float64)
    c["ident"] = np.eye(128)
    sm = np.ones((128, 512)); sm[:, ::64] = 0.0
    c["scanm"] = sm
    half = 64
    fr = (np.float32(10000.0) ** (-(np.arange(half, dtype=np.float32)) / np.float32(half))).astype(np.float64)
    c["freq"] = np.concatenate([fr, fr])[:, None]
    c["sgn"] = np.concatenate([-np.ones(64), np.ones(64)])[:, None]
    ic = np.zeros((128, 4, 16))
    for g, w in enumerate((2, 4, 8, 16)):
        ic[:, g, :] = (1.0 / np.minimum(np.arange(16) + 1, w))[None, :]
    c["invc"] = ic.reshape(128, 64)
    c["eps"] = np.array([1e-6, 1e-5, 64e-5, 1e-18, np.pi / 2])[None, :].repeat(128, 0)
    off = {}
    cols = []
    o = 0
    for k, v in c.items():
        off[k] = (o, v.shape[1])
        o += v.shape[1]
        cols.append(v)
    return np.concatenate(cols, 1).astype(np.float32), off


CONSTS, COFF = make_consts()
NCONST = CONSTS.shape[1]


def prep_weights(inp):
    L = inp["w_in"].shape[0]
    w_in = np.asarray(inp["w_in"], np.float32)
    wblk = np.zeros((L, NBLKW, 128, 4096), np.float32)

    def inblk(cols):
        a = w_in[:, :, cols]
        return a.reshape(L, 8, 128, len(cols)).transpose(0, 2, 1, 3)

    def put(b, cols):
        a = inblk(cols)
        n = a.shape[3]
        tmp = np.zeros((L, 128, 8, 512), np.float32)
        tmp[:, :, :, :n] = a
        wblk[:, b] = tmp.reshape(L, 128, 4096)

    ar = np.arange
    sw = np.concatenate([np.concatenate([ar(h * 128 + 64, h * 128 + 128), ar(h * 128, h * 128 + 64)]) for h in range(4)])
    put(0, ar(0, 512)); put(1, ar(512, 1024))
    put(2, ar(1024, 1536)); put(3, 1024 + sw)
    put(4, ar(1536, 2048)); put(5, 1536 + sw)
    put(6, ar(2048, 2560)); put(7, ar(2560, 3072))
    put(8, ar(3072, 3584)); put(9, ar(3648, 4160)); put(10, ar(4160, 4672))
    put(11, np.concatenate([ar(3584, 3648), ar(4672, 4736)]))
    put(12, ar(4736, 5248))
    wb = np.asarray(inp["w_branch"], np.float32)
    for n in range(3):
        wblk[:, 13 + 3 * n] = wb[:, n].reshape(L, 4, 128, 1024).transpose(0, 2, 1, 3).reshape(L, 128, 4096)
        for hf in range(2):
            put(13 + 3 * n + 1 + hf, 5248 + n * 1024 + hf * 512 + ar(512))
    wo = np.asarray(inp["w_out"], np.float32)
    for hf in range(2):
        a = wo[:, :, hf * 512:(hf + 1) * 512].reshape(L, 8, 128, 512).transpose(0, 2, 1, 3)
        wblk[:, 22 + hf] = a.reshape(L, 128, 4096)
    wsm = np.zeros((L, 128, 1536), np.float32)
    pw = np.asarray(inp["pool_w"], np.float32)
    wsm[:, :, 0:512] = pw.transpose(0, 2, 1, 3).reshape(L, 128, 512)
    wsm[:, 0:64, 512:1024] = np.asarray(inp["rwkv_w2"], np.float32)
    wsm[:, 64:128, 1024:1536] = np.asarray(inp["rwkv_a2"], np.float32)

    def fm(v):
        return np.asarray(v, np.float32).reshape(L, 4, 128).transpose(0, 2, 1)
    mu = np.asarray(inp["rwkv_shift_mu"], np.float32)
    vecs = np.zeros((L, 128, NV), np.float32)
    vecs[:, :, 0:4] = fm(inp["pool_scale"])
    vecs[:, :, 4:8] = fm(mu[:, 0:512])
    vecs[:, :, 8:12] = fm(mu[:, 576:1088])
    vecs[:, :, 12:16] = fm(mu[:, 1088:1600])
    vecs[:, 0:64, 16] = mu[:, 512:576]
    vecs[:, 64:128, 16] = mu[:, 1600:1664]
    vecs[:, :, 17:21] = fm(inp["rwkv_w0"])
    vecs[:, :, 21:25] = fm(inp["rwkv_a0"])
    vecs[:, :, 25:29] = fm(inp["rwkv_k_k"])
    vecs[:, :, 29:33] = fm(inp["rwkv_k_a"])
    vecs[:, :, 33:37] = fm(np.asarray(inp["rwkv_r_k"], np.float32).reshape(L, 512))
    rows = np.stack([np.asarray(inp["ret_norm_g"], np.float32), np.asarray(inp["rwkv_ln_w"], np.float32),
                     np.asarray(inp["rwkv_ln_b"], np.float32)], 1)
    ngb = np.concatenate([np.asarray(inp["norm_g"], np.float32), np.asarray(inp["b_ada"], np.float32)], 1)
    return dict(wblk=wblk, wsm=wsm, vecs=vecs, rows=np.ascontiguousarray(rows), ngb=np.ascontiguousarray(ngb),
                wada=np.ascontiguousarray(np.asarray(inp["w_ada"], np.float32)),
                fg=np.asarray(inp["final_g"], np.float32).reshape(1, D), consts=CONSTS)


def build(T, L, NB=2, dbg=None):
    TT = NB * 128
    NT = T // TT
    NCH = NB * 2
    CPB = 512 // TT
    nc = bass.Bass("TRN2", target_bir_lowering=False)
    S = Sched(nc)

    def dram(name, shape, dt, kind):
        return nc.dram_tensor(name, shape, dt, kind=kind).ap()
    x_in = dram("x", [T, D], F32, "ExternalInput")
    c_in = dram("c", [128, 8], F32, "ExternalInput")
    pos_in = dram("pos", [1, T], I32, "ExternalInput")
    wblk = dram("wblk", [L, NBLKW, 128, 4096], F32, "ExternalInput")
    wsm_in = dram("wsm", [L, 128, 1536], F32, "ExternalInput")
    vecs_in = dram("vecs", [L, 128, NV], F32, "ExternalInput")
    rows_in = dram("rows", [L, 3, 512], F32, "ExternalInput")
    ngb_in = dram("ngb", [L, 4096], F32, "ExternalInput")
    wada_in = dram("wada", [L, D, 3 * D], F32, "ExternalInput")
    fg_in = dram("fg", [1, D], F32, "ExternalInput")
    consts_in = dram("consts", [128, NCONST], F32, "ExternalInput")
    out = dram("out", [T, D], F32, "ExternalOutput")
    dbg_o = dram("dbg_o", [128, 20 * TT], BF16, "ExternalOutput") if dbg == "dump" else None
    b_dbg = Buf("dbg")
    dbg_f = dram("dbg_f", [128, 16384], F32, "ExternalOutput") if dbg == "dump" else None
    dstate = {"off": 0, "map": {}, "on": False}
    global DUMP_MAP
    DUMP_MAP = dstate["map"]
    dstage = None

    def dump(name, ap, rd, width):
        if dbg != "dump" or not dstate["on"]:
            return
        o = dstate["off"]
        dstate["map"][name] = (o, width)
        st_ = hn
        op("dve", lambda e: e.tensor_copy(out=st_[:, 0:width], in_=ap), rd, [st_])
        dma("sp", lambda e: e.dma_start(out=dbg_f[:, o:o + width], in_=st_[:, 0:width]), "dbgf", [st_], [b_dbg])
        dstate["off"] += width
    wsc = dram("wsc", [L, NBLKW, 128, 4096], BF16, "Internal")
    xs = dram("xs", [T, D], F32, "Internal")
    b_wsc = [Buf("wsc%d" % l) for l in range(L)]
    b_xs = [Buf("xs%d" % i) for i in range(NT)]
    b_out = [Buf("out%d" % i) for i in range(NT)]

    def tile(name, shape, dt):
        return Tile(nc, name, shape, dt)

    CT = tile("consts", [128, NCONST], F32)

    def cst(name):
        o, n = COFF[name]
        return CT[:, o:o + n]
    identb = tile("identb", [128, 128], BF16)
    onesb = tile("onesb", [128, 128], BF16)
    cbc = tile("cbc", [128, 8, 128], F32)
    cfm = tile("cfm", [128, 8], F32)
    modbc = tile("modbc", [128, 3 * D], F32)
    gbc = tile("gbc", [128, D], F32)
    rows3 = tile("rows3", [128, 3, 512], F32)
    vecs = tile("vecs", [128, NV], F32)
    wsmb = tile("wsmb", [128, 1536], BF16)
    big = tile("big", [128, 8 * TT], F32)
    bada = tile("bada", [128, TT], F32)
    class _Alias:
        def __init__(self, t, lo, hi, shp=None):
            self.t, self.lo, self.hi, self.b, self.shp = t, lo, hi, t.b, shp

        def __getitem__(self, idx):
            v = self.t[:, self.lo:self.hi]
            if self.shp:
                v = v.rearrange("p (h i) -> p h i", i=self.shp)
            return v[idx]
    xtl = [tile("xt%d" % i, [128, NB, D], F32) for i in range(2)]
    hT = tile("hT", [128, 8, TT], BF16)
    hn = _Alias(big, 0, D)
    hb = tile("hb", [128, D], BF16)
    junk = hb
    wsmf = big
    st4 = tile("st4", [128, 16], F32)
    NSLOT = 3
    wslot = [tile("wslot%d" % i, [128, 4096], BF16) for i in range(NSLOT)]
    FW = TT + 16
    Ft = [tile("F%d" % i, [128, 4, FW], F32) for i in range(4)]
    NRING = 5
    ringt = [tile("ring%d" % i, [128, FW], F32) for i in range(NRING)]
    wkn = "stg dd sig ag cs csx Pin Pex Q kk sqb rn kp prb tb".split()
    wk = {n: tile("wk_" + n, [128, FW], F32) for n in wkn}
    WK2_PENDING = True
    BB = [tile("BB%d" % i, [128, 4, TT], BF16) for i in range(6)]
    artm = [tile("artm%d" % i, [128, 4, NB, 2, 128], BF16) for i in range(2)]
    Vm = [tile("Vm%d" % i, [128, 4, TT], BF16) for i in range(2)]
    opool = tile("opool", [128, 4, TT], BF16)
    oret = tile("oret", [128, 4, TT], BF16)
    orw = tile("orw", [128, 4, TT], BF16)
    puh = tile("puh", [128, 4, 16], F32)
    hal = tile("hal", [128, 16], F32)
    Rst = tile("Rst", [128, 4, 128], F32)
    Rb = tile("Rb", [128, 4, 128], BF16)
    Hst = [tile("Hst%d" % g, [128, 2, 64], F32) for g in range(2)]
    Hb = [tile("Hb%d" % g, [128, 2, 64], BF16) for g in range(2)]
    ksF = tile("ksF", [128, 4, TT], F32)

    Sm = _Alias(hb, 0, 512, 128)
    lor = _Alias(hb, 512, 512 + TT)
    AX1 = tile("AX1", [128, 8, 256], BF16)
    AX2 = tile("AX2", [128, 8, 256], BF16)
    BR = [[tile("BR%d_%d" % (i, g), [128, 4, 256], BF16) for g in range(2)] for i in range(2)]
    Am = [[tile("Am%d_%d" % (i, g), [128, 4, 128], BF16) for g in range(2)] for i in range(2)]
    Gs = [tile("Gs%d" % g, [128, 256], BF16) for g in range(2)]
    fdummy = tile("fdummy", [128, 2], F32)
    nvec = tile("nvec", [128, 8], F32)

    class ATile:
        def __init__(self, name, base, lo):
            self.b = Buf(name)
            self.base, self.lo = base, lo

        def __getitem__(self, idx):
            v = self.base[:]
            if len(v.shape) == 3:
                v = v.rearrange("p h x -> p (h x)")
            return v.bitcast(F32)[:, self.lo:self.lo + TT][idx]
    wk2 = {}
    _regions = [(AX1, 4), (AX2, 4), (BR[0][0], 2), (BR[0][1], 2), (BR[1][0], 2), (BR[1][1], 2)]
    _names = "sig ag cs csx Pin Pex Q kk sqb rn kp prb tb".split()
    _ri = 0
    for base_, n_ in _regions:
        for i_ in range(n_):
            if _ri < len(_names):
                wk2[_names[_ri]] = ATile("wk2_" + _names[_ri], base_, i_ * TT)
                _ri += 1
    assert _ri == len(_names)
    Us = [[tile("Us%d_%d" % (i, g), [128, 256], BF16) for g in range(2)] for i in range(2)]
    WCt = tile("WCt", [128, 4, NCH], F32)

    class _PosI:
        b = bada.b

        def __getitem__(self, idx):
            return bada[:, 0:TT].bitcast(I32)[idx]
    posi = _PosI()
    psum_all = nc.alloc_psum_tensor("psum_all", [128, 4096], F32)

    class Bank:
        def __init__(self, i):
            self.i = i
            self.b = Buf("bank%d" % i)

        def __getitem__(self, idx):
            return psum_all[:, self.i * 512:(self.i + 1) * 512][idx]
    banks = [Bank(i) for i in range(8)]

    def mbank(i0, n):
        return psum_all[:, i0 * 512:(i0 + n) * 512]
    ring5t = [tile("ringw%d" % i, [128, 512], F32) for i in range(4)]
    state = {"bank": 0, "ring": 0, "ring5": 0}

    def ring5():
        r = ring5t[state["ring5"] % 4]
        state["ring5"] += 1
        return r

    def pb():
        b = banks[state["bank"] % 8]
        state["bank"] += 1
        return b

    def ring():
        r = ringt[state["ring"] % NRING]
        state["ring"] += 1
        return r

    def op(eng, fn, r, w):
        return S.op(eng, fn, [t.b if hasattr(t, "b") else t for t in r], [t.b if hasattr(t, "b") else t for t in w])

    def dma(eng, fn, key, r, w):
        return S.dma(eng, fn, key, [t.b if hasattr(t, "b") else t for t in r], [t.b if hasattr(t, "b") else t for t in w])

    def cast_layer(l):
        key = "cast%d" % l
        tok = None
        for b in range(NBLKW):
            for hf in range(2):
                tok = S.dma("pool", lambda e, l=l, b=b, hf=hf: e.dma_start(
                    out=wsc[l, b, :, hf * 2048:(hf + 1) * 2048], in_=wblk[l, b, :, hf * 2048:(hf + 1) * 2048]), key)
        b_wsc[l].lw = tok

    order = [0, 1, 2, 3, 4, 5, 6, 7, 11, 8, 9, 10, 12] + [14, 15, 13, 17, 18, 16, 20, 21, 19, 22, 23]
    seq = [(l, b) for l in range(L) for _ in range(NT) for b in order]
    wstate = {"next_load": 0, "next_get": 0}
    wreleased = [False] * len(seq)

    def w_pump():
        while wstate["next_load"] < len(seq):
            n = wstate["next_load"]
            if n >= NSLOT and not wreleased[n - NSLOT]:
                break
            l, b = seq[n]
            s = n % NSLOT
            if b == 11:
                dma("sp", lambda e, l=l, b=b, s=s: e.dma_start(
                    out=wslot[s][:].rearrange("p (k c) -> p k c", c=512)[:, :, 0:128],
                    in_=wsc[l, b].rearrange("p (k c) -> p k c", c=512)[:, :, 0:128]), "ws%d" % s, [b_wsc[l]], [wslot[s]])
            else:
                dma("sp", lambda e, l=l, b=b, s=s: e.dma_start(out=wslot[s][:], in_=wsc[l, b]), "ws%d" % s, [b_wsc[l]], [wslot[s]])
            wstate["next_load"] += 1

    def wget(l, b):
        n = wstate["next_get"]
        assert seq[n] == (l, b), (seq[n], l, b)
        w_pump()
        assert wstate["next_load"] > n, "weight block not loadable (too many held)"
        wstate["next_get"] += 1
        wstate["last"] = n
        t_ = wslot[n % NSLOT]
        t_.widx = n
        return t_

    def wrel(*tiles):
        for t_ in tiles:
            wreleased[t_.widx] = True
        w_pump()

    def w8(wt):
        return wt[:].rearrange("p (k c) -> p k c", c=512)

    dma("sp", lambda e: e.dma_start(out=CT[:], in_=consts_in), "ld_c", [], [CT])
    dma("sp", lambda e: e.dma_start(out=cfm[:], in_=c_in), "ld_c2", [], [cfm])
    cast_layer(0)
    op("act", lambda e: e.activation(out=identb[:], in_=cst("ident"), func=AF.Copy), [CT], [identb])
    op("act", lambda e: e.activation(out=onesb[:], in_=cst("onesbd"), func=AF.Copy), [CT], [onesb])
    op("act", lambda e: e.activation(out=cfm[:], in_=cfm[:], func=AF.Silu), [cfm], [cfm])
    for k in range(8):
        op("dve", lambda e, k=k: e.tensor_scalar(out=cbc[:, k, :], in0=cst("ident"), scalar1=0.0, scalar2=cfm[:, k:k + 1],
                                                 op0=ALU.mult, op1=ALU.add), [CT, cfm], [cbc])
    for t_ in (artm[0], artm[1], Vm[0], Vm[1], Us[0][0], Us[0][1], Us[1][0], Us[1][1], Gs[0], Gs[1]):
        op("pool", lambda e, t_=t_: e.memset(t_[:], 0.0), [], [t_])
    eps_n = cst("eps")[:, 0:1]
    eps_r = cst("eps")[:, 1:2]
    eps_w = cst("eps")[:, 2:3]
    eps_k = cst("eps")[:, 3:4]

    def fm_proj(wt, nch, evac, ncols=128, k_parts=8):
        c = 0
        while c < nch:
            n = min(CPB, nch - c)
            bk = pb()
            for i in range(n):
                for k in range(8):
                    op("pe", lambda e, bk=bk, i=i, k=k, cc=c + i: e.matmul(
                        bk[0:ncols, i * TT:(i + 1) * TT], lhsT=w8(wt)[:, k, cc * 128:cc * 128 + ncols], rhs=hT[:, k, :],
                        start=(k == 0), stop=(k == 7)), [wt, hT], [bk])
            evac(bk, c, n)
            c += n

    def bview(bk, n):
        return bk[:, 0:n * TT].rearrange("p (n t) -> p n t", t=TT)

    for l in range(L):
        S.epoch = l
        if l + 1 < L:
            cast_layer(l + 1)
        dma("sp", lambda e, l=l: e.dma_start(out=vecs[:], in_=vecs_in[l]), "ld_v", [], [vecs])
        dma("sp", lambda e, l=l: e.dma_start(out=wsmf[:, 0:1536], in_=wsm_in[l]), "ld_w", [], [wsmf])
        dma("sp", lambda e, l=l: e.dma_start(out=rows3[:].rearrange("p a c -> p (a c)"),
                                             in_=rows_in[l].rearrange("a c -> (a c)").partition_broadcast(128)), "ld_r", [], [rows3])
        dma("sp", lambda e, l=l: e.dma_start(out=gbc[:], in_=ngb_in[l, 0:D].partition_broadcast(128)), "ld_g", [], [gbc])
        op("act", lambda e: e.activation(out=wsmb[:], in_=wsmf[:, 0:1536], func=AF.Copy), [wsmf], [wsmb])
        NWB = 3 * D // TT
        for nb in range(NWB):
            dma("sp", lambda e, l=l, nb=nb: e.dma_start(
                out=big[:].rearrange("p (k c) -> p k c", c=TT),
                in_=wada_in[l, :, nb * TT:(nb + 1) * TT].rearrange("(k p) c -> p k c", p=128)), "ld_a", [], [big])
            dma("sp", lambda e, l=l, nb=nb: e.dma_start(
                out=bada[:, 0:TT], in_=ngb_in[l, D + nb * TT:D + (nb + 1) * TT].partition_broadcast(128)), "ld_b", [], [bada])
            bk = pb()
            for k in range(8):
                op("pe", lambda e, bk=bk, k=k: e.matmul(bk[:, 0:TT], lhsT=cbc[:, k, :], rhs=big[:, k * TT:(k + 1) * TT],
                                                        start=(k == 0), stop=(k == 7)), [cbc, big], [bk])
            op("dve", lambda e, bk=bk, nb=nb: e.tensor_tensor(out=modbc[:, nb * TT:(nb + 1) * TT], in0=bk[:, 0:TT], in1=bada[:, 0:TT],
                                                              op=ALU.add), [bk, bada], [modbc])
        op("dve", lambda e: e.scalar_tensor_tensor(out=modbc[:, D:2 * D], in0=modbc[:, D:2 * D], scalar=1.0, in1=gbc[:],
                                                   op0=ALU.add, op1=ALU.mult), [modbc, gbc], [modbc])
        B_bc = modbc[:, 0:D]
        A_bc = modbc[:, D:2 * D]
        G_bc = modbc[:, 2 * D:3 * D]
        op("pool", lambda e: e.memset(puh[:], 0.0), [], [puh])
        op("pool", lambda e: e.memset(hal[:], 0.0), [], [hal])
        op("pool", lambda e: e.memset(Rst[:], 0.0), [], [Rst])
        op("pool", lambda e: e.memset(Rb[:], 0.0), [], [Rb])
        for g_ in range(2):
            op("pool", lambda e, g_=g_: e.memset(Hst[g_][:], 0.0), [], [Hst[g_]])
            op("pool", lambda e, g_=g_: e.memset(Hb[g_][:], 0.0), [], [Hb[g_]])
        op("dve", lambda e: e.tensor_scalar(out=nvec[:], in0=vecs[:, 17:25], scalar1=-1.0, scalar2=None, op0=ALU.mult), [vecs], [nvec])
        poolw = wsmb[:, 0:512].rearrange("p (g d) -> p g d", d=128)
        lw2 = wsmb[:, 512:1024]
        la2 = wsmb[:, 1024:1536]

        def V(i, n=4):
            return vecs[:, i:i + n]

        for it in range(NT):
            t0 = it * TT
            gi = l * NT + it
            xt = xtl[gi % 2]

            def load_x(l_, it_):
                g_ = l_ * NT + it_
                src = x_in if l_ == 0 else xs
                rd = [] if l_ == 0 else [b_xs[it_]]
                dma("sp", lambda e: e.dma_start(out=xtl[g_ % 2][:], in_=src[it_ * TT:(it_ + 1) * TT, :].rearrange("(j p) d -> p j d", p=128)),
                    "ld_x%d" % (g_ % 2), rd, [xtl[g_ % 2]])
            if gi == 0 or NT == 1:
                load_x(l, it)
            if dbg == "pro":
                S.emit()
                return nc
            dstate["on"] = (l == 0 and it == NT - 1)
            for j in range(NB):
                op("act", lambda e, j=j: e.activation(out=junk[:], in_=xt[:, j, :], func=AF.Square, accum_out=st4[:, j:j + 1]),
                   [xt], [junk, st4])
            op("act", lambda e: e.activation(out=st4[:, 4:4 + NB], in_=st4[:, 0:NB], func=AF.Sqrt, bias=eps_n, scale=1.0 / D),
               [st4, CT], [st4])
            op("dve", lambda e: e.reciprocal(out=st4[:, 8:8 + NB], in_=st4[:, 4:4 + NB]), [st4], [st4])
            for j in range(NB):
                op("dve", lambda e, j=j: e.scalar_tensor_tensor(out=hn[:], in0=xt[:, j, :], scalar=st4[:, 8 + j:9 + j], in1=A_bc,
                                                                op0=ALU.mult, op1=ALU.mult), [xt, st4, modbc], [hn])
                op("dve", lambda e: e.tensor_tensor(out=hb[:], in0=hn[:], in1=B_bc, op=ALU.add), [hn, modbc], [hb])
                bk = pb()
                bkb = bk[:].bitcast(BF16)
                for k in range(8):
                    op("pe", lambda e, bkb=bkb, k=k: e.transpose(out=bkb[:, k * 128:(k + 1) * 128], in_=hb[:, k * 128:(k + 1) * 128],
                                                                  identity=identb[:]), [hb, identb], [bk])
                op("act", lambda e, bkb=bkb, j=j: e.activation(out=hT[:, :, j * 128:(j + 1) * 128],
                                                               in_=bkb.rearrange("p (k t) -> p k t", t=128), func=AF.Copy), [bk], [hT])
            if dbg == "N":
                S.emit()
                return nc
            pu, sa, sb, sg = Ft
            op("dve", lambda e: e.tensor_copy(out=pu[:, :, 0:16], in_=puh[:]), [puh], [pu])
            wt = wget(l, 0)
            fm_proj(wt, 4, lambda bk, c, n: op("act", lambda e: e.activation(out=pu[:, c:c + n, 16:16 + TT], in_=bview(bk, n), func=AF.Copy),
                                               [bk], [pu]))
            wrel(wt)
            wt = wget(l, 1)
            fm_proj(wt, 4, lambda bk, c, n: op("act", lambda e: e.activation(out=sg[:, c:c + n, 0:TT], in_=bview(bk, n), func=AF.Silu),
                                               [bk], [sg]))
            wrel(wt)
            op("pool", lambda e: e.tensor_copy(out=puh[:], in_=pu[:, :, TT:TT + 16]), [pu], [puh])
            for g in range(4):
                cur = pu
                for m in range(g + 1):
                    sh = 1 << m
                    lo = (1 << (m + 1)) - 1
                    dst = sa if (m % 2 == 0) else sb
                    op("dve", lambda e, g=g, cur=cur, dst=dst, sh=sh, lo=lo: e.tensor_tensor(
                        out=dst[:, g, lo:FW], in0=cur[:, g, lo:FW], in1=cur[:, g, lo - sh:FW - sh], op=ALU.add), [cur], [dst])
                    cur = dst
                wg = float(1 << (g + 1))
                dpool = BB[0]
                op("dve", lambda e, g=g, cur=cur, wg=wg: e.scalar_tensor_tensor(
                    out=dpool[:, g, :], in0=cur[:, g, 16:FW], scalar=1.0 / wg, in1=pu[:, g, 16:FW], op0=ALU.mult, op1=ALU.subtract),
                    [cur, pu], [dpool])
                if it == 0:
                    r1 = ring()
                    op("dve", lambda e, g=g, cur=cur, r1=r1: e.tensor_tensor(out=r1[:, 0:16], in0=cur[:, g, 16:32],
                                                                             in1=cst("invc")[:, g * 16:(g + 1) * 16], op=ALU.mult), [cur, CT], [r1])
                    op("dve", lambda e, g=g, r1=r1: e.tensor_tensor(out=dpool[:, g, 0:16], in0=r1[:, 0:16], in1=pu[:, g, 16:32],
                                                                    op=ALU.subtract), [r1, pu], [dpool])
            for g in range(4):
                bk = pb()
                op("pe", lambda e, bk=bk, g=g: e.matmul(bk[:, 0:TT], lhsT=poolw[:, g, :], rhs=BB[0][:, g, :], start=True, stop=True),
                   [wsmb, BB[0]], [bk])
                op("dve", lambda e, bk=bk, g=g: e.scalar_tensor_tensor(out=opool[:, g, :], in0=bk[:, 0:TT], scalar=V(0)[:, g:g + 1],
                                                                        in1=sg[:, g, 0:TT], op0=ALU.mult, op1=ALU.mult), [bk, vecs, sg], [opool])
            if dbg == "P":
                S.emit()
                return nc
            if it + 1 < NT:
                load_x(l, it + 1)
            elif l + 1 < L and NT > 1:
                load_x(l + 1, 0)
            tabs = Ft[0]
            sret = Ft[1]
            qr, qd, kr, ktm, vtm = BB[0], BB[1], BB[2], BB[3], BB[4]
            dma("sp", lambda e, t0=t0: e.dma_start(out=posi[:], in_=pos_in[0, t0:t0 + TT].partition_broadcast(128)), "ld_p", [], [posi])
            ang, nn, r2 = ring(), ring(), ring()
            op("dve", lambda e: e.tensor_copy(out=ang[:, 0:TT], in_=posi[:]), [posi], [ang])
            op("dve", lambda e: e.tensor_scalar(out=ang[:, 0:TT], in0=ang[:, 0:TT], scalar1=cst("freq"), scalar2=None, op0=ALU.mult),
               [ang, CT], [ang])
            op("dve", lambda e: e.tensor_scalar(out=nn[:, 0:TT], in0=ang[:, 0:TT], scalar1=float(1.0 / (2 * np.pi)), scalar2=None,
                                                op0=ALU.mult), [ang], [nn])
            op("dve", lambda e: e.tensor_copy(out=posi[:], in_=nn[:, 0:TT]), [nn], [posi])
            op("dve", lambda e: e.tensor_copy(out=nn[:, 0:TT], in_=posi[:]), [posi], [nn])
            c1 = 6.28125
            c2 = float(2 * np.pi - 6.28125)
            op("dve", lambda e: e.scalar_tensor_tensor(out=r2[:, 0:TT], in0=nn[:, 0:TT], scalar=-c1, in1=ang[:, 0:TT],
                                                       op0=ALU.mult, op1=ALU.add), [nn, ang], [r2])
            op("dve", lambda e: e.scalar_tensor_tensor(out=r2[:, 0:TT], in0=nn[:, 0:TT], scalar=-c2, in1=r2[:, 0:TT],
                                                       op0=ALU.mult, op1=ALU.add), [nn, r2], [r2])
            ws_, wc_, mk_ = ring(), ring(), ring()
            twopi = float(2 * np.pi)
            op("dve", lambda e: e.tensor_scalar(out=mk_[:, 0:TT], in0=r2[:, 0:TT], scalar1=0.0, scalar2=float(np.pi), op0=ALU.add, op1=ALU.is_gt),
               [r2], [mk_])
            op("dve", lambda e: e.scalar_tensor_tensor(out=ws_[:, 0:TT], in0=mk_[:, 0:TT], scalar=-twopi, in1=r2[:, 0:TT], op0=ALU.mult, op1=ALU.add),
               [mk_, r2], [ws_])
            op("dve", lambda e: e.tensor_scalar(out=mk_[:, 0:TT], in0=r2[:, 0:TT], scalar1=float(np.pi / 2), scalar2=float(np.pi), op0=ALU.add,
                                                op1=ALU.is_gt), [r2], [mk_])
            op("dve", lambda e: e.scalar_tensor_tensor(out=wc_[:, 0:TT], in0=mk_[:, 0:TT], scalar=-twopi, in1=r2[:, 0:TT], op0=ALU.mult, op1=ALU.add),
               [mk_, r2], [wc_])
            op("act", lambda e: e.activation(out=tabs[:, 0, 0:TT], in_=wc_[:, 0:TT], func=AF.Sin, bias=cst("eps")[:, 4:5]), [wc_, CT], [tabs])
            op("act", lambda e: e.activation(out=tabs[:, 1, 0:TT], in_=ws_[:, 0:TT], func=AF.Sin, scale=cst("sgn")), [ws_, CT], [tabs])
            op("dve", lambda e: e.tensor_scalar(out=tabs[:, 2:4, 0:TT], in0=tabs[:, 0:2, 0:TT], scalar1=float(128.0 ** -0.5), scalar2=None,
                                                 op0=ALU.mult), [tabs], [tabs])

            def rope(dst, dq, tc):
                wa = wget(l, 2 if dst is qr else 4)
                wb_ = wget(l, 3 if dst is qr else 5)
                for h in range(4):
                    ba, bb = pb(), pb()
                    for (bk_, wt_) in ((ba, wa), (bb, wb_)):
                        for k in range(8):
                            op("pe", lambda e, bk_=bk_, wt_=wt_, k=k, h=h: e.matmul(
                                bk_[:, 0:TT], lhsT=w8(wt_)[:, k, h * 128:(h + 1) * 128], rhs=hT[:, k, :], start=(k == 0), stop=(k == 7)),
                                [wt_, hT], [bk_])
                    t1, t2 = ring(), ring()
                    op("dve", lambda e, ba=ba, t1=t1: e.tensor_tensor(out=t1[:, 0:TT], in0=ba[:, 0:TT], in1=tabs[:, tc, 0:TT], op=ALU.mult),
                       [ba, tabs], [t1])
                    op("dve", lambda e, bb=bb, t2=t2: e.tensor_tensor(out=t2[:, 0:TT], in0=bb[:, 0:TT], in1=tabs[:, tc + 1, 0:TT], op=ALU.mult),
                       [bb, tabs], [t2])
                    if dq is None:
                        op("dve", lambda e, t1=t1, t2=t2, h=h: e.tensor_tensor(out=dst[:, h, :], in0=t1[:, 0:TT], in1=t2[:, 0:TT], op=ALU.add),
                           [t1, t2], [dst])
                    else:
                        op("dve", lambda e, t1=t1, t2=t2: e.tensor_tensor(out=t1[:, 0:TT], in0=t1[:, 0:TT], in1=t2[:, 0:TT], op=ALU.add),
                           [t1, t2], [t1])
                        op("act", lambda e, t1=t1, h=h: e.activation(out=dst[:, h, :], in_=t1[:, 0:TT], func=AF.Copy), [t1], [dst])
                        op("dve", lambda e, t1=t1, h=h: e.tensor_tensor(
                            out=dq[:, h, :].rearrange("p (j i) -> p j i", i=128), in0=t1[:, 0:TT].rearrange("p (j i) -> p j i", i=128),
                            in1=cst("decq")[:, h * 128:(h + 1) * 128].unsqueeze(1).broadcast_to([128, NB, 128]), op=ALU.mult), [t1, CT], [dq])
                wrel(wa, wb_)
            dump("tabs", tabs[:, :, 0:TT], [tabs], 4 * TT) if False else None
            for i_ in range(4):
                dump("tab%d" % i_, tabs[:, i_, 0:TT], [tabs], TT)
            rope(qr, qd, 0)
            rope(kr, None, 2)
            dump("qr", qr[:].rearrange("p c t -> p (c t)"), [qr], 4 * TT)
            dump("qd", qd[:].rearrange("p c t -> p (c t)"), [qd], 4 * TT)
            dump("kr", kr[:].rearrange("p c t -> p (c t)"), [kr], 4 * TT)
            wt = wget(l, 6)
            for j in range(NB):
                bk = pb()
                for k in range(8):
                    op("pe", lambda e, bk=bk, k=k, j=j, wt=wt: e.matmul(bk[:, :], lhsT=hT[:, k, j * 128:(j + 1) * 128], rhs=w8(wt)[:, k, :],
                                                                      start=(k == 0), stop=(k == 7)), [wt, hT], [bk])
                op("act", lambda e, bk=bk, j=j: e.activation(out=vtm[:, :, j * 128:(j + 1) * 128],
                                                             in_=bk[:, :].rearrange("p (h e) -> p h e", e=128), func=AF.Copy), [bk], [vtm])
            wrel(wt)
            wt = wget(l, 7)
            fm_proj(wt, 4, lambda bk, c, n: op("act", lambda e: e.activation(out=sret[:, c:c + n, 0:TT], in_=bview(bk, n), func=AF.Silu),
                                               [bk], [sret]))
            wrel(wt)
            for j in range(NB):
                js = slice(j * 128, (j + 1) * 128)
                bk = pb()
                bkb = bk[:].bitcast(BF16)
                for h in range(4):
                    op("pe", lambda e, bkb=bkb, h=h, js=js: e.transpose(out=bkb[:, h * 128:(h + 1) * 128], in_=kr[:, h, js], identity=identb[:]),
                       [kr, identb], [bk])
                op("dve", lambda e, bkb=bkb, js=js: e.tensor_tensor(
                    out=ktm[:, :, js], in0=bkb[:, 0:512].rearrange("p (h d) -> p h d", d=128),
                    in1=cst("ktail").unsqueeze(2).broadcast_to([128, 4, 128]), op=ALU.mult), [bk, CT], [ktm])
                bs = pb()
                for h in range(4):
                    op("pe", lambda e, bs=bs, h=h, js=js: e.matmul(bs[:, h * 128:(h + 1) * 128], lhsT=kr[:, h, js], rhs=qr[:, h, js],
                                                                   start=True, stop=True), [kr, qr], [bs])
                op("dve", lambda e, bs=bs: e.tensor_tensor(out=Sm[:].rearrange("p h i -> p (h i)"), in0=bs[:, :], in1=cst("maskT"), op=ALU.mult),
                   [bs, CT], [Sm])
                bo = pb()
                for h in range(4):
                    op("pe", lambda e, bo=bo, h=h, js=js: e.matmul(bo[:, h * 128:(h + 1) * 128], lhsT=Sm[:, h, :], rhs=vtm[:, h, js],
                                                                   start=True, stop=False), [Sm, vtm], [bo])
                    op("pe", lambda e, bo=bo, h=h, js=js: e.matmul(bo[:, h * 128:(h + 1) * 128], lhsT=qd[:, h, js], rhs=Rb[:, h, :],
                                                                   start=False, stop=True), [qd, Rb], [bo])
                bkv = pb()
                for h in range(4):
                    op("pe", lambda e, bkv=bkv, h=h, js=js: e.matmul(bkv[:, h * 128:(h + 1) * 128], lhsT=ktm[:, h, js], rhs=vtm[:, h, js],
                                                                     start=True, stop=True), [ktm, vtm], [bkv])
                for h in range(4):
                    op("dve", lambda e, bkv=bkv, h=h: e.scalar_tensor_tensor(out=Rst[:, h, :], in0=Rst[:, h, :], scalar=float(GAM[h] ** 128),
                                                                              in1=bkv[:, h * 128:(h + 1) * 128], op0=ALU.mult, op1=ALU.add),
                       [Rst, bkv], [Rst])
                op("act", lambda e: e.activation(out=Rb[:], in_=Rst[:], func=AF.Copy), [Rst], [Rb])
                osb, sq, on = ring5(), ring5(), ring5()
                op("act", lambda e, bo=bo, osb=osb: e.activation(out=osb[:, 0:512], in_=bo[:, :], func=AF.Copy), [bo], [osb])
                op("act", lambda e, bo=bo, sq=sq: e.activation(out=sq[:, 0:512], in_=bo[:, :], func=AF.Square), [bo], [sq])
                if j == NB - 1:
                    dump("osb", osb[:, 0:512], [osb], 512)
                    dump("vtm", vtm[:].rearrange("p c t -> p (c t)"), [vtm], 4 * TT)
                    dump("ktm", ktm[:].rearrange("p c t -> p (c t)"), [ktm], 4 * TT)
                    dump("Sm", Sm[:].rearrange("p c t -> p (c t)"), [Sm], 512)
                head_norm(op, st4, osb, sq, on, 4, 128, eps_r, CT)
                if j == NB - 1:
                    dump("on", on[:, 0:512], [on], 512)
                op("dve", lambda e, on=on: e.tensor_tensor(out=on[:, 0:512], in0=on[:, 0:512], in1=rows3[:, 0, :], op=ALU.mult), [on, rows3], [on])
                bt_ = pb()
                for h in range(4):
                    op("pe", lambda e, bt_=bt_, h=h, on=on: e.transpose(out=bt_[:, h * 128:(h + 1) * 128], in_=on[:, h * 128:(h + 1) * 128],
                                                                        identity=cst("ident")), [on, CT], [bt_])
                op("dve", lambda e, bt_=bt_, js=js: e.tensor_tensor(out=oret[:, :, js], in0=bt_[:, :].rearrange("p (h t) -> p h t", t=128),
                                                                    in1=sret[:, :, js], op=ALU.mult), [bt_, sret], [oret])
            if dbg == "R":
                S.emit()
                return nc
            bon = Ft[0]
            srw = Ft[1]
            vsf = Ft[2]
            btt, ktt, vbb, Vtm, btm, ktm2 = BB
            stg_i = [0]

            rsF = Ft[3]

            class CV:
                def __init__(self, t, c):
                    self.t, self.c, self.b = t, c, t.b

                def __getitem__(self, idx):
                    return self.t[idx[0], self.c, idx[1]]

            def shifted(wt_, cc, idx, dst_ap, dst_t):
                bk = pb()
                for k in range(8):
                    op("pe", lambda e, bk=bk, k=k: e.matmul(bk[:, 0:TT], lhsT=w8(wt_)[:, k, cc * 128:(cc + 1) * 128], rhs=hT[:, k, :],
                                                            start=(k == 0), stop=(k == 7)), [wt_, hT], [bk])
                stg, dd = wk["stg"], wk["dd"]
                op("act", lambda e: e.activation(out=stg[:, 1:1 + TT], in_=bk[:, 0:TT], func=AF.Copy), [bk], [stg])
                op("dve", lambda e: e.tensor_copy(out=stg[:, 0:1], in_=hal[:, idx:idx + 1]), [hal], [stg])
                op("dve", lambda e: e.tensor_copy(out=hal[:, idx:idx + 1], in_=stg[:, TT:TT + 1]), [stg], [hal])
                op("dve", lambda e: e.tensor_tensor(out=dd[:, 0:TT], in0=stg[:, 0:TT], in1=stg[:, 1:1 + TT], op=ALU.subtract), [stg], [dd])
                mu_i = (4 + idx) if idx < 12 else 16
                op("dve", lambda e: e.scalar_tensor_tensor(out=dst_ap, in0=dd[:, 0:TT], scalar=vecs[:, mu_i:mu_i + 1], in1=stg[:, 1:1 + TT],
                                                           op0=ALU.mult, op1=ALU.add), [dd, vecs, stg], [dst_t])
            wl_ = wget(l, 11)
            lot = wk["sig"]
            shifted(wl_, 0, 12, lot[:, 0:TT], lot)
            wrel(wl_)
            op("act", lambda e: e.activation(out=lor[0:64, :], in_=lot[0:64, 0:TT], func=AF.Tanh), [lot], [lor])
            op("act", lambda e: e.activation(out=lor[64:128, :], in_=lot[64:128, 0:TT], func=AF.Copy), [lot], [lor])
            wr_ = wget(l, 8)
            for c in range(4):
                shifted(wr_, c, c, rsF[:, c, 0:TT], rsF)
            wrel(wr_)
            wk_ = wget(l, 9)
            for c in range(4):
                shifted(wk_, c, 4 + c, ksF[:, c, 0:TT], ksF)
            wrel(wk_)
            wv_ = wget(l, 10)
            for c in range(4):
                shifted(wv_, c, 8 + c, vsf[:, c, 0:TT], vsf)
                op("pool", lambda e, c=c: e.tensor_copy(out=vbb[:, c, :], in_=vsf[:, c, 0:TT]), [vsf], [vbb])
            wrel(wv_)
            fence_r = [AX1, AX2] + [BR[i_][g_] for i_ in range(2) for g_ in range(2)]
            op("pool", lambda e: e.memset(fdummy[:], 0.0), fence_r, list(wk2.values()))
            clists = []
            for c in range(4):
                cur_ = []

                def q(eng, fn, r, w, cur_=cur_):
                    cur_.append((eng, fn, r, w))
                clists.append(cur_)
                WK = wk if c % 2 == 0 else wk2
                rs_, ks_ = CV(rsF, c), CV(ksF, c)
                bw, ba_ = pb(), pb()
                q("pe", lambda e, bw=bw, c=c: e.matmul(bw[:, 0:TT], lhsT=lw2[:, c * 128:(c + 1) * 128], rhs=lor[:, :], start=True, stop=True),
                   [wsmb, lor], [bw])
                q("pe", lambda e, ba_=ba_, c=c: e.matmul(ba_[:, 0:TT], lhsT=la2[:, c * 128:(c + 1) * 128], rhs=lor[:, :],
                                                          start=True, stop=True), [wsmb, lor], [ba_])
                sig, ag = WK["sig"], WK["ag"]
                q("act", lambda e, bw=bw, sig=sig, c=c: e.activation(out=sig[:, 0:TT], in_=bw[:, 0:TT], func=AF.Sigmoid, bias=V(17)[:, c:c + 1]),
                   [bw, vecs], [sig])
                q("act", lambda e, ba_=ba_, ag=ag, c=c: e.activation(out=ag[:, 0:TT], in_=ba_[:, 0:TT], func=AF.Sigmoid, bias=V(21)[:, c:c + 1]),
                   [ba_, vecs], [ag])
                cs, csx = WK["cs"], WK["csx"]
                q("dve", lambda e, sig=sig, cs=cs: e.tensor_tensor_scan(out=cs[:, 0:TT], data0=cst("scanm")[:, 0:TT], data1=sig[:, 0:TT],
                                                                         initial=0.0, op0=ALU.mult, op1=ALU.add), [sig, CT], [cs])
                q("dve", lambda e, sig=sig, cs=cs, csx=csx: e.tensor_tensor(out=csx[:, 0:TT], in0=cs[:, 0:TT], in1=sig[:, 0:TT], op=ALU.subtract),
                   [cs, sig], [csx])
                Pin, Pex, Q = WK["Pin"], WK["Pex"], WK["Q"]
                q("act", lambda e, cs=cs, Pin=Pin: e.activation(out=Pin[:, 0:TT], in_=cs[:, 0:TT], func=AF.Exp, scale=-C0), [cs], [Pin])
                q("act", lambda e, csx=csx, Pex=Pex: e.activation(out=Pex[:, 0:TT], in_=csx[:, 0:TT], func=AF.Exp, scale=-C0), [csx], [Pex])
                q("act", lambda e, cs=cs, Q=Q: e.activation(out=Q[:, 0:TT], in_=cs[:, 0:TT], func=AF.Exp, scale=C0), [cs], [Q])
                q("pool", lambda e, Pin=Pin, c=c: e.tensor_copy(out=WCt[:, c, :], in_=Pin[:, 63:TT:64]), [Pin], [WCt])
                kk, sqb, rn = WK["kk"], WK["sqb"], WK["rn"]
                q("dve", lambda e, ks_=ks_, kk=kk, c=c: e.tensor_scalar(out=kk[:, 0:TT], in0=ks_[:, 0:TT], scalar1=V(25)[:, c:c + 1], scalar2=None,
                                                                         op0=ALU.mult), [ks_, vecs], [kk])
                sqv = sqb[:, 0:TT // 2].bitcast(BF16)
                q("dve", lambda e, kk=kk, sqv=sqv: e.tensor_tensor(out=sqv, in0=kk[:, 0:TT], in1=kk[:, 0:TT], op=ALU.mult), [kk], [sqb])
                bn_ = pb()
                q("pe", lambda e, bn_=bn_, sqv=sqv: e.matmul(bn_[:, 0:TT], lhsT=onesb[:], rhs=sqv, start=True, stop=True), [onesb, sqb], [bn_])
                q("act", lambda e, bn_=bn_, rn=rn: e.activation(out=rn[:, 0:TT], in_=bn_[:, 0:TT], func=AF.Ln, bias=eps_k), [bn_, CT], [rn])
                q("act", lambda e, rn=rn: e.activation(out=rn[:, 0:TT], in_=rn[:, 0:TT], func=AF.Exp, scale=-0.5), [rn], [rn])
                q("dve", lambda e, kk=kk, rn=rn: e.tensor_tensor(out=kk[:, 0:TT], in0=kk[:, 0:TT], in1=rn[:, 0:TT], op=ALU.mult), [kk, rn], [kk])
                kp = WK["kp"]
                q("dve", lambda e, ag=ag, kp=kp, c=c: e.tensor_scalar(out=kp[:, 0:TT], in0=ag[:, 0:TT], scalar1=-1.0, scalar2=V(29)[:, c:c + 1],
                                                                        op0=ALU.add, op1=ALU.mult), [ag, vecs], [kp])
                q("dve", lambda e, kp=kp, ks_=ks_: e.scalar_tensor_tensor(out=kp[:, 0:TT], in0=kp[:, 0:TT], scalar=1.0, in1=ks_[:, 0:TT],
                                                                            op0=ALU.add, op1=ALU.mult), [kp, ks_], [kp])
                prb = WK["prb"]
                prv = prb[:, 0:TT // 2].bitcast(BF16)
                q("dve", lambda e, rs_=rs_, kp=kp, prv=prv, c=c: e.scalar_tensor_tensor(out=prv, in0=rs_[:, 0:TT], scalar=V(33)[:, c:c + 1],
                                                                                        in1=kp[:, 0:TT], op0=ALU.mult, op1=ALU.mult),
                   [rs_, vecs, kp], [prb])
                bb_ = pb()
                q("pe", lambda e, bb_=bb_, prv=prv: e.matmul(bb_[:, 0:TT], lhsT=onesb[:], rhs=prv, start=True, stop=True), [onesb, prb], [bb_])
                q("dve", lambda e, bb_=bb_, c=c: e.tensor_tensor(out=bon[:, c, 0:TT], in0=bb_[:, 0:TT], in1=vsf[:, c, 0:TT], op=ALU.mult),
                   [bb_, vsf], [bon])
                for hp in range(2):
                    rw_ = slice(hp * 64, hp * 64 + 64)
                    q("dve", lambda e, kk=kk, Pex=Pex, hp=hp, rw_=rw_, c=c: e.scalar_tensor_tensor(
                        out=artm[hp][rw_, c, :, 0, :], in0=kk[rw_, 0:TT].rearrange("p (j i) -> p j i", i=128), scalar=-1.0,
                        in1=Pex[rw_, 0:TT].rearrange("p (j i) -> p j i", i=128), op0=ALU.mult, op1=ALU.mult), [kk, Pex], [artm[hp]])
                    q("pool", lambda e, rs_=rs_, Pin=Pin, hp=hp, rw_=rw_, c=c: e.tensor_tensor(
                        out=artm[hp][rw_, c, :, 1, :], in0=rs_[rw_, 0:TT].rearrange("p (j i) -> p j i", i=128),
                        in1=Pin[rw_, 0:TT].rearrange("p (j i) -> p j i", i=128), op=ALU.mult), [rs_, Pin], [artm[hp]])
                tb = WK["tb"]
                q("dve", lambda e, kk=kk, ag=ag, tb=tb: e.tensor_tensor(out=tb[:, 0:TT], in0=kk[:, 0:TT], in1=ag[:, 0:TT], op=ALU.mult),
                   [kk, ag], [tb])
                q("dve", lambda e, tb=tb, Q=Q, c=c: e.tensor_tensor(out=btt[:, c, :], in0=tb[:, 0:TT], in1=Q[:, 0:TT], op=ALU.mult),
                   [tb, Q], [btt])
                q("pool", lambda e, kp=kp, Q=Q, c=c: e.tensor_tensor(out=ktt[:, c, :], in0=kp[:, 0:TT], in1=Q[:, 0:TT], op=ALU.mult),
                   [kp, Q], [ktt])
            sa_, sb_ = clists[0] + clists[2], clists[1] + clists[3]
            k_ = len(clists[0]) // 2
            merged = list(sa_[:k_])
            ia_, ib_ = k_, 0
            while ia_ < len(sa_) or ib_ < len(sb_):
                if ib_ < len(sb_):
                    merged.append(sb_[ib_])
                    ib_ += 1
                if ia_ < len(sa_):
                    merged.append(sa_[ia_])
                    ia_ += 1
            for (eng_, fn_, r_, w_) in merged:
                op(eng_, fn_, r_, w_)
            op("pool", lambda e: e.memset(fdummy[:], 0.0), list(wk2.values()), fence_r)
            wg_ = wget(l, 12)
            fm_proj(wg_, 4, lambda bk, c, n: op("act", lambda e: e.activation(out=srw[:, c:c + n, 0:TT], in_=bview(bk, n), func=AF.Silu),
                                                [bk], [srw]))
            wrel(wg_)
            def part1(j):
                js = slice(j * 128, (j + 1) * 128)
                for (srcT, dstT) in ((vbb, Vtm), (btt, btm), (ktt, ktm2)):
                    bk = pb()
                    bkb = bk[:].bitcast(BF16)
                    for c in range(4):
                        op("pe", lambda e, bkb=bkb, c=c, srcT=srcT, js=js: e.transpose(out=bkb[:, c * 128:(c + 1) * 128], in_=srcT[:, c, js],
                                                                                      identity=identb[:]), [srcT, identb], [bk])
                    op("act", lambda e, bkb=bkb, dstT=dstT, js=js: e.activation(out=dstT[:, :, js], in_=bkb[:, 0:512].rearrange("p (c k) -> p c k", k=128),
                                                                                func=AF.Copy), [bk], [dstT])
                    if dstT is Vtm:
                        for cp_ in range(2):
                            rw_ = slice(cp_ * 64, cp_ * 64 + 64)
                            op("act", lambda e, bkb=bkb, js=js, cp_=cp_, rw_=rw_: e.activation(
                                out=Vm[cp_][rw_, :, js], in_=bkb[rw_, 0:512].rearrange("p (c k) -> p c k", k=128), func=AF.Copy), [bk], [Vm[cp_]])
                for c in range(4):
                    b1, b2, b3 = pb(), pb(), pb()
                    for hp in range(2):
                        arv = artm[hp][:, c, j, :, :].rearrange("p a i -> p (a i)")
                        op("pe", lambda e, b1=b1, hp=hp, c=c, arv=arv, js=js: e.matmul(b1[:, hp * 256:(hp + 1) * 256], lhsT=btt[:, c, js],
                                                                                       rhs=arv, start=True, stop=True), [btt, artm[hp]], [b1])
                        op("pe", lambda e, b2=b2, hp=hp, c=c, arv=arv, js=js: e.matmul(b2[:, hp * 256:(hp + 1) * 256], lhsT=ktt[:, c, js],
                                                                                       rhs=arv, start=True, stop=True), [ktt, artm[hp]], [b2])
                        op("pe", lambda e, b3=b3, hp=hp, c=c, js=js: e.matmul(b3[:, hp * 128:(hp + 1) * 128], lhsT=artm[hp][:, c, j, 0, :],
                                                                              rhs=btt[:, c, js], start=True, stop=True), [btt, artm[hp]], [b3])
                    op("dve", lambda e, b1=b1, c=c: e.tensor_tensor(out=AX1[:, 2 * c:2 * c + 2, :].rearrange("p h x -> p (h x)"), in0=b1[:, :],
                                                                    in1=cst("MU"), op=ALU.mult), [b1, CT], [AX1])
                    op("dve", lambda e, b2=b2, c=c: e.tensor_tensor(out=AX2[:, 2 * c:2 * c + 2, :].rearrange("p h x -> p (h x)"), in0=b2[:, :],
                                                                    in1=cst("MU"), op=ALU.mult), [b2, CT], [AX2])
                    am_ = Am[0][c // 2]
                    op("dve", lambda e, b3=b3, c=c, am_=am_: e.tensor_tensor(out=am_[:, (c % 2) * 2:(c % 2) * 2 + 2, :].rearrange("p h x -> p (h x)"),
                                                                             in0=b3[:, 0:256], in1=cst("ML"), op=ALU.mult), [b3, CT], [am_])
            def part2(j):
                js = slice(j * 128, (j + 1) * 128)
                for g in range(2):
                    hsl = slice(4 * g, 4 * g + 4)
                    op("act", lambda e, g=g, hsl=hsl: e.activation(out=BR[0][g][:, :, 0:128], in_=AX1[:, hsl, 0:128], func=AF.Copy), [AX1], [BR[0][g]])
                    op("dve", lambda e, g=g, hsl=hsl: e.tensor_tensor(out=BR[0][g][:, :, 128:256], in0=AX1[:, hsl, 0:128],
                                                                       in1=identb[:].unsqueeze(1).broadcast_to([128, 4, 128]), op=ALU.add),
                       [AX1, identb], [BR[0][g]])
                for m in range(6):
                    cur, nxt = m % 2, (m + 1) % 2
                    last = (m == 5)
                    for g in range(2):
                        bqg = [banks[2 * g], banks[2 * g + 1]]
                        bag = banks[4 + g]
                        for hh in range(4):
                            bq = bqg[hh // 2]
                            hp = hh % 2
                            if m == 0:
                                op("pe", lambda e, bq=bq, hp=hp, hh=hh, g=g: e.matmul(bq[:, hp * 256:hp * 256 + 128], lhsT=Am[0][g][:, hh, :],
                                                                                      rhs=BR[0][g][:, hh, 0:128], start=True, stop=True),
                                   [Am[0][g], BR[0][g]], [bq])
                            elif not last:
                                op("pe", lambda e, bq=bq, hp=hp, hh=hh, g=g, cur=cur: e.matmul(bq[:, hp * 256:(hp + 1) * 256], lhsT=Am[cur][g][:, hh, :],
                                                                                               rhs=BR[cur][g][:, hh, :], start=True, stop=True),
                                   [Am[cur][g], BR[cur][g]], [bq])
                            else:
                                op("pe", lambda e, bq=bq, hp=hp, hh=hh, g=g, cur=cur: e.matmul(bq[:, hp * 256 + 128:(hp + 1) * 256], lhsT=Am[cur][g][:, hh, :],
                                                                                               rhs=BR[cur][g][:, hh, 128:256], start=True, stop=True),
                                   [Am[cur][g], BR[cur][g]], [bq])
                            if not last:
                                op("pe", lambda e, bag=bag, hh=hh, g=g, cur=cur: e.matmul(bag[:, hh * 128:(hh + 1) * 128], lhsT=BR[cur][g][:, hh, 0:128],
                                                                                          rhs=Am[cur][g][:, hh, :], start=True, stop=True),
                                   [Am[cur][g], BR[cur][g]], [bag])
                        bqv = mbank(2 * g, 2).rearrange("p (h x) -> p h x", x=256)
                        if not last:
                            op("act", lambda e, bqv=bqv, nxt=nxt, g=g: e.activation(out=BR[nxt][g][:, :, 0:128], in_=bqv[:, :, 0:128], func=AF.Copy),
                               bqg, [BR[nxt][g]])
                            op("act", lambda e, nxt=nxt, g=g, bag=bag: e.activation(out=Am[nxt][g][:], in_=bag[:, :].rearrange("p (h x) -> p h x", x=128),
                                                                                    func=AF.Copy), [bag], [Am[nxt][g]])
                        if m == 0:
                            op("dve", lambda e, nxt=nxt, g=g: e.tensor_copy(out=BR[nxt][g][:, :, 128:256], in_=BR[0][g][:, :, 128:256]),
                               [BR[0][g]], [BR[nxt][g]])
                        else:
                            op("dve", lambda e, bqv=bqv, nxt=nxt, cur=cur, g=g: e.tensor_tensor(out=BR[nxt][g][:, :, 128:256], in0=bqv[:, :, 128:256],
                                                                                                in1=BR[cur][g][:, :, 128:256], op=ALU.add),
                               bqg + [BR[cur][g]], [BR[nxt][g]])
                TTm = BR[0]
                by = banks[6]
                for cp in range(2):
                    co = cp * 64
                    cc = 2 * j + cp
                    bgs, bus, bhs = [banks[0], banks[1]], [banks[2], banks[3]], [banks[4], banks[5]]

                    def hinfo(g, hh):
                        h = 4 * g + hh
                        c, hp, po = h // 2, h % 2, (h % 2) * 64
                        chs = slice(j * 128 + po, j * 128 + po + 64)
                        return h, c, hp, po, chs, slice(hh * 64, (hh + 1) * 64), slice(h * 64, (h + 1) * 64)
                    for g in range(2):
                        for hh in range(4):
                            h, c, hp, po, chs, ls, hs = hinfo(g, hh)
                            op("pe", lambda e, g=g, c=c, hp=hp, ls=ls: e.matmul(
                                bgs[g][co:co + 64, ls], lhsT=artm[hp][:, c, j, 0, co:co + 64], rhs=Hb[g][:, c % 2, :], start=True, stop=False),
                                [artm[hp], Hb[g]], [bgs[g]])
                            op("pe", lambda e, g=g, h=h, c=c, ls=ls, chs=chs: e.matmul(
                                bgs[g][co:co + 64, ls], lhsT=AX2[:, h, co:co + 64], rhs=Vtm[:, c, chs], start=False, stop=True),
                                [AX2, Vtm], [bgs[g]])
                    for g in range(2):
                        op("act", lambda e, g=g: e.activation(out=Gs[g][co:co + 64, :], in_=bgs[g][co:co + 64, 0:256], func=AF.Copy), [bgs[g]], [Gs[g]])
                    for g in range(2):
                        for hh in range(4):
                            h, c, hp, po, chs, ls, hs = hinfo(g, hh)
                            op("pe", lambda e, g=g, hh=hh, ls=ls: e.matmul(bus[g][co:co + 64, ls], lhsT=TTm[g][:, hh, 128 + co:128 + co + 64],
                                                                           rhs=Gs[g][:, ls], start=True, stop=True), [TTm[g], Gs[g]], [bus[g]])
                    for g in range(2):
                        op("act", lambda e, g=g: e.activation(out=Us[cp][g][co:co + 64, :], in_=bus[g][co:co + 64, 0:256], func=AF.Copy),
                           [bus[g]], [Us[cp][g]])
                    for g in range(2):
                        for hh in range(4):
                            h, c, hp, po, chs, ls, hs = hinfo(g, hh)
                            op("pe", lambda e, g=g, c=c, hp=hp, hs=hs: e.matmul(by[co:co + 64, hs], lhsT=artm[hp][:, c, j, 1, co:co + 64],
                                                                                rhs=Hb[g][:, c % 2, :], start=True, stop=False), [artm[hp], Hb[g]], [by])
                            op("pe", lambda e, g=g, h=h, hs=hs, ls=ls: e.matmul(by[co:co + 64, hs], lhsT=AX1[:, h, 128 + co:128 + co + 64],
                                                                                rhs=Us[cp][g][:, ls], start=False, stop=False), [AX1, Us[cp][g]], [by])
                            op("pe", lambda e, h=h, c=c, hs=hs, chs=chs: e.matmul(by[co:co + 64, hs], lhsT=AX2[:, h, 128 + co:128 + co + 64],
                                                                                  rhs=Vtm[:, c, chs], start=False, stop=True), [AX2, Vtm], [by])
                            op("pe", lambda e, g=g, c=c, po=po, ls=ls, chs=chs: e.matmul(bhs[g][po:po + 64, (c % 2) * 64:(c % 2 + 1) * 64], lhsT=btm[:, c, chs],
                                                                                         rhs=Us[cp][g][:, ls], start=True, stop=False), [btm, Us[cp][g]], [bhs[g]])
                            op("pe", lambda e, g=g, c=c, po=po, chs=chs: e.matmul(bhs[g][po:po + 64, (c % 2) * 64:(c % 2 + 1) * 64], lhsT=ktm2[:, c, chs],
                                                                                  rhs=Vm[cp][:, c, chs], start=False, stop=True), [ktm2, Vm[cp]], [bhs[g]])
                        Htmp = ring5()
                        op("dve", lambda e, g=g, Htmp=Htmp: e.tensor_tensor(out=Htmp[:, 0:128], in0=bhs[g][:, 0:128],
                                                                            in1=Hst[g][:].rearrange("p c v -> p (c v)"), op=ALU.add), [bhs[g], Hst[g]], [Htmp])
                        op("dve", lambda e, g=g, Htmp=Htmp: e.tensor_tensor(out=Hst[g][:], in0=Htmp[:, 0:128].rearrange("p (c v) -> p c v", v=64),
                                                                            in1=WCt[:, 2 * g:2 * g + 2, cc:cc + 1].broadcast_to([128, 2, 64]), op=ALU.mult),
                           [Htmp, WCt], [Hst[g]])
                        op("act", lambda e, g=g: e.activation(out=Hb[g][:], in_=Hst[g][:], func=AF.Copy), [Hst[g]], [Hb[g]])
            def part3(j):
                js = slice(j * 128, (j + 1) * 128)
                by = banks[6]
                ysb, ysq, yn = ring5(), ring5(), ring5()
                op("act", lambda e, ysb=ysb: e.activation(out=ysb[:, 0:512], in_=by[:, :], func=AF.Copy), [by], [ysb])
                op("act", lambda e, ysq=ysq: e.activation(out=ysq[:, 0:512], in_=by[:, :], func=AF.Square), [by], [ysq])
                head_norm(op, st4, ysb, ysq, yn, 8, 64, eps_w, CT)
                op("dve", lambda e, yn=yn: e.tensor_tensor(out=yn[:, 0:512], in0=yn[:, 0:512], in1=rows3[:, 1, :], op=ALU.mult), [yn, rows3], [yn])
                op("dve", lambda e, yn=yn: e.tensor_tensor(out=yn[:, 0:512], in0=yn[:, 0:512], in1=rows3[:, 2, :], op=ALU.add), [yn, rows3], [yn])
                bt_ = pb()
                for c in range(4):
                    op("pe", lambda e, bt_=bt_, c=c, yn=yn: e.transpose(out=bt_[:, c * 128:(c + 1) * 128], in_=yn[:, c * 128:(c + 1) * 128],
                                                                        identity=cst("ident")), [yn, CT], [bt_])
                yf = ring5()
                op("dve", lambda e, bt_=bt_, yf=yf, js=js: e.tensor_tensor(out=yf[:, 0:512].rearrange("p (c t) -> p c t", t=128),
                                                                           in0=bt_[:, :].rearrange("p (c t) -> p c t", t=128), in1=bon[:, :, js], op=ALU.add),
                   [bt_, bon], [yf])
                op("dve", lambda e, yf=yf, js=js: e.tensor_tensor(out=orw[:, :, js], in0=yf[:, 0:512].rearrange("p (c t) -> p c t", t=128),
                                                                   in1=srw[:, :, js], op=ALU.mult), [yf, srw], [orw])
            for j in range(NB):
                part1(j)
                part2(j)
                part3(j)
            if dbg == "W":
                S.emit()
                return nc
            if dbg == "dump" and l == 0 and it == NT - 1:
                for i_, t_ in enumerate((opool, oret, orw)):
                    dma("sp", lambda e, i_=i_, t_=t_: e.dma_start(out=dbg_o[:, i_ * 4 * TT:(i_ + 1) * 4 * TT], in_=t_[:].rearrange("p c t -> p (c t)")),
                        "dbg", [t_], [b_dbg])
                dma("sp", lambda e: e.dma_start(out=dbg_o[:, 12 * TT:20 * TT], in_=hT[:].rearrange("p c t -> p (c t)")), "dbg", [hT], [b_dbg])
            macc = big
            mT0, mT1 = BB[0], BB[1]
            obr = [opool, oret, orw]
            gts = [Ft[2], Ft[3]]
            for n in range(3):
                for hf in range(2):
                    wgl = wget(l, 13 + 3 * n + 1 + hf)
                    c = 0
                    while c < 4:
                        nn_ = min(CPB, 4 - c)
                        bgl = pb()
                        for i in range(nn_):
                            for k in range(8):
                                op("pe", lambda e, bgl=bgl, k=k, wgl=wgl, i=i, cc=c + i: e.matmul(
                                    bgl[:, i * TT:(i + 1) * TT], lhsT=w8(wgl)[:, k, cc * 128:(cc + 1) * 128], rhs=hT[:, k, :],
                                    start=(k == 0), stop=(k == 7)), [wgl, hT], [bgl])
                        op("act", lambda e, bgl=bgl, c=c, nn_=nn_, hf=hf: e.activation(out=gts[hf][:, c:c + nn_, 0:TT], in_=bview(bgl, nn_), func=AF.Sigmoid),
                           [bgl], [gts[hf]])
                        c += nn_
                    wrel(wgl)
                wbn = wget(l, 13 + 3 * n)
                wbv = wbn[:].rearrange("p (k d) -> p k d", d=1024)
                for dp in range(0, 8, CPB):
                    byn = pb()
                    for i in range(CPB):
                        dc = dp + i
                        for k4 in range(4):
                            op("pe", lambda e, byn=byn, k4=k4, wbv=wbv, dc=dc, i=i, n=n, wbn=wbn: e.matmul(
                                byn[:, i * TT:(i + 1) * TT], lhsT=wbv[:, k4, dc * 128:(dc + 1) * 128], rhs=obr[n][:, k4, :],
                                start=(k4 == 0), stop=(k4 == 3)), [wbn, obr[n]], [byn])
                    hf, dq0 = dp // 4, dp % 4
                    gv = gts[hf][:, dq0:dq0 + CPB, 0:TT]
                    ms = macc[:, dp * TT:(dp + CPB) * TT].rearrange("p (i t) -> p i t", t=TT)
                    yv = bview(byn, CPB)
                    if n == 0:
                        op("dve", lambda e, yv=yv, gv=gv, ms=ms: e.tensor_tensor(out=ms, in0=yv, in1=gv, op=ALU.mult), [byn, gts[hf]], [macc])
                    else:
                        tmp = ring5()
                        tv = tmp[:, 0:CPB * TT].rearrange("p (i t) -> p i t", t=TT)
                        op("dve", lambda e, yv=yv, gv=gv, tv=tv: e.tensor_tensor(out=tv, in0=yv, in1=gv, op=ALU.mult), [byn, gts[hf]], [tmp])
                        if n == 1:
                            op("pool", lambda e, tv=tv, ms=ms: e.tensor_tensor(out=ms, in0=ms, in1=tv, op=ALU.add), [tmp, macc], [macc])
                        else:
                            mt_ = (mT0 if dp < 4 else mT1)
                            op("pool", lambda e, tv=tv, ms=ms, mt_=mt_, dq0=dq0: e.tensor_tensor(out=mt_[:, dq0:dq0 + CPB, :], in0=ms, in1=tv, op=ALU.add),
                               [tmp, macc], [mt_])
                wrel(wbn)
            for hf in range(2):
                wo_ = wget(l, 22 + hf)
                for j in range(NB):
                    bo = pb()
                    for dc in range(8):
                        mt_ = (mT0 if dc < 4 else mT1)
                        op("pe", lambda e, bo=bo, dc=dc, mt_=mt_, j=j, wo_=wo_: e.matmul(bo[:, :], lhsT=mt_[:, dc % 4, j * 128:(j + 1) * 128], rhs=w8(wo_)[:, dc, :],
                                                                                         start=(dc == 0), stop=(dc == 7)), [mt_, wo_], [bo])
                    tmp = ring5()
                    op("dve", lambda e, bo=bo, tmp=tmp, hf=hf: e.tensor_tensor(out=tmp[:, 0:512], in0=bo[:, :], in1=G_bc[:, hf * 512:(hf + 1) * 512], op=ALU.mult),
                       [bo, modbc], [tmp])
                    op("pool", lambda e, tmp=tmp, hf=hf, j=j: e.tensor_tensor(out=xt[:, j, hf * 512:(hf + 1) * 512], in0=xt[:, j, hf * 512:(hf + 1) * 512],
                                                                              in1=tmp[:, 0:512], op=ALU.add), [tmp, xt], [xt])
                wrel(wo_)
            if l < L - 1:
                dma("sp", lambda e, t0=t0: e.dma_start(out=xs[t0:t0 + TT, :].rearrange("(j p) d -> p j d", p=128), in_=xt[:]), "st_x", [xt], [b_xs[it]])
            else:
                if it == 0:
                    dma("sp", lambda e: e.dma_start(out=gbc[:], in_=fg_in[0, :].partition_broadcast(128)), "ld_g", [], [gbc])
                for j in range(NB):
                    op("act", lambda e, j=j: e.activation(out=junk[:], in_=xt[:, j, :], func=AF.Square, accum_out=st4[:, j:j + 1]), [xt], [junk, st4])
                op("act", lambda e: e.activation(out=st4[:, 4:4 + NB], in_=st4[:, 0:NB], func=AF.Sqrt, bias=eps_n, scale=1.0 / D), [st4, CT], [st4])
                op("dve", lambda e: e.reciprocal(out=st4[:, 8:8 + NB], in_=st4[:, 4:4 + NB]), [st4], [st4])
                for j in range(NB):
                    op("dve", lambda e, j=j: e.scalar_tensor_tensor(out=xt[:, j, :], in0=xt[:, j, :], scalar=st4[:, 8 + j:9 + j], in1=gbc[:],
                                                                    op0=ALU.mult, op1=ALU.mult), [xt, st4, gbc], [xt])
                dma("sp", lambda e, t0=t0: e.dma_start(out=out[t0:t0 + TT, :].rearrange("(j p) d -> p j d", p=128), in_=xt[:]), "st_o", [xt], [b_out[it]])
    S.wait_all("sp", b_out + [b_dbg])
    S.emit()
    return nc


def head_norm(op, st4, xs_, sq, on, nh, hd, eps_ap, CT):
    s1 = st4[:, 0:nh]
    xv = xs_[:, 0:nh * hd].rearrange("p (h d) -> p h d", d=hd)
    qv = sq[:, 0:nh * hd].rearrange("p (h d) -> p h d", d=hd)
    ov = on[:, 0:nh * hd].rearrange("p (h d) -> p h d", d=hd)
    m = st4[:, 0:nh]
    v = st4[:, 8:8 + nh]
    op("dve", lambda e: e.tensor_reduce(out=m, in_=xv, axis=AX.X, op=ALU.add), [xs_], [st4])
    op("dve", lambda e: e.tensor_reduce(out=v, in_=qv, axis=AX.X, op=ALU.add), [sq], [st4])
    op("dve", lambda e: e.tensor_scalar(out=m, in0=m, scalar1=1.0 / hd, scalar2=None, op0=ALU.mult), [st4], [st4])
    msq = sq[:, 0:nh]
    op("dve", lambda e: e.tensor_tensor(out=msq, in0=m, in1=m, op=ALU.mult), [st4], [sq])
    op("dve", lambda e: e.scalar_tensor_tensor(out=v, in0=v, scalar=1.0 / hd, in1=msq, op0=ALU.mult, op1=ALU.subtract), [st4, sq], [st4])
    op("act", lambda e: e.activation(out=v, in_=v, func=AF.Sqrt, bias=eps_ap), [st4, CT], [st4])
    op("dve", lambda e: e.reciprocal(out=v, in_=v), [st4], [st4])
    op("dve", lambda e: e.tensor_tensor(out=ov, in0=xv, in1=m.unsqueeze(2).broadcast_to([128, nh, hd]), op=ALU.subtract), [xs_, st4], [on])
    op("dve", lambda e: e.tensor_tensor(out=ov, in0=ov, in1=v.unsqueeze(2).broadcast_to([128, nh, hd]), op=ALU.mult), [on, st4], [on])


_CACHE = {}


def run(inputs, T, L, NB=2, n_cores=8, dbg=None):
    B = inputs["x"].shape[0]
    wd = prep_weights(inputs)
    key = (T, L, NB, dbg)
    if key not in _CACHE:
        _CACHE[key] = build(T, L, NB, dbg)
    nc = _CACHE[key]
    in_maps = []
    x = np.asarray(inputs["x"], np.float32)
    c = np.asarray(inputs["c"], np.float32)
    pos = np.asarray(inputs["positions"], np.int32)
    for core in range(n_cores):
        b = core % B
        m = dict(wd)
        m["x"] = np.ascontiguousarray(x[b])
        m["c"] = np.ascontiguousarray(c[b].reshape(8, 128).T)
        m["pos"] = np.ascontiguousarray(pos[b].reshape(1, T))
        in_maps.append(m)
    import os
    res = run_bass_kernel_spmd(nc, in_maps, core_ids=list(range(n_cores)), **({'trace': True} if os.environ.get('KTRACE') else {}))
    if os.environ.get('KTRACE'):
        print('EXEC_TIME_NS', res.exec_time_ns)
    if dbg == "dump":
        global LAST_DBG
        LAST_DBG = np.asarray(res.results[0]["dbg_o"]).astype(np.float32)
        global LAST_DBGF
        LAST_DBGF = np.asarray(res.results[0]["dbg_f"]).astype(np.float32)
    return np.stack([np.asarray(res.results[b]["out"], np.float32) for b in range(B)], 0)


def kernel(**inputs):
    T = inputs["x"].shape[1]
    L = inputs["w_in"].shape[0]
    return run(inputs, T, L)
```

```python
import contextlib
import numpy as np
import concourse.bass as bass
import concourse.mybir as mybir
from concourse.bass_utils import run_bass_kernel_spmd

F32 = mybir.dt.float32
BF16 = mybir.dt.bfloat16
I32 = mybir.dt.int32
AF = mybir.ActivationFunctionType
ALU = mybir.AluOpType
AX = mybir.AxisListType

D = 1024
W = 512
DIN = 8320
NBLKW = 24
NV = 40
C0 = float(np.exp(-0.5))
GAM = [1.0 - 2.0 ** (-5.0 - h) for h in range(4)]
ENGS = ["pe", "dve", "act", "pool", "sp"]


class Buf:
    __slots__ = ("name", "lw", "rd")

    def __init__(self, name):
        self.name = name
        self.lw = None
        self.rd = []


class _Rec:
    def __getattr__(self, name):
        return lambda *a, **k: (name, a, k)


_REC = _Rec()


class Sched:
    def __init__(self, nc):
        self.nc = nc
        self.ops = {e: [] for e in ENGS}
        self.cnt = {}
        self.keys = []
        self.epoch = 0

    def _key(self, eng):
        return "%s_%d" % (eng, self.epoch)

    def _deps(self, reads, writes, eng=None):
        deps = {}
        pre = None if eng is None else eng + "_"
        for b in reads:
            t = b.lw
            if t is not None and deps.get(t[0], 0) < t[1]:
                deps[t[0]] = t[1]
        for b in writes:
            t = b.lw
            if t is not None and deps.get(t[0], 0) < t[1] and not (pre and t[0].startswith(pre)):
                deps[t[0]] = t[1]
            for t in b.rd:
                if deps.get(t[0], 0) < t[1] and not (pre and t[0].startswith(pre)):
                    deps[t[0]] = t[1]
        return deps

    def _commit(self, tok, reads, writes):
        for b in reads:
            b.rd.append(tok)
        for b in writes:
            b.lw = tok
            b.rd = []

    def _bump(self, key, inc):
        if key not in self.cnt:
            self.cnt[key] = 0
            self.keys.append(key)
        self.cnt[key] += inc
        return (key, self.cnt[key])

    def op(self, eng, fn, reads=(), writes=()):
        deps = self._deps(reads, writes, eng)
        key = self._key(eng)
        tok = self._bump(key, 1)
        self.ops[eng].append((fn(_REC), deps, key, 1))
        self._commit(tok, reads, writes)
        return tok

    def dma(self, eng, fn, semkey, reads=(), writes=()):
        deps = self._deps(reads, writes)
        tok = self._bump(semkey, 16)
        self.ops[eng].append((fn(_REC), deps, semkey, 16))
        self._commit(tok, reads, writes)
        return tok

    def wait_all(self, eng, bufs):
        deps = self._deps(bufs, ())
        self.ops[eng].append((None, deps, None, 0))

    def emit(self):
        nc = self.nc
        with contextlib.ExitStack() as st:
            sems = {}
            for i, k in enumerate(self.keys):
                sems[k] = st.enter_context(nc.semaphore("s%d" % i))
            block = st.enter_context(nc.Block())

            def run(engname, e):
                seen = {}
                for fn, deps, key, inc in self.ops[engname]:
                    for k, v in deps.items():
                        if engname == "pe" and k.startswith("pe_"):
                            continue
                        if seen.get(k, 0) >= v:
                            continue
                        e.wait_ge(sems[k], v)
                        seen[k] = v
                    if fn is not None:
                        getattr(e, fn[0])(*fn[1], **fn[2]).then_inc(sems[key], inc)

            @block.tensor
            def _(e):
                run("pe", e)

            @block.vector
            def _(e):
                run("dve", e)

            @block.scalar
            def _(e):
                run("act", e)

            @block.gpsimd
            def _(e):
                run("pool", e)

            @block.sync
            def _(e):
                run("sp", e)


class Tile:
    def __init__(self, nc, name, shape, dtype, psum=False):
        if psum:
            self.t = nc.alloc_psum_tensor("p_" + name, shape, dtype)
        else:
            self.t = nc.alloc_sbuf_tensor("t_" + name, shape, dtype)
        self.b = Buf(name)

    def __getitem__(self, idx):
        return self.t[idx]


def make_consts():
    c = {}
    p = np.arange(128)
    i = np.arange(128)
    mt = np.zeros((128, 4, 128), np.float64)
    dq = np.zeros((128, 4, 128), np.float64)
    kt = np.zeros((128, 4), np.float64)
    for h in range(4):
        g = GAM[h]
        diff = i[None, :] - p[:, None]
        mt[:, h, :] = np.where(diff >= 0, g ** np.maximum(diff, 0), 0.0)
        dq[:, h, :] = (g ** (i + 1.0))[None, :]
        kt[:, h] = g ** (127.0 - p)
    c["maskT"] = mt.reshape(128, 512)
    c["decq"] = dq.reshape(128, 512)
    c["ktail"] = kt
    same = (p[:, None] // 64) == (i[None, :] // 64)
    mus = (same & (i[None, :] > p[:, None])).astype(np.float64)
    mui = (same & (i[None, :] >= p[:, None])).astype(np.float64)
    mls = (same & (i[None, :] < p[:, None])).astype(np.float64)
    mu2 = np.concatenate([mus, mui], 1)
    c["MU"] = np.concatenate([mu2, mu2], 1)
    c["ML"] = np.concatenate([mls, mls], 1)
    c["onesbd"] = same.astype(np.float64)
    c["ident"] = np.eye(128)
    sm = np.ones((128, 512)); sm[:, ::64] = 0.0
    c["scanm"] = sm
    half = 64
    fr = (np.float32(10000.0) ** (-(np.arange(half, dtype=np.float32)) / np.float32(half))).astype(np.float64)
    c["freq"] = np.concatenate([fr, fr])[:, None]
    c["sgn"] = np.concatenate([-np.ones(64), np.ones(64)])[:, None]
    ic = np.zeros((128, 4, 16))
    for g, w in enumerate((2, 4, 8, 16)):
        ic[:, g, :] = (1.0 / np.minimum(np.arange(16) + 1, w))[None, :]
    c["invc"] = ic.reshape(128, 64)
    c["eps"] = np.array([1e-6, 1e-5, 64e-5, 1e-18, np.pi / 2])[None, :].repeat(128, 0)
    off = {}
    cols = []
    o = 0
    for k, v in c.items():
        off[k] = (o, v.shape[1])
        o += v.shape[1]
        cols.append(v)
    return np.concatenate(cols, 1).astype(np.float32), off


CONSTS, COFF = make_consts()
NCONST = CONSTS.shape[1]


def prep_weights(inp):
    L = inp["w_in"].shape[0]
    w_in = np.asarray(inp["w_in"], np.float32)
    wblk = np.zeros((L, NBLKW, 128, 4096), np.float32)

    def inblk(cols):
        a = w_in[:, :, cols]
        return a.reshape(L, 8, 128, len(cols)).transpose(0, 2, 1, 3)

    def put(b, cols):
        a = inblk(cols)
        n = a.shape[3]
        tmp = np.zeros((L, 128, 8, 512), np.float32)
        tmp[:, :, :, :n] = a
        wblk[:, b] = tmp.reshape(L, 128, 4096)

    ar = np.arange
    sw = np.concatenate([np.concatenate([ar(h * 128 + 64, h * 128 + 128), ar(h * 128, h * 128 + 64)]) for h in range(4)])
    put(0, ar(0, 512)); put(1, ar(512, 1024))
    put(2, ar(1024, 1536)); put(3, 1024 + sw)
    put(4, ar(1536, 2048)); put(5, 1536 + sw)
    put(6, ar(2048, 2560)); put(7, ar(2560, 3072))
    put(8, ar(3072, 3584)); put(9, ar(3648, 4160)); put(10, ar(4160, 4672))
    put(11, np.concatenate([ar(3584, 3648), ar(4672, 4736)]))
    put(12, ar(4736, 5248))
    wb = np.asarray(inp["w_branch"], np.float32)
    for n in range(3):
        wblk[:, 13 + 3 * n] = wb[:, n].reshape(L, 4, 128, 1024).transpose(0, 2, 1, 3).reshape(L, 128, 4096)
        for hf in range(2):
            put(13 + 3 * n + 1 + hf, 5248 + n * 1024 + hf * 512 + ar(512))
    wo = np.asarray(inp["w_out"], np.float32)
    for hf in range(2):
        a = wo[:, :, hf * 512:(hf + 1) * 512].reshape(L, 8, 128, 512).transpose(0, 2, 1, 3)
        wblk[:, 22 + hf] = a.reshape(L, 128, 4096)
    wsm = np.zeros((L, 128, 1536), np.float32)
    pw = np.asarray(inp["pool_w"], np.float32)
    wsm[:, :, 0:512] = pw.transpose(0, 2, 1, 3).reshape(L, 128, 512)
    wsm[:, 0:64, 512:1024] = np.asarray(inp["rwkv_w2"], np.float32)
    wsm[:, 64:128, 1024:1536] = np.asarray(inp["rwkv_a2"], np.float32)

    def fm(v):
        return np.asarray(v, np.float32).reshape(L, 4, 128).transpose(0, 2, 1)
    mu = np.asarray(inp["rwkv_shift_mu"], np.float32)
    vecs = np.zeros((L, 128, NV), np.float32)
    vecs[:, :, 0:4] = fm(inp["pool_scale"])
    vecs[:, :, 4:8] = fm(mu[:, 0:512])
    vecs[:, :, 8:12] = fm(mu[:, 576:1088])
    vecs[:, :, 12:16] = fm(mu[:, 1088:1600])
    vecs[:, 0:64, 16] = mu[:, 512:576]
    vecs[:, 64:128, 16] = mu[:, 1600:1664]
    vecs[:, :, 17:21] = fm(inp["rwkv_w0"])
    vecs[:, :, 21:25] = fm(inp["rwkv_a0"])
    vecs[:, :, 25:29] = fm(inp["rwkv_k_k"])
    vecs[:, :, 29:33] = fm(inp["rwkv_k_a"])
    vecs[:, :, 33:37] = fm(np.asarray(inp["rwkv_r_k"], np.float32).reshape(L, 512))
    rows = np.stack([np.asarray(inp["ret_norm_g"], np.float32), np.asarray(inp["rwkv_ln_w"], np.float32),
                     np.asarray(inp["rwkv_ln_b"], np.float32)], 1)
    ngb = np.concatenate([np.asarray(inp["norm_g"], np.float32), np.asarray(inp["b_ada"], np.float32)], 1)
    return dict(wblk=wblk, wsm=wsm, vecs=vecs, rows=np.ascontiguousarray(rows), ngb=np.ascontiguousarray(ngb),
                wada=np.ascontiguousarray(np.asarray(inp["w_ada"], np.float32)),
                fg=np.asarray(inp["final_g"], np.float32).reshape(1, D), consts=CONSTS)


def build(T, L, NB=2, dbg=None):
    TT = NB * 128
    NT = T // TT
    NCH = NB * 2
    CPB = 512 // TT
    nc = bass.Bass("TRN2", target_bir_lowering=False)
    S = Sched(nc)

    def dram(name, shape, dt, kind):
        return nc.dram_tensor(name, shape, dt, kind=kind).ap()
    x_in = dram("x", [T, D], F32, "ExternalInput")
    c_in = dram("c", [128, 8], F32, "ExternalInput")
    pos_in = dram("pos", [1, T], I32, "ExternalInput")
    wblk = dram("wblk", [L, NBLKW, 128, 4096], F32, "ExternalInput")
    wsm_in = dram("wsm", [L, 128, 1536], F32, "ExternalInput")
    vecs_in = dram("vecs", [L, 128, NV], F32, "ExternalInput")
    rows_in = dram("rows", [L, 3, 512], F32, "ExternalInput")
    ngb_in = dram("ngb", [L, 4096], F32, "ExternalInput")
    wada_in = dram("wada", [L, D, 3 * D], F32, "ExternalInput")
    fg_in = dram("fg", [1, D], F32, "ExternalInput")
    consts_in = dram("consts", [128, NCONST], F32, "ExternalInput")
    out = dram("out", [T, D], F32, "ExternalOutput")
    dbg_o = dram("dbg_o", [128, 20 * TT], BF16, "ExternalOutput") if dbg == "dump" else None
    b_dbg = Buf("dbg")
    dbg_f = dram("dbg_f", [128, 16384], F32, "ExternalOutput") if dbg == "dump" else None
    dstate = {"off": 0, "map": {}, "on": False}
    global DUMP_MAP
    DUMP_MAP = dstate["map"]
    dstage = None

    def dump(name, ap, rd, width):
        if dbg != "dump" or not dstate["on"]:
            return
        o = dstate["off"]
        dstate["map"][name] = (o, width)
        st_ = hn
        op("dve", lambda e: e.tensor_copy(out=st_[:, 0:width], in_=ap), rd, [st_])
        dma("sp", lambda e: e.dma_start(out=dbg_f[:, o:o + width], in_=st_[:, 0:width]), "dbgf", [st_], [b_dbg])
        dstate["off"] += width
    wsc = dram("wsc", [L, NBLKW, 128, 4096], BF16, "Internal")
    xs = dram("xs", [T, D], F32, "Internal")
    b_wsc = [Buf("wsc%d" % l) for l in range(L)]
    b_xs = [Buf("xs%d" % i) for i in range(NT)]
    b_out = [Buf("out%d" % i) for i in range(NT)]

    def tile(name, shape, dt):
        return Tile(nc, name, shape, dt)

    CT = tile("consts", [128, NCONST], F32)

    def cst(name):
        o, n = COFF[name]
        return CT[:, o:o + n]
    identb = tile("identb", [128, 128], BF16)
    onesb = tile("onesb", [128, 128], BF16)
    cbc = tile("cbc", [128, 8, 128], F32)
    cfm = tile("cfm", [128, 8], F32)
    modbc = tile("modbc", [128, 3 * D], F32)
    gbc = tile("gbc", [128, D], F32)
    rows3 = tile("rows3", [128, 3, 512], F32)
    vecs = tile("vecs", [128, NV], F32)
    wsmb = tile("wsmb", [128, 1536], BF16)
    big = tile("big", [128, 8 * TT], F32)
    bada = tile("bada", [128, TT], F32)
    class _Alias:
        def __init__(self, t, lo, hi, shp=None):
            self.t, self.lo, self.hi, self.b, self.shp = t, lo, hi, t.b, shp

        def __getitem__(self, idx):
            v = self.t[:, self.lo:self.hi]
            if self.shp:
                v = v.rearrange("p (h i) -> p h i", i=self.shp)
            return v[idx]
    xtl = [tile("xt%d" % i, [128, NB, D], F32) for i in range(2)]
    hT = tile("hT", [128, 8, TT], BF16)
    hn = _Alias(big, 0, D)
    hb = tile("hb", [128, D], BF16)
    junk = hb
    wsmf = big
    st4 = tile("st4", [128, 16], F32)
    NSLOT = 3
    wslot = [tile("wslot%d" % i, [128, 4096], BF16) for i in range(NSLOT)]
    FW = TT + 16
    Ft = [tile("F%d" % i, [128, 4, FW], F32) for i in range(4)]
    NRING = 5
    ringt = [tile("ring%d" % i, [128, FW], F32) for i in range(NRING)]
    wkn = "stg dd sig ag cs csx Pin Pex Q kk sqb rn kp prb tb".split()
    wk = {n: tile("wk_" + n, [128, FW], F32) for n in wkn}
    WK2_PENDING = True
    BB = [tile("BB%d" % i, [128, 4, TT], BF16) for i in range(6)]
    artm = [tile("artm%d" % i, [128, 4, NB, 2, 128], BF16) for i in range(2)]
    Vm = [tile("Vm%d" % i, [128, 4, TT], BF16) for i in range(2)]
    opool = tile("opool", [128, 4, TT], BF16)
    oret = tile("oret", [128, 4, TT], BF16)
    orw = tile("orw", [128, 4, TT], BF16)
    puh = tile("puh", [128, 4, 16], F32)
    hal = tile("hal", [128, 16], F32)
    Rst = tile("Rst", [128, 4, 128], F32)
    Rb = tile("Rb", [128, 4, 128], BF16)
    Hst = [tile("Hst%d" % g, [128, 2, 64], F32) for g in range(2)]
    Hb = [tile("Hb%d" % g, [128, 2, 64], BF16) for g in range(2)]
    ksF = tile("ksF", [128, 4, TT], F32)

    Sm = _Alias(hb, 0, 512, 128)
    lor = _Alias(hb, 512, 512 + TT)
    AX1 = tile("AX1", [128, 8, 256], BF16)
    AX2 = tile("AX2", [128, 8, 256], BF16)
    BR = [[tile("BR%d_%d" % (i, g), [128, 4, 256], BF16) for g in range(2)] for i in range(2)]
    Am = [[tile("Am%d_%d" % (i, g), [128, 4, 128], BF16) for g in range(2)] for i in range(2)]
    Gs = [tile("Gs%d" % g, [128, 256], BF16) for g in range(2)]
    fdummy = tile("fdummy", [128, 2], F32)
    nvec = tile("nvec", [128, 8], F32)

    class ATile:
        def __init__(self, name, base, lo):
            self.b = Buf(name)
            self.base, self.lo = base, lo

        def __getitem__(self, idx):
            v = self.base[:]
            if len(v.shape) == 3:
                v = v.rearrange("p h x -> p (h x)")
            return v.bitcast(F32)[:, self.lo:self.lo + TT][idx]
    wk2 = {}
    _regions = [(AX1, 4), (AX2, 4), (BR[0][0], 2), (BR[0][1], 2), (BR[1][0], 2), (BR[1][1], 2)]
    _names = "sig ag cs csx Pin Pex Q kk sqb rn kp prb tb".split()
    _ri = 0
    for base_, n_ in _regions:
        for i_ in range(n_):
            if _ri < len(_names):
                wk2[_names[_ri]] = ATile("wk2_" + _names[_ri], base_, i_ * TT)
                _ri += 1
    assert _ri == len(_names)
    Us = [[tile("Us%d_%d" % (i, g), [128, 256], BF16) for g in range(2)] for i in range(2)]
    WCt = tile("WCt", [128, 4, NCH], F32)

    class _PosI:
        b = bada.b

        def __getitem__(self, idx):
            return bada[:, 0:TT].bitcast(I32)[idx]
    posi = _PosI()
    psum_all = nc.alloc_psum_tensor("psum_all", [128, 4096], F32)

    class Bank:
        def __init__(self, i):
            self.i = i
            self.b = Buf("bank%d" % i)

        def __getitem__(self, idx):
            return psum_all[:, self.i * 512:(self.i + 1) * 512][idx]
    banks = [Bank(i) for i in range(8)]

    def mbank(i0, n):
        return psum_all[:, i0 * 512:(i0 + n) * 512]
    ring5t = [tile("ringw%d" % i, [128, 512], F32) for i in range(4)]
    state = {"bank": 0, "ring": 0, "ring5": 0}

    def ring5():
        r = ring5t[state["ring5"] % 4]
        state["ring5"] += 1
        return r

    def pb():
        b = banks[state["bank"] % 8]
        state["bank"] += 1
        return b

    def ring():
        r = ringt[state["ring"] % NRING]
        state["ring"] += 1
        return r

    def op(eng, fn, r, w):
        return S.op(eng, fn, [t.b if hasattr(t, "b") else t for t in r], [t.b if hasattr(t, "b") else t for t in w])

    def dma(eng, fn, key, r, w):
        return S.dma(eng, fn, key, [t.b if hasattr(t, "b") else t for t in r], [t.b if hasattr(t, "b") else t for t in w])

    def cast_layer(l):
        key = "cast%d" % l
        tok = None
        for b in range(NBLKW):
            for hf in range(2):
                tok = S.dma("pool", lambda e, l=l, b=b, hf=hf: e.dma_start(
                    out=wsc[l, b, :, hf * 2048:(hf + 1) * 2048], in_=wblk[l, b, :, hf * 2048:(hf + 1) * 2048]), key)
        b_wsc[l].lw = tok

    order = [0, 1, 2, 3, 4, 5, 6, 7, 11, 8, 9, 10, 12] + [14, 15, 13, 17, 18, 16, 20, 21, 19, 22, 23]
    seq = [(l, b) for l in range(L) for _ in range(NT) for b in order]
    wstate = {"next_load": 0, "next_get": 0}
    wreleased = [False] * len(seq)

    def w_pump():
        while wstate["next_load"] < len(seq):
            n = wstate["next_load"]
            if n >= NSLOT and not wreleased[n - NSLOT]:
                break
            l, b = seq[n]
            s = n % NSLOT
            if b == 11:
                dma("sp", lambda e, l=l, b=b, s=s: e.dma_start(
                    out=wslot[s][:].rearrange("p (k c) -> p k c", c=512)[:, :, 0:128],
                    in_=wsc[l, b].rearrange("p (k c) -> p k c", c=512)[:, :, 0:128]), "ws%d" % s, [b_wsc[l]], [wslot[s]])
            else:
                dma("sp", lambda e, l=l, b=b, s=s: e.dma_start(out=wslot[s][:], in_=wsc[l, b]), "ws%d" % s, [b_wsc[l]], [wslot[s]])
            wstate["next_load"] += 1

    def wget(l, b):
        n = wstate["next_get"]
        assert seq[n] == (l, b), (seq[n], l, b)
        w_pump()
        assert wstate["next_load"] > n, "weight block not loadable (too many held)"
        wstate["next_get"] += 1
        wstate["last"] = n
        t_ = wslot[n % NSLOT]
        t_.widx = n
        return t_

    def wrel(*tiles):
        for t_ in tiles:
            wreleased[t_.widx] = True
        w_pump()

    def w8(wt):
        return wt[:].rearrange("p (k c) -> p k c", c=512)

    dma("sp", lambda e: e.dma_start(out=CT[:], in_=consts_in), "ld_c", [], [CT])
    dma("sp", lambda e: e.dma_start(out=cfm[:], in_=c_in), "ld_c2", [], [cfm])
    cast_layer(0)
    op("act", lambda e: e.activation(out=identb[:], in_=cst("ident"), func=AF.Copy), [CT], [identb])
    op("act", lambda e: e.activation(out=onesb[:], in_=cst("onesbd"), func=AF.Copy), [CT], [onesb])
    op("act", lambda e: e.activation(out=cfm[:], in_=cfm[:], func=AF.Silu), [cfm], [cfm])
    for k in range(8):
        op("dve", lambda e, k=k: e.tensor_scalar(out=cbc[:, k, :], in0=cst("ident"), scalar1=0.0, scalar2=cfm[:, k:k + 1],
                                                 op0=ALU.mult, op1=ALU.add), [CT, cfm], [cbc])
    for t_ in (artm[0], artm[1], Vm[0], Vm[1], Us[0][0], Us[0][1], Us[1][0], Us[1][1], Gs[0], Gs[1]):
        op("pool", lambda e, t_=t_: e.memset(t_[:], 0.0), [], [t_])
    eps_n = cst("eps")[:, 0:1]
    eps_r = cst("eps")[:, 1:2]
    eps_w = cst("eps")[:, 2:3]
    eps_k = cst("eps")[:, 3:4]

    def fm_proj(wt, nch, evac, ncols=128, k_parts=8):
        c = 0
        while c < nch:
            n = min(CPB, nch - c)
            bk = pb()
            for i in range(n):
                for k in range(8):
                    op("pe", lambda e, bk=bk, i=i, k=k, cc=c + i: e.matmul(
                        bk[0:ncols, i * TT:(i + 1) * TT], lhsT=w8(wt)[:, k, cc * 128:cc * 128 + ncols], rhs=hT[:, k, :],
                        start=(k == 0), stop=(k == 7)), [wt, hT], [bk])
            evac(bk, c, n)
            c += n

    def bview(bk, n):
        return bk[:, 0:n * TT].rearrange("p (n t) -> p n t", t=TT)

    for l in range(L):
        S.epoch = l
        if l + 1 < L:
            cast_layer(l + 1)
        dma("sp", lambda e, l=l: e.dma_start(out=vecs[:], in_=vecs_in[l]), "ld_v", [], [vecs])
        dma("sp", lambda e, l=l: e.dma_start(out=wsmf[:, 0:1536], in_=wsm_in[l]), "ld_w", [], [wsmf])
        dma("sp", lambda e, l=l: e.dma_start(out=rows3[:].rearrange("p a c -> p (a c)"),
                                             in_=rows_in[l].rearrange("a c -> (a c)").partition_broadcast(128)), "ld_r", [], [rows3])
        dma("sp", lambda e, l=l: e.dma_start(out=gbc[:], in_=ngb_in[l, 0:D].partition_broadcast(128)), "ld_g", [], [gbc])
        op("act", lambda e: e.activation(out=wsmb[:], in_=wsmf[:, 0:1536], func=AF.Copy), [wsmf], [wsmb])
        NWB = 3 * D // TT
        for nb in range(NWB):
            dma("sp", lambda e, l=l, nb=nb: e.dma_start(
                out=big[:].rearrange("p (k c) -> p k c", c=TT),
                in_=wada_in[l, :, nb * TT:(nb + 1) * TT].rearrange("(k p) c -> p k c", p=128)), "ld_a", [], [big])
            dma("sp", lambda e, l=l, nb=nb: e.dma_start(
                out=bada[:, 0:TT], in_=ngb_in[l, D + nb * TT:D + (nb + 1) * TT].partition_broadcast(128)), "ld_b", [], [bada])
            bk = pb()
            for k in range(8):
                op("pe", lambda e, bk=bk, k=k: e.matmul(bk[:, 0:TT], lhsT=cbc[:, k, :], rhs=big[:, k * TT:(k + 1) * TT],
                                                        start=(k == 0), stop=(k == 7)), [cbc, big], [bk])
            op("dve", lambda e, bk=bk, nb=nb: e.tensor_tensor(out=modbc[:, nb * TT:(nb + 1) * TT], in0=bk[:, 0:TT], in1=bada[:, 0:TT],
                                                              op=ALU.add), [bk, bada], [modbc])
        op("dve", lambda e: e.scalar_tensor_tensor(out=modbc[:, D:2 * D], in0=modbc[:, D:2 * D], scalar=1.0, in1=gbc[:],
                                                   op0=ALU.add, op1=ALU.mult), [modbc, gbc], [modbc])
        B_bc = modbc[:, 0:D]
        A_bc = modbc[:, D:2 * D]
        G_bc = modbc[:, 2 * D:3 * D]
        op("pool", lambda e: e.memset(puh[:], 0.0), [], [puh])
        op("pool", lambda e: e.memset(hal[:], 0.0), [], [hal])
        op("pool", lambda e: e.memset(Rst[:], 0.0), [], [Rst])
        op("pool", lambda e: e.memset(Rb[:], 0.0), [], [Rb])
        for g_ in range(2):
            op("pool", lambda e, g_=g_: e.memset(Hst[g_][:], 0.0), [], [Hst[g_]])
            op("pool", lambda e, g_=g_: e.memset(Hb[g_][:], 0.0), [], [Hb[g_]])
        op("dve", lambda e: e.tensor_scalar(out=nvec[:], in0=vecs[:, 17:25], scalar1=-1.0, scalar2=None, op0=ALU.mult), [vecs], [nvec])
        poolw = wsmb[:, 0:512].rearrange("p (g d) -> p g d", d=128)
        lw2 = wsmb[:, 512:1024]
        la2 = wsmb[:, 1024:1536]

        def V(i, n=4):
            return vecs[:, i:i + n]

        for it in range(NT):
            t0 = it * TT
            gi = l * NT + it
            xt = xtl[gi % 2]

            def load_x(l_, it_):
                g_ = l_ * NT + it_
                src = x_in if l_ == 0 else xs
                rd = [] if l_ == 0 else [b_xs[it_]]
                dma("sp", lambda e: e.dma_start(out=xtl[g_ % 2][:], in_=src[it_ * TT:(it_ + 1) * TT, :].rearrange("(j p) d -> p j d", p=128)),
                    "ld_x%d" % (g_ % 2), rd, [xtl[g_ % 2]])
            if gi == 0 or NT == 1:
                load_x(l, it)
            if dbg == "pro":
                S.emit()
                return nc
            dstate["on"] = (l == 0 and it == NT - 1)
            for j in range(NB):
                op("act", lambda e, j=j: e.activation(out=junk[:], in_=xt[:, j, :], func=AF.Square, accum_out=st4[:, j:j + 1]),
                   [xt], [junk, st4])
            op("act", lambda e: e.activation(out=st4[:, 4:4 + NB], in_=st4[:, 0:NB], func=AF.Sqrt, bias=eps_n, scale=1.0 / D),
               [st4, CT], [st4])
            op("dve", lambda e: e.reciprocal(out=st4[:, 8:8 + NB], in_=st4[:, 4:4 + NB]), [st4], [st4])
            for j in range(NB):
                op("dve", lambda e, j=j: e.scalar_tensor_tensor(out=hn[:], in0=xt[:, j, :], scalar=st4[:, 8 + j:9 + j], in1=A_bc,
                                                                op0=ALU.mult, op1=ALU.mult), [xt, st4, modbc], [hn])
                op("dve", lambda e: e.tensor_tensor(out=hb[:], in0=hn[:], in1=B_bc, op=ALU.add), [hn, modbc], [hb])
                bk = pb()
                bkb = bk[:].bitcast(BF16)
                for k in range(8):
                    op("pe", lambda e, bkb=bkb, k=k: e.transpose(out=bkb[:, k * 128:(k + 1) * 128], in_=hb[:, k * 128:(k + 1) * 128],
                                                                  identity=identb[:]), [hb, identb], [bk])
                op("act", lambda e, bkb=bkb, j=j: e.activation(out=hT[:, :, j * 128:(j + 1) * 128],
                                                               in_=bkb.rearrange("p (k t) -> p k t", t=128), func=AF.Copy), [bk], [hT])
            if dbg == "N":
                S.emit()
                return nc
            pu, sa, sb, sg = Ft
            op("dve", lambda e: e.tensor_copy(out=pu[:, :, 0:16], in_=puh[:]), [puh], [pu])
            wt = wget(l, 0)
            fm_proj(wt, 4, lambda bk, c, n: op("act", lambda e: e.activation(out=pu[:, c:c + n, 16:16 + TT], in_=bview(bk, n), func=AF.Copy),
                                               [bk], [pu]))
            wrel(wt)
            wt = wget(l, 1)
            fm_proj(wt, 4, lambda bk, c, n: op("act", lambda e: e.activation(out=sg[:, c:c + n, 0:TT], in_=bview(bk, n), func=AF.Silu),
                                               [bk], [sg]))
            wrel(wt)
            op("pool", lambda e: e.tensor_copy(out=puh[:], in_=pu[:, :, TT:TT + 16]), [pu], [puh])
            for g in range(4):
                cur = pu
                for m in range(g + 1):
                    sh = 1 << m
                    lo = (1 << (m + 1)) - 1
                    dst = sa if (m % 2 == 0) else sb
                    op("dve", lambda e, g=g, cur=cur, dst=dst, sh=sh, lo=lo: e.tensor_tensor(
                        out=dst[:, g, lo:FW], in0=cur[:, g, lo:FW], in1=cur[:, g, lo - sh:FW - sh], op=ALU.add), [cur], [dst])
                    cur = dst
                wg = float(1 << (g + 1))
                dpool = BB[0]
                op("dve", lambda e, g=g, cur=cur, wg=wg: e.scalar_tensor_tensor(
                    out=dpool[:, g, :], in0=cur[:, g, 16:FW], scalar=1.0 / wg, in1=pu[:, g, 16:FW], op0=ALU.mult, op1=ALU.subtract),
                    [cur, pu], [dpool])
                if it == 0:
                    r1 = ring()
                    op("dve", lambda e, g=g, cur=cur, r1=r1: e.tensor_tensor(out=r1[:, 0:16], in0=cur[:, g, 16:32],
                                                                             in1=cst("invc")[:, g * 16:(g + 1) * 16], op=ALU.mult), [cur, CT], [r1])
                    op("dve", lambda e, g=g, r1=r1: e.tensor_tensor(out=dpool[:, g, 0:16], in0=r1[:, 0:16], in1=pu[:, g, 16:32],
                                                                    op=ALU.subtract), [r1, pu], [dpool])
            for g in range(4):
                bk = pb()
                op("pe", lambda e, bk=bk, g=g: e.matmul(bk[:, 0:TT], lhsT=poolw[:, g, :], rhs=BB[0][:, g, :], start=True, stop=True),
                   [wsmb, BB[0]], [bk])
                op("dve", lambda e, bk=bk, g=g: e.scalar_tensor_tensor(out=opool[:, g, :], in0=bk[:, 0:TT], scalar=V(0)[:, g:g + 1],
                                                                        in1=sg[:, g, 0:TT], op0=ALU.mult, op1=ALU.mult), [bk, vecs, sg], [opool])
            if dbg == "P":
                S.emit()
                return nc
            if it + 1 < NT:
                load_x(l, it + 1)
            elif l + 1 < L and NT > 1:
                load_x(l + 1, 0)
            tabs = Ft[0]
            sret = Ft[1]
            qr, qd, kr, ktm, vtm = BB[0], BB[1], BB[2], BB[3], BB[4]
            dma("sp", lambda e, t0=t0: e.dma_start(out=posi[:], in_=pos_in[0, t0:t0 + TT].partition_broadcast(128)), "ld_p", [], [posi])
            ang, nn, r2 = ring(), ring(), ring()
            op("dve", lambda e: e.tensor_copy(out=ang[:, 0:TT], in_=posi[:]), [posi], [ang])
            op("dve", lambda e: e.tensor_scalar(out=ang[:, 0:TT], in0=ang[:, 0:TT], scalar1=cst("freq"), scalar2=None, op0=ALU.mult),
               [ang, CT], [ang])
            op("dve", lambda e: e.tensor_scalar(out=nn[:, 0:TT], in0=ang[:, 0:TT], scalar1=float(1.0 / (2 * np.pi)), scalar2=None,
                                                op0=ALU.mult), [ang], [nn])
            op("dve", lambda e: e.tensor_copy(out=posi[:], in_=nn[:, 0:TT]), [nn], [posi])
            op("dve", lambda e: e.tensor_copy(out=nn[:, 0:TT], in_=posi[:]), [posi], [nn])
            c1 = 6.28125
            c2 = float(2 * np.pi - 6.28125)
            op("dve", lambda e: e.scalar_tensor_tensor(out=r2[:, 0:TT], in0=nn[:, 0:TT], scalar=-c1, in1=ang[:, 0:TT],
                                                       op0=ALU.mult, op1=ALU.add), [nn, ang], [r2])
            op("dve", lambda e: e.scalar_tensor_tensor(out=r2[:, 0:TT], in0=nn[:, 0:TT], scalar=-c2, in1=r2[:, 0:TT],
                                                       op0=ALU.mult, op1=ALU.add), [nn, r2], [r2])
            ws_, wc_, mk_ = ring(), ring(), ring()
            twopi = float(2 * np.pi)
            op("dve", lambda e: e.tensor_scalar(out=mk_[:, 0:TT], in0=r2[:, 0:TT], scalar1=0.0, scalar2=float(np.pi), op0=ALU.add, op1=ALU.is_gt),
               [r2], [mk_])
            op("dve", lambda e: e.scalar_tensor_tensor(out=ws_[:, 0:TT], in0=mk_[:, 0:TT], scalar=-twopi, in1=r2[:, 0:TT], op0=ALU.mult, op1=ALU.add),
               [mk_, r2], [ws_])
            op("dve", lambda e: e.tensor_scalar(out=mk_[:, 0:TT], in0=r2[:, 0:TT], scalar1=float(np.pi / 2), scalar2=float(np.pi), op0=ALU.add,
                                                op1=ALU.is_gt), [r2], [mk_])
            op("dve", lambda e: e.scalar_tensor_tensor(out=wc_[:, 0:TT], in0=mk_[:, 0:TT], scalar=-twopi, in1=r2[:, 0:TT], op0=ALU.mult, op1=ALU.add),
               [mk_, r2], [wc_])
            op("act", lambda e: e.activation(out=tabs[:, 0, 0:TT], in_=wc_[:, 0:TT], func=AF.Sin, bias=cst("eps")[:, 4:5]), [wc_, CT], [tabs])
            op("act", lambda e: e.activation(out=tabs[:, 1, 0:TT], in_=ws_[:, 0:TT], func=AF.Sin, scale=cst("sgn")), [ws_, CT], [tabs])
            op("dve", lambda e: e.tensor_scalar(out=tabs[:, 2:4, 0:TT], in0=tabs[:, 0:2, 0:TT], scalar1=float(128.0 ** -0.5), scalar2=None,
                                                 op0=ALU.mult), [tabs], [tabs])

            def rope(dst, dq, tc):
                wa = wget(l, 2 if dst is qr else 4)
                wb_ = wget(l, 3 if dst is qr else 5)
                for h in range(4):
                    ba, bb = pb(), pb()
                    for (bk_, wt_) in ((ba, wa), (bb, wb_)):
                        for k in range(8):
                            op("pe", lambda e, bk_=bk_, wt_=wt_, k=k, h=h: e.matmul(
                                bk_[:, 0:TT], lhsT=w8(wt_)[:, k, h * 128:(h + 1) * 128], rhs=hT[:, k, :], start=(k == 0), stop=(k == 7)),
                                [wt_, hT], [bk_])
                    t1, t2 = ring(), ring()
                    op("dve", lambda e, ba=ba, t1=t1: e.tensor_tensor(out=t1[:, 0:TT], in0=ba[:, 0:TT], in1=tabs[:, tc, 0:TT], op=ALU.mult),
                       [ba, tabs], [t1])
                    op("dve", lambda e, bb=bb, t2=t2: e.tensor_tensor(out=t2[:, 0:TT], in0=bb[:, 0:TT], in1=tabs[:, tc + 1, 0:TT], op=ALU.mult),
                       [bb, tabs], [t2])
                    if dq is None:
                        op("dve", lambda e, t1=t1, t2=t2, h=h: e.tensor_tensor(out=dst[:, h, :], in0=t1[:, 0:TT], in1=t2[:, 0:TT], op=ALU.add),
                           [t1, t2], [dst])
                    else:
                        op("dve", lambda e, t1=t1, t2=t2: e.tensor_tensor(out=t1[:, 0:TT], in0=t1[:, 0:TT], in1=t2[:, 0:TT], op=ALU.add),
                           [t1, t2], [t1])
                        op("act", lambda e, t1=t1, h=h: e.activation(out=dst[:, h, :], in_=t1[:, 0:TT], func=AF.Copy), [t1], [dst])
                        op("dve", lambda e, t1=t1, h=h: e.tensor_tensor(
                            out=dq[:, h, :].rearrange("p (j i) -> p j i", i=128), in0=t1[:, 0:TT].rearrange("p (j i) -> p j i", i=128),
                            in1=cst("decq")[:, h * 128:(h + 1) * 128].unsqueeze(1).broadcast_to([128, NB, 128]), op=ALU.mult), [t1, CT], [dq])
                wrel(wa, wb_)
            dump("tabs", tabs[:, :, 0:TT], [tabs], 4 * TT) if False else None
            for i_ in range(4):
                dump("tab%d" % i_, tabs[:, i_, 0:TT], [tabs], TT)
            rope(qr, qd, 0)
            rope(kr, None, 2)
            dump("qr", qr[:].rearrange("p c t -> p (c t)"), [qr], 4 * TT)
            dump("qd", qd[:].rearrange("p c t -> p (c t)"), [qd], 4 * TT)
            dump("kr", kr[:].rearrange("p c t -> p (c t)"), [kr], 4 * TT)
            wt = wget(l, 6)
            for j in range(NB):
                bk = pb()
                for k in range(8):
                    op("pe", lambda e, bk=bk, k=k, j=j, wt=wt: e.matmul(bk[:, :], lhsT=hT[:, k, j * 128:(j + 1) * 128], rhs=w8(wt)[:, k, :],
                                                                      start=(k == 0), stop=(k == 7)), [wt, hT], [bk])
                op("act", lambda e, bk=bk, j=j: e.activation(out=vtm[:, :, j * 128:(j + 1) * 128],
                                                             in_=bk[:, :].rearrange("p (h e) -> p h e", e=128), func=AF.Copy), [bk], [vtm])
            wrel(wt)
            wt = wget(l, 7)
            fm_proj(wt, 4, lambda bk, c, n: op("act", lambda e: e.activation(out=sret[:, c:c + n, 0:TT], in_=bview(bk, n), func=AF.Silu),
                                               [bk], [sret]))
            wrel(wt)
            for j in range(NB):
                js = slice(j * 128, (j + 1) * 128)
                bk = pb()
                bkb = bk[:].bitcast(BF16)
                for h in range(4):
                    op("pe", lambda e, bkb=bkb, h=h, js=js: e.transpose(out=bkb[:, h * 128:(h + 1) * 128], in_=kr[:, h, js], identity=identb[:]),
                       [kr, identb], [bk])
                op("dve", lambda e, bkb=bkb, js=js: e.tensor_tensor(
                    out=ktm[:, :, js], in0=bkb[:, 0:512].rearrange("p (h d) -> p h d", d=128),
                    in1=cst("ktail").unsqueeze(2).broadcast_to([128, 4, 128]), op=ALU.mult), [bk, CT], [ktm])
                bs = pb()
                for h in range(4):
                    op("pe", lambda e, bs=bs, h=h, js=js: e.matmul(bs[:, h * 128:(h + 1) * 128], lhsT=kr[:, h, js], rhs=qr[:, h, js],
                                                                   start=True, stop=True), [kr, qr], [bs])
                op("dve", lambda e, bs=bs: e.tensor_tensor(out=Sm[:].rearrange("p h i -> p (h i)"), in0=bs[:, :], in1=cst("maskT"), op=ALU.mult),
                   [bs, CT], [Sm])
                bo = pb()
                for h in range(4):
                    op("pe", lambda e, bo=bo, h=h, js=js: e.matmul(bo[:, h * 128:(h + 1) * 128], lhsT=Sm[:, h, :], rhs=vtm[:, h, js],
                                                                   start=True, stop=False), [Sm, vtm], [bo])
                    op("pe", lambda e, bo=bo, h=h, js=js: e.matmul(bo[:, h * 128:(h + 1) * 128], lhsT=qd[:, h, js], rhs=Rb[:, h, :],
                                                                   start=False, stop=True), [qd, Rb], [bo])
                bkv = pb()
                for h in range(4):
                    op("pe", lambda e, bkv=bkv, h=h, js=js: e.matmul(bkv[:, h * 128:(h + 1) * 128], lhsT=ktm[:, h, js], rhs=vtm[:, h, js],
                                                                     start=True, stop=True), [ktm, vtm], [bkv])
                for h in range(4):
                    op("dve", lambda e, bkv=bkv, h=h: e.scalar_tensor_tensor(out=Rst[:, h, :], in0=Rst[:, h, :], scalar=float(GAM[h] ** 128),
                                                                              in1=bkv[:, h * 128:(h + 1) * 128], op0=ALU.mult, op1=ALU.add),
                       [Rst, bkv], [Rst])
                op("act", lambda e: e.activation(out=Rb[:], in_=Rst[:], func=AF.Copy), [Rst], [Rb])
                osb, sq, on = ring5(), ring5(), ring5()
                op("act", lambda e, bo=bo, osb=osb: e.activation(out=osb[:, 0:512], in_=bo[:, :], func=AF.Copy), [bo], [osb])
                op("act", lambda e, bo=bo, sq=sq: e.activation(out=sq[:, 0:512], in_=bo[:, :], func=AF.Square), [bo], [sq])
                if j == NB - 1:
                    dump("osb", osb[:, 0:512], [osb], 512)
                    dump("vtm", vtm[:].rearrange("p c t -> p (c t)"), [vtm], 4 * TT)
                    dump("ktm", ktm[:].rearrange("p c t -> p (c t)"), [ktm], 4 * TT)
                    dump("Sm", Sm[:].rearrange("p c t -> p (c t)"), [Sm], 512)
                head_norm(op, st4, osb, sq, on, 4, 128, eps_r, CT)
                if j == NB - 1:
                    dump("on", on[:, 0:512], [on], 512)
                op("dve", lambda e, on=on: e.tensor_tensor(out=on[:, 0:512], in0=on[:, 0:512], in1=rows3[:, 0, :], op=ALU.mult), [on, rows3], [on])
                bt_ = pb()
                for h in range(4):
                    op("pe", lambda e, bt_=bt_, h=h, on=on: e.transpose(out=bt_[:, h * 128:(h + 1) * 128], in_=on[:, h * 128:(h + 1) * 128],
                                                                        identity=cst("ident")), [on, CT], [bt_])
                op("dve", lambda e, bt_=bt_, js=js: e.tensor_tensor(out=oret[:, :, js], in0=bt_[:, :].rearrange("p (h t) -> p h t", t=128),
                                                                    in1=sret[:, :, js], op=ALU.mult), [bt_, sret], [oret])
            if dbg == "R":
                S.emit()
                return nc
            bon = Ft[0]
            srw = Ft[1]
            vsf = Ft[2]
            btt, ktt, vbb, Vtm, btm, ktm2 = BB
            stg_i = [0]

            rsF = Ft[3]

            class CV:
                def __init__(self, t, c):
                    self.t, self.c, self.b = t, c, t.b

                def __getitem__(self, idx):
                    return self.t[idx[0], self.c, idx[1]]

            def shifted(wt_, cc, idx, dst_ap, dst_t):
                bk = pb()
                for k in range(8):
                    op("pe", lambda e, bk=bk, k=k: e.matmul(bk[:, 0:TT], lhsT=w8(wt_)[:, k, cc * 128:(cc + 1) * 128], rhs=hT[:, k, :],
                                                            start=(k == 0), stop=(k == 7)), [wt_, hT], [bk])
                stg, dd = wk["stg"], wk["dd"]
                op("act", lambda e: e.activation(out=stg[:, 1:1 + TT], in_=bk[:, 0:TT], func=AF.Copy), [bk], [stg])
                op("dve", lambda e: e.tensor_copy(out=stg[:, 0:1], in_=hal[:, idx:idx + 1]), [hal], [stg])
                op("dve", lambda e: e.tensor_copy(out=hal[:, idx:idx + 1], in_=stg[:, TT:TT + 1]), [stg], [hal])
                op("dve", lambda e: e.tensor_tensor(out=dd[:, 0:TT], in0=stg[:, 0:TT], in1=stg[:, 1:1 + TT], op=ALU.subtract), [stg], [dd])
                mu_i = (4 + idx) if idx < 12 else 16
                op("dve", lambda e: e.scalar_tensor_tensor(out=dst_ap, in0=dd[:, 0:TT], scalar=vecs[:, mu_i:mu_i + 1], in1=stg[:, 1:1 + TT],
                                                           op0=ALU.mult, op1=ALU.add), [dd, vecs, stg], [dst_t])
            wl_ = wget(l, 11)
            lot = wk["sig"]
            shifted(wl_, 0, 12, lot[:, 0:TT], lot)
            wrel(wl_)
            op("act", lambda e: e.activation(out=lor[0:64, :], in_=lot[0:64, 0:TT], func=AF.Tanh), [lot], [lor])
            op("act", lambda e: e.activation(out=lor[64:128, :], in_=lot[64:128, 0:TT], func=AF.Copy), [lot], [lor])
            wr_ = wget(l, 8)
            for c in range(4):
                shifted(wr_, c, c, rsF[:, c, 0:TT], rsF)
            wrel(wr_)
            wk_ = wget(l, 9)
            for c in range(4):
                shifted(wk_, c, 4 + c, ksF[:, c, 0:TT], ksF)
            wrel(wk_)
            wv_ = wget(l, 10)
            for c in range(4):
                shifted(wv_, c, 8 + c, vsf[:, c, 0:TT], vsf)
                op("pool", lambda e, c=c: e.tensor_copy(out=vbb[:, c, :], in_=vsf[:, c, 0:TT]), [vsf], [vbb])
            wrel(wv_)
            fence_r = [AX1, AX2] + [BR[i_][g_] for i_ in range(2) for g_ in range(2)]
            op("pool", lambda e: e.memset(fdummy[:], 0.0), fence_r, list(wk2.values()))
            clists = []
            for c in range(4):
                cur_ = []

                def q(eng, fn, r, w, cur_=cur_):
                    cur_.append((eng, fn, r, w))
                clists.append(cur_)
                WK = wk if c % 2 == 0 else wk2
                rs_, ks_ = CV(rsF, c), CV(ksF, c)
                bw, ba_ = pb(), pb()
                q("pe", lambda e, bw=bw, c=c: e.matmul(bw[:, 0:TT], lhsT=lw2[:, c * 128:(c + 1) * 128], rhs=lor[:, :], start=True, stop=True),
                   [wsmb, lor], [bw])
                q("pe", lambda e, ba_=ba_, c=c: e.matmul(ba_[:, 0:TT], lhsT=la2[:, c * 128:(c + 1) * 128], rhs=lor[:, :],
                                                          start=True, stop=True), [wsmb, lor], [ba_])
                sig, ag = WK["sig"], WK["ag"]
                q("act", lambda e, bw=bw, sig=sig, c=c: e.activation(out=sig[:, 0:TT], in_=bw[:, 0:TT], func=AF.Sigmoid, bias=V(17)[:, c:c + 1]),
                   [bw, vecs], [sig])
                q("act", lambda e, ba_=ba_, ag=ag, c=c: e.activation(out=ag[:, 0:TT], in_=ba_[:, 0:TT], func=AF.Sigmoid, bias=V(21)[:, c:c + 1]),
                   [ba_, vecs], [ag])
                cs, csx = WK["cs"], WK["csx"]
                q("dve", lambda e, sig=sig, cs=cs: e.tensor_tensor_scan(out=cs[:, 0:TT], data0=cst("scanm")[:, 0:TT], data1=sig[:, 0:TT],
                                                                         initial=0.0, op0=ALU.mult, op1=ALU.add), [sig, CT], [cs])
                q("dve", lambda e, sig=sig, cs=cs, csx=csx: e.tensor_tensor(out=csx[:, 0:TT], in0=cs[:, 0:TT], in1=sig[:, 0:TT], op=ALU.subtract),
                   [cs, sig], [csx])
                Pin, Pex, Q = WK["Pin"], WK["Pex"], WK["Q"]
                q("act", lambda e, cs=cs, Pin=Pin: e.activation(out=Pin[:, 0:TT], in_=cs[:, 0:TT], func=AF.Exp, scale=-C0), [cs], [Pin])
                q("act", lambda e, csx=csx, Pex=Pex: e.activation(out=Pex[:, 0:TT], in_=csx[:, 0:TT], func=AF.Exp, scale=-C0), [csx], [Pex])
                q("act", lambda e, cs=cs, Q=Q: e.activation(out=Q[:, 0:TT], in_=cs[:, 0:TT], func=AF.Exp, scale=C0), [cs], [Q])
                q("pool", lambda e, Pin=Pin, c=c: e.tensor_copy(out=WCt[:, c, :], in_=Pin[:, 63:TT:64]), [Pin], [WCt])
                kk, sqb, rn = WK["kk"], WK["sqb"], WK["rn"]
                q("dve", lambda e, ks_=ks_, kk=kk, c=c: e.tensor_scalar(out=kk[:, 0:TT], in0=ks_[:, 0:TT], scalar1=V(25)[:, c:c + 1], scalar2=None,
                                                                         op0=ALU.mult), [ks_, vecs], [kk])
                sqv = sqb[:, 0:TT // 2].bitcast(BF16)
                q("dve", lambda e, kk=kk, sqv=sqv: e.tensor_tensor(out=sqv, in0=kk[:, 0:TT], in1=kk[:, 0:TT], op=ALU.mult), [kk], [sqb])
                bn_ = pb()
                q("pe", lambda e, bn_=bn_, sqv=sqv: e.matmul(bn_[:, 0:TT], lhsT=onesb[:], rhs=sqv, start=True, stop=True), [onesb, sqb], [bn_])
                q("act", lambda e, bn_=bn_, rn=rn: e.activation(out=rn[:, 0:TT], in_=bn_[:, 0:TT], func=AF.Ln, bias=eps_k), [bn_, CT], [rn])
                q("act", lambda e, rn=rn: e.activation(out=rn[:, 0:TT], in_=rn[:, 0:TT], func=AF.Exp, scale=-0.5), [rn], [rn])
                q("dve", lambda e, kk=kk, rn=rn: e.tensor_tensor(out=kk[:, 0:TT], in0=kk[:, 0:TT], in1=rn[:, 0:TT], op=ALU.mult), [kk, rn], [kk])
                kp = WK["kp"]
                q("dve", lambda e, ag=ag, kp=kp, c=c: e.tensor_scalar(out=kp[:, 0:TT], in0=ag[:, 0:TT], scalar1=-1.0, scalar2=V(29)[:, c:c + 1],
                                                                        op0=ALU.add, op1=ALU.mult), [ag, vecs], [kp])
                q("dve", lambda e, kp=kp, ks_=ks_: e.scalar_tensor_tensor(out=kp[:, 0:TT], in0=kp[:, 0:TT], scalar=1.0, in1=ks_[:, 0:TT],
                                                                            op0=ALU.add, op1=ALU.mult), [kp, ks_], [kp])
                prb = WK["prb"]
                prv = prb[:, 0:TT // 2].bitcast(BF16)
                q("dve", lambda e, rs_=rs_, kp=kp, prv=prv, c=c: e.scalar_tensor_tensor(out=prv, in0=rs_[:, 0:TT], scalar=V(33)[:, c:c + 1],
                                                                                        in1=kp[:, 0:TT], op0=ALU.mult, op1=ALU.mult),
                   [rs_, vecs, kp], [prb])
                bb_ = pb()
                q("pe", lambda e, bb_=bb_, prv=prv: e.matmul(bb_[:, 0:TT], lhsT=onesb[:], rhs=prv, start=True, stop=True), [onesb, prb], [bb_])
                q("dve", lambda e, bb_=bb_, c=c: e.tensor_tensor(out=bon[:, c, 0:TT], in0=bb_[:, 0:TT], in1=vsf[:, c, 0:TT], op=ALU.mult),
                   [bb_, vsf], [bon])
                for hp in range(2):
                    rw_ = slice(hp * 64, hp * 64 + 64)
                    q("dve", lambda e, kk=kk, Pex=Pex, hp=hp, rw_=rw_, c=c: e.scalar_tensor_tensor(
                        out=artm[hp][rw_, c, :, 0, :], in0=kk[rw_, 0:TT].rearrange("p (j i) -> p j i", i=128), scalar=-1.0,
                        in1=Pex[rw_, 0:TT].rearrange("p (j i) -> p j i", i=128), op0=ALU.mult, op1=ALU.mult), [kk, Pex], [artm[hp]])
                    q("pool", lambda e, rs_=rs_, Pin=Pin, hp=hp, rw_=rw_, c=c: e.tensor_tensor(
                        out=artm[hp][rw_, c, :, 1, :], in0=rs_[rw_, 0:TT].rearrange("p (j i) -> p j i", i=128),
                        in1=Pin[rw_, 0:TT].rearrange("p (j i) -> p j i", i=128), op=ALU.mult), [rs_, Pin], [artm[hp]])
                tb = WK["tb"]
                q("dve", lambda e, kk=kk, ag=ag, tb=tb: e.tensor_tensor(out=tb[:, 0:TT], in0=kk[:, 0:TT], in1=ag[:, 0:TT], op=ALU.mult),
                   [kk, ag], [tb])
                q("dve", lambda e, tb=tb, Q=Q, c=c: e.tensor_tensor(out=btt[:, c, :], in0=tb[:, 0:TT], in1=Q[:, 0:TT], op=ALU.mult),
                   [tb, Q], [btt])
                q("pool", lambda e, kp=kp, Q=Q, c=c: e.tensor_tensor(out=ktt[:, c, :], in0=kp[:, 0:TT], in1=Q[:, 0:TT], op=ALU.mult),
                   [kp, Q], [ktt])
            sa_, sb_ = clists[0] + clists[2], clists[1] + clists[3]
            k_ = 0
            merged = list(sa_[:k_])
            ia_, ib_ = k_, 0
            while ia_ < len(sa_) or ib_ < len(sb_):
                if ib_ < len(sb_):
                    merged.append(sb_[ib_])
                    ib_ += 1
                if ia_ < len(sa_):
                    merged.append(sa_[ia_])
                    ia_ += 1
            for (eng_, fn_, r_, w_) in merged:
                op(eng_, fn_, r_, w_)
            op("pool", lambda e: e.memset(fdummy[:], 0.0), list(wk2.values()), fence_r)
            wg_ = wget(l, 12)
            fm_proj(wg_, 4, lambda bk, c, n: op("act", lambda e: e.activation(out=srw[:, c:c + n, 0:TT], in_=bview(bk, n), func=AF.Silu),
                                                [bk], [srw]))
            wrel(wg_)
            def part1(j):
                js = slice(j * 128, (j + 1) * 128)
                for (srcT, dstT) in ((vbb, Vtm), (btt, btm), (ktt, ktm2)):
                    bk = pb()
                    bkb = bk[:].bitcast(BF16)
                    for c in range(4):
                        op("pe", lambda e, bkb=bkb, c=c, srcT=srcT, js=js: e.transpose(out=bkb[:, c * 128:(c + 1) * 128], in_=srcT[:, c, js],
                                                                                      identity=identb[:]), [srcT, identb], [bk])
                    op("act", lambda e, bkb=bkb, dstT=dstT, js=js: e.activation(out=dstT[:, :, js], in_=bkb[:, 0:512].rearrange("p (c k) -> p c k", k=128),
                                                                                func=AF.Copy), [bk], [dstT])
                    if dstT is Vtm:
                        for cp_ in range(2):
                            rw_ = slice(cp_ * 64, cp_ * 64 + 64)
                            op("act", lambda e, bkb=bkb, js=js, cp_=cp_, rw_=rw_: e.activation(
                                out=Vm[cp_][rw_, :, js], in_=bkb[rw_, 0:512].rearrange("p (c k) -> p c k", k=128), func=AF.Copy), [bk], [Vm[cp_]])
                for c in range(4):
                    b1, b2, b3 = pb(), pb(), pb()
                    for hp in range(2):
                        arv = artm[hp][:, c, j, :, :].rearrange("p a i -> p (a i)")
                        op("pe", lambda e, b1=b1, hp=hp, c=c, arv=arv, js=js: e.matmul(b1[:, hp * 256:(hp + 1) * 256], lhsT=btt[:, c, js],
                                                                                       rhs=arv, start=True, stop=True), [btt, artm[hp]], [b1])
                        op("pe", lambda e, b2=b2, hp=hp, c=c, arv=arv, js=js: e.matmul(b2[:, hp * 256:(hp + 1) * 256], lhsT=ktt[:, c, js],
                                                                                       rhs=arv, start=True, stop=True), [ktt, artm[hp]], [b2])
                        op("pe", lambda e, b3=b3, hp=hp, c=c, js=js: e.matmul(b3[:, hp * 128:(hp + 1) * 128], lhsT=artm[hp][:, c, j, 0, :],
                                                                              rhs=btt[:, c, js], start=True, stop=True), [btt, artm[hp]], [b3])
                    op("dve", lambda e, b1=b1, c=c: e.tensor_tensor(out=AX1[:, 2 * c:2 * c + 2, :].rearrange("p h x -> p (h x)"), in0=b1[:, :],
                                                                    in1=cst("MU"), op=ALU.mult), [b1, CT], [AX1])
                    op("dve", lambda e, b2=b2, c=c: e.tensor_tensor(out=AX2[:, 2 * c:2 * c + 2, :].rearrange("p h x -> p (h x)"), in0=b2[:, :],
                                                                    in1=cst("MU"), op=ALU.mult), [b2, CT], [AX2])
                    am_ = Am[0][c // 2]
                    op("dve", lambda e, b3=b3, c=c, am_=am_: e.tensor_tensor(out=am_[:, (c % 2) * 2:(c % 2) * 2 + 2, :].rearrange("p h x -> p (h x)"),
                                                                             in0=b3[:, 0:256], in1=cst("ML"), op=ALU.mult), [b3, CT], [am_])
            def part2(j):
                js = slice(j * 128, (j + 1) * 128)
                for g in range(2):
                    hsl = slice(4 * g, 4 * g + 4)
                    op("act", lambda e, g=g, hsl=hsl: e.activation(out=BR[0][g][:, :, 0:128], in_=AX1[:, hsl, 0:128], func=AF.Copy), [AX1], [BR[0][g]])
                    op("dve", lambda e, g=g, hsl=hsl: e.tensor_tensor(out=BR[0][g][:, :, 128:256], in0=AX1[:, hsl, 0:128],
                                                                       in1=identb[:].unsqueeze(1).broadcast_to([128, 4, 128]), op=ALU.add),
                       [AX1, identb], [BR[0][g]])
                for m in range(6):
                    cur, nxt = m % 2, (m + 1) % 2
                    last = (m == 5)
                    for g in range(2):
                        bqg = [banks[2 * g], banks[2 * g + 1]]
                        bag = banks[4 + g]
                        for hh in range(4):
                            bq = bqg[hh // 2]
                            hp = hh % 2
                            if m == 0:
                                op("pe", lambda e, bq=bq, hp=hp, hh=hh, g=g: e.matmul(bq[:, hp * 256:hp * 256 + 128], lhsT=Am[0][g][:, hh, :],
                                                                                      rhs=BR[0][g][:, hh, 0:128], start=True, stop=True),
                                   [Am[0][g], BR[0][g]], [bq])
                            elif not last:
                                op("pe", lambda e, bq=bq, hp=hp, hh=hh, g=g, cur=cur: e.matmul(bq[:, hp * 256:(hp + 1) * 256], lhsT=Am[cur][g][:, hh, :],
                                                                                               rhs=BR[cur][g][:, hh, :], start=True, stop=True),
                                   [Am[cur][g], BR[cur][g]], [bq])
                            else:
                                op("pe", lambda e, bq=bq, hp=hp, hh=hh, g=g, cur=cur: e.matmul(bq[:, hp * 256 + 128:(hp + 1) * 256], lhsT=Am[cur][g][:, hh, :],
                                                                                               rhs=BR[cur][g][:, hh, 128:256], start=True, stop=True),
                                   [Am[cur][g], BR[cur][g]], [bq])
                            if not last:
                                op("pe", lambda e, bag=bag, hh=hh, g=g, cur=cur: e.matmul(bag[:, hh * 128:(hh + 1) * 128], lhsT=BR[cur][g][:, hh, 0:128],
                                                                                          rhs=Am[cur][g][:, hh, :], start=True, stop=True),
                                   [Am[cur][g], BR[cur][g]], [bag])
                        bqv = mbank(2 * g, 2).rearrange("p (h x) -> p h x", x=256)
                        if not last:
                            op("act", lambda e, bqv=bqv, nxt=nxt, g=g: e.activation(out=BR[nxt][g][:, :, 0:128], in_=bqv[:, :, 0:128], func=AF.Copy),
                               bqg, [BR[nxt][g]])
                            op("act", lambda e, nxt=nxt, g=g, bag=bag: e.activation(out=Am[nxt][g][:], in_=bag[:, :].rearrange("p (h x) -> p h x", x=128),
                                                                                    func=AF.Copy), [bag], [Am[nxt][g]])
                        if m == 0:
                            op("dve", lambda e, nxt=nxt, g=g: e.tensor_copy(out=BR[nxt][g][:, :, 128:256], in_=BR[0][g][:, :, 128:256]),
                               [BR[0][g]], [BR[nxt][g]])
                        else:
                            op("dve", lambda e, bqv=bqv, nxt=nxt, cur=cur, g=g: e.tensor_tensor(out=BR[nxt][g][:, :, 128:256], in0=bqv[:, :, 128:256],
                                                                                                in1=BR[cur][g][:, :, 128:256], op=ALU.add),
                               bqg + [BR[cur][g]], [BR[nxt][g]])
                TTm = BR[0]
                by = banks[6]
                for cp in range(2):
                    co = cp * 64
                    cc = 2 * j + cp
                    bgs, bus, bhs = [banks[0], banks[1]], [banks[2], banks[3]], [banks[4], banks[5]]

                    def hinfo(g, hh):
                        h = 4 * g + hh
                        c, hp, po = h // 2, h % 2, (h % 2) * 64
                        chs = slice(j * 128 + po, j * 128 + po + 64)
                        return h, c, hp, po, chs, slice(hh * 64, (hh + 1) * 64), slice(h * 64, (h + 1) * 64)
                    for g in range(2):
                        for hh in range(4):
                            h, c, hp, po, chs, ls, hs = hinfo(g, hh)
                            op("pe", lambda e, g=g, c=c, hp=hp, ls=ls: e.matmul(
                                bgs[g][co:co + 64, ls], lhsT=artm[hp][:, c, j, 0, co:co + 64], rhs=Hb[g][:, c % 2, :], start=True, stop=False),
                                [artm[hp], Hb[g]], [bgs[g]])
                            op("pe", lambda e, g=g, h=h, c=c, ls=ls, chs=chs: e.matmul(
                                bgs[g][co:co + 64, ls], lhsT=AX2[:, h, co:co + 64], rhs=Vtm[:, c, chs], start=False, stop=True),
                                [AX2, Vtm], [bgs[g]])
                    for g in range(2):
                        op("act", lambda e, g=g: e.activation(out=Gs[g][co:co + 64, :], in_=bgs[g][co:co + 64, 0:256], func=AF.Copy), [bgs[g]], [Gs[g]])
                    for g in range(2):
                        for hh in range(4):
                            h, c, hp, po, chs, ls, hs = hinfo(g, hh)
                            op("pe", lambda e, g=g, hh=hh, ls=ls: e.matmul(bus[g][co:co + 64, ls], lhsT=TTm[g][:, hh, 128 + co:128 + co + 64],
                                                                           rhs=Gs[g][:, ls], start=True, stop=True), [TTm[g], Gs[g]], [bus[g]])
                    for g in range(2):
                        op("act", lambda e, g=g: e.activation(out=Us[cp][g][co:co + 64, :], in_=bus[g][co:co + 64, 0:256], func=AF.Copy),
                           [bus[g]], [Us[cp][g]])
                    for g in range(2):
                        for hh in range(4):
                            h, c, hp, po, chs, ls, hs = hinfo(g, hh)
                            op("pe", lambda e, g=g, c=c, hp=hp, hs=hs: e.matmul(by[co:co + 64, hs], lhsT=artm[hp][:, c, j, 1, co:co + 64],
                                                                                rhs=Hb[g][:, c % 2, :], start=True, stop=False), [artm[hp], Hb[g]], [by])
                            op("pe", lambda e, g=g, h=h, hs=hs, ls=ls: e.matmul(by[co:co + 64, hs], lhsT=AX1[:, h, 128 + co:128 + co + 64],
                                                                                rhs=Us[cp][g][:, ls], start=False, stop=False), [AX1, Us[cp][g]], [by])
                            op("pe", lambda e, h=h, c=c, hs=hs, chs=chs: e.matmul(by[co:co + 64, hs], lhsT=AX2[:, h, 128 + co:128 + co + 64],
                                                                                  rhs=Vtm[:, c, chs], start=False, stop=True), [AX2, Vtm], [by])
                            op("pe", lambda e, g=g, c=c, po=po, ls=ls, chs=chs: e.matmul(bhs[g][po:po + 64, (c % 2) * 64:(c % 2 + 1) * 64], lhsT=btm[:, c, chs],
                                                                                         rhs=Us[cp][g][:, ls], start=True, stop=False), [btm, Us[cp][g]], [bhs[g]])
                            op("pe", lambda e, g=g, c=c, po=po, chs=chs: e.matmul(bhs[g][po:po + 64, (c % 2) * 64:(c % 2 + 1) * 64], lhsT=ktm2[:, c, chs],
                                                                                  rhs=Vm[cp][:, c, chs], start=False, stop=True), [ktm2, Vm[cp]], [bhs[g]])
                        Htmp = ring5()
                        op("dve", lambda e, g=g, Htmp=Htmp: e.tensor_tensor(out=Htmp[:, 0:128], in0=bhs[g][:, 0:128],
                                                                            in1=Hst[g][:].rearrange("p c v -> p (c v)"), op=ALU.add), [bhs[g], Hst[g]], [Htmp])
                        op("dve", lambda e, g=g, Htmp=Htmp: e.tensor_tensor(out=Hst[g][:], in0=Htmp[:, 0:128].rearrange("p (c v) -> p c v", v=64),
                                                                            in1=WCt[:, 2 * g:2 * g + 2, cc:cc + 1].broadcast_to([128, 2, 64]), op=ALU.mult),
                           [Htmp, WCt], [Hst[g]])
                        op("act", lambda e, g=g: e.activation(out=Hb[g][:], in_=Hst[g][:], func=AF.Copy), [Hst[g]], [Hb[g]])
            def part3(j):
                js = slice(j * 128, (j + 1) * 128)
                by = banks[6]
                ysb, ysq, yn = ring5(), ring5(), ring5()
                op("act", lambda e, ysb=ysb: e.activation(out=ysb[:, 0:512], in_=by[:, :], func=AF.Copy), [by], [ysb])
                op("act", lambda e, ysq=ysq: e.activation(out=ysq[:, 0:512], in_=by[:, :], func=AF.Square), [by], [ysq])
                head_norm(op, st4, ysb, ysq, yn, 8, 64, eps_w, CT)
                op("dve", lambda e, yn=yn: e.tensor_tensor(out=yn[:, 0:512], in0=yn[:, 0:512], in1=rows3[:, 1, :], op=ALU.mult), [yn, rows3], [yn])
                op("dve", lambda e, yn=yn: e.tensor_tensor(out=yn[:, 0:512], in0=yn[:, 0:512], in1=rows3[:, 2, :], op=ALU.add), [yn, rows3], [yn])
                bt_ = pb()
                for c in range(4):
                    op("pe", lambda e, bt_=bt_, c=c, yn=yn: e.transpose(out=bt_[:, c * 128:(c + 1) * 128], in_=yn[:, c * 128:(c + 1) * 128],
                                                                        identity=cst("ident")), [yn, CT], [bt_])
                yf = ring5()
                op("dve", lambda e, bt_=bt_, yf=yf, js=js: e.tensor_tensor(out=yf[:, 0:512].rearrange("p (c t) -> p c t", t=128),
                                                                           in0=bt_[:, :].rearrange("p (c t) -> p c t", t=128), in1=bon[:, :, js], op=ALU.add),
                   [bt_, bon], [yf])
                op("dve", lambda e, yf=yf, js=js: e.tensor_tensor(out=orw[:, :, js], in0=yf[:, 0:512].rearrange("p (c t) -> p c t", t=128),
                                                                   in1=srw[:, :, js], op=ALU.mult), [yf, srw], [orw])
            for j in range(NB):
                part1(j)
                part2(j)
                part3(j)
            if dbg == "W":
                S.emit()
                return nc
            if dbg == "dump" and l == 0 and it == NT - 1:
                for i_, t_ in enumerate((opool, oret, orw)):
                    dma("sp", lambda e, i_=i_, t_=t_: e.dma_start(out=dbg_o[:, i_ * 4 * TT:(i_ + 1) * 4 * TT], in_=t_[:].rearrange("p c t -> p (c t)")),
                        "dbg", [t_], [b_dbg])
                dma("sp", lambda e: e.dma_start(out=dbg_o[:, 12 * TT:20 * TT], in_=hT[:].rearrange("p c t -> p (c t)")), "dbg", [hT], [b_dbg])
            macc = big
            mT0, mT1 = BB[0], BB[1]
            obr = [opool, oret, orw]
            gts = [Ft[2], Ft[3]]
            for n in range(3):
                for hf in range(2):
                    wgl = wget(l, 13 + 3 * n + 1 + hf)
                    c = 0
                    while c < 4:
                        nn_ = min(CPB, 4 - c)
                        bgl = pb()
                        for i in range(nn_):
                            for k in range(8):
                                op("pe", lambda e, bgl=bgl, k=k, wgl=wgl, i=i, cc=c + i: e.matmul(
                                    bgl[:, i * TT:(i + 1) * TT], lhsT=w8(wgl)[:, k, cc * 128:(cc + 1) * 128], rhs=hT[:, k, :],
                                    start=(k == 0), stop=(k == 7)), [wgl, hT], [bgl])
                        op("act", lambda e, bgl=bgl, c=c, nn_=nn_, hf=hf: e.activation(out=gts[hf][:, c:c + nn_, 0:TT], in_=bview(bgl, nn_), func=AF.Sigmoid),
                           [bgl], [gts[hf]])
                        c += nn_
                    wrel(wgl)
                wbn = wget(l, 13 + 3 * n)
                wbv = wbn[:].rearrange("p (k d) -> p k d", d=1024)
                for dp in range(0, 8, CPB):
                    byn = pb()
                    for i in range(CPB):
                        dc = dp + i
                        for k4 in range(4):
                            op("pe", lambda e, byn=byn, k4=k4, wbv=wbv, dc=dc, i=i, n=n, wbn=wbn: e.matmul(
                                byn[:, i * TT:(i + 1) * TT], lhsT=wbv[:, k4, dc * 128:(dc + 1) * 128], rhs=obr[n][:, k4, :],
                                start=(k4 == 0), stop=(k4 == 3)), [wbn, obr[n]], [byn])
                    hf, dq0 = dp // 4, dp % 4
                    gv = gts[hf][:, dq0:dq0 + CPB, 0:TT]
                    ms = macc[:, dp * TT:(dp + CPB) * TT].rearrange("p (i t) -> p i t", t=TT)
                    yv = bview(byn, CPB)
                    if n == 0:
                        op("dve", lambda e, yv=yv, gv=gv, ms=ms: e.tensor_tensor(out=ms, in0=yv, in1=gv, op=ALU.mult), [byn, gts[hf]], [macc])
                    else:
                        tmp = ring5()
                        tv = tmp[:, 0:CPB * TT].rearrange("p (i t) -> p i t", t=TT)
                        op("dve", lambda e, yv=yv, gv=gv, tv=tv: e.tensor_tensor(out=tv, in0=yv, in1=gv, op=ALU.mult), [byn, gts[hf]], [tmp])
                        if n == 1:
                            op("pool", lambda e, tv=tv, ms=ms: e.tensor_tensor(out=ms, in0=ms, in1=tv, op=ALU.add), [tmp, macc], [macc])
                        else:
                            mt_ = (mT0 if dp < 4 else mT1)
                            op("pool", lambda e, tv=tv, ms=ms, mt_=mt_, dq0=dq0: e.tensor_tensor(out=mt_[:, dq0:dq0 + CPB, :], in0=ms, in1=tv, op=ALU.add),
                               [tmp, macc], [mt_])
                wrel(wbn)
            for hf in range(2):
                wo_ = wget(l, 22 + hf)
                for j in range(NB):
                    bo = pb()
                    for dc in range(8):
                        mt_ = (mT0 if dc < 4 else mT1)
                        op("pe", lambda e, bo=bo, dc=dc, mt_=mt_, j=j, wo_=wo_: e.matmul(bo[:, :], lhsT=mt_[:, dc % 4, j * 128:(j + 1) * 128], rhs=w8(wo_)[:, dc, :],
                                                                                         start=(dc == 0), stop=(dc == 7)), [mt_, wo_], [bo])
                    tmp = ring5()
                    op("dve", lambda e, bo=bo, tmp=tmp, hf=hf: e.tensor_tensor(out=tmp[:, 0:512], in0=bo[:, :], in1=G_bc[:, hf * 512:(hf + 1) * 512], op=ALU.mult),
                       [bo, modbc], [tmp])
                    op("pool", lambda e, tmp=tmp, hf=hf, j=j: e.tensor_tensor(out=xt[:, j, hf * 512:(hf + 1) * 512], in0=xt[:, j, hf * 512:(hf + 1) * 512],
                                                                              in1=tmp[:, 0:512], op=ALU.add), [tmp, xt], [xt])
                wrel(wo_)
            if l < L - 1:
                dma("sp", lambda e, t0=t0: e.dma_start(out=xs[t0:t0 + TT, :].rearrange("(j p) d -> p j d", p=128), in_=xt[:]), "st_x", [xt], [b_xs[it]])
            else:
                if it == 0:
                    dma("sp", lambda e: e.dma_start(out=gbc[:], in_=fg_in[0, :].partition_broadcast(128)), "ld_g", [], [gbc])
                for j in range(NB):
                    op("act", lambda e, j=j: e.activation(out=junk[:], in_=xt[:, j, :], func=AF.Square, accum_out=st4[:, j:j + 1]), [xt], [junk, st4])
                op("act", lambda e: e.activation(out=st4[:, 4:4 + NB], in_=st4[:, 0:NB], func=AF.Sqrt, bias=eps_n, scale=1.0 / D), [st4, CT], [st4])
                op("dve", lambda e: e.reciprocal(out=st4[:, 8:8 + NB], in_=st4[:, 4:4 + NB]), [st4], [st4])
                for j in range(NB):
                    op("dve", lambda e, j=j: e.scalar_tensor_tensor(out=xt[:, j, :], in0=xt[:, j, :], scalar=st4[:, 8 + j:9 + j], in1=gbc[:],
                                                                    op0=ALU.mult, op1=ALU.mult), [xt, st4, gbc], [xt])
                dma("sp", lambda e, t0=t0: e.dma_start(out=out[t0:t0 + TT, :].rearrange("(j p) d -> p j d", p=128), in_=xt[:]), "st_o", [xt], [b_out[it]])
    S.wait_all("sp", b_out + [b_dbg])
    S.emit()
    return nc


def head_norm(op, st4, xs_, sq, on, nh, hd, eps_ap, CT):
    s1 = st4[:, 0:nh]
    xv = xs_[:, 0:nh * hd].rearrange("p (h d) -> p h d", d=hd)
    qv = sq[:, 0:nh * hd].rearrange("p (h d) -> p h d", d=hd)
    ov = on[:, 0:nh * hd].rearrange("p (h d) -> p h d", d=hd)
    m = st4[:, 0:nh]
    v = st4[:, 8:8 + nh]
    op("dve", lambda e: e.tensor_reduce(out=m, in_=xv, axis=AX.X, op=ALU.add), [xs_], [st4])
    op("dve", lambda e: e.tensor_reduce(out=v, in_=qv, axis=AX.X, op=ALU.add), [sq], [st4])
    op("dve", lambda e: e.tensor_scalar(out=m, in0=m, scalar1=1.0 / hd, scalar2=None, op0=ALU.mult), [st4], [st4])
    msq = sq[:, 0:nh]
    op("dve", lambda e: e.tensor_tensor(out=msq, in0=m, in1=m, op=ALU.mult), [st4], [sq])
    op("dve", lambda e: e.scalar_tensor_tensor(out=v, in0=v, scalar=1.0 / hd, in1=msq, op0=ALU.mult, op1=ALU.subtract), [st4, sq], [st4])
    op("act", lambda e: e.activation(out=v, in_=v, func=AF.Sqrt, bias=eps_ap), [st4, CT], [st4])
    op("dve", lambda e: e.reciprocal(out=v, in_=v), [st4], [st4])
    op("dve", lambda e: e.tensor_tensor(out=ov, in0=xv, in1=m.unsqueeze(2).broadcast_to([128, nh, hd]), op=ALU.subtract), [xs_, st4], [on])
    op("dve", lambda e: e.tensor_tensor(out=ov, in0=ov, in1=v.unsqueeze(2).broadcast_to([128, nh, hd]), op=ALU.mult), [on, st4], [on])


_CACHE = {}


def run(inputs, T, L, NB=2, n_cores=8, dbg=None):
    B = inputs["x"].shape[0]
    wd = prep_weights(inputs)
    key = (T, L, NB, dbg)
    if key not in _CACHE:
        _CACHE[key] = build(T, L, NB, dbg)
    nc = _CACHE[key]
    in_maps = []
    x = np.asarray(inputs["x"], np.float32)
    c = np.asarray(inputs["c"], np.float32)
    pos = np.asarray(inputs["positions"], np.int32)
    for core in range(n_cores):
        b = core % B
        m = dict(wd)
        m["x"] = np.ascontiguousarray(x[b])
        m["c"] = np.ascontiguousarray(c[b].reshape(8, 128).T)
        m["pos"] = np.ascontiguousarray(pos[b].reshape(1, T))
        in_maps.append(m)
    import os
    res = run_bass_kernel_spmd(nc, in_maps, core_ids=list(range(n_cores)), **({'trace': True} if os.environ.get('KTRACE') else {}))
    if os.environ.get('KTRACE'):
        print('EXEC_TIME_NS', res.exec_time_ns)
    if dbg == "dump":
        global LAST_DBG
        LAST_DBG = np.asarray(res.results[0]["dbg_o"]).astype(np.float32)
        global LAST_DBGF
        LAST_DBGF = np.asarray(res.results[0]["dbg_f"]).astype(np.float32)
    return np.stack([np.asarray(res.results[b]["out"], np.float32) for b in range(B)], 0)


def kernel(**inputs):
    T = inputs["x"].shape[1]
    L = inputs["w_in"].shape[0]
    return run(inputs, T, L)
```
